# Optimizing a Trainium2 kernel written in Bass

```python
import jax, jax.numpy as jnp
from jax import lax
import numpy as np

D_MODEL = 2048
BATCH = 4
SEQ = 2048
DEPTH = 4

GRID_W = 64
CTX_LEN = 256
EPS = 1e-6
MLA_HEADS = D_MODEL // 256
QK_NOPE = 128
QK_ROPE = 64
V_DIM = 128
Q_LORA = D_MODEL // 4
KV_LORA = D_MODEL // 8
Q_BLOCK = 128
ROPE_BASE = 10000.0
AXIS_PAIRS = QK_ROPE // 4
ATTN_SCALE = (QK_NOPE + QK_ROPE) ** -0.5
HG_HEADS = D_MODEL // 256
HG_DK = 128
HG_DV = 128
HG_KW = HG_HEADS * HG_DK
HG_VW = HG_HEADS * HG_DV
CHUNK = 64
CONV_WIDTH = 31
D_FF = (11 * D_MODEL) // 4
FFN_CONV_WIDTH = 3
MOD_STD = 0.5
N_EVEN = (DEPTH + 1) // 2
N_ODD = DEPTH // 2
IN_SIZES = (Q_LORA, KV_LORA, QK_ROPE, HG_KW, HG_KW, HG_KW, HG_VW, HG_VW)
MIX_WIDTH = MLA_HEADS * V_DIM + HG_VW

kernel_name = "hybrid_mla_hgrn2_conformer_convffn_dit"


def rmsnorm(x, g):
    x32 = x.astype(jnp.float32)
    y = x32 * lax.rsqrt(jnp.mean(x32 * x32, axis=-1, keepdims=True) + EPS)
    return (y * g.astype(jnp.float32)).astype(x.dtype)


def layernorm(x, g, b):
    x32 = x.astype(jnp.float32)
    mu = jnp.mean(x32, axis=-1, keepdims=True)
    xc = x32 - mu
    var = jnp.mean(xc * xc, axis=-1, keepdims=True)
    return (xc * lax.rsqrt(var + EPS) * g.astype(jnp.float32) + b.astype(jnp.float32)).astype(x.dtype)


def dwconv(x, w):
    k = w.shape[0]
    pad = (k - 1) // 2
    return lax.conv_general_dilated(x, w[:, None, :].astype(x.dtype), (1,), [(pad, pad)],
                                    dimension_numbers=('NWC', 'WIO', 'NWC'),
                                    feature_group_count=x.shape[-1])


def split_cols(p, sizes):
    offs, acc = [], 0
    for s in sizes[:-1]:
        acc += s
        offs.append(acc)
    return jnp.split(p, offs, axis=-1)


def rope_tables(n):
    rows = n // GRID_W
    row = jnp.repeat(jnp.arange(rows), GRID_W)
    col = jnp.tile(jnp.arange(GRID_W), rows)
    pos = jnp.stack([row, col], axis=-1).astype(jnp.float32)
    freqs = ROPE_BASE ** (-jnp.arange(AXIS_PAIRS, dtype=jnp.float32) / AXIS_PAIRS)
    ang = pos[:, :, None] * freqs
    return jnp.cos(ang)[:, None], jnp.sin(ang)[:, None]


def apply_axial_rope(x, cos, sin):
    xs = x.astype(jnp.float32).reshape(*x.shape[:-1], 2, 2, AXIS_PAIRS)
    x1, x2 = xs[..., 0, :], xs[..., 1, :]
    out = jnp.stack([x1 * cos - x2 * sin, x2 * cos + x1 * sin], axis=-2)
    return out.reshape(x.shape)


def mla_qkv(cq, ckv, kr, q_norm, w_q_up, kv_norm, w_kv_up, rope):
    b, t, _ = cq.shape
    q = (rmsnorm(cq, q_norm) @ w_q_up).reshape(b, t, MLA_HEADS, QK_NOPE + QK_ROPE).astype(jnp.float32)
    kv = (rmsnorm(ckv, kv_norm) @ w_kv_up).reshape(b, t, MLA_HEADS, QK_NOPE + V_DIM).astype(jnp.float32)
    q_nope, q_rope = q[..., :QK_NOPE], q[..., QK_NOPE:]
    k_nope, v = kv[..., :QK_NOPE], kv[..., QK_NOPE:]
    k_rope = kr.astype(jnp.float32)[:, :, None, :]
    if rope is not None:
        cos, sin = rope
        q_rope = apply_axial_rope(q_rope, cos, sin)
        k_rope = apply_axial_rope(k_rope, cos, sin)
    k_rope = jnp.broadcast_to(k_rope, (b, t, MLA_HEADS, QK_ROPE))
    return (jnp.concatenate([q_nope, q_rope], axis=-1),
            jnp.concatenate([k_nope, k_rope], axis=-1), v)


def softmax_attend(q, k, v):
    s = jnp.einsum('bqhd,bkhd->bhqk', q, k) * ATTN_SCALE
    p = jax.nn.softmax(s, axis=-1)
    return jnp.einsum('bhqk,bkhd->bqhd', p, v)


def blocked_attend(q, k, v):
    b, t, h, d = q.shape
    nb = t // Q_BLOCK
    qb = q.reshape(b, nb, Q_BLOCK, h, d).swapaxes(0, 1)
    o = lax.map(lambda qq: softmax_attend(qq, k, v), qb)
    return o.swapaxes(0, 1).reshape(b, t, h, v.shape[-1])


def hgrn_gates(f_raw, lb):
    log_f = jnp.logaddexp(jnp.log(lb), jnp.log1p(-lb) + jax.nn.log_sigmoid(f_raw))
    k = (1.0 - lb) * jax.nn.sigmoid(-f_raw)
    return log_f, k


def gla_scan(q, k, v, log_f, s0):
    b, h, t, _ = q.shape
    dv = v.shape[-1]
    nc = t // CHUNK

    def chunks(a):
        return a.reshape(b, h, nc, CHUNK, a.shape[-1]).transpose(2, 0, 1, 3, 4)

    mask = jnp.tril(jnp.ones((CHUNK, CHUNK), dtype=bool))[:, :, None]

    def step(s, inp):
        qc, kc, vc, gc = inp
        cum = jnp.cumsum(gc, axis=2)
        inter = jnp.einsum('bhtk,bhkv->bhtv', qc * jnp.exp(cum), s)
        rel = cum[:, :, :, None, :] - cum[:, :, None, :, :]
        decay = jnp.where(mask, jnp.exp(jnp.minimum(rel, 0.0)), 0.0)
        att = jnp.einsum('bhtk,bhsk,bhtsk->bhts', qc, kc, decay)
        intra = jnp.einsum('bhts,bhsv->bhtv', att, vc)
        last = cum[:, :, -1:, :]
        s_new = (jnp.exp(last[:, :, 0, :, None]) * s
                 + jnp.einsum('bhsk,bhsv->bhkv', kc * jnp.exp(last - cum), vc))
        return s_new, inter + intra

    s_fin, o = lax.scan(step, s0, (chunks(q), chunks(k), chunks(v), chunks(log_f)))
    return s_fin, o.transpose(1, 2, 0, 3, 4).reshape(b, h, t, dv)


def bi_scan(q, k_f, lf_f, k_b, lf_b, v, s_f, s_b):
    flip = lambda a: jnp.flip(a, axis=2)
    sf, of = gla_scan(q, k_f, v, lf_f, s_f)
    sb, ob = gla_scan(flip(q), flip(k_b), flip(v), flip(lf_b), s_b)
    return sf, sb, of + flip(ob)


def mla_hgrn_mixer(u_lat, u_ctx, lb_f, lb_b, w_in, q_norm, w_q_up, kv_norm, w_kv_up, hg_norm, w_out, with_ctx):
    b, t, _ = u_lat.shape
    lat = split_cols(u_lat @ w_in, IN_SIZES)
    cx = split_cols(u_ctx @ w_in, IN_SIZES)
    q_l, k_l, v_l = mla_qkv(lat[0], lat[1], lat[2], q_norm, w_q_up, kv_norm, w_kv_up, rope_tables(t))
    q_c, k_c, v_c = mla_qkv(cx[0], cx[1], cx[2], q_norm, w_q_up, kv_norm, w_kv_up, None)
    att_l = blocked_attend(q_l, jnp.concatenate([k_c, k_l], axis=1), jnp.concatenate([v_c, v_l], axis=1))
    lbf = lb_f.reshape(HG_HEADS, 1, HG_DK)
    lbb = lb_b.reshape(HG_HEADS, 1, HG_DK)

    def heads(a, d):
        bb, tt, _ = a.shape
        return a.reshape(bb, tt, HG_HEADS, d).transpose(0, 2, 1, 3).astype(jnp.float32)

    def hg_inputs(parts):
        lf_f, k_f = hgrn_gates(heads(parts[4], HG_DK), lbf)
        lf_b, k_b = hgrn_gates(heads(parts[5], HG_DK), lbb)
        return heads(parts[3], HG_DK), k_f, lf_f, k_b, lf_b, heads(parts[6], HG_DV)

    zeros = jnp.zeros((b, HG_HEADS, HG_DK, HG_DV), jnp.float32)
    s_f, s_b, o_c = bi_scan(*hg_inputs(cx), zeros, zeros)
    _, _, o_l = bi_scan(*hg_inputs(lat), s_f, s_b)

    def merge(att, o, g):
        bb, tt, _ = g.shape
        hg = rmsnorm(o, hg_norm).transpose(0, 2, 1, 3).reshape(bb, tt, HG_VW).astype(g.dtype) * jax.nn.silu(g)
        return jnp.concatenate([att.reshape(bb, tt, -1).astype(g.dtype), hg], axis=-1) @ w_out

    y_lat = merge(att_l, o_l, lat[7])
    y_ctx = merge(softmax_attend(q_c, k_c, v_c), o_c, cx[7]) if with_ctx else None
    return y_lat, y_ctx


def conformer_conv(u, w_pw1, b_pw1, w_dw, b_dw, ln_g, ln_b, w_pw2, b_pw2):
    a, gate = jnp.split(u @ w_pw1 + b_pw1, 2, axis=-1)
    h = a * jax.nn.sigmoid(gate)
    h = dwconv(h, w_dw) + b_dw
    h = jax.nn.silu(layernorm(h, ln_g, ln_b))
    return h @ w_pw2 + b_pw2


def conv_ffn(u, w_up, w_conv, b_conv, w_down):
    h = dwconv(u @ w_up, w_conv) + b_conv
    val, gate = jnp.split(h, 2, axis=-1)
    return (jax.nn.silu(gate) * val) @ w_down


def setup_inputs(seed: int = 0) -> dict:
    key = jax.random.key(seed)
    ks = jax.random.split(key, 32)
    d = D_MODEL
    n_in = sum(IN_SIZES)

    def nrm(k, shape, s):
        return jax.random.normal(k, shape, jnp.float32) * s

    return {
        "x": nrm(ks[0], (BATCH, SEQ, d), 1.0),
        "c": nrm(ks[1], (BATCH, d), 1.0),
        "ctx": nrm(ks[2], (BATCH, CTX_LEN, d), 1.0),
        "c_ctx": nrm(ks[3], (d,), 1.0),
        "w_mod": nrm(ks[4], (DEPTH, d, 6 * d), MOD_STD * d ** -0.5),
        "b_mod": nrm(ks[5], (DEPTH, 6 * d), 0.01),
        "norm_gains": 1.0 + nrm(ks[6], (DEPTH, 4, d), 0.02),
        "w_in_ab": nrm(ks[7], (N_EVEN, d, n_in), d ** -0.5),
        "mla_q_norm": 1.0 + nrm(ks[8], (N_EVEN, Q_LORA), 0.02),
        "w_q_up": nrm(ks[9], (N_EVEN, Q_LORA, MLA_HEADS * (QK_NOPE + QK_ROPE)), Q_LORA ** -0.5),
        "mla_kv_norm": 1.0 + nrm(ks[10], (N_EVEN, KV_LORA), 0.02),
        "w_kv_up": nrm(ks[11], (N_EVEN, KV_LORA, MLA_HEADS * (QK_NOPE + V_DIM)), KV_LORA ** -0.5),
        "hgrn_lb": nrm(ks[12], (2, N_EVEN, HG_KW), 0.1),
        "hgrn_norm": 1.0 + nrm(ks[13], (N_EVEN, HG_DV), 0.02),
        "w_out_ab": nrm(ks[14], (N_EVEN, MIX_WIDTH, d), MIX_WIDTH ** -0.5),
        "conv_w_pw1": nrm(ks[15], (N_ODD, d, 2 * d), d ** -0.5),
        "conv_b_pw1": nrm(ks[16], (N_ODD, 2 * d), 0.01),
        "conv_w_dw": nrm(ks[17], (N_ODD, CONV_WIDTH, d), CONV_WIDTH ** -0.5),
        "conv_b_dw": nrm(ks[18], (N_ODD, d), 0.01),
        "conv_ln_g": 1.0 + nrm(ks[19], (N_ODD, d), 0.02),
        "conv_ln_b": nrm(ks[20], (N_ODD, d), 0.01),
        "conv_w_pw2": nrm(ks[21], (N_ODD, d, d), d ** -0.5),
        "conv_b_pw2": nrm(ks[22], (N_ODD, d), 0.01),
        "ffn_w_up": nrm(ks[23], (DEPTH, d, 2 * D_FF), d ** -0.5),
        "ffn_w_conv": nrm(ks[24], (DEPTH, FFN_CONV_WIDTH, 2 * D_FF), FFN_CONV_WIDTH ** -0.5),
        "ffn_b_conv": nrm(ks[25], (DEPTH, 2 * D_FF), 0.01),
        "ffn_w_down": nrm(ks[26], (DEPTH, D_FF, d), D_FF ** -0.5),
    }


def reference(x, c, ctx, c_ctx, w_mod, b_mod, norm_gains, w_in_ab, mla_q_norm, w_q_up, mla_kv_norm, w_kv_up,
              hgrn_lb, hgrn_norm, w_out_ab, conv_w_pw1, conv_b_pw1, conv_w_dw, conv_b_dw, conv_ln_g, conv_ln_b,
              conv_w_pw2, conv_b_pw2, ffn_w_up, ffn_w_conv, ffn_b_conv, ffn_w_down):
    lb = jnp.cumsum(jax.nn.softmax(hgrn_lb.astype(jnp.float32), axis=1), axis=1)
    lb = lb - lb[:, :1]
    sc = jax.nn.silu(c)
    scc = jax.nn.silu(c_ctx)
    h_ctx = ctx
    for l in range(DEPTH):
        last = l == DEPTH - 1
        even = l % 2 == 0
        j = l // 2
        m = [mm[:, None, :] for mm in jnp.split(sc @ w_mod[l] + b_mod[l], 6, axis=-1)]
        mc = jnp.split(scc @ w_mod[l] + b_mod[l], 6, axis=-1)
        g_pre1, g_post1, g_pre2, g_post2 = norm_gains[l]
        ffn_p = (ffn_w_up[l], ffn_w_conv[l], ffn_b_conv[l], ffn_w_down[l])
        u_lat = rmsnorm(x, g_pre1) * (1.0 + m[1]) + m[0]
        u_ctx = rmsnorm(h_ctx, g_pre1) * (1.0 + mc[1]) + mc[0] if (even or not last) else None
        if even:
            y_lat, y_ctx = mla_hgrn_mixer(u_lat, u_ctx, lb[0, j], lb[1, j], w_in_ab[j], mla_q_norm[j], w_q_up[j],
                                          mla_kv_norm[j], w_kv_up[j], hgrn_norm[j], w_out_ab[j], not last)
        else:
            conv_p = (conv_w_pw1[j], conv_b_pw1[j], conv_w_dw[j], conv_b_dw[j], conv_ln_g[j], conv_ln_b[j],
                      conv_w_pw2[j], conv_b_pw2[j])
            y_lat = conformer_conv(u_lat, *conv_p)
            y_ctx = None if last else conformer_conv(u_ctx, *conv_p)
        x = x + m[2] * rmsnorm(y_lat, g_post1)
        u_lat = rmsnorm(x, g_pre2) * (1.0 + m[4]) + m[3]
        x = x + m[5] * rmsnorm(conv_ffn(u_lat, *ffn_p), g_post2)
        if not last:
            h_ctx = h_ctx + mc[2] * rmsnorm(y_ctx, g_post1)
            u_ctx = rmsnorm(h_ctx, g_pre2) * (1.0 + mc[4]) + mc[3]
            h_ctx = h_ctx + mc[5] * rmsnorm(conv_ffn(u_ctx, *ffn_p), g_post2)
    return x
```

```python
import contextlib
import numpy as np
import concourse.bass as bass
import concourse.mybir as mybir
from concourse.bass_utils import run_bass_kernel_spmd

dt = mybir.dt
F32 = dt.float32
BF16 = dt.bfloat16
AF = mybir.ActivationFunctionType
ALU = mybir.AluOpType

ENG_NAMES = ("pe", "act", "dve", "pool", "sp")

D = 2048
DEPTH = 4
NCTX = 256
NLAT = 2048
DFF = 5632
EPS = 1e-6
ATTN_SCALE = 192.0 ** -0.5
NCOL = 2352
CTX0 = 16
LAT0 = 288
PIECES = [(CTX0, 256, 1)] + [(LAT0 + 512 * i, 512, 0) for i in range(4)]
WINS = [(CTX0, 1)] + [(LAT0 + 256 * i, 0) for i in range(8)]
ST_RUNS = [[(0, CTX0, 256, 1), (256, LAT0, 512, 0)],
           [(0, 800, 512, 0), (512, 1312, 256, 0)],
           [(0, 1568, 512, 0), (512, 2080, 256, 0)]]
CELL_BOUNDS = np.array([0, 16, 272] + [288 + 256 * i for i in range(9)])


def cells(c0, n):
    a = int(np.searchsorted(CELL_BOUNDS, c0, side="right") - 1)
    b = int(np.searchsorted(CELL_BOUNDS, c0 + n - 1, side="right") - 1)
    return list(range(a, b + 1))


def ck(name, c0, n):
    return [(name, i) for i in cells(c0, n)]


def blkcol(tb):
    return CTX0 + 128 * tb if tb < 2 else LAT0 + 128 * (tb - 2)


class Op:
    __slots__ = ("eng", "fn", "deps", "signal", "tok", "lane")

    def __init__(self, eng, fn, lane=None):
        self.eng = eng
        self.fn = fn
        self.deps = []
        self.signal = lane is not None
        self.tok = None
        self.lane = lane


class Reg:
    __slots__ = ("w", "r")

    def __init__(self):
        self.w = None
        self.r = []


class Prog:
    def __init__(self, nc, n_lanes=8):
        self.nc = nc
        self.ops = {e: [] for e in ENG_NAMES}
        self.regs = {}
        self.n_lanes = n_lanes
        self.lane_last = {}
        self.lane_rr = {e: 0 for e in ENG_NAMES}

    def reg(self, key):
        r = self.regs.get(key)
        if r is None:
            r = self.regs[key] = Reg()
        return r

    def _add(self, op, reads, writes):
        deps = op.deps
        for k in reads:
            r = self.reg(k)
            if r.w is not None:
                deps.append(r.w)
        for k in writes:
            r = self.reg(k)
            if r.w is not None:
                deps.append(r.w)
            deps.extend(r.r)
        for k in reads:
            self.reg(k).r.append(op)
        for k in writes:
            r = self.reg(k)
            r.w = op
            r.r = []
        if op.eng == "pe":
            op.deps = deps = [d for d in deps if d.eng != "pe" or d.lane is not None]
        for d in deps:
            d.signal = True
        self.ops[op.eng].append(op)
        return op

    def op(self, eng, fn, reads=(), writes=()):
        return self._add(Op(eng, fn), reads, writes)

    def dma(self, eng, fn, reads=(), writes=()):
        lane = (eng, self.lane_rr[eng] % self.n_lanes)
        self.lane_rr[eng] += 1
        op = Op(eng, fn, lane=lane)
        prev = self.lane_last.get(lane)
        if prev is not None:
            op.deps.append(prev)
            prev.signal = True
        self.lane_last[lane] = op
        return self._add(op, reads, writes)

    def barrier(self):
        lasts = list(self.lane_last.values())
        for e in ENG_NAMES:
            for o in reversed(self.ops[e]):
                if o.fn is not None:
                    lasts.append(o)
                    break
        for e in ENG_NAMES:
            op = Op(e, None)
            op.deps = [d for d in lasts if not (d.eng == e and d.lane is None)]
            for d in op.deps:
                d.signal = True
            self.ops[e].append(op)
        self.regs = {}

    def emit(self, stack):
        nc = self.nc
        sems = {e: stack.enter_context(nc.semaphore("sem_" + e)) for e in ENG_NAMES}
        lane_sems = {}
        lane_cnt = {}
        for e in ENG_NAMES:
            for i in range(self.n_lanes):
                if (e, i) in self.lane_last:
                    lane_sems[(e, i)] = stack.enter_context(nc.semaphore("lane_%s%d" % (e, i)))
                    lane_cnt[(e, i)] = 0
        for e in ENG_NAMES:
            c = 0
            for op in self.ops[e]:
                if op.lane is not None:
                    lane_cnt[op.lane] += 16
                    op.tok = (lane_sems[op.lane], lane_cnt[op.lane])
                elif op.signal:
                    c += 1
                    op.tok = (sems[e], c)
        final_waits = [(lane_sems[l], lane_cnt[l]) for l in lane_sems]
        block = stack.enter_context(nc.Block())
        ops = self.ops

        def replay(e, h):
            seen = {}
            for op in ops[e]:
                for d in op.deps:
                    sem, val = d.tok
                    k = sem.num
                    if seen.get(k, 0) < val:
                        seen[k] = val
                        h.wait_ge(sem, val)
                if op.fn is None:
                    continue
                inst = op.fn(h)
                if op.lane is not None:
                    inst.then_inc(op.tok[0], 16)
                elif op.signal:
                    inst.then_inc(op.tok[0], 1)
            if e == "sp":
                for sem, val in final_waits:
                    if val:
                        h.wait_ge(sem, val)

        @block.tensor
        def _(h):
            replay("pe", h)

        @block.scalar
        def _(h):
            replay("act", h)

        @block.vector
        def _(h):
            replay("dve", h)

        @block.gpsimd
        def _(h):
            replay("pool", h)

        @block.sync
        def _(h):
            replay("sp", h)


class Rot:
    def __init__(self, kb, name, n, shape, dtype):
        self.items = [(kb.sb("%s%d" % (name, i), shape, dtype), "%s%d" % (name, i)) for i in range(n)]
        self.i = 0

    def next(self):
        it = self.items[self.i % len(self.items)]
        self.i += 1
        return it


class PVLayout:
    def __init__(self):
        self.off = {}
        self.n = 0

    def add(self, name, ncols):
        self.off[name] = self.n
        self.n += ncols


def pv_layout():
    L = PVLayout()
    L.add("cc", 32)
    for l in range(DEPTH):
        L.add("bmod%d" % l, 96)
        L.add("gains%d" % l, 64)
        L.add("fwc%d" % l, 3 * 88)
        L.add("fbc%d" % l, 88)
    for j in range(2):
        L.add("qn%d" % j, 4)
        L.add("kvn%d" % j, 2)
        L.add("hgn%d" % j, 1)
        L.add("bpw1%d" % j, 32)
        L.add("wdw%d" % j, 31 * 16)
        L.add("bdw%d" % j, 16)
        L.add("lng%d" % j, 16)
        L.add("lnb%d" % j, 16)
        L.add("bpw2%d" % j, 16)
    L.add("hlb", 32)
    return L


PVL = pv_layout()


def fm(v):
    v = np.asarray(v, np.float32)
    return np.ascontiguousarray(v.reshape(-1, 128).T)


class KB:
    def __init__(self, nc, st):
        self.nc = nc
        self.st = st
        self.P = Prog(nc)
        self.banks = [st.enter_context(nc.psum_tensor("pb%d" % i, [128, 512], F32)) for i in range(8)]
        self.brr = 0
        self.R = {}
        self.E = {}

    def phase(self, name):
        self.P.barrier()
        C = Carver(self.arena)
        R, E = self.R, self.E
        if name == "mod":
            R["wup"] = C.rot("wup", 2, [128, 16, 512], BF16)
        elif name in ("prenorm", "ln"):
            R["xin"] = C.take([128, 16, 512], F32)
        elif name in ("ffn_up", "ffn_down"):
            R["aT44"] = C.take([128, 44, 768], BF16)
            R["uw"] = [C.take([128, 16, 258], BF16) for _ in range(3)]
            R["hvg"] = C.rot("hvg", 6, [128, 256], F32)
            if name == "ffn_up":
                R["wup"] = C.rot("wup", 2, [128, 16, 512], BF16)
            else:
                R["wfin"] = C.rot("wfin", 2, [128, 44, 256], BF16)
        elif name == "final16":
            R["aT16"] = C.take([128, 16, 768], BF16)
            R["wfin"] = C.rot("wfin", 2, [128, 16, 256], BF16)
        elif name == "odd1":
            R["upiece"] = C.rot("upiece", 2, [128, 16, 512], BF16)
            R["hfull"] = C.rot("hfull", 2, [128, NCOL], F32)
            R["acc"] = C.rot("acc", 2, [128, NCOL], F32)
            R["wup"] = C.rot("wup", 2, [128, 16, 256], BF16)
            for ap, key in R["hfull"].items:
                self.memset(ap, 0.0, [key], eng="dve")
        elif name in ("mla_prep", "mla_heads"):
            E["rope"] = C.take([64, 2, NCOL], F32)
            E["cqn"] = C.take([128, 4, NCOL], BF16)
            E["ckvn"] = C.take([128, 2, NCOL], BF16)
            E["krot"] = C.take([64, NCOL], BF16)
            if name == "mla_prep":
                R["upiece"] = C.rot("upiece", 2, [128, 16, 512], BF16)
                E["wprep"] = C.take([128, 16, 896], BF16)
                E["cf"] = C.take([128, 4, 512], F32)
            else:
                E["wq"] = C.take([128, 4, 2048], BF16)
                E["wkv"] = C.take([128, 2, 2048], BF16)
                E["qn"] = C.take([128, NCOL], BF16)
                E["qrot"] = C.take([64, NCOL], BF16)
                E["kn"] = C.take([128, NCOL], BF16)
                E["V"] = C.take([128, 18, 128], BF16)
        elif name == "hgrn":
            R["upiece"] = C.rot("upiece", 1, [128, 16, 256], BF16)
            E["wh"] = C.take([128, 16, 640], BF16)
            for nm in ("OT", "F0", "F1", "T1", "T2"):
                E[nm] = C.take([128, NCOL], F32)
            for nm in ("Q", "SG", "KK0", "KK1", "QT0", "QT1", "KT0", "KT1", "KH0", "KH1", "maskR"):
                E[nm] = C.take([128, NCOL], BF16)
            E["Vh"] = C.take([32, 72, 128], BF16)
            for nm in ("MS0", "MS1", "FD0", "FD1", "tmpc"):
                E[nm] = C.take([128, 72], F32)
            for nm in ("S00", "S01", "S10", "S11"):
                E[nm] = C.take([128, 128], F32)
            for nm in ("SM00", "SM01", "SM10", "SM11"):
                E[nm] = C.take([128, 128], BF16)
            E["atm"] = [C.take([32, 4, 32], BF16) for _ in range(2)]
            E["kht"] = [C.take([32, 4, 128], BF16) for _ in range(2)]
            self.memset(E["maskR"], 1.0, ["maskR"], eng="dve")
            for (s0, nch) in ((CTX0, 8), (LAT0, 64)):
                self.memset(E["maskR"][:, s0:s0 + nch * 32].rearrange("p (c t) -> p c t", t=32)[:, :, 0:1], 0.0, ["maskR"], eng="dve")
        else:
            raise ValueError(name)

    def sb(self, name, shape, dtype):
        return self.st.enter_context(self.nc.sbuf_tensor("s_" + name, shape, dtype))

    def dram(self, name, shape, dtype):
        return self.nc.dram_tensor(name, shape, dtype, kind="Internal").ap()

    def bank(self, lo=0, hi=6):
        i = lo + self.brr % (hi - lo)
        self.brr += 1
        return self.banks[i], ("pb", i)

    def mm(self, out, lhsT, rhs, start, stop, reads, writes):
        self.P.op("pe", lambda h: h.matmul(out, lhsT=lhsT, rhs=rhs, start=start, stop=stop), reads, writes)

    def act(self, out, in_, func, reads, writes, bias=0.0, scale=1.0):
        self.P.op("act", lambda h: h.activation(out=out, in_=in_, func=func, bias=bias, scale=scale), reads, writes)

    def tt(self, out, in0, in1, op, reads, writes, eng="dve"):
        self.P.op(eng, lambda h: h.tensor_tensor(out=out, in0=in0, in1=in1, op=op), reads, writes)

    def stt(self, out, in0, scalar, in1, op0, op1, reads, writes):
        self.P.op("dve", lambda h: h.scalar_tensor_tensor(out=out, in0=in0, scalar=scalar, in1=in1, op0=op0, op1=op1), reads, writes)

    def ts(self, out, in0, s1, s2, op0, op1, reads, writes):
        self.P.op("dve", lambda h: h.tensor_scalar(out=out, in0=in0, scalar1=s1, scalar2=s2, op0=op0, op1=op1), reads, writes)

    def recip(self, out, in_, reads, writes):
        self.P.op("dve", lambda h: h.reciprocal(out=out, in_=in_), reads, writes)

    def copy(self, out, in_, reads, writes, eng="dve"):
        if eng == "act":
            self.P.op("act", lambda h: h.activation(out=out, in_=in_, func=AF.Identity), reads, writes)
        else:
            self.P.op(eng, lambda h: h.tensor_copy(out=out, in_=in_), reads, writes)

    def memset(self, ap, val, writes, eng="pool"):
        self.P.op(eng, lambda h: h.memset(ap, val), (), writes)

    def dma(self, out, in_, reads, writes, eng="sp"):
        self.P.dma(eng, lambda h: h.dma_start(out=out, in_=in_), reads, writes)

    def rsqrt_from(self, out, in_, scale, reads, writes):
        self.act(out, in_, AF.Sqrt, reads, writes, bias=self.epsb[:, 0:1], scale=scale)
        self.recip(out, out, writes, writes)

    def setup(self, io):
        self.io = io
        P = self.P
        self.xT = self.dram("xT_d", [D, NCOL], F32)
        self.uT = self.dram("uT_d", [D, NCOL], BF16)
        self.hT = self.dram("hT_d", [D, NCOL], BF16)
        self.hc = self.dram("hc_d", [D, NCOL], F32)
        self.yT = self.dram("yT_d", [D, 768], F32)
        self.pv = self.sb("pv", [128, PVL.n], F32)
        self.ones_f = self.sb("ones_f", [128, 128], F32)
        self.ones_b = self.sb("ones_b", [128, 128], BF16)
        self.id_f = self.sb("id_f", [128, 128], F32)
        self.id_b = self.sb("id_b", [128, 128], BF16)
        self.epsb = self.sb("epsb", [128, 1], F32)
        self.zer = self.sb("zer", [128, 16], BF16)
        self.mods = self.sb("mods", [128, DEPTH * 6 * 16 * 2], F32)
        self.modT = self.sb("modT", [128, DEPTH * 96 * 2], F32)
        self.scb = self.sb("scb", [128, 32], BF16)
        self.lbt = self.sb("lbt", [128, 2 * 8 * 2], F32)
        self.lb0 = self.sb("lb0", [128, 2], F32)
        self.mask_f = self.sb("mask_f", [32, 32], F32)
        self.mask_b = self.sb("mask_b", [32, 32], F32)
        self.dma(self.pv[:], io["pv"], (), ["pv"])
        self.memset(self.ones_f[:], 1.0, ["ones_f"])
        self.memset(self.ones_b[:], 1.0, ["ones_b"])
        self.memset(self.epsb[:], EPS, ["epsb"])
        self.memset(self.zer[:], 0.0, ["zer"])
        self.memset(self.id_f[:], 0.0, ["id_f"])
        P.op("pool", lambda h: h.affine_select(out=self.id_f[:], in_=self.id_f[:], pattern=[[-1, 128]], compare_op=ALU.not_equal,
                                               fill=1.0, base=0, channel_multiplier=1), ["id_f"], ["id_f"])
        self.copy(self.id_b[:], self.id_f[:], ["id_f"], ["id_b"])
        for di, (m, pstep, cmul) in enumerate(((self.mask_f, 1, -1), (self.mask_b, -1, 1))):
            key = "mask%d" % di
            self.memset(m[:], 1.0, [key])
            P.op("pool", lambda h, m=m, ps=pstep, cm=cmul: h.affine_select(out=m[:], in_=m[:], pattern=[[ps, 32]], compare_op=ALU.is_ge,
                                                                            fill=0.0, base=0, channel_multiplier=cm), [key], [key])
        for c in range(16):
            self.dma(self.xT[c * 128:(c + 1) * 128, :], io["xT0"][c * 128:(c + 1) * 128, :], (), [("xT", i) for i in range(12)])
        for c in range(16):
            for p0 in (0, 272, 2336):
                self.dma(self.uT[c * 128:(c + 1) * 128, p0:p0 + 16], self.zer[:], ["zer"], ck("uT", p0, 16))
        o = PVL.off["cc"]
        self.act(self.scb[:], self.pv[:, o:o + 32], AF.Silu, ["pv"], ["scb"])
        o = PVL.off["hlb"]
        hl = self.pv[:, o:o + 32].rearrange("p (d j h) -> p d j h", d=2, j=2)
        lbv = self.lbt[:].rearrange("p (d h t) -> p d h t", d=2, h=8)
        self.tt(lbv[:, :, :, 0], hl[:, :, 1, :], hl[:, :, 0, :], ALU.subtract, ["pv"], ["lbt"])
        self.act(lbv[:, :, :, 1], lbv[:, :, :, 0], AF.Sigmoid, ["lbt"], ["lbt"], scale=-1.0)
        self.act(lbv[:, :, :, 0], lbv[:, :, :, 0], AF.Sigmoid, ["lbt"], ["lbt"])
        self.memset(self.lb0[:, 0:1], 0.0, ["lb0"])
        self.memset(self.lb0[:, 1:2], 1.0, ["lb0"])

    def lbs(self, j, d, h):
        if j == 0:
            return self.lb0[:, 0:1], self.lb0[:, 1:2], "lb0"
        o = (d * 8 + h) * 2
        return self.lbt[:, o:o + 1], self.lbt[:, o + 1:o + 2], "lbt"

    def mod(self, l, which, c, kind):
        o = (((l * 6 + which) * 16 + c) * 2) + kind
        return self.mods[:, o:o + 1]

    def modulation(self, wrot):
        io = self.io
        scv = self.scb[:].rearrange("p (c k) -> p c k", k=2)
        for l in range(DEPTH):
            bk, bkey = self.banks[7], ("pb", 7)
            for blk in range(24):
                wt, wkey = wrot.next()
                self.dma(wt[:, :, :], io["w_mod"][l, :, blk * 512:(blk + 1) * 512].rearrange("(k p) m -> p k m", p=128), (), [wkey], eng="pool")
                for jj in range(4):
                    oc = blk * 4 + jj
                    for k in range(16):
                        self.mm(bk[:, oc * 2:oc * 2 + 2], wt[:, k, jj * 128:(jj + 1) * 128], scv[:, k, :], k == 0, k == 15,
                                [wkey, "scb"], [bkey])
            o = PVL.off["bmod%d" % l]
            mt = self.modT[:, l * 192:(l + 1) * 192].rearrange("p (o k) -> p o k", k=2)
            self.tt(mt, bk[:, 0:192].rearrange("p (o k) -> p o k", k=2), self.pv[:, o:o + 96].unsqueeze(2).to_broadcast([128, 96, 2]),
                    ALU.add, [bkey, "pv"], ["modT"])
            go = PVL.off["gains%d" % l]
            md = self.mods[:, l * 192:(l + 1) * 192].rearrange("p (w c k) -> p w c k", w=6, c=16)
            mt6 = self.modT[:, l * 192:(l + 1) * 192].rearrange("p (w c k) -> p w c k", w=6, c=16)
            for (dst, gi, mi, plus1) in ((0, 0, 1, True), (2, 1, 2, False), (3, 2, 4, True), (5, 3, 5, False)):
                gb = self.pv[:, go + gi * 16:go + (gi + 1) * 16].unsqueeze(2).to_broadcast([128, 16, 2])
                if plus1:
                    self.stt(md[:, dst], mt6[:, mi], 1.0, gb, ALU.add, ALU.mult, ["modT", "pv"], ["mods"])
                else:
                    self.tt(md[:, dst], mt6[:, mi], gb, ALU.mult, ["modT", "pv"], ["mods"])
            self.copy(md[:, 1], mt6[:, 0], ["modT"], ["mods"])
            self.copy(md[:, 4], mt6[:, 3], ["modT"], ["mods"])

    def prenorm(self, l, sub):
        self.phase("prenorm")
        R = self.R
        wS, wB = (0, 1) if sub == 1 else (3, 4)
        for (c0, n, kind) in PIECES:
            xin = R["xin"]
            for c in range(16):
                self.dma(xin[:, c, :n], self.xT[c * 128:(c + 1) * 128, c0:c0 + n], ck("xT", c0, n), [("xin", c)])
            bk, bkey = self.bank()
            for c in range(16):
                sq, sqk = R["sq"].next()
                self.act(sq[:, :n], xin[:, c, :n], AF.Square, [("xin", c)], [sqk])
                self.mm(bk[:, :n], self.ones_f[:], sq[:, :n], c == 0, c == 15, ["ones_f", sqk], [bkey])
            rs, rsk = R["rstd"].next()
            self.rsqrt_from(rs[:, :n], bk[:, :n], 1.0 / D, [bkey, "epsb"], [rsk])
            for c in range(16):
                tmp, tk = R["tmpf"].next()
                self.tt(tmp[:, :n], xin[:, c, :n], rs[:, :n], ALU.mult, [("xin", c), rsk], [tk])
                ub, uk = R["ub"].next()
                self.act(ub[:, :n], tmp[:, :n], AF.Identity, [tk, "mods"], [uk], bias=self.mod(l, wB, c, kind), scale=self.mod(l, wS, c, kind))
                self.dma(self.uT[c * 128:(c + 1) * 128, c0:c0 + n], ub[:, :n], [uk], ck("uT", c0, n))

    def final_linear(self, l, sub, s, aT, akeys, Kc, w_ap, bias_off):
        R = self.R
        wG = 2 if sub == 1 else 5
        parts = [(0, 512, 6), (512, 256, 7)]
        for blk in range(8):
            wt, wkey = R["wfin"].next()
            self.dma(wt[:, :Kc, :], w_ap[:, blk * 256:(blk + 1) * 256].rearrange("(k p) m -> p k m", p=128), (), [wkey], eng="pool")
            for jj in range(2):
                c = blk * 2 + jj
                for (n0, nn, sb_) in parts:
                    bk, bkey = self.bank()
                    for k in range(Kc):
                        self.mm(bk[:, :nn], wt[:, k, jj * 128:(jj + 1) * 128], aT[:, k, n0:n0 + nn], k == 0, k == Kc - 1,
                                [wkey] + akeys, [bkey])
                    ysb, yk = R["tmpf"].next()
                    bias = 0.0 if bias_off is None else self.pv[:, bias_off + c:bias_off + c + 1]
                    self.act(ysb[:, :nn], bk[:, :nn], AF.Identity, [bkey, "pv"], [yk], bias=bias)
                    sq, sqk = R["sq"].next()
                    self.act(sq[:, :nn], ysb[:, :nn], AF.Square, [yk], [sqk])
                    self.mm(self.banks[sb_][:, :nn], self.ones_f[:], sq[:, :nn], c == 0, c == 15, ["ones_f", sqk], [("pb", sb_)])
                    self.dma(self.yT[c * 128:(c + 1) * 128, n0:n0 + nn], ysb[:, :nn], [yk], [("yT", c)])
        rs = R["rstd768"]
        self.rsqrt_from(rs[:, 0:512], self.banks[6][:, :512], 1.0 / D, [("pb", 6), "epsb"], ["rs768a"])
        self.rsqrt_from(rs[:, 512:768], self.banks[7][:, :256], 1.0 / D, [("pb", 7), "epsb"], ["rs768b"])
        for c in range(16):
            for (lo, c0, n, kind) in ST_RUNS[s]:
                yl, ylk = R["tmpf"].next()
                xl, xlk = R["sq"].next()
                self.dma(yl[:, :n], self.yT[c * 128:(c + 1) * 128, lo:lo + n], [("yT", c)], [ylk])
                self.dma(xl[:, :n], self.xT[c * 128:(c + 1) * 128, c0:c0 + n], ck("xT", c0, n), [xlk])
                self.tt(yl[:, :n], yl[:, :n], rs[:, lo:lo + n], ALU.mult, [ylk, "rs768a", "rs768b"], [ylk])
                self.stt(xl[:, :n], yl[:, :n], self.mod(l, wG, c, kind), xl[:, :n], ALU.mult, ALU.add, [ylk, xlk, "mods"], [xlk])
                self.dma(self.xT[c * 128:(c + 1) * 128, c0:c0 + n], xl[:, :n], [xlk], ck("xT", c0, n))

    def final_from_dram(self, l, sub, w_ap, bias_off):
        self.phase("final16")
        R = self.R
        for s in range(3):
            aT = R["aT16"]
            for (lo, c0, n, kind) in ST_RUNS[s]:
                self.dma(aT[:, :, lo:lo + n], self.hT[:, c0:c0 + n].rearrange("(k p) n -> p k n", p=128), ck("hT", c0, n), [("aT16", lo)])
            self.final_linear(l, sub, s, aT, [("aT16", 0), ("aT16", 256), ("aT16", 512)], 16, w_ap, bias_off)

    def ffn(self, l):
        io = self.io
        R = self.R
        self.prenorm(l, 2)
        wo = PVL.off["fwc%d" % l]
        bo = PVL.off["fbc%d" % l]
        for s in range(3):
            self.phase("ffn_up")
            aT = R["aT44"]
            uws = []
            for wi in range(3):
                c0, kind = WINS[3 * s + wi]
                uw = R["uw"][wi]
                self.dma(uw[:, :, :], self.uT[:, c0 - 1:c0 + 257].rearrange("(k p) n -> p k n", p=128), ck("uT", c0 - 1, 258), [("uw", wi)])
                uws.append(uw)
            for jj in range(22):
                wu, wkey = R["wup"].next()
                self.dma(wu[:, :, 0:256], io["ffn_w_up"][l, :, jj * 256:(jj + 1) * 256].rearrange("(k p) m -> p k m", p=128), (), [wkey + "a"], eng="pool")
                self.dma(wu[:, :, 256:512], io["ffn_w_up"][l, :, DFF + jj * 256:DFF + (jj + 1) * 256].rearrange("(k p) m -> p k m", p=128), (), [wkey + "b"], eng="pool")
                for pj in range(2):
                    j = jj * 2 + pj
                    for wi in range(3):
                        res = []
                        for half, chn in ((0, j), (1, 44 + j)):
                            bk, bkey = self.bank()
                            for k in range(16):
                                self.mm(bk[:, :258], wu[:, k, half * 256 + pj * 128:half * 256 + (pj + 1) * 128], uws[wi][:, k, :], k == 0, k == 15,
                                        [wkey + "a", wkey + "b", ("uw", wi)], [bkey])
                            hh, hk = R["hvg"].next()
                            w0 = self.pv[:, wo + chn:wo + chn + 1]
                            w1 = self.pv[:, wo + 88 + chn:wo + 88 + chn + 1]
                            w2 = self.pv[:, wo + 176 + chn:wo + 176 + chn + 1]
                            bb = self.pv[:, bo + chn:bo + chn + 1]
                            self.act(hh[:, :], bk[:, 1:257], AF.Identity, [bkey, "pv"], [hk], bias=bb, scale=w1)
                            self.stt(hh[:, :], bk[:, 0:256], w0, hh[:, :], ALU.mult, ALU.add, [bkey, hk, "pv"], [hk])
                            self.stt(hh[:, :], bk[:, 2:258], w2, hh[:, :], ALU.mult, ALU.add, [bkey, hk, "pv"], [hk])
                            res.append((hh, hk))
                        (hv, hvk), (hg, hgk) = res
                        self.act(hg[:, :], hg[:, :], AF.Silu, [hgk], [hgk])
                        self.tt(aT[:, j, wi * 256:(wi + 1) * 256], hg[:, :], hv[:, :], ALU.mult, [hgk, hvk], [("aT44", wi)])
            self.phase("ffn_down")
            self.final_linear(l, 2, s, R["aT44"], [], 44, io["ffn_w_down"][l], None)

    def conformer(self, l):
        io = self.io
        R = self.R
        j = l // 2
        self.prenorm(l, 1)
        self.phase("odd1")
        b1 = PVL.off["bpw1%d" % j]
        wdw = PVL.off["wdw%d" % j]
        bdw = PVL.off["bdw%d" % j]
        for c in range(16):
            wt, wkey = R["wup"].next()
            self.dma(wt[:, :, 0:128], io["conv_w_pw1"][j, :, c * 128:(c + 1) * 128].rearrange("(k p) m -> p k m", p=128), (), [wkey + "a"], eng="pool")
            self.dma(wt[:, :, 128:256], io["conv_w_pw1"][j, :, D + c * 128:D + (c + 1) * 128].rearrange("(k p) m -> p k m", p=128), (), [wkey + "b"], eng="pool")
            hf, hfk = R["hfull"].next()
            for (c0, n, kind) in PIECES:
                up, upk = R["upiece"].next()
                self.dma(up[:, :, :n], self.uT[:, c0:c0 + n].rearrange("(k p) n -> p k n", p=128), ck("uT", c0, n), [upk])
                ba, bak = self.bank()
                bg, bgk = self.bank()
                for k in range(16):
                    self.mm(ba[:, :n], wt[:, k, 0:128], up[:, k, :n], k == 0, k == 15, [wkey + "a", wkey + "b", upk], [bak])
                for k in range(16):
                    self.mm(bg[:, :n], wt[:, k, 128:256], up[:, k, :n], k == 0, k == 15, [wkey + "a", wkey + "b", upk], [bgk])
                sg, sgk = R["tmpf"].next()
                self.act(sg[:, :n], bg[:, :n], AF.Sigmoid, [bgk, "pv"], [sgk], bias=self.pv[:, b1 + 16 + c:b1 + 16 + c + 1])
                self.stt(hf[:, c0:c0 + n], ba[:, :n], self.pv[:, b1 + c:b1 + c + 1], sg[:, :n], ALU.add, ALU.mult, [bak, sgk, "pv"], [hfk])
            acc, acck = R["acc"].next()
            lo, hi = CTX0, LAT0 + NLAT
            nn = hi - lo
            self.act(acc[:, lo:hi], hf[:, lo - 15:hi - 15], AF.Identity, [hfk, "pv"], [acck],
                     bias=self.pv[:, bdw + c:bdw + c + 1], scale=self.pv[:, wdw + c:wdw + c + 1])
            for t in range(1, 31):
                self.stt(acc[:, lo:hi], hf[:, lo - 15 + t:hi - 15 + t], self.pv[:, wdw + t * 16 + c:wdw + t * 16 + c + 1], acc[:, lo:hi],
                         ALU.mult, ALU.add, [hfk, acck, "pv"], [acck])
            for (s0, sn) in ((CTX0, NCTX), (LAT0, NLAT)):
                self.dma(self.hc[c * 128:(c + 1) * 128, s0:s0 + sn], acc[:, s0:s0 + sn], [acck], ck("hc", s0, sn))
        self.phase("ln")
        lg = PVL.off["lng%d" % j]
        lbo = PVL.off["lnb%d" % j]
        for (c0, n, kind) in PIECES:
            xin = R["xin"]
            for c in range(16):
                self.dma(xin[:, c, :n], self.hc[c * 128:(c + 1) * 128, c0:c0 + n], ck("hc", c0, n), [("xin", c)])
            bm, bmk = self.bank()
            bq, bqk = self.bank()
            for c in range(16):
                sq, sqk = R["sq"].next()
                self.act(sq[:, :n], xin[:, c, :n], AF.Square, [("xin", c)], [sqk])
                self.mm(bm[:, :n], self.ones_f[:], xin[:, c, :n], c == 0, c == 15, ["ones_f", ("xin", c)], [bmk])
                self.mm(bq[:, :n], self.ones_f[:], sq[:, :n], c == 0, c == 15, ["ones_f", sqk], [bqk])
            mean, mk = R["rstd"].next()
            rs, rsk = R["rstd"].next()
            self.act(mean[:, :n], bm[:, :n], AF.Identity, [bmk], [mk], scale=1.0 / D)
            m2, m2k = R["tmpf"].next()
            self.tt(m2[:, :n], mean[:, :n], mean[:, :n], ALU.mult, [mk], [m2k])
            self.stt(m2[:, :n], bq[:, :n], 1.0 / D, m2[:, :n], ALU.mult, ALU.subtract, [bqk, m2k], [m2k])
            self.rsqrt_from(rs[:, :n], m2[:, :n], 1.0, [m2k, "epsb"], [rsk])
            for c in range(16):
                tmp, tk = R["tmpf"].next()
                self.tt(tmp[:, :n], xin[:, c, :n], mean[:, :n], ALU.subtract, [("xin", c), mk], [tk])
                self.tt(tmp[:, :n], tmp[:, :n], rs[:, :n], ALU.mult, [tk, rsk], [tk])
                ub, uk = R["ub"].next()
                self.act(ub[:, :n], tmp[:, :n], AF.Silu, [tk, "pv"], [uk], bias=self.pv[:, lbo + c:lbo + c + 1], scale=self.pv[:, lg + c:lg + c + 1])
                self.dma(self.hT[c * 128:(c + 1) * 128, c0:c0 + n], ub[:, :n], [uk], ck("hT", c0, n))
        self.final_from_dram(l, 1, io["conv_w_pw2"][j], PVL.off["bpw2%d" % j])

    def even(self, l):
        io = self.io
        R, E = self.R, self.E
        j = l // 2
        self.prenorm(l, 1)
        if self.stop_at == "pre":
            return
        self.phase("mla_prep")
        wp = E["wprep"]
        self.dma(wp[:, :, 0:832], io["w_in_ab"][j, :, 0:832].rearrange("(k p) m -> p k m", p=128), (), ["wprep_a"], eng="pool")
        self.dma(wp[:, :, 832:896], io["w_kr_sw"][j].rearrange("(k p) m -> p k m", p=128), (), ["wprep_b"], eng="pool")
        self.dma(E["rope"][:, :, :], io["rope"], (), ["rope"])
        wpk = ["wprep_a", "wprep_b"]
        qn_o = PVL.off["qn%d" % j]
        kvn_o = PVL.off["kvn%d" % j]
        cqn, ckvn, krot = E["cqn"], E["ckvn"], E["krot"]
        for (c0, n, kind) in PIECES:
            up, upk = R["upiece"].next()
            self.dma(up[:, :, :n], self.uT[:, c0:c0 + n].rearrange("(k p) n -> p k n", p=128), ck("uT", c0, n), [upk])
            for (nm, nch, col_off, dst, gain_o) in (("cq", 4, 0, cqn, qn_o), ("ckv", 2, 512, ckvn, kvn_o)):
                sbk, sbkey = self.bank()
                cf = E["cf"]
                for c in range(nch):
                    bk, bkey = self.bank()
                    for k in range(16):
                        self.mm(bk[:, :n], wp[:, k, col_off + c * 128:col_off + (c + 1) * 128], up[:, k, :n], k == 0, k == 15, wpk + [upk], [bkey])
                    self.act(cf[:, c, :n], bk[:, :n], AF.Identity, [bkey], [("xin", c)])
                    sq, sqk = R["sq"].next()
                    self.act(sq[:, :n], bk[:, :n], AF.Square, [bkey], [sqk])
                    self.mm(sbk[:, :n], self.ones_f[:], sq[:, :n], c == 0, c == nch - 1, ["ones_f", sqk], [sbkey])
                rs, rsk = R["rstd"].next()
                self.rsqrt_from(rs[:, :n], sbk[:, :n], 1.0 / (nch * 128), [sbkey, "epsb"], [rsk])
                for c in range(nch):
                    tmp, tk = R["tmpf"].next()
                    self.tt(tmp[:, :n], cf[:, c, :n], rs[:, :n], ALU.mult, [("xin", c), rsk], [tk])
                    self.act(dst[:, c, c0:c0 + n], tmp[:, :n], AF.Identity, [tk, "pv"], ck(nm + "n", c0, n), scale=self.pv[:, gain_o + c:gain_o + c + 1])
            bk, bkey = self.bank()
            bs, bskey = self.bank()
            for k in range(16):
                self.mm(bk[0:64, :n], wp[:, k, 768:832], up[:, k, :n], k == 0, k == 15, wpk + [upk], [bkey])
            for k in range(16):
                self.mm(bs[0:64, :n], wp[:, k, 832:896], up[:, k, :n], k == 0, k == 15, wpk + [upk], [bskey])
            self.rope_combine(krot[:, c0:c0 + n], bk, bkey, bs, bskey, c0, n, ck("krot", c0, n))
        if self.stop_at == "prep":
            return
        self.phase("mla_heads")
        cqn, ckvn, krot = E["cqn"], E["ckvn"], E["krot"]
        self.dma(E["wq"][:, :, :], io["w_q_full"][j].rearrange("(k p) m -> p k m", p=128), (), ["wq"], eng="pool")
        self.dma(E["wkv"][:, :, :], io["w_kv_up"][j].rearrange("(k p) m -> p k m", p=128), (), ["wkv"], eng="pool")
        wq, wkv = E["wq"], E["wkv"]
        for h in range(8):
            qn, qrot, kn, V = E["qn"], E["qrot"], E["kn"], E["V"]
            for pi, (c0, n, kind) in enumerate(PIECES):
                bk, bkey = self.bank()
                for k in range(4):
                    self.mm(bk[:, :n], wq[:, k, h * 256:h * 256 + 128], cqn[:, k, c0:c0 + n], k == 0, k == 3, ["wq"] + ck("cqn", c0, n), [bkey])
                self.copy(qn[:, c0:c0 + n], bk[:, :n], [bkey], ck("qn", c0, n), eng="act")
                b1, b1k = self.bank()
                b2, b2k = self.bank()
                for k in range(4):
                    self.mm(b1[0:64, :n], wq[:, k, h * 256 + 128:h * 256 + 192], cqn[:, k, c0:c0 + n], k == 0, k == 3, ["wq"] + ck("cqn", c0, n), [b1k])
                for k in range(4):
                    self.mm(b2[0:64, :n], wq[:, k, h * 256 + 192:h * 256 + 256], cqn[:, k, c0:c0 + n], k == 0, k == 3, ["wq"] + ck("cqn", c0, n), [b2k])
                self.rope_combine(qrot[:, c0:c0 + n], b1, b1k, b2, b2k, c0, n, ck("qrot", c0, n))
                bk, bkey = self.bank()
                for k in range(2):
                    self.mm(bk[:, :n], wkv[:, k, h * 256:h * 256 + 128], ckvn[:, k, c0:c0 + n], k == 0, k == 1, ["wkv"] + ck("ckvn", c0, n), [bkey])
                self.copy(kn[:, c0:c0 + n], bk[:, :n], [bkey], ck("kn", c0, n), eng="act")
                bk, bkey = self.bank()
                for k in range(2):
                    self.mm(bk[:, :n], wkv[:, k, h * 256 + 128:h * 256 + 256], ckvn[:, k, c0:c0 + n], k == 0, k == 1, ["wkv"] + ck("ckvn", c0, n), [bkey])
                vt, vtk = R["ub"].next()
                self.copy(vt[:, :n], bk[:, :n], [bkey], [vtk], eng="act")
                self.to_tokmajor(vt, vtk, c0, n, V, "V")
            for (c0, n, kind) in PIECES:
                kbs = [0, 1] if kind == 1 else list(range(18))
                ob, obk = self.banks[6], ("pb", 6)
                db, dbk = self.banks[7], ("pb", 7)
                for i, kb in enumerate(kbs):
                    kc = blkcol(kb)
                    sbk, sbkey = self.bank()
                    self.mm(sbk[:, :n], kn[:, kc:kc + 128], qn[:, c0:c0 + n], True, False, ck("kn", kc, 128) + ck("qn", c0, n), [sbkey])
                    self.mm(sbk[:, :n], krot[:, kc:kc + 128], qrot[:, c0:c0 + n], False, True, ck("krot", kc, 128) + ck("qrot", c0, n), [sbkey])
                    pt, ptk = R["ub"].next()
                    self.act(pt[:, :n], sbk[:, :n], AF.Exp, [sbkey], [ptk], scale=ATTN_SCALE)
                    self.mm(ob[:, :n], V[:, kb, :], pt[:, :n], i == 0, i == len(kbs) - 1, [("V", kb), ptk], [obk])
                    self.mm(db[:, :n], self.ones_b[:], pt[:, :n], i == 0, i == len(kbs) - 1, ["ones_b", ptk], [dbk])
                rd, rdk = R["tmpf"].next()
                self.recip(rd[:, :n], db[:, :n], [dbk], [rdk])
                at, atk = R["ub"].next()
                self.tt(at[:, :n], ob[:, :n], rd[:, :n], ALU.mult, [obk, rdk], [atk])
                self.dma(self.hT[h * 128:(h + 1) * 128, c0:c0 + n], at[:, :n], [atk], ck("hT", c0, n))
        if self.stop_at == "mla":
            return
        self.phase("hgrn")
        for h in range(8):
            self.hgrn_head(l, j, h)
        if self.stop_at == "hgrn":
            return
        self.final_from_dram(l, 1, io["w_out_ab"][j], None)

    def rope_combine(self, dst, b1, b1k, b2, b2k, c0, n, wkeys):
        R, E = self.R, self.E
        rope = E["rope"]
        t1, t1k = R["tmpf"].next()
        t2, t2k = R["tmpf"].next()
        self.tt(t1[0:64, :n], b1[0:64, :n], rope[:, 0, c0:c0 + n], ALU.mult, [b1k, "rope"], [t1k])
        self.tt(t2[0:64, :n], b2[0:64, :n], rope[:, 1, c0:c0 + n], ALU.mult, [b2k, "rope"], [t2k])
        self.tt(dst, t1[0:64, :n], t2[0:64, :n], ALU.add, [t1k, t2k], wkeys)

    def to_tokmajor(self, src, srck, c0, n, dst, dname, blk=128):
        per = 128 // blk
        for q in range(n // blk):
            col = c0 + q * blk
            tb = (col - CTX0) // blk if col < LAT0 else 2 * per + (col - LAT0) // blk
            bk, bkey = self.bank()
            bkb = bk.bitcast(BF16)
            self.P.op("pe", lambda h, bkb=bkb, q=q: h.transpose(bkb[0:blk, 0:128], src[:, q * blk:(q + 1) * blk], self.id_b[:]), [srck, "id_b"], [bkey])
            self.copy(dst[:, tb, :], bkb[0:blk, 0:128], [bkey], [(dname, tb)])

    def hgrn_head(self, l, j, h):
        io = self.io
        R, E = self.R, self.E
        wh = E["wh"]
        for gi, base in enumerate((832, 1856, 2880, 3904, 4928)):
            self.dma(wh[:, :, gi * 128:(gi + 1) * 128], io["w_in_ab"][j, :, base + h * 128:base + (h + 1) * 128].rearrange("(k p) m -> p k m", p=128),
                     (), [("wh", gi)], eng="pool")
        whk = [("wh", g) for g in range(5)]
        Q, SG, OT, Vh = E["Q"], E["SG"], E["OT"], E["Vh"]
        Fd = [E["F0"], E["F1"]]
        KKd = [E["KK0"], E["KK1"]]
        lo, hi = CTX0, LAT0 + NLAT
        for (c0, kind) in WINS:
            n = 256
            up, upk = R["upiece"].next()
            self.dma(up[:, :, :n], self.uT[:, c0:c0 + n].rearrange("(k p) n -> p k n", p=128), ck("uT", c0, n), [upk])
            bks = []
            for gi in range(5):
                bk, bkey = self.bank(0, 8)
                for k in range(16):
                    self.mm(bk[:, :n], wh[:, k, gi * 128:(gi + 1) * 128], up[:, k, :n], k == 0, k == 15, whk + [upk], [bkey])
                bks.append((bk, bkey))
            self.copy(Q[:, c0:c0 + n], bks[0][0][:, :n], [bks[0][1]], ["Q"], eng="act")
            for d in range(2):
                bk, bkey = bks[1 + d]
                lb, oml, lbk = self.lbs(j, d, h)
                self.act(Fd[d][:, c0:c0 + n], bk[:, :n], AF.Sigmoid, [bkey], ["F%d" % d])
                self.act(KKd[d][:, c0:c0 + n], bk[:, :n], AF.Sigmoid, [bkey], ["KK%d" % d], scale=-1.0)
                self.ts(Fd[d][:, c0:c0 + n], Fd[d][:, c0:c0 + n], oml, lb, ALU.mult, ALU.add, ["F%d" % d, lbk], ["F%d" % d])
                self.P.op("dve", lambda hh, o=KKd[d][:, c0:c0 + n], s=oml: hh.tensor_scalar_mul(out=o, in0=o, scalar1=s), ["KK%d" % d, lbk], ["KK%d" % d])
            self.act(SG[:, c0:c0 + n], bks[4][0][:, :n], AF.Silu, [bks[4][1]], ["SG"])
            vt, vtk = R["ub"].next()
            self.copy(vt[:, :n], bks[3][0][:, :n], [bks[3][1]], [vtk], eng="act")
            self.to_tokmajor(vt, vtk, c0, n, Vh, "Vh", blk=32)
        seqs = ((CTX0, 8), (LAT0, 64))
        CH, MID, LAST = 32, 15, 31

        def cv(t, s0, nch):
            return t[:, s0:s0 + nch * CH].rearrange("p (c t) -> p c t", t=CH)

        QT, KT, KH = [E["QT0"], E["QT1"]], [E["KT0"], E["KT1"]], [E["KH0"], E["KH1"]]
        MS, FDc = [E["MS0"], E["MS1"]], [E["FD0"], E["FD1"]]
        T1, T2 = E["T1"], E["T2"]
        for d in range(2):
            F, KK = Fd[d], KKd[d]
            fk, kkk = "F%d" % d, "KK%d" % d
            self.act(F[:, lo:hi], F[:, lo:hi], AF.Ln, [fk], [fk])
            for (s0, nch) in seqs:
                self.P.op("dve", lambda hh, s0=s0, nch=nch, F=F: hh.tensor_tensor_scan(out=T1[:, s0:s0 + nch * CH], data0=E["maskR"][:, s0:s0 + nch * CH],
                                                                                  data1=F[:, s0:s0 + nch * CH], initial=0.0, op0=ALU.mult, op1=ALU.add),
                          [fk, "maskR"], ["T1"])
            ci0 = 0
            for (s0, nch) in seqs:
                G = cv(T1, s0, nch)
                X = cv(T2, s0, nch)
                if d == 0:
                    self.act(FDc[d][:, ci0:ci0 + nch], G[:, :, LAST], AF.Exp, ["T1"], ["FD%d" % d])
                    self.act(MS[d][:, ci0:ci0 + nch], G[:, :, MID], AF.Exp, ["T1"], ["MS%d" % d])
                    self.tt(X, G, G[:, :, MID:MID + 1].to_broadcast([128, nch, CH]), ALU.subtract, ["T1"], ["T2"])
                    self.tt(G, G[:, :, LAST:LAST + 1].to_broadcast([128, nch, CH]), G, ALU.subtract, ["T1"], ["T1"])
                else:
                    self.act(FDc[d][:, ci0:ci0 + nch], G[:, :, LAST], AF.Exp, ["T1"], ["FD%d" % d])
                    self.tt(X, G, cv(F, s0, nch), ALU.subtract, ["T1", fk], ["T2"])
                    tmpc = E["tmpc"]
                    self.tt(tmpc[:, 0:nch], G[:, :, LAST], X[:, :, MID + 1], ALU.subtract, ["T1", "T2"], ["tmpc"])
                    self.act(MS[d][:, ci0:ci0 + nch], tmpc[:, 0:nch], AF.Exp, ["tmpc"], ["MS%d" % d])
                    self.tt(G, X[:, :, MID + 1:MID + 2].to_broadcast([128, nch, CH]), X, ALU.subtract, ["T2"], ["T1"])
                ci0 += nch
            (e1, e1k), (e3, e3k) = ((T2, "T2"), (T1, "T1")) if d == 0 else ((T1, "T1"), (T2, "T2"))
            self.act(F[:, lo:hi], e1[:, lo:hi], AF.Exp, [e1k], [fk])
            self.tt(QT[d][:, lo:hi], Q[:, lo:hi], F[:, lo:hi], ALU.mult, ["Q", fk], ["QT%d" % d])
            self.act(F[:, lo:hi], e1[:, lo:hi], AF.Exp, [e1k], [fk], scale=-1.0)
            self.tt(KT[d][:, lo:hi], KK[:, lo:hi], F[:, lo:hi], ALU.mult, [kkk, fk], ["KT%d" % d])
            self.act(F[:, lo:hi], e3[:, lo:hi], AF.Exp, [e3k], [fk])
            self.tt(KH[d][:, lo:hi], KK[:, lo:hi], F[:, lo:hi], ALU.mult, [kkk, fk], ["KH%d" % d])
        order = [list(range(18)), [1, 0] + list(range(17, 1, -1))]
        S = [[E["S00"], E["S01"]], [E["S10"], E["S11"]]]
        SM = [[E["SM00"], E["SM01"]], [E["SM10"], E["SM11"]]]
        masks = [self.mask_f, self.mask_b]
        cnt = [0, 0]
        for d in range(2):
            self.memset(S[d][0][:], 0.0, [("S", d, 0)], eng="dve")
            self.memset(SM[d][0][:], 0.0, [("SM", d, 0)], eng="dve")
        visited = set()
        for step in range(18):
            for d in range(2):
                tb = order[d][step]
                col = blkcol(tb)
                pb0 = 4 * d
                atb = self.banks[pb0]
                trbb = self.banks[pb0 + 1].bitcast(BF16)
                ob, obk = self.banks[pb0 + 2], ("pb", pb0 + 2)
                ubk, ubkk = self.banks[pb0 + 3], ("pb", pb0 + 3)
                atm, kht = E["atm"][d], E["kht"][d]
                chs = (0, 1, 2, 3) if d == 0 else (3, 2, 1, 0)
                atk, trk = ("pb", pb0), ("pb", pb0 + 1)
                for ch in chs:
                    cc = col + ch * 32
                    self.mm(atb[0:32, ch * 32:(ch + 1) * 32], KT[d][:, cc:cc + 32], QT[d][:, cc:cc + 32], True, True,
                            ["KT%d" % d, "QT%d" % d], [atk])
                for ch in chs:
                    cc = col + ch * 32
                    self.P.op("pe", lambda hh, trbb=trbb, d=d, cc=cc, ch=ch: hh.transpose(trbb[0:32, ch * 128:(ch + 1) * 128], KH[d][:, cc:cc + 32], self.id_b[:]),
                              ["KH%d" % d, "id_b"], [trk])
                self.tt(atm[:, :, :], atb[0:32, 0:128].rearrange("p (c t) -> p c t", t=32), masks[d][:, :].unsqueeze(1).to_broadcast([32, 4, 32]),
                        ALU.mult, [atk, "mask%d" % d], [("atm", d)])
                self.copy(kht[:, :, :], trbb[0:32, 0:512].rearrange("p (c t) -> p c t", t=128), [trk], [("kht", d)], eng="act")
                for qi, ch in enumerate(chs):
                    ci = tb * 4 + ch
                    cc = col + ch * 32
                    cur = cnt[d] % 2
                    nxt = 1 - cur
                    self.mm(ob[:, ch * 32:(ch + 1) * 32], Vh[:, ci, :], atm[:, ch, :], True, False, [("Vh", ci), ("atm", d)], [obk])
                    self.mm(ob[:, ch * 32:(ch + 1) * 32], SM[d][cur][:, :], QT[d][:, cc:cc + 32], False, True, [("SM", d, cur), "QT%d" % d], [obk])
                    self.mm(ubk[:, 0:128], kht[:, ch, :], Vh[:, ci, :], True, True, [("kht", d), ("Vh", ci)], [ubkk])
                    self.stt(S[d][nxt][:, :], S[d][cur][:, :], FDc[d][:, ci:ci + 1], ubk[:, 0:128], ALU.mult, ALU.add,
                             [("S", d, cur), "FD%d" % d, ubkk], [("S", d, nxt)])
                    if qi < 3:
                        nci = tb * 4 + chs[qi + 1]
                    elif step < 17:
                        nci = order[d][step + 1] * 4 + (0 if d == 0 else 3)
                    else:
                        nci = None
                    if nci is not None:
                        self.act(SM[d][nxt][:, :], S[d][nxt][:, :], AF.Identity, [("S", d, nxt), "MS%d" % d], [("SM", d, nxt)], scale=MS[d][:, nci:nci + 1])
                    cnt[d] += 1
                if tb not in visited:
                    visited.add(tb)
                    self.copy(OT[:, col:col + 128], ob[:, 0:128], [obk], [("OT", tb)], eng="act")
                else:
                    self.tt(OT[:, col:col + 128], ob[:, 0:128], OT[:, col:col + 128], ALU.add, [obk, ("OT", tb)], [("OT", tb)])
        hg_o = PVL.off["hgn%d" % j]
        otk = [("OT", tb) for tb in range(18)]
        for (c0, n, kind) in PIECES:
            sq, sqk = R["sq"].next()
            self.act(sq[:, :n], OT[:, c0:c0 + n], AF.Square, otk, [sqk])
            bk, bkey = self.bank()
            self.mm(bk[:, :n], self.ones_f[:], sq[:, :n], True, True, ["ones_f", sqk], [bkey])
            rs, rsk = R["rstd"].next()
            self.rsqrt_from(rs[:, :n], bk[:, :n], 1.0 / 128, [bkey, "epsb"], [rsk])
            tmp, tk = R["tmpf"].next()
            self.tt(tmp[:, :n], OT[:, c0:c0 + n], rs[:, :n], ALU.mult, otk + [rsk], [tk])
            ub, uk = R["ub"].next()
            self.stt(ub[:, :n], tmp[:, :n], self.pv[:, hg_o:hg_o + 1], SG[:, c0:c0 + n], ALU.mult, ALU.mult, [tk, "pv", "SG"], [uk])
            self.dma(self.hT[(8 + h) * 128:(9 + h) * 128, c0:c0 + n], ub[:, :n], [uk], ck("hT", c0, n))


_WSHAPES = {
    "w_mod": [DEPTH, D, 6 * D], "w_in_ab": [2, D, 5952], "w_kr_sw": [2, D, 64], "w_q_full": [2, 512, 2048],
    "w_kv_up": [2, 256, 2048], "w_out_ab": [2, D, D], "conv_w_pw1": [2, D, 2 * D], "conv_w_pw2": [2, D, D],
    "ffn_w_up": [DEPTH, D, 2 * DFF], "ffn_w_down": [DEPTH, DFF, D],
}


def build_program(stop=None, layers=None):
    nc = bass.Bass("TRN2", target_bir_lowering=False)
    io = {}
    io["xT0"] = nc.dram_tensor("xT0", [D, NCOL], F32, kind="ExternalInput").ap()
    io["pv"] = nc.dram_tensor("pv", [128, PVL.n], F32, kind="ExternalInput").ap()
    io["rope"] = nc.dram_tensor("rope", [64, 2, NCOL], F32, kind="ExternalInput").ap()
    for k, shp in _WSHAPES.items():
        io[k] = nc.dram_tensor(k, shp, F32, kind="ExternalInput").ap()
    outT = nc.dram_tensor("outT", [D, NCOL], F32, kind="ExternalOutput").ap()
    dbg_h = nc.dram_tensor("dbg_h", [D, NCOL], BF16, kind="ExternalOutput").ap() if stop is not None else None
    dbg_u = nc.dram_tensor("dbg_u", [D, NCOL], BF16, kind="ExternalOutput").ap() if stop is not None else None
    st = contextlib.ExitStack()
    with st:
        kb = KB(nc, st)
        kb.setup(io)
        R = kb.R
        R["sq"] = Rot(kb, "sq", 4, [128, 512], F32)
        R["tmpf"] = Rot(kb, "tmpf", 4, [128, 512], F32)
        R["ub"] = Rot(kb, "ub", 4, [128, 512], BF16)
        R["rstd"] = Rot(kb, "rstd", 3, [128, 512], F32)
        R["rstd768"] = kb.sb("rstd768", [128, 768], F32)
        kb.arena = kb.sb("arena", [128, ARENA_BYTES // 2], BF16)
        kb.stop_at = stop[1] if stop else None
        if kb.stop_at != "setup":
            kb.phase("mod")
            kb.modulation(R["wup"])
        for l in (range(DEPTH) if layers is None else layers):
            if kb.stop_at in ("setup", "mod"):
                break
            if l % 2 == 0:
                kb.even(l)
            else:
                kb.conformer(l)
            if stop is not None and stop[0] == l and stop[1] != "ffn":
                break
            kb.ffn(l)
            if stop == (l, "ffn"):
                break
        kb.P.barrier()
        for c in range(16):
            kb.dma(outT[c * 128:(c + 1) * 128, :], kb.xT[c * 128:(c + 1) * 128, :], (), [("outT", c)])
            if dbg_h is not None:
                kb.dma(dbg_h[c * 128:(c + 1) * 128, :], kb.hT[c * 128:(c + 1) * 128, :], (), [("dbgh", c)])
                kb.dma(dbg_u[c * 128:(c + 1) * 128, :], kb.uT[c * 128:(c + 1) * 128, :], (), [("dbgu", c)])
        kb.P.emit(st)
    return nc


ARENA_BYTES = 156 * 1024


class Carver:
    def __init__(self, arena):
        self.arena = arena
        self.off = 0

    def take(self, shape, dtype):
        free = int(np.prod(shape[1:]))
        nbytes = free * (4 if dtype == F32 else 2)
        assert self.off + nbytes <= ARENA_BYTES, (self.off, nbytes)
        ap = self.arena[0:shape[0], self.off // 2:(self.off + nbytes) // 2]
        if dtype == F32:
            ap = ap.bitcast(F32)
        self.off += (nbytes + 63) // 64 * 64
        names = "abcdefg"[:len(shape) - 1]
        if len(shape) > 2:
            pat = "p (%s) -> p %s" % (" ".join(names), " ".join(names))
            ap = ap.rearrange(pat, **{names[i]: shape[1 + i] for i in range(1, len(shape) - 1)})
        return ap

    def rot(self, name, n, shape, dtype):
        return RotAP([(self.take(shape, dtype), "%s%d" % (name, i)) for i in range(n)])


class RotAP:
    def __init__(self, items):
        self.items = items
        self.i = 0

    def next(self):
        it = self.items[self.i % len(self.items)]
        self.i += 1
        return it


def rope_tables_host():
    tab = np.zeros((64, 2, NCOL), np.float32)
    tab[:, 0, :] = 1.0
    t = np.arange(NLAT)
    pos = np.stack([t // 64, t % 64], 0).astype(np.float32)
    freqs = (10000.0 ** (-np.arange(16, dtype=np.float32) / 16)).astype(np.float32)
    for i in range(64):
        a, half, p = i // 32, (i % 32) // 16, i % 16
        ang = (pos[a] * freqs[p]).astype(np.float32)
        tab[i, 0, LAT0:LAT0 + NLAT] = np.cos(ang)
        tab[i, 1, LAT0:LAT0 + NLAT] = np.sin(ang) * (-1.0 if half == 0 else 1.0)
    return tab


SWAP = np.array([(i // 32) * 32 + (1 - (i % 32) // 16) * 16 + i % 16 for i in range(64)])


def host_inputs(inputs, b):
    f = lambda k: np.asarray(inputs[k], np.float32)
    m = {}
    xT0 = np.zeros((D, NCOL), np.float32)
    xT0[:, CTX0:CTX0 + NCTX] = f("ctx")[b].T
    xT0[:, LAT0:LAT0 + NLAT] = f("x")[b].T
    m["xT0"] = xT0
    pv = np.zeros((128, PVL.n), np.float32)

    def put(name, arr):
        o = PVL.off[name]
        pv[:, o:o + arr.shape[1]] = arr

    cc = np.stack([fm(f("c")[b]), fm(f("c_ctx"))], -1).reshape(128, 32)
    put("cc", cc)
    for l in range(DEPTH):
        put("bmod%d" % l, fm(f("b_mod")[l]))
        put("gains%d" % l, np.concatenate([fm(f("norm_gains")[l, i]) for i in range(4)], 1))
        put("fwc%d" % l, np.concatenate([fm(f("ffn_w_conv")[l, k]) for k in range(3)], 1))
        put("fbc%d" % l, fm(f("ffn_b_conv")[l]))
    for j in range(2):
        put("qn%d" % j, fm(f("mla_q_norm")[j]))
        put("kvn%d" % j, fm(f("mla_kv_norm")[j]))
        put("hgn%d" % j, fm(f("hgrn_norm")[j]))
        put("bpw1%d" % j, fm(f("conv_b_pw1")[j]))
        wd = f("conv_w_dw")[j]
        put("wdw%d" % j, np.concatenate([fm(wd[k]) for k in range(31)], 1))
        put("bdw%d" % j, fm(f("conv_b_dw")[j]))
        put("lng%d" % j, fm(f("conv_ln_g")[j]))
        put("lnb%d" % j, fm(f("conv_ln_b")[j]))
        put("bpw2%d" % j, fm(f("conv_b_pw2")[j]))
    hl = f("hgrn_lb")
    put("hlb", np.concatenate([fm(hl[d, jj]) for d in range(2) for jj in range(2)], 1))
    m["pv"] = pv
    return m


def shared_inputs(inputs):
    f = lambda k: np.ascontiguousarray(np.asarray(inputs[k], np.float32))
    m = {k: f(k) for k in ("w_mod", "w_in_ab", "w_kv_up", "w_out_ab", "conv_w_pw1", "conv_w_pw2", "ffn_w_up", "ffn_w_down")}
    m["w_kr_sw"] = np.ascontiguousarray(m["w_in_ab"][:, :, 768:832][:, :, SWAP])
    wq = f("w_q_up").reshape(2, 512, 8, 192)
    m["w_q_full"] = np.ascontiguousarray(np.concatenate([wq, wq[:, :, :, 128:][:, :, :, SWAP]], -1).reshape(2, 512, 2048))
    m["rope"] = rope_tables_host()
    return m


_NC_CACHE = {}


def kernel(**inputs):
    n = 4
    if "full" not in _NC_CACHE:
        _NC_CACHE["full"] = build_program()
    nc = _NC_CACHE["full"]
    sh = shared_inputs(inputs)
    in_maps = []
    for b in range(n):
        m = dict(sh)
        m.update(host_inputs(inputs, b))
        in_maps.append(m)
    res = run_bass_kernel_spmd(nc, in_maps, core_ids=list(range(n)))
    out = np.stack([np.ascontiguousarray(res.results[b]["outT"][:, LAT0:LAT0 + NLAT].T) for b in range(n)], 0)
    return out.astype(np.float32)
```

```python
import contextlib
import numpy as np
import concourse.bass as bass
import concourse.mybir as mybir
from concourse.bass_utils import run_bass_kernel_spmd

dt = mybir.dt
F32 = dt.float32
BF16 = dt.bfloat16
AF = mybir.ActivationFunctionType
ALU = mybir.AluOpType

ENG_NAMES = ("pe", "act", "dve", "pool", "sp")

D = 2048
DEPTH = 4
NCTX = 256
NLAT = 2048
DFF = 5632
EPS = 1e-6
ATTN_SCALE = 192.0 ** -0.5
NCOL = 2352
CTX0 = 16
LAT0 = 288
PIECES = [(CTX0, 256, 1)] + [(LAT0 + 512 * i, 512, 0) for i in range(4)]
WINS = [(CTX0, 1)] + [(LAT0 + 256 * i, 0) for i in range(8)]
ST_RUNS = [[(0, CTX0, 256, 1), (256, LAT0, 512, 0)],
           [(0, 800, 512, 0), (512, 1312, 256, 0)],
           [(0, 1568, 512, 0), (512, 2080, 256, 0)]]
CELL_BOUNDS = np.array([0, 16, 272] + [288 + 256 * i for i in range(9)])


def cells(c0, n):
    a = int(np.searchsorted(CELL_BOUNDS, c0, side="right") - 1)
    b = int(np.searchsorted(CELL_BOUNDS, c0 + n - 1, side="right") - 1)
    return list(range(a, b + 1))


def ck(name, c0, n):
    return [(name, i) for i in cells(c0, n)]


def blkcol(tb):
    return CTX0 + 128 * tb if tb < 2 else LAT0 + 128 * (tb - 2)


class Op:
    __slots__ = ("eng", "fn", "deps", "signal", "tok", "lane")

    def __init__(self, eng, fn, lane=None):
        self.eng = eng
        self.fn = fn
        self.deps = []
        self.signal = lane is not None
        self.tok = None
        self.lane = lane


class Reg:
    __slots__ = ("w", "r")

    def __init__(self):
        self.w = None
        self.r = []


class Prog:
    def __init__(self, nc, n_lanes=8):
        self.nc = nc
        self.ops = {e: [] for e in ENG_NAMES}
        self.regs = {}
        self.n_lanes = n_lanes
        self.lane_last = {}
        self.lane_rr = {e: 0 for e in ENG_NAMES}

    def reg(self, key):
        r = self.regs.get(key)
        if r is None:
            r = self.regs[key] = Reg()
        return r

    def _add(self, op, reads, writes):
        deps = op.deps
        for k in reads:
            r = self.reg(k)
            if r.w is not None:
                deps.append(r.w)
        for k in writes:
            r = self.reg(k)
            if r.w is not None:
                deps.append(r.w)
            deps.extend(r.r)
        for k in reads:
            self.reg(k).r.append(op)
        for k in writes:
            r = self.reg(k)
            r.w = op
            r.r = []
        if op.eng == "pe":
            op.deps = deps = [d for d in deps if d.eng != "pe" or d.lane is not None]
        for d in deps:
            d.signal = True
        self.ops[op.eng].append(op)
        return op

    def op(self, eng, fn, reads=(), writes=()):
        return self._add(Op(eng, fn), reads, writes)

    def dma(self, eng, fn, reads=(), writes=()):
        lane = (eng, self.lane_rr[eng] % self.n_lanes)
        self.lane_rr[eng] += 1
        op = Op(eng, fn, lane=lane)
        prev = self.lane_last.get(lane)
        if prev is not None:
            op.deps.append(prev)
            prev.signal = True
        self.lane_last[lane] = op
        return self._add(op, reads, writes)

    def barrier(self):
        lasts = list(self.lane_last.values())
        for e in ENG_NAMES:
            for o in reversed(self.ops[e]):
                if o.fn is not None:
                    lasts.append(o)
                    break
        for e in ENG_NAMES:
            op = Op(e, None)
            op.deps = [d for d in lasts if not (d.eng == e and d.lane is None)]
            for d in op.deps:
                d.signal = True
            self.ops[e].append(op)
        self.regs = {}

    def emit(self, stack):
        nc = self.nc
        sems = {e: stack.enter_context(nc.semaphore("sem_" + e)) for e in ENG_NAMES}
        lane_sems = {}
        lane_cnt = {}
        for e in ENG_NAMES:
            for i in range(self.n_lanes):
                if (e, i) in self.lane_last:
                    lane_sems[(e, i)] = stack.enter_context(nc.semaphore("lane_%s%d" % (e, i)))
                    lane_cnt[(e, i)] = 0
        for e in ENG_NAMES:
            c = 0
            for op in self.ops[e]:
                if op.lane is not None:
                    lane_cnt[op.lane] += 16
                    op.tok = (lane_sems[op.lane], lane_cnt[op.lane])
                elif op.signal:
                    c += 1
                    op.tok = (sems[e], c)
        final_waits = [(lane_sems[l], lane_cnt[l]) for l in lane_sems]
        block = stack.enter_context(nc.Block())
        ops = self.ops

        def replay(e, h):
            seen = {}
            for op in ops[e]:
                for d in op.deps:
                    sem, val = d.tok
                    k = sem.num
                    if seen.get(k, 0) < val:
                        seen[k] = val
                        h.wait_ge(sem, val)
                if op.fn is None:
                    continue
                inst = op.fn(h)
                if op.lane is not None:
                    inst.then_inc(op.tok[0], 16)
                elif op.signal:
                    inst.then_inc(op.tok[0], 1)
            if e == "sp":
                for sem, val in final_waits:
                    if val:
                        h.wait_ge(sem, val)

        @block.tensor
        def _(h):
            replay("pe", h)

        @block.scalar
        def _(h):
            replay("act", h)

        @block.vector
        def _(h):
            replay("dve", h)

        @block.gpsimd
        def _(h):
            replay("pool", h)

        @block.sync
        def _(h):
            replay("sp", h)


class Rot:
    def __init__(self, kb, name, n, shape, dtype):
        self.items = [(kb.sb("%s%d" % (name, i), shape, dtype), "%s%d" % (name, i)) for i in range(n)]
        self.i = 0

    def next(self):
        it = self.items[self.i % len(self.items)]
        self.i += 1
        return it


class PVLayout:
    def __init__(self):
        self.off = {}
        self.n = 0

    def add(self, name, ncols):
        self.off[name] = self.n
        self.n += ncols


def pv_layout():
    L = PVLayout()
    L.add("cc", 32)
    for l in range(DEPTH):
        L.add("bmod%d" % l, 96)
        L.add("gains%d" % l, 64)
        L.add("fwc%d" % l, 3 * 88)
        L.add("fbc%d" % l, 88)
    for j in range(2):
        L.add("qn%d" % j, 4)
        L.add("kvn%d" % j, 2)
        L.add("hgn%d" % j, 1)
        L.add("bpw1%d" % j, 32)
        L.add("wdw%d" % j, 31 * 16)
        L.add("bdw%d" % j, 16)
        L.add("lng%d" % j, 16)
        L.add("lnb%d" % j, 16)
        L.add("bpw2%d" % j, 16)
    L.add("hlb", 32)
    return L


PVL = pv_layout()


def fm(v):
    v = np.asarray(v, np.float32)
    return np.ascontiguousarray(v.reshape(-1, 128).T)


class KB:
    def __init__(self, nc, st):
        self.nc = nc
        self.st = st
        self.P = Prog(nc)
        self.banks = [st.enter_context(nc.psum_tensor("pb%d" % i, [128, 512], F32)) for i in range(8)]
        self.brr = 0
        self.R = {}
        self.E = {}

    def phase(self, name):
        self.P.barrier()
        C = Carver(self.arena)
        R, E = self.R, self.E
        if name == "mod":
            R["wup"] = C.rot("wup", 2, [128, 16, 512], BF16)
        elif name in ("prenorm", "ln"):
            R["xin"] = C.take([128, 16, 512], F32)
        elif name in ("ffn_up", "ffn_down"):
            R["aT44"] = C.take([128, 44, 768], BF16)
            R["uw"] = [C.take([128, 16, 258], BF16) for _ in range(3)]
            R["hvg"] = C.rot("hvg", 6, [128, 256], F32)
            if name == "ffn_up":
                R["wup"] = C.rot("wup", 2, [128, 16, 512], BF16)
            else:
                R["wfin"] = C.rot("wfin", 2, [128, 44, 256], BF16)
        elif name == "final16":
            R["aT16"] = C.take([128, 16, 768], BF16)
            R["wfin"] = C.rot("wfin", 2, [128, 16, 256], BF16)
        elif name == "odd1":
            R["upiece"] = C.rot("upiece", 2, [128, 16, 512], BF16)
            R["hfull"] = C.rot("hfull", 2, [128, NCOL], BF16)
            R["dg"] = C.rot("dg", 2, [128, 31, 128], BF16)
            R["wup"] = C.rot("wup", 2, [128, 16, 256], BF16)
            for ap, key in R["hfull"].items:
                self.memset(ap, 0.0, [key], eng="dve")
        elif name in ("mla_prep", "mla_heads"):
            E["rope"] = C.take([64, 2, NCOL], F32)
            E["cqn"] = C.take([128, 4, NCOL], BF16)
            E["ckvn"] = C.take([128, 2, NCOL], BF16)
            E["krot"] = C.take([64, NCOL], BF16)
            if name == "mla_prep":
                R["upiece"] = C.rot("upiece", 2, [128, 16, 512], BF16)
                E["wprep"] = C.take([128, 16, 896], BF16)
                E["cf"] = C.take([128, 4, 512], F32)
            else:
                E["wq"] = C.take([128, 4, 2048], BF16)
                E["wkv"] = C.take([128, 2, 2048], BF16)
                E["qn"] = C.take([128, NCOL], BF16)
                E["qrot"] = C.take([64, NCOL], BF16)
                E["kn"] = C.take([128, NCOL], BF16)
                E["V"] = C.take([128, 18, 128], BF16)
        elif name == "hgrn":
            R["upiece"] = C.rot("upiece", 1, [128, 16, 256], BF16)
            E["wh"] = C.take([128, 16, 640], BF16)
            for nm in ("OT", "F0", "F1", "T1"):
                E[nm] = C.take([128, NCOL], F32)
            for nm in ("Q", "SG", "KK0", "KK1", "QT0", "QT1", "KT0", "KT1", "KH0", "KH1", "maskR"):
                E[nm] = C.take([128, NCOL], BF16)
            E["Vh"] = C.take([32, 72, 128], BF16)
            for nm in ("MS0", "MS1", "FD0", "FD1", "tmpc"):
                E[nm] = C.take([128, 72], F32)
            for nm in ("Sr0", "Sr1"):
                E[nm] = C.take([128, 8, 128], F32)
            for nm in ("SMr0", "SMr1"):
                E[nm] = C.take([128, 8, 128], BF16)
            E["atm"] = [C.take([32, 2, 4, 32], BF16) for _ in range(2)]
            E["kht"] = [C.take([32, 4, 128], BF16) for _ in range(2)]
            self.memset(E["maskR"], 1.0, ["maskR"], eng="dve")
            for (s0, nch) in ((CTX0, 8), (LAT0, 64)):
                self.memset(E["maskR"][:, s0:s0 + nch * 32].rearrange("p (c t) -> p c t", t=32)[:, :, 0:1], 0.0, ["maskR"], eng="dve")
        else:
            raise ValueError(name)

    def sb(self, name, shape, dtype):
        return self.st.enter_context(self.nc.sbuf_tensor("s_" + name, shape, dtype))

    def dram(self, name, shape, dtype):
        return self.nc.dram_tensor(name, shape, dtype, kind="Internal").ap()

    def bank(self, lo=0, hi=6):
        i = lo + self.brr % (hi - lo)
        self.brr += 1
        return self.banks[i], ("pb", i)

    def mm(self, out, lhsT, rhs, start, stop, reads, writes):
        self.P.op("pe", lambda h: h.matmul(out, lhsT=lhsT, rhs=rhs, start=start, stop=stop), reads, writes)

    def act(self, out, in_, func, reads, writes, bias=0.0, scale=1.0):
        self.P.op("act", lambda h: h.activation(out=out, in_=in_, func=func, bias=bias, scale=scale), reads, writes)

    def tt(self, out, in0, in1, op, reads, writes, eng="dve"):
        self.P.op(eng, lambda h: h.tensor_tensor(out=out, in0=in0, in1=in1, op=op), reads, writes)

    def stt(self, out, in0, scalar, in1, op0, op1, reads, writes):
        self.P.op("dve", lambda h: h.scalar_tensor_tensor(out=out, in0=in0, scalar=scalar, in1=in1, op0=op0, op1=op1), reads, writes)

    def ts(self, out, in0, s1, s2, op0, op1, reads, writes):
        self.P.op("dve", lambda h: h.tensor_scalar(out=out, in0=in0, scalar1=s1, scalar2=s2, op0=op0, op1=op1), reads, writes)

    def recip(self, out, in_, reads, writes):
        self.P.op("dve", lambda h: h.reciprocal(out=out, in_=in_), reads, writes)

    def copy(self, out, in_, reads, writes, eng="dve"):
        if eng == "act":
            self.P.op("act", lambda h: h.activation(out=out, in_=in_, func=AF.Identity), reads, writes)
        else:
            self.P.op(eng, lambda h: h.tensor_copy(out=out, in_=in_), reads, writes)

    def memset(self, ap, val, writes, eng="pool"):
        self.P.op(eng, lambda h: h.memset(ap, val), (), writes)

    def dma(self, out, in_, reads, writes, eng="sp"):
        self.P.dma(eng, lambda h: h.dma_start(out=out, in_=in_), reads, writes)

    def rsqrt_from(self, out, in_, scale, reads, writes):
        self.act(out, in_, AF.Sqrt, reads, writes, bias=self.epsb[:, 0:1], scale=scale)
        self.recip(out, out, writes, writes)

    def setup(self, io):
        self.io = io
        P = self.P
        self.xT = self.dram("xT_d", [D, NCOL], F32)
        self.uT = self.dram("uT_d", [D, NCOL], BF16)
        self.hT = self.dram("hT_d", [D, NCOL], BF16)
        self.hc = self.dram("hc_d", [D, NCOL], F32)
        self.yT = self.dram("yT_d", [D, 768], F32)
        self.pv = self.sb("pv", [128, PVL.n], F32)
        self.ones_f = self.sb("ones_f", [128, 128], F32)
        self.ones_b = self.sb("ones_b", [128, 128], BF16)
        self.id_f = self.sb("id_f", [128, 128], F32)
        self.id_b = self.sb("id_b", [128, 128], BF16)
        self.epsb = self.sb("epsb", [128, 1], F32)
        self.zer = self.sb("zer", [128, 16], BF16)
        self.mods = self.sb("mods", [128, DEPTH * 6 * 16 * 2], F32)
        self.modT = self.sb("modT", [128, DEPTH * 96 * 2], F32)
        self.scb = self.sb("scb", [128, 32], BF16)
        self.lbt = self.sb("lbt", [128, 2 * 8 * 2], F32)
        self.lb0 = self.sb("lb0", [128, 2], F32)
        self.mask_f = self.sb("mask_f", [32, 32], F32)
        self.mask_b = self.sb("mask_b", [32, 32], F32)
        self.dma(self.pv[:], io["pv"], (), ["pv"])
        self.memset(self.ones_f[:], 1.0, ["ones_f"])
        self.memset(self.ones_b[:], 1.0, ["ones_b"])
        self.memset(self.epsb[:], EPS, ["epsb"])
        self.memset(self.zer[:], 0.0, ["zer"])
        self.memset(self.id_f[:], 0.0, ["id_f"])
        P.op("pool", lambda h: h.affine_select(out=self.id_f[:], in_=self.id_f[:], pattern=[[-1, 128]], compare_op=ALU.not_equal,
                                               fill=1.0, base=0, channel_multiplier=1), ["id_f"], ["id_f"])
        self.copy(self.id_b[:], self.id_f[:], ["id_f"], ["id_b"])
        for di, (m, pstep, cmul) in enumerate(((self.mask_f, 1, -1), (self.mask_b, -1, 1))):
            key = "mask%d" % di
            self.memset(m[:], 1.0, [key])
            P.op("pool", lambda h, m=m, ps=pstep, cm=cmul: h.affine_select(out=m[:], in_=m[:], pattern=[[ps, 32]], compare_op=ALU.is_ge,
                                                                            fill=0.0, base=0, channel_multiplier=cm), [key], [key])
        for c in range(16):
            self.dma(self.xT[c * 128:(c + 1) * 128, :], io["xT0"][c * 128:(c + 1) * 128, :], (), [("xT", i) for i in range(12)])
        for c in range(16):
            for p0 in (0, 272, 2336):
                self.dma(self.uT[c * 128:(c + 1) * 128, p0:p0 + 16], self.zer[:], ["zer"], ck("uT", p0, 16))
        o = PVL.off["cc"]
        self.act(self.scb[:], self.pv[:, o:o + 32], AF.Silu, ["pv"], ["scb"])
        o = PVL.off["hlb"]
        hl = self.pv[:, o:o + 32].rearrange("p (d j h) -> p d j h", d=2, j=2)
        lbv = self.lbt[:].rearrange("p (d h t) -> p d h t", d=2, h=8)
        self.tt(lbv[:, :, :, 0], hl[:, :, 1, :], hl[:, :, 0, :], ALU.subtract, ["pv"], ["lbt"])
        self.act(lbv[:, :, :, 1], lbv[:, :, :, 0], AF.Sigmoid, ["lbt"], ["lbt"], scale=-1.0)
        self.act(lbv[:, :, :, 0], lbv[:, :, :, 0], AF.Sigmoid, ["lbt"], ["lbt"])
        self.memset(self.lb0[:, 0:1], 0.0, ["lb0"])
        self.memset(self.lb0[:, 1:2], 1.0, ["lb0"])

    def lbs(self, j, d, h):
        if j == 0:
            return self.lb0[:, 0:1], self.lb0[:, 1:2], "lb0"
        o = (d * 8 + h) * 2
        return self.lbt[:, o:o + 1], self.lbt[:, o + 1:o + 2], "lbt"

    def mod(self, l, which, c, kind):
        o = (((l * 6 + which) * 16 + c) * 2) + kind
        return self.mods[:, o:o + 1]

    def modulation(self, wrot):
        io = self.io
        scv = self.scb[:].rearrange("p (c k) -> p c k", k=2)
        for l in range(DEPTH):
            bk, bkey = self.banks[7], ("pb", 7)
            for blk in range(24):
                wt, wkey = wrot.next()
                self.dma(wt[:, :, :], io["w_mod"][l, :, blk * 512:(blk + 1) * 512].rearrange("(k p) m -> p k m", p=128), (), [wkey], eng="pool")
                for jj in range(4):
                    oc = blk * 4 + jj
                    for k in range(16):
                        self.mm(bk[:, oc * 2:oc * 2 + 2], wt[:, k, jj * 128:(jj + 1) * 128], scv[:, k, :], k == 0, k == 15,
                                [wkey, "scb"], [bkey])
            o = PVL.off["bmod%d" % l]
            mt = self.modT[:, l * 192:(l + 1) * 192].rearrange("p (o k) -> p o k", k=2)
            self.tt(mt, bk[:, 0:192].rearrange("p (o k) -> p o k", k=2), self.pv[:, o:o + 96].unsqueeze(2).to_broadcast([128, 96, 2]),
                    ALU.add, [bkey, "pv"], ["modT"])
            go = PVL.off["gains%d" % l]
            md = self.mods[:, l * 192:(l + 1) * 192].rearrange("p (w c k) -> p w c k", w=6, c=16)
            mt6 = self.modT[:, l * 192:(l + 1) * 192].rearrange("p (w c k) -> p w c k", w=6, c=16)
            for (dst, gi, mi, plus1) in ((0, 0, 1, True), (2, 1, 2, False), (3, 2, 4, True), (5, 3, 5, False)):
                gb = self.pv[:, go + gi * 16:go + (gi + 1) * 16].unsqueeze(2).to_broadcast([128, 16, 2])
                if plus1:
                    self.stt(md[:, dst], mt6[:, mi], 1.0, gb, ALU.add, ALU.mult, ["modT", "pv"], ["mods"])
                else:
                    self.tt(md[:, dst], mt6[:, mi], gb, ALU.mult, ["modT", "pv"], ["mods"])
            self.copy(md[:, 1], mt6[:, 0], ["modT"], ["mods"])
            self.copy(md[:, 4], mt6[:, 3], ["modT"], ["mods"])

    def prenorm(self, l, sub):
        self.phase("prenorm")
        R = self.R
        wS, wB = (0, 1) if sub == 1 else (3, 4)
        for (c0, n, kind) in PIECES:
            xin = R["xin"]
            for c in range(16):
                self.dma(xin[:, c, :n], self.xT[c * 128:(c + 1) * 128, c0:c0 + n], ck("xT", c0, n), [("xin", c)])
            bk, bkey = self.bank()
            for c in range(16):
                sq, sqk = R["sq"].next()
                self.act(sq[:, :n], xin[:, c, :n], AF.Square, [("xin", c)], [sqk])
                self.mm(bk[:, :n], self.ones_f[:], sq[:, :n], c == 0, c == 15, ["ones_f", sqk], [bkey])
            rs, rsk = R["rstd"].next()
            self.rsqrt_from(rs[:, :n], bk[:, :n], 1.0 / D, [bkey, "epsb"], [rsk])
            for c in range(16):
                tmp, tk = R["tmpf"].next()
                self.tt(tmp[:, :n], xin[:, c, :n], rs[:, :n], ALU.mult, [("xin", c), rsk], [tk])
                ub, uk = R["ub"].next()
                self.act(ub[:, :n], tmp[:, :n], AF.Identity, [tk, "mods"], [uk], bias=self.mod(l, wB, c, kind), scale=self.mod(l, wS, c, kind))
                self.dma(self.uT[c * 128:(c + 1) * 128, c0:c0 + n], ub[:, :n], [uk], ck("uT", c0, n))

    def final_linear(self, l, sub, s, aT, akeys, Kc, w_ap, bias_off):
        R = self.R
        wG = 2 if sub == 1 else 5
        parts = [(0, 512, 6), (512, 256, 7)]
        for blk in range(8):
            wt, wkey = R["wfin"].next()
            self.dma(wt[:, :Kc, :], w_ap[:, blk * 256:(blk + 1) * 256].rearrange("(k p) m -> p k m", p=128), (), [wkey], eng="pool")
            for jj in range(2):
                c = blk * 2 + jj
                for (n0, nn, sb_) in parts:
                    bk, bkey = self.bank()
                    for k in range(Kc):
                        self.mm(bk[:, :nn], wt[:, k, jj * 128:(jj + 1) * 128], aT[:, k, n0:n0 + nn], k == 0, k == Kc - 1,
                                [wkey] + akeys, [bkey])
                    ysb, yk = R["tmpf"].next()
                    bias = 0.0 if bias_off is None else self.pv[:, bias_off + c:bias_off + c + 1]
                    self.act(ysb[:, :nn], bk[:, :nn], AF.Identity, [bkey, "pv"], [yk], bias=bias)
                    sq, sqk = R["sq"].next()
                    self.act(sq[:, :nn], ysb[:, :nn], AF.Square, [yk], [sqk])
                    self.mm(self.banks[sb_][:, :nn], self.ones_f[:], sq[:, :nn], c == 0, c == 15, ["ones_f", sqk], [("pb", sb_)])
                    self.dma(self.yT[c * 128:(c + 1) * 128, n0:n0 + nn], ysb[:, :nn], [yk], [("yT", c)])
        rs = R["rstd768"]
        self.rsqrt_from(rs[:, 0:512], self.banks[6][:, :512], 1.0 / D, [("pb", 6), "epsb"], ["rs768a"])
        self.rsqrt_from(rs[:, 512:768], self.banks[7][:, :256], 1.0 / D, [("pb", 7), "epsb"], ["rs768b"])
        for c in range(16):
            for (lo, c0, n, kind) in ST_RUNS[s]:
                yl, ylk = R["tmpf"].next()
                xl, xlk = R["sq"].next()
                self.dma(yl[:, :n], self.yT[c * 128:(c + 1) * 128, lo:lo + n], [("yT", c)], [ylk])
                self.dma(xl[:, :n], self.xT[c * 128:(c + 1) * 128, c0:c0 + n], ck("xT", c0, n), [xlk])
                self.tt(yl[:, :n], yl[:, :n], rs[:, lo:lo + n], ALU.mult, [ylk, "rs768a", "rs768b"], [ylk])
                self.stt(xl[:, :n], yl[:, :n], self.mod(l, wG, c, kind), xl[:, :n], ALU.mult, ALU.add, [ylk, xlk, "mods"], [xlk])
                self.dma(self.xT[c * 128:(c + 1) * 128, c0:c0 + n], xl[:, :n], [xlk], ck("xT", c0, n))

    def final_from_dram(self, l, sub, w_ap, bias_off):
        self.phase("final16")
        R = self.R
        for s in range(3):
            aT = R["aT16"]
            for (lo, c0, n, kind) in ST_RUNS[s]:
                self.dma(aT[:, :, lo:lo + n], self.hT[:, c0:c0 + n].rearrange("(k p) n -> p k n", p=128), ck("hT", c0, n), [("aT16", lo)])
            self.final_linear(l, sub, s, aT, [("aT16", 0), ("aT16", 256), ("aT16", 512)], 16, w_ap, bias_off)

    def ffn(self, l):
        io = self.io
        R = self.R
        self.prenorm(l, 2)
        wo = PVL.off["fwc%d" % l]
        bo = PVL.off["fbc%d" % l]
        for s in range(3):
            self.phase("ffn_up")
            aT = R["aT44"]
            uws = []
            for wi in range(3):
                c0, kind = WINS[3 * s + wi]
                uw = R["uw"][wi]
                self.dma(uw[:, :, :], self.uT[:, c0 - 1:c0 + 257].rearrange("(k p) n -> p k n", p=128), ck("uT", c0 - 1, 258), [("uw", wi)])
                uws.append(uw)
            for jj in range(22):
                wu, wkey = R["wup"].next()
                self.dma(wu[:, :, 0:256], io["ffn_w_up"][l, :, jj * 256:(jj + 1) * 256].rearrange("(k p) m -> p k m", p=128), (), [wkey + "a"], eng="pool")
                self.dma(wu[:, :, 256:512], io["ffn_w_up"][l, :, DFF + jj * 256:DFF + (jj + 1) * 256].rearrange("(k p) m -> p k m", p=128), (), [wkey + "b"], eng="pool")
                for pj in range(2):
                    j = jj * 2 + pj
                    for wi in range(3):
                        res = []
                        for half, chn in ((0, j), (1, 44 + j)):
                            bk, bkey = self.bank()
                            for k in range(16):
                                self.mm(bk[:, :258], wu[:, k, half * 256 + pj * 128:half * 256 + (pj + 1) * 128], uws[wi][:, k, :], k == 0, k == 15,
                                        [wkey + "a", wkey + "b", ("uw", wi)], [bkey])
                            hh, hk = R["hvg"].next()
                            w0 = self.pv[:, wo + chn:wo + chn + 1]
                            w1 = self.pv[:, wo + 88 + chn:wo + 88 + chn + 1]
                            w2 = self.pv[:, wo + 176 + chn:wo + 176 + chn + 1]
                            bb = self.pv[:, bo + chn:bo + chn + 1]
                            self.act(hh[:, :], bk[:, 1:257], AF.Identity, [bkey, "pv"], [hk], bias=bb, scale=w1)
                            self.stt(hh[:, :], bk[:, 0:256], w0, hh[:, :], ALU.mult, ALU.add, [bkey, hk, "pv"], [hk])
                            self.stt(hh[:, :], bk[:, 2:258], w2, hh[:, :], ALU.mult, ALU.add, [bkey, hk, "pv"], [hk])
                            res.append((hh, hk))
                        (hv, hvk), (hg, hgk) = res
                        self.act(hg[:, :], hg[:, :], AF.Silu, [hgk], [hgk])
                        self.tt(aT[:, j, wi * 256:(wi + 1) * 256], hg[:, :], hv[:, :], ALU.mult, [hgk, hvk], [("aT44", wi)])
            self.phase("ffn_down")
            self.final_linear(l, 2, s, R["aT44"], [], 44, io["ffn_w_down"][l], None)

    def conformer(self, l):
        io = self.io
        R = self.R
        j = l // 2
        self.prenorm(l, 1)
        self.phase("odd1")
        b1 = PVL.off["bpw1%d" % j]
        wdw = PVL.off["wdw%d" % j]
        bdw = PVL.off["bdw%d" % j]
        for c in range(16):
            wt, wkey = R["wup"].next()
            self.dma(wt[:, :, 0:128], io["conv_w_pw1"][j, :, c * 128:(c + 1) * 128].rearrange("(k p) m -> p k m", p=128), (), [wkey + "a"], eng="pool")
            self.dma(wt[:, :, 128:256], io["conv_w_pw1"][j, :, D + c * 128:D + (c + 1) * 128].rearrange("(k p) m -> p k m", p=128), (), [wkey + "b"], eng="pool")
            hf, hfk = R["hfull"].next()
            for (c0, n, kind) in PIECES:
                up, upk = R["upiece"].next()
                self.dma(up[:, :, :n], self.uT[:, c0:c0 + n].rearrange("(k p) n -> p k n", p=128), ck("uT", c0, n), [upk])
                ba, bak = self.bank()
                bg, bgk = self.bank()
                for k in range(16):
                    self.mm(ba[:, :n], wt[:, k, 0:128], up[:, k, :n], k == 0, k == 15, [wkey + "a", wkey + "b", upk], [bak])
                for k in range(16):
                    self.mm(bg[:, :n], wt[:, k, 128:256], up[:, k, :n], k == 0, k == 15, [wkey + "a", wkey + "b", upk], [bgk])
                sg, sgk = R["tmpf"].next()
                self.act(sg[:, :n], bg[:, :n], AF.Sigmoid, [bgk, "pv"], [sgk], bias=self.pv[:, b1 + 16 + c:b1 + 16 + c + 1])
                self.stt(hf[:, c0:c0 + n], ba[:, :n], self.pv[:, b1 + c:b1 + c + 1], sg[:, :n], ALU.add, ALU.mult, [bak, sgk, "pv"], [hfk])
            dg, dgk = R["dg"].next()
            wv = self.pv[:, wdw:wdw + 496].rearrange("p (t c) -> p t c", c=16)[:, :, c:c + 1]
            self.tt(dg[:, :, :], self.id_b[:].unsqueeze(1).to_broadcast([128, 31, 128]), wv.to_broadcast([128, 31, 128]), ALU.mult,
                    ["id_b", "pv"], [dgk])
            for (c0, n, kind) in PIECES:
                bk, bkey = self.bank()
                for t in range(31):
                    self.mm(bk[:, :n], dg[:, t, :], hf[:, c0 - 15 + t:c0 - 15 + t + n], t == 0, t == 30, [dgk, hfk], [bkey])
                oc, ock = R["tmpf"].next()
                self.act(oc[:, :n], bk[:, :n], AF.Identity, [bkey, "pv"], [ock], bias=self.pv[:, bdw + c:bdw + c + 1])
                self.dma(self.hc[c * 128:(c + 1) * 128, c0:c0 + n], oc[:, :n], [ock], ck("hc", c0, n))
        self.phase("ln")
        lg = PVL.off["lng%d" % j]
        lbo = PVL.off["lnb%d" % j]
        for (c0, n, kind) in PIECES:
            xin = R["xin"]
            for c in range(16):
                self.dma(xin[:, c, :n], self.hc[c * 128:(c + 1) * 128, c0:c0 + n], ck("hc", c0, n), [("xin", c)])
            bm, bmk = self.bank()
            bq, bqk = self.bank()
            for c in range(16):
                sq, sqk = R["sq"].next()
                self.act(sq[:, :n], xin[:, c, :n], AF.Square, [("xin", c)], [sqk])
                self.mm(bm[:, :n], self.ones_f[:], xin[:, c, :n], c == 0, c == 15, ["ones_f", ("xin", c)], [bmk])
                self.mm(bq[:, :n], self.ones_f[:], sq[:, :n], c == 0, c == 15, ["ones_f", sqk], [bqk])
            mean, mk = R["rstd"].next()
            rs, rsk = R["rstd"].next()
            self.act(mean[:, :n], bm[:, :n], AF.Identity, [bmk], [mk], scale=1.0 / D)
            m2, m2k = R["tmpf"].next()
            self.tt(m2[:, :n], mean[:, :n], mean[:, :n], ALU.mult, [mk], [m2k])
            self.stt(m2[:, :n], bq[:, :n], 1.0 / D, m2[:, :n], ALU.mult, ALU.subtract, [bqk, m2k], [m2k])
            self.rsqrt_from(rs[:, :n], m2[:, :n], 1.0, [m2k, "epsb"], [rsk])
            for c in range(16):
                tmp, tk = R["tmpf"].next()
                self.tt(tmp[:, :n], xin[:, c, :n], mean[:, :n], ALU.subtract, [("xin", c), mk], [tk])
                self.tt(tmp[:, :n], tmp[:, :n], rs[:, :n], ALU.mult, [tk, rsk], [tk])
                ub, uk = R["ub"].next()
                self.act(ub[:, :n], tmp[:, :n], AF.Silu, [tk, "pv"], [uk], bias=self.pv[:, lbo + c:lbo + c + 1], scale=self.pv[:, lg + c:lg + c + 1])
                self.dma(self.hT[c * 128:(c + 1) * 128, c0:c0 + n], ub[:, :n], [uk], ck("hT", c0, n))
        self.final_from_dram(l, 1, io["conv_w_pw2"][j], PVL.off["bpw2%d" % j])

    def even(self, l):
        io = self.io
        R, E = self.R, self.E
        j = l // 2
        self.prenorm(l, 1)
        if self.stop_at == "pre":
            return
        self.phase("mla_prep")
        wp = E["wprep"]
        self.dma(wp[:, :, 0:832], io["w_in_ab"][j, :, 0:832].rearrange("(k p) m -> p k m", p=128), (), ["wprep_a"], eng="pool")
        self.dma(wp[:, :, 832:896], io["w_kr_sw"][j].rearrange("(k p) m -> p k m", p=128), (), ["wprep_b"], eng="pool")
        self.dma(E["rope"][:, :, :], io["rope"], (), ["rope"])
        wpk = ["wprep_a", "wprep_b"]
        qn_o = PVL.off["qn%d" % j]
        kvn_o = PVL.off["kvn%d" % j]
        cqn, ckvn, krot = E["cqn"], E["ckvn"], E["krot"]
        for (c0, n, kind) in PIECES:
            up, upk = R["upiece"].next()
            self.dma(up[:, :, :n], self.uT[:, c0:c0 + n].rearrange("(k p) n -> p k n", p=128), ck("uT", c0, n), [upk])
            for (nm, nch, col_off, dst, gain_o) in (("cq", 4, 0, cqn, qn_o), ("ckv", 2, 512, ckvn, kvn_o)):
                sbk, sbkey = self.bank()
                cf = E["cf"]
                for c in range(nch):
                    bk, bkey = self.bank()
                    for k in range(16):
                        self.mm(bk[:, :n], wp[:, k, col_off + c * 128:col_off + (c + 1) * 128], up[:, k, :n], k == 0, k == 15, wpk + [upk], [bkey])
                    self.act(cf[:, c, :n], bk[:, :n], AF.Identity, [bkey], [("xin", c)])
                    sq, sqk = R["sq"].next()
                    self.act(sq[:, :n], bk[:, :n], AF.Square, [bkey], [sqk])
                    self.mm(sbk[:, :n], self.ones_f[:], sq[:, :n], c == 0, c == nch - 1, ["ones_f", sqk], [sbkey])
                rs, rsk = R["rstd"].next()
                self.rsqrt_from(rs[:, :n], sbk[:, :n], 1.0 / (nch * 128), [sbkey, "epsb"], [rsk])
                for c in range(nch):
                    tmp, tk = R["tmpf"].next()
                    self.tt(tmp[:, :n], cf[:, c, :n], rs[:, :n], ALU.mult, [("xin", c), rsk], [tk])
                    self.act(dst[:, c, c0:c0 + n], tmp[:, :n], AF.Identity, [tk, "pv"], ck(nm + "n", c0, n), scale=self.pv[:, gain_o + c:gain_o + c + 1])
            bk, bkey = self.bank()
            bs, bskey = self.bank()
            for k in range(16):
                self.mm(bk[0:64, :n], wp[:, k, 768:832], up[:, k, :n], k == 0, k == 15, wpk + [upk], [bkey])
            for k in range(16):
                self.mm(bs[0:64, :n], wp[:, k, 832:896], up[:, k, :n], k == 0, k == 15, wpk + [upk], [bskey])
            self.rope_combine(krot[:, c0:c0 + n], bk, bkey, bs, bskey, c0, n, ck("krot", c0, n))
        if self.stop_at == "prep":
            return
        self.phase("mla_heads")
        cqn, ckvn, krot = E["cqn"], E["ckvn"], E["krot"]
        self.dma(E["wq"][:, :, :], io["w_q_full"][j].rearrange("(k p) m -> p k m", p=128), (), ["wq"], eng="pool")
        self.dma(E["wkv"][:, :, :], io["w_kv_up"][j].rearrange("(k p) m -> p k m", p=128), (), ["wkv"], eng="pool")
        wq, wkv = E["wq"], E["wkv"]
        for h in range(8):
            qn, qrot, kn, V = E["qn"], E["qrot"], E["kn"], E["V"]
            for pi, (c0, n, kind) in enumerate(PIECES):
                bk, bkey = self.bank()
                for k in range(4):
                    self.mm(bk[:, :n], wq[:, k, h * 256:h * 256 + 128], cqn[:, k, c0:c0 + n], k == 0, k == 3, ["wq"] + ck("cqn", c0, n), [bkey])
                self.copy(qn[:, c0:c0 + n], bk[:, :n], [bkey], ck("qn", c0, n), eng="act")
                b1, b1k = self.bank()
                b2, b2k = self.bank()
                for k in range(4):
                    self.mm(b1[0:64, :n], wq[:, k, h * 256 + 128:h * 256 + 192], cqn[:, k, c0:c0 + n], k == 0, k == 3, ["wq"] + ck("cqn", c0, n), [b1k])
                for k in range(4):
                    self.mm(b2[0:64, :n], wq[:, k, h * 256 + 192:h * 256 + 256], cqn[:, k, c0:c0 + n], k == 0, k == 3, ["wq"] + ck("cqn", c0, n), [b2k])
                self.rope_combine(qrot[:, c0:c0 + n], b1, b1k, b2, b2k, c0, n, ck("qrot", c0, n))
                bk, bkey = self.bank()
                for k in range(2):
                    self.mm(bk[:, :n], wkv[:, k, h * 256:h * 256 + 128], ckvn[:, k, c0:c0 + n], k == 0, k == 1, ["wkv"] + ck("ckvn", c0, n), [bkey])
                self.copy(kn[:, c0:c0 + n], bk[:, :n], [bkey], ck("kn", c0, n), eng="act")
                bk, bkey = self.bank()
                for k in range(2):
                    self.mm(bk[:, :n], wkv[:, k, h * 256 + 128:h * 256 + 256], ckvn[:, k, c0:c0 + n], k == 0, k == 1, ["wkv"] + ck("ckvn", c0, n), [bkey])
                vt, vtk = R["ub"].next()
                self.copy(vt[:, :n], bk[:, :n], [bkey], [vtk], eng="act")
                self.to_tokmajor(vt, vtk, c0, n, V, "V")
            for (c0, n, kind) in PIECES:
                kbs = [0, 1] if kind == 1 else list(range(18))
                ob, obk = self.banks[6], ("pb", 6)
                db, dbk = self.banks[7], ("pb", 7)
                pend = None
                nk = len(kbs)
                for i in range(nk + 1):
                    if i < nk:
                        kb = kbs[i]
                        kc = blkcol(kb)
                        sbk, sbkey = self.bank()
                        self.mm(sbk[:, :n], kn[:, kc:kc + 128], qn[:, c0:c0 + n], True, False, ck("kn", kc, 128) + ck("qn", c0, n), [sbkey])
                        self.mm(sbk[:, :n], krot[:, kc:kc + 128], qrot[:, c0:c0 + n], False, True, ck("krot", kc, 128) + ck("qrot", c0, n), [sbkey])
                        pt, ptk = R["ub"].next()
                        self.act(pt[:, :n], sbk[:, :n], AF.Exp, [sbkey], [ptk], scale=ATTN_SCALE)
                        cur = (i, kb, pt, ptk)
                    if pend is not None:
                        pi, pkb, ppt, pptk = pend
                        self.mm(ob[:, :n], V[:, pkb, :], ppt[:, :n], pi == 0, pi == nk - 1, [("V", pkb), pptk], [obk])
                        self.mm(db[:, :n], self.ones_b[:], ppt[:, :n], pi == 0, pi == nk - 1, ["ones_b", pptk], [dbk])
                    pend = cur if i < nk else None
                rd, rdk = R["tmpf"].next()
                self.recip(rd[:, :n], db[:, :n], [dbk], [rdk])
                at, atk = R["ub"].next()
                self.tt(at[:, :n], ob[:, :n], rd[:, :n], ALU.mult, [obk, rdk], [atk])
                self.dma(self.hT[h * 128:(h + 1) * 128, c0:c0 + n], at[:, :n], [atk], ck("hT", c0, n))
        if self.stop_at == "mla":
            return
        self.phase("hgrn")
        for h in range(8):
            self.hgrn_head(l, j, h)
        if self.stop_at == "hgrn":
            return
        self.final_from_dram(l, 1, io["w_out_ab"][j], None)

    def rope_combine(self, dst, b1, b1k, b2, b2k, c0, n, wkeys):
        R, E = self.R, self.E
        rope = E["rope"]
        t1, t1k = R["tmpf"].next()
        t2, t2k = R["tmpf"].next()
        self.tt(t1[0:64, :n], b1[0:64, :n], rope[:, 0, c0:c0 + n], ALU.mult, [b1k, "rope"], [t1k])
        self.tt(t2[0:64, :n], b2[0:64, :n], rope[:, 1, c0:c0 + n], ALU.mult, [b2k, "rope"], [t2k])
        self.tt(dst, t1[0:64, :n], t2[0:64, :n], ALU.add, [t1k, t2k], wkeys)

    def to_tokmajor(self, src, srck, c0, n, dst, dname, blk=128):
        per = 128 // blk
        for q in range(n // blk):
            col = c0 + q * blk
            tb = (col - CTX0) // blk if col < LAT0 else 2 * per + (col - LAT0) // blk
            bk, bkey = self.bank()
            bkb = bk.bitcast(BF16)
            self.P.op("pe", lambda h, bkb=bkb, q=q: h.transpose(bkb[0:blk, 0:128], src[:, q * blk:(q + 1) * blk], self.id_b[:]), [srck, "id_b"], [bkey])
            self.copy(dst[:, tb, :], bkb[0:blk, 0:128], [bkey], [(dname, tb)])

    def hgrn_head(self, l, j, h):
        io = self.io
        R, E = self.R, self.E
        wh = E["wh"]
        for gi, base in enumerate((832, 1856, 2880, 3904, 4928)):
            self.dma(wh[:, :, gi * 128:(gi + 1) * 128], io["w_in_ab"][j, :, base + h * 128:base + (h + 1) * 128].rearrange("(k p) m -> p k m", p=128),
                     (), [("wh", gi)], eng="pool")
        whk = [("wh", g) for g in range(5)]
        Q, SG, OT, Vh = E["Q"], E["SG"], E["OT"], E["Vh"]
        Fd = [E["F0"], E["F1"]]
        KKd = [E["KK0"], E["KK1"]]
        lo, hi = CTX0, LAT0 + NLAT
        for (c0, kind) in WINS:
            n = 256
            up, upk = R["upiece"].next()
            self.dma(up[:, :, :n], self.uT[:, c0:c0 + n].rearrange("(k p) n -> p k n", p=128), ck("uT", c0, n), [upk])
            bks = []
            for gi in range(5):
                bk, bkey = self.bank(0, 8)
                for k in range(16):
                    self.mm(bk[:, :n], wh[:, k, gi * 128:(gi + 1) * 128], up[:, k, :n], k == 0, k == 15, whk + [upk], [bkey])
                bks.append((bk, bkey))
            self.copy(Q[:, c0:c0 + n], bks[0][0][:, :n], [bks[0][1]], ["Q"], eng="act")
            for d in range(2):
                bk, bkey = bks[1 + d]
                lb, oml, lbk = self.lbs(j, d, h)
                self.act(Fd[d][:, c0:c0 + n], bk[:, :n], AF.Sigmoid, [bkey], ["F%d" % d])
                self.act(KKd[d][:, c0:c0 + n], bk[:, :n], AF.Sigmoid, [bkey], ["KK%d" % d], scale=-1.0)
                self.ts(Fd[d][:, c0:c0 + n], Fd[d][:, c0:c0 + n], oml, lb, ALU.mult, ALU.add, ["F%d" % d, lbk], ["F%d" % d])
                self.P.op("dve", lambda hh, o=KKd[d][:, c0:c0 + n], s=oml: hh.tensor_scalar_mul(out=o, in0=o, scalar1=s), ["KK%d" % d, lbk], ["KK%d" % d])
            self.act(SG[:, c0:c0 + n], bks[4][0][:, :n], AF.Silu, [bks[4][1]], ["SG"])
            vt, vtk = R["ub"].next()
            self.copy(vt[:, :n], bks[3][0][:, :n], [bks[3][1]], [vtk], eng="act")
            self.to_tokmajor(vt, vtk, c0, n, Vh, "Vh", blk=32)
        seqs = ((CTX0, 8), (LAT0, 64))
        CH, MID, LAST = 32, 15, 31

        def cv(t, s0, nch):
            return t[:, s0:s0 + nch * CH].rearrange("p (c t) -> p c t", t=CH)

        QT, KT, KH = [E["QT0"], E["QT1"]], [E["KT0"], E["KT1"]], [E["KH0"], E["KH1"]]
        MS, FDc = [E["MS0"], E["MS1"]], [E["FD0"], E["FD1"]]
        T1 = E["T1"]
        for d in range(2):
            F, KK = Fd[d], KKd[d]
            fk, kkk = "F%d" % d, "KK%d" % d
            self.act(F[:, lo:hi], F[:, lo:hi], AF.Ln, [fk], [fk])
            for (s0, nch) in seqs:
                self.P.op("dve", lambda hh, s0=s0, nch=nch, F=F: hh.tensor_tensor_scan(out=T1[:, s0:s0 + nch * CH], data0=E["maskR"][:, s0:s0 + nch * CH],
                                                                                  data1=F[:, s0:s0 + nch * CH], initial=0.0, op0=ALU.mult, op1=ALU.add),
                          [fk, "maskR"], ["T1"])
            ci0 = 0
            for (s0, nch) in seqs:
                G = cv(T1, s0, nch)
                Fv = cv(F, s0, nch)
                self.act(FDc[d][:, ci0:ci0 + nch], G[:, :, LAST], AF.Exp, ["T1"], ["FD%d" % d])
                if d == 0:
                    self.act(MS[d][:, ci0:ci0 + nch], G[:, :, MID], AF.Exp, ["T1"], ["MS%d" % d])
                    self.tt(Fv, G, G[:, :, MID:MID + 1].to_broadcast([128, nch, CH]), ALU.subtract, ["T1"], [fk])
                    self.tt(G, G[:, :, LAST:LAST + 1].to_broadcast([128, nch, CH]), G, ALU.subtract, ["T1"], ["T1"])
                else:
                    self.tt(Fv, G, Fv, ALU.subtract, ["T1", fk], [fk])
                    tmpc = E["tmpc"]
                    self.tt(tmpc[:, 0:nch], G[:, :, LAST], Fv[:, :, MID + 1], ALU.subtract, ["T1", fk], ["tmpc"])
                    self.act(MS[d][:, ci0:ci0 + nch], tmpc[:, 0:nch], AF.Exp, ["tmpc"], ["MS%d" % d])
                    self.tt(G, Fv[:, :, MID + 1:MID + 2].to_broadcast([128, nch, CH]), Fv, ALU.subtract, [fk], ["T1"])
                ci0 += nch
            (e1, e1k), (e3, e3k) = ((F, fk), (T1, "T1")) if d == 0 else ((T1, "T1"), (F, fk))
            self.act(e3[:, lo:hi], e3[:, lo:hi], AF.Exp, [e3k], [e3k])
            self.tt(KH[d][:, lo:hi], KK[:, lo:hi], e3[:, lo:hi], ALU.mult, [kkk, e3k], ["KH%d" % d])
            self.act(e3[:, lo:hi], e1[:, lo:hi], AF.Exp, [e1k], [e3k])
            self.tt(QT[d][:, lo:hi], Q[:, lo:hi], e3[:, lo:hi], ALU.mult, ["Q", e3k], ["QT%d" % d])
            self.act(e3[:, lo:hi], e1[:, lo:hi], AF.Exp, [e1k], [e3k], scale=-1.0)
            self.tt(KT[d][:, lo:hi], KK[:, lo:hi], e3[:, lo:hi], ALU.mult, [kkk, e3k], ["KT%d" % d])
        order = [list(range(18)), [1, 0] + list(range(17, 1, -1))]
        Sr = [E["Sr0"], E["Sr1"]]
        SMr = [E["SMr0"], E["SMr1"]]
        masks = [self.mask_f, self.mask_b]
        for d in range(2):
            self.memset(Sr[d][:, 0, :], 0.0, [("S", d, 0)], eng="dve")
        visited = set()

        def emit_PU(step, d):
            tb = order[d][step]
            col = blkcol(tb)
            pb0 = 4 * d
            atb = self.banks[pb0]
            trbb = self.banks[pb0 + 1].bitcast(BF16)
            ubk, ubkk = self.banks[pb0 + 3], ("pb", pb0 + 3)
            atm, kht = E["atm"][d], E["kht"][d]
            chs = (0, 1, 2, 3) if d == 0 else (3, 2, 1, 0)
            atk, trk = ("pb", pb0), ("pb", pb0 + 1)
            for ch in chs:
                cc = col + ch * 32
                self.mm(atb[0:32, ch * 32:(ch + 1) * 32], KT[d][:, cc:cc + 32], QT[d][:, cc:cc + 32], True, True, ["KT%d" % d, "QT%d" % d], [atk])
            for ch in chs:
                cc = col + ch * 32
                self.P.op("pe", lambda hh, trbb=trbb, d=d, cc=cc, ch=ch: hh.transpose(trbb[0:32, ch * 128:(ch + 1) * 128], KH[d][:, cc:cc + 32], self.id_b[:]),
                          ["KH%d" % d, "id_b"], [trk])
            self.tt(atm[:, step % 2, :, :], atb[0:32, 0:128].rearrange("p (c t) -> p c t", t=32), masks[d][:, :].unsqueeze(1).to_broadcast([32, 4, 32]),
                    ALU.mult, [atk, "mask%d" % d], [("atm", d, step % 2)])
            self.copy(kht[:, :, :], trbb[0:32, 0:512].rearrange("p (c t) -> p c t", t=128), [trk], [("kht", d)], eng="act")
            for qi, ch in enumerate(chs):
                ci = tb * 4 + ch
                self.mm(ubk[:, qi * 128:(qi + 1) * 128], kht[:, ch, :], Vh[:, ci, :], True, True, [("kht", d), ("Vh", ci)], [ubkk])

        def emit_S(step, d):
            tb = order[d][step]
            ubk, ubkk = self.banks[4 * d + 3], ("pb", 4 * d + 3)
            chs = (0, 1, 2, 3) if d == 0 else (3, 2, 1, 0)
            for qi, ch in enumerate(chs):
                ci = tb * 4 + ch
                g = step * 4 + qi
                self.act(SMr[d][:, g % 8, :], Sr[d][:, g % 8, :], AF.Identity, [("S", d, g % 8), "MS%d" % d], [("SM", d, g % 8)], scale=MS[d][:, ci:ci + 1])
                self.stt(Sr[d][:, (g + 1) % 8, :], Sr[d][:, g % 8, :], FDc[d][:, ci:ci + 1], ubk[:, qi * 128:(qi + 1) * 128], ALU.mult, ALU.add,
                         [("S", d, g % 8), "FD%d" % d, ubkk], [("S", d, (g + 1) % 8)])

        def emit_O(step, d):
            tb = order[d][step]
            col = blkcol(tb)
            ob, obk = self.banks[4 * d + 2], ("pb", 4 * d + 2)
            atm = E["atm"][d]
            chs = (0, 1, 2, 3) if d == 0 else (3, 2, 1, 0)
            for qi, ch in enumerate(chs):
                ci = tb * 4 + ch
                cc = col + ch * 32
                g = step * 4 + qi
                self.mm(ob[:, ch * 32:(ch + 1) * 32], Vh[:, ci, :], atm[:, step % 2, ch, :], True, False, [("Vh", ci), ("atm", d, step % 2)], [obk])
                self.mm(ob[:, ch * 32:(ch + 1) * 32], SMr[d][:, g % 8, :], QT[d][:, cc:cc + 32], False, True, [("SM", d, g % 8), "QT%d" % d], [obk])
            if tb not in visited:
                visited.add(tb)
                self.copy(OT[:, col:col + 128], ob[:, 0:128], [obk], [("OT", tb)], eng="act")
            else:
                self.tt(OT[:, col:col + 128], ob[:, 0:128], OT[:, col:col + 128], ALU.add, [obk, ("OT", tb)], [("OT", tb)])

        for step in range(18):
            for d in range(2):
                emit_PU(step, d)
            for d in range(2):
                emit_S(step, d)
            if step >= 1:
                for d in range(2):
                    emit_O(step - 1, d)
        for d in range(2):
            emit_O(17, d)
        hg_o = PVL.off["hgn%d" % j]
        otk = [("OT", tb) for tb in range(18)]
        for (c0, n, kind) in PIECES:
            sq, sqk = R["sq"].next()
            self.act(sq[:, :n], OT[:, c0:c0 + n], AF.Square, otk, [sqk])
            bk, bkey = self.bank()
            self.mm(bk[:, :n], self.ones_f[:], sq[:, :n], True, True, ["ones_f", sqk], [bkey])
            rs, rsk = R["rstd"].next()
            self.rsqrt_from(rs[:, :n], bk[:, :n], 1.0 / 128, [bkey, "epsb"], [rsk])
            tmp, tk = R["tmpf"].next()
            self.tt(tmp[:, :n], OT[:, c0:c0 + n], rs[:, :n], ALU.mult, otk + [rsk], [tk])
            ub, uk = R["ub"].next()
            self.stt(ub[:, :n], tmp[:, :n], self.pv[:, hg_o:hg_o + 1], SG[:, c0:c0 + n], ALU.mult, ALU.mult, [tk, "pv", "SG"], [uk])
            self.dma(self.hT[(8 + h) * 128:(9 + h) * 128, c0:c0 + n], ub[:, :n], [uk], ck("hT", c0, n))


_WSHAPES = {
    "w_mod": [DEPTH, D, 6 * D], "w_in_ab": [2, D, 5952], "w_kr_sw": [2, D, 64], "w_q_full": [2, 512, 2048],
    "w_kv_up": [2, 256, 2048], "w_out_ab": [2, D, D], "conv_w_pw1": [2, D, 2 * D], "conv_w_pw2": [2, D, D],
    "ffn_w_up": [DEPTH, D, 2 * DFF], "ffn_w_down": [DEPTH, DFF, D],
}


def build_program(stop=None, layers=None):
    nc = bass.Bass("TRN2", target_bir_lowering=False)
    io = {}
    io["xT0"] = nc.dram_tensor("xT0", [D, NCOL], F32, kind="ExternalInput").ap()
    io["pv"] = nc.dram_tensor("pv", [128, PVL.n], F32, kind="ExternalInput").ap()
    io["rope"] = nc.dram_tensor("rope", [64, 2, NCOL], F32, kind="ExternalInput").ap()
    for k, shp in _WSHAPES.items():
        io[k] = nc.dram_tensor(k, shp, F32, kind="ExternalInput").ap()
    outT = nc.dram_tensor("outT", [D, NCOL], F32, kind="ExternalOutput").ap()
    dbg_h = nc.dram_tensor("dbg_h", [D, NCOL], BF16, kind="ExternalOutput").ap() if stop is not None else None
    dbg_u = nc.dram_tensor("dbg_u", [D, NCOL], BF16, kind="ExternalOutput").ap() if stop is not None else None
    st = contextlib.ExitStack()
    with st:
        kb = KB(nc, st)
        kb.setup(io)
        R = kb.R
        R["sq"] = Rot(kb, "sq", 4, [128, 512], F32)
        R["tmpf"] = Rot(kb, "tmpf", 4, [128, 512], F32)
        R["ub"] = Rot(kb, "ub", 4, [128, 512], BF16)
        R["rstd"] = Rot(kb, "rstd", 3, [128, 512], F32)
        R["rstd768"] = kb.sb("rstd768", [128, 768], F32)
        kb.arena = kb.sb("arena", [128, ARENA_BYTES // 2], BF16)
        kb.stop_at = stop[1] if stop else None
        if kb.stop_at != "setup":
            kb.phase("mod")
            kb.modulation(R["wup"])
        for l in (range(DEPTH) if layers is None else layers):
            if kb.stop_at in ("setup", "mod"):
                break
            if l % 2 == 0:
                kb.even(l)
            else:
                kb.conformer(l)
            if stop is not None and stop[0] == l and stop[1] != "ffn":
                break
            kb.ffn(l)
            if stop == (l, "ffn"):
                break
        kb.P.barrier()
        for c in range(16):
            kb.dma(outT[c * 128:(c + 1) * 128, :], kb.xT[c * 128:(c + 1) * 128, :], (), [("outT", c)])
            if dbg_h is not None:
                kb.dma(dbg_h[c * 128:(c + 1) * 128, :], kb.hT[c * 128:(c + 1) * 128, :], (), [("dbgh", c)])
                kb.dma(dbg_u[c * 128:(c + 1) * 128, :], kb.uT[c * 128:(c + 1) * 128, :], (), [("dbgu", c)])
        kb.P.emit(st)
    return nc


ARENA_BYTES = 156 * 1024


class Carver:
    def __init__(self, arena):
        self.arena = arena
        self.off = 0

    def take(self, shape, dtype):
        free = int(np.prod(shape[1:]))
        nbytes = free * (4 if dtype == F32 else 2)
        assert self.off + nbytes <= ARENA_BYTES, (self.off, nbytes)
        ap = self.arena[0:shape[0], self.off // 2:(self.off + nbytes) // 2]
        if dtype == F32:
            ap = ap.bitcast(F32)
        self.off += (nbytes + 63) // 64 * 64
        names = "abcdefg"[:len(shape) - 1]
        if len(shape) > 2:
            pat = "p (%s) -> p %s" % (" ".join(names), " ".join(names))
            ap = ap.rearrange(pat, **{names[i]: shape[1 + i] for i in range(1, len(shape) - 1)})
        return ap

    def rot(self, name, n, shape, dtype):
        return RotAP([(self.take(shape, dtype), "%s%d" % (name, i)) for i in range(n)])


class RotAP:
    def __init__(self, items):
        self.items = items
        self.i = 0

    def next(self):
        it = self.items[self.i % len(self.items)]
        self.i += 1
        return it


def rope_tables_host():
    tab = np.zeros((64, 2, NCOL), np.float32)
    tab[:, 0, :] = 1.0
    t = np.arange(NLAT)
    pos = np.stack([t // 64, t % 64], 0).astype(np.float32)
    freqs = (10000.0 ** (-np.arange(16, dtype=np.float32) / 16)).astype(np.float32)
    for i in range(64):
        a, half, p = i // 32, (i % 32) // 16, i % 16
        ang = (pos[a] * freqs[p]).astype(np.float32)
        tab[i, 0, LAT0:LAT0 + NLAT] = np.cos(ang)
        tab[i, 1, LAT0:LAT0 + NLAT] = np.sin(ang) * (-1.0 if half == 0 else 1.0)
    return tab


SWAP = np.array([(i // 32) * 32 + (1 - (i % 32) // 16) * 16 + i % 16 for i in range(64)])


def host_inputs(inputs, b):
    f = lambda k: np.asarray(inputs[k], np.float32)
    m = {}
    xT0 = np.zeros((D, NCOL), np.float32)
    xT0[:, CTX0:CTX0 + NCTX] = f("ctx")[b].T
    xT0[:, LAT0:LAT0 + NLAT] = f("x")[b].T
    m["xT0"] = xT0
    pv = np.zeros((128, PVL.n), np.float32)

    def put(name, arr):
        o = PVL.off[name]
        pv[:, o:o + arr.shape[1]] = arr

    cc = np.stack([fm(f("c")[b]), fm(f("c_ctx"))], -1).reshape(128, 32)
    put("cc", cc)
    for l in range(DEPTH):
        put("bmod%d" % l, fm(f("b_mod")[l]))
        put("gains%d" % l, np.concatenate([fm(f("norm_gains")[l, i]) for i in range(4)], 1))
        put("fwc%d" % l, np.concatenate([fm(f("ffn_w_conv")[l, k]) for k in range(3)], 1))
        put("fbc%d" % l, fm(f("ffn_b_conv")[l]))
    for j in range(2):
        put("qn%d" % j, fm(f("mla_q_norm")[j]))
        put("kvn%d" % j, fm(f("mla_kv_norm")[j]))
        put("hgn%d" % j, fm(f("hgrn_norm")[j]))
        put("bpw1%d" % j, fm(f("conv_b_pw1")[j]))
        wd = f("conv_w_dw")[j]
        put("wdw%d" % j, np.concatenate([fm(wd[k]) for k in range(31)], 1))
        put("bdw%d" % j, fm(f("conv_b_dw")[j]))
        put("lng%d" % j, fm(f("conv_ln_g")[j]))
        put("lnb%d" % j, fm(f("conv_ln_b")[j]))
        put("bpw2%d" % j, fm(f("conv_b_pw2")[j]))
    hl = f("hgrn_lb")
    put("hlb", np.concatenate([fm(hl[d, jj]) for d in range(2) for jj in range(2)], 1))
    m["pv"] = pv
    return m


def shared_inputs(inputs):
    f = lambda k: np.ascontiguousarray(np.asarray(inputs[k], np.float32))
    m = {k: f(k) for k in ("w_mod", "w_in_ab", "w_kv_up", "w_out_ab", "conv_w_pw1", "conv_w_pw2", "ffn_w_up", "ffn_w_down")}
    m["w_kr_sw"] = np.ascontiguousarray(m["w_in_ab"][:, :, 768:832][:, :, SWAP])
    wq = f("w_q_up").reshape(2, 512, 8, 192)
    m["w_q_full"] = np.ascontiguousarray(np.concatenate([wq, wq[:, :, :, 128:][:, :, :, SWAP]], -1).reshape(2, 512, 2048))
    m["rope"] = rope_tables_host()
    return m


_NC_CACHE = {}


def kernel(**inputs):
    n = 4
    if "full" not in _NC_CACHE:
        _NC_CACHE["full"] = build_program()
    nc = _NC_CACHE["full"]
    sh = shared_inputs(inputs)
    in_maps = []
    for b in range(n):
        m = dict(sh)
        m.update(host_inputs(inputs, b))
        in_maps.append(m)
    res = run_bass_kernel_spmd(nc, in_maps, core_ids=list(range(n)))
    out = np.stack([np.ascontiguousarray(res.results[b]["outT"][:, LAT0:LAT0 + NLAT].T) for b in range(n)], 0)
    return out.astype(np.float32)
```

```python
import contextlib
import numpy as np
import concourse.bass as bass
import concourse.mybir as mybir
from concourse.bass_utils import run_bass_kernel_spmd

dt = mybir.dt
F32 = dt.float32
BF16 = dt.bfloat16
AF = mybir.ActivationFunctionType
ALU = mybir.AluOpType

ENG_NAMES = ("pe", "act", "dve", "pool", "sp")

D = 2048
DEPTH = 4
NCTX = 256
NLAT = 2048
DFF = 5632
EPS = 1e-6
ATTN_SCALE = 192.0 ** -0.5
NCOL = 2352
CTX0 = 16
LAT0 = 288
PIECES = [(CTX0, 256, 1)] + [(LAT0 + 512 * i, 512, 0) for i in range(4)]
WINS = [(CTX0, 1)] + [(LAT0 + 256 * i, 0) for i in range(8)]
ST_RUNS = [[(0, CTX0, 256, 1), (256, LAT0, 512, 0)],
           [(0, 800, 512, 0), (512, 1312, 256, 0)],
           [(0, 1568, 512, 0), (512, 2080, 256, 0)]]
CELL_BOUNDS = np.array([0, 16, 272] + [288 + 256 * i for i in range(9)])


def cells(c0, n):
    a = int(np.searchsorted(CELL_BOUNDS, c0, side="right") - 1)
    b = int(np.searchsorted(CELL_BOUNDS, c0 + n - 1, side="right") - 1)
    return list(range(a, b + 1))


def ck(name, c0, n):
    return [(name, i) for i in cells(c0, n)]


def blkcol(tb):
    return CTX0 + 128 * tb if tb < 2 else LAT0 + 128 * (tb - 2)


class Op:
    __slots__ = ("eng", "fn", "deps", "signal", "tok", "lane")

    def __init__(self, eng, fn, lane=None):
        self.eng = eng
        self.fn = fn
        self.deps = []
        self.signal = lane is not None
        self.tok = None
        self.lane = lane


class Reg:
    __slots__ = ("w", "r")

    def __init__(self):
        self.w = None
        self.r = []


class Prog:
    def __init__(self, nc, n_lanes=8):
        self.nc = nc
        self.ops = {e: [] for e in ENG_NAMES}
        self.regs = {}
        self.n_lanes = n_lanes
        self.lane_last = {}
        self.lane_rr = {e: 0 for e in ENG_NAMES}

    def reg(self, key):
        r = self.regs.get(key)
        if r is None:
            r = self.regs[key] = Reg()
        return r

    def _add(self, op, reads, writes):
        deps = op.deps
        for k in reads:
            r = self.reg(k)
            if r.w is not None:
                deps.append(r.w)
        for k in writes:
            r = self.reg(k)
            if r.w is not None:
                deps.append(r.w)
            deps.extend(r.r)
        for k in reads:
            self.reg(k).r.append(op)
        for k in writes:
            r = self.reg(k)
            r.w = op
            r.r = []
        if op.eng == "pe":
            op.deps = deps = [d for d in deps if d.eng != "pe" or d.lane is not None]
        for d in deps:
            d.signal = True
        self.ops[op.eng].append(op)
        return op

    def op(self, eng, fn, reads=(), writes=()):
        return self._add(Op(eng, fn), reads, writes)

    def dma(self, eng, fn, reads=(), writes=()):
        lane = (eng, self.lane_rr[eng] % self.n_lanes)
        self.lane_rr[eng] += 1
        op = Op(eng, fn, lane=lane)
        prev = self.lane_last.get(lane)
        if prev is not None:
            op.deps.append(prev)
            prev.signal = True
        self.lane_last[lane] = op
        return self._add(op, reads, writes)

    def barrier(self):
        lasts = list(self.lane_last.values())
        for e in ENG_NAMES:
            for o in reversed(self.ops[e]):
                if o.fn is not None:
                    lasts.append(o)
                    break
        for e in ENG_NAMES:
            op = Op(e, None)
            op.deps = [d for d in lasts if not (d.eng == e and d.lane is None)]
            for d in op.deps:
                d.signal = True
            self.ops[e].append(op)
        self.regs = {}

    def emit(self, stack):
        nc = self.nc
        sems = {e: stack.enter_context(nc.semaphore("sem_" + e)) for e in ENG_NAMES}
        lane_sems = {}
        lane_cnt = {}
        for e in ENG_NAMES:
            for i in range(self.n_lanes):
                if (e, i) in self.lane_last:
                    lane_sems[(e, i)] = stack.enter_context(nc.semaphore("lane_%s%d" % (e, i)))
                    lane_cnt[(e, i)] = 0
        for e in ENG_NAMES:
            c = 0
            for op in self.ops[e]:
                if op.lane is not None:
                    lane_cnt[op.lane] += 16
                    op.tok = (lane_sems[op.lane], lane_cnt[op.lane])
                elif op.signal:
                    c += 1
                    op.tok = (sems[e], c)
        final_waits = [(lane_sems[l], lane_cnt[l]) for l in lane_sems]
        block = stack.enter_context(nc.Block())
        ops = self.ops

        def replay(e, h):
            seen = {}
            for op in ops[e]:
                for d in op.deps:
                    sem, val = d.tok
                    k = sem.num
                    if seen.get(k, 0) < val:
                        seen[k] = val
                        h.wait_ge(sem, val)
                if op.fn is None:
                    continue
                inst = op.fn(h)
                if op.lane is not None:
                    inst.then_inc(op.tok[0], 16)
                elif op.signal:
                    inst.then_inc(op.tok[0], 1)
            if e == "sp":
                for sem, val in final_waits:
                    if val:
                        h.wait_ge(sem, val)

        @block.tensor
        def _(h):
            replay("pe", h)

        @block.scalar
        def _(h):
            replay("act", h)

        @block.vector
        def _(h):
            replay("dve", h)

        @block.gpsimd
        def _(h):
            replay("pool", h)

        @block.sync
        def _(h):
            replay("sp", h)


class Rot:
    def __init__(self, kb, name, n, shape, dtype):
        self.items = [(kb.sb("%s%d" % (name, i), shape, dtype), "%s%d" % (name, i)) for i in range(n)]
        self.i = 0

    def next(self):
        it = self.items[self.i % len(self.items)]
        self.i += 1
        return it


class PVLayout:
    def __init__(self):
        self.off = {}
        self.n = 0

    def add(self, name, ncols):
        self.off[name] = self.n
        self.n += ncols


def pv_layout():
    L = PVLayout()
    L.add("cc", 32)
    for l in range(DEPTH):
        L.add("bmod%d" % l, 96)
        L.add("gains%d" % l, 64)
        L.add("fwc%d" % l, 3 * 88)
        L.add("fbc%d" % l, 88)
    for j in range(2):
        L.add("qn%d" % j, 4)
        L.add("kvn%d" % j, 2)
        L.add("hgn%d" % j, 1)
        L.add("bpw1%d" % j, 32)
        L.add("wdw%d" % j, 31 * 16)
        L.add("bdw%d" % j, 16)
        L.add("lng%d" % j, 16)
        L.add("lnb%d" % j, 16)
        L.add("bpw2%d" % j, 16)
    L.add("hlb", 32)
    return L


PVL = pv_layout()


def fm(v):
    v = np.asarray(v, np.float32)
    return np.ascontiguousarray(v.reshape(-1, 128).T)


class KB:
    def __init__(self, nc, st):
        self.nc = nc
        self.st = st
        self.P = Prog(nc)
        self.banks = [st.enter_context(nc.psum_tensor("pb%d" % i, [128, 512], F32)) for i in range(8)]
        self.brr = 0
        self.R = {}
        self.E = {}

    def phase(self, name):
        self.P.barrier()
        C = Carver(self.arena)
        R, E = self.R, self.E
        if name == "mod":
            R["wup"] = C.rot("wup", 2, [128, 16, 512], BF16)
        elif name in ("prenorm", "ln"):
            R["xin"] = C.take([128, 16, 512], F32)
        elif name == "ffn":
            R["aT44"] = C.take([128, 44, 768], BF16)
            R["uw"] = [C.take([128, 16, 258], BF16) for _ in range(3)]
            R["hvg"] = C.rot("hvg", 6, [128, 256], F32)
            R["wup"] = C.rot("wup", 2, [128, 16, 512], BF16)
            R["wfin"] = C.rot("wfin", 2, [128, 44, 128], BF16)
            R["rstd768"] = [C.take([128, 768], F32) for _ in range(2)]
        elif name == "final16":
            R["aT16"] = C.take([128, 16, 768], BF16)
            R["wfin"] = C.rot("wfin", 2, [128, 16, 256], BF16)
            R["rstd768"] = [C.take([128, 768], F32) for _ in range(2)]
        elif name == "odd1":
            R["upiece"] = C.rot("upiece", 2, [128, 16, 512], BF16)
            R["hfull"] = C.rot("hfull", 2, [128, NCOL], BF16)
            R["dg"] = C.rot("dg", 2, [128, 31, 128], BF16)
            R["wup"] = C.rot("wup", 2, [128, 16, 256], BF16)
            for ap, key in R["hfull"].items:
                self.memset(ap, 0.0, [key], eng="dve")
        elif name in ("mla_prep", "mla_heads"):
            E["rope"] = C.take([64, 2, NCOL], F32)
            E["cqn"] = C.take([128, 4, NCOL], BF16)
            E["ckvn"] = C.take([128, 2, NCOL], BF16)
            E["krot"] = C.take([64, NCOL], BF16)
            if name == "mla_prep":
                R["upiece"] = C.rot("upiece", 2, [128, 16, 512], BF16)
                E["wprep"] = C.take([128, 16, 896], BF16)
                E["cf"] = C.take([128, 4, 512], F32)
            else:
                E["wq"] = C.take([128, 4, 2048], BF16)
                E["wkv"] = C.take([128, 2, 2048], BF16)
                E["qn"] = C.take([128, NCOL], BF16)
                E["qrot"] = C.take([64, NCOL], BF16)
                E["kn"] = C.take([128, NCOL], BF16)
                E["V"] = C.take([128, 18, 128], BF16)
        elif name == "hgrn":
            R["upiece"] = C.rot("upiece", 3, [128, 16, 256], BF16)
            E["wh"] = C.take([128, 16, 640], BF16)
            E["OT"] = C.take([128, NCOL], F32)
            for nm in ("SG", "QT0", "QT1", "KT0", "KT1", "KH0", "KH1"):
                E[nm] = C.take([128, NCOL], BF16)
            E["Vh"] = C.take([32, 72, 128], BF16)
            for nm in ("MS0", "MS1", "FD0", "FD1", "tmpc"):
                E[nm] = C.take([128, 72], F32)
            for nm in ("Sr0", "Sr1"):
                E[nm] = C.take([128, 8, 128], F32)
            for nm in ("SMr0", "SMr1"):
                E[nm] = C.take([128, 8, 128], BF16)
            E["atm"] = [C.take([32, 2, 4, 32], BF16) for _ in range(2)]
            E["kht"] = [C.take([32, 4, 128], BF16) for _ in range(2)]
            E["gt"] = [[C.take([128, 256], F32) for _ in range(7)] for _ in range(3)]
            E["maskP"] = C.take([128, 256], F32)
            self.memset(E["maskP"], 1.0, ["maskP"], eng="dve")
            self.memset(E["maskP"].rearrange("p (c t) -> p c t", t=32)[:, :, 0:1], 0.0, ["maskP"], eng="dve")
        else:
            raise ValueError(name)

    def sb(self, name, shape, dtype):
        return self.st.enter_context(self.nc.sbuf_tensor("s_" + name, shape, dtype))

    def dram(self, name, shape, dtype):
        return self.nc.dram_tensor(name, shape, dtype, kind="Internal").ap()

    def bank(self, lo=0, hi=6):
        i = lo + self.brr % (hi - lo)
        self.brr += 1
        return self.banks[i], ("pb", i)

    def mm(self, out, lhsT, rhs, start, stop, reads, writes):
        self.P.op("pe", lambda h: h.matmul(out, lhsT=lhsT, rhs=rhs, start=start, stop=stop), reads, writes)

    def act(self, out, in_, func, reads, writes, bias=0.0, scale=1.0):
        self.P.op("act", lambda h: h.activation(out=out, in_=in_, func=func, bias=bias, scale=scale), reads, writes)

    def tt(self, out, in0, in1, op, reads, writes, eng="dve"):
        self.P.op(eng, lambda h: h.tensor_tensor(out=out, in0=in0, in1=in1, op=op), reads, writes)

    def stt(self, out, in0, scalar, in1, op0, op1, reads, writes):
        self.P.op("dve", lambda h: h.scalar_tensor_tensor(out=out, in0=in0, scalar=scalar, in1=in1, op0=op0, op1=op1), reads, writes)

    def ts(self, out, in0, s1, s2, op0, op1, reads, writes):
        self.P.op("dve", lambda h: h.tensor_scalar(out=out, in0=in0, scalar1=s1, scalar2=s2, op0=op0, op1=op1), reads, writes)

    def recip(self, out, in_, reads, writes):
        self.P.op("dve", lambda h: h.reciprocal(out=out, in_=in_), reads, writes)

    def copy(self, out, in_, reads, writes, eng="dve"):
        if eng == "act":
            self.P.op("act", lambda h: h.activation(out=out, in_=in_, func=AF.Identity), reads, writes)
        else:
            self.P.op(eng, lambda h: h.tensor_copy(out=out, in_=in_), reads, writes)

    def memset(self, ap, val, writes, eng="pool"):
        self.P.op(eng, lambda h: h.memset(ap, val), (), writes)

    def dma(self, out, in_, reads, writes, eng="sp"):
        self.P.dma(eng, lambda h: h.dma_start(out=out, in_=in_), reads, writes)

    def rsqrt_from(self, out, in_, scale, reads, writes):
        self.act(out, in_, AF.Sqrt, reads, writes, bias=self.epsb[:, 0:1], scale=scale)
        self.recip(out, out, writes, writes)

    def setup(self, io):
        self.io = io
        P = self.P
        self.xT = self.dram("xT_d", [D, NCOL], F32)
        self.uT = self.dram("uT_d", [D, NCOL], BF16)
        self.hT = self.dram("hT_d", [D, NCOL], BF16)
        self.hc = self.dram("hc_d", [D, NCOL], F32)
        self.yT2 = [self.dram("yT_d%d" % i, [D, 768], F32) for i in range(2)]
        self.pv = self.sb("pv", [128, PVL.n], F32)
        self.ones_f = self.sb("ones_f", [128, 128], F32)
        self.ones_b = self.sb("ones_b", [128, 128], BF16)
        self.id_f = self.sb("id_f", [128, 128], F32)
        self.id_b = self.sb("id_b", [128, 128], BF16)
        self.epsb = self.sb("epsb", [128, 1], F32)
        self.zer = self.sb("zer", [128, 16], BF16)
        self.mods = self.sb("mods", [128, DEPTH * 6 * 16 * 2], F32)
        self.modT = self.sb("modT", [128, DEPTH * 96 * 2], F32)
        self.scb = self.sb("scb", [128, 32], BF16)
        self.lbt = self.sb("lbt", [128, 2 * 8 * 2], F32)
        self.lb0 = self.sb("lb0", [128, 2], F32)
        self.mask_f = self.sb("mask_f", [32, 32], F32)
        self.mask_b = self.sb("mask_b", [32, 32], F32)
        self.dma(self.pv[:], io["pv"], (), ["pv"])
        self.memset(self.ones_f[:], 1.0, ["ones_f"])
        self.memset(self.ones_b[:], 1.0, ["ones_b"])
        self.memset(self.epsb[:], EPS, ["epsb"])
        self.memset(self.zer[:], 0.0, ["zer"])
        self.memset(self.id_f[:], 0.0, ["id_f"])
        P.op("pool", lambda h: h.affine_select(out=self.id_f[:], in_=self.id_f[:], pattern=[[-1, 128]], compare_op=ALU.not_equal,
                                               fill=1.0, base=0, channel_multiplier=1), ["id_f"], ["id_f"])
        self.copy(self.id_b[:], self.id_f[:], ["id_f"], ["id_b"])
        for di, (m, pstep, cmul) in enumerate(((self.mask_f, 1, -1), (self.mask_b, -1, 1))):
            key = "mask%d" % di
            self.memset(m[:], 1.0, [key])
            P.op("pool", lambda h, m=m, ps=pstep, cm=cmul: h.affine_select(out=m[:], in_=m[:], pattern=[[ps, 32]], compare_op=ALU.is_ge,
                                                                            fill=0.0, base=0, channel_multiplier=cm), [key], [key])
        for c in range(16):
            self.dma(self.xT[c * 128:(c + 1) * 128, :], io["xT0"][c * 128:(c + 1) * 128, :], (), [("xT", i) for i in range(12)])
        for c in range(16):
            for p0 in (0, 272, 2336):
                self.dma(self.uT[c * 128:(c + 1) * 128, p0:p0 + 16], self.zer[:], ["zer"], ck("uT", p0, 16))
        o = PVL.off["cc"]
        self.act(self.scb[:], self.pv[:, o:o + 32], AF.Silu, ["pv"], ["scb"])
        o = PVL.off["hlb"]
        hl = self.pv[:, o:o + 32].rearrange("p (d j h) -> p d j h", d=2, j=2)
        lbv = self.lbt[:].rearrange("p (d h t) -> p d h t", d=2, h=8)
        self.tt(lbv[:, :, :, 0], hl[:, :, 1, :], hl[:, :, 0, :], ALU.subtract, ["pv"], ["lbt"])
        self.act(lbv[:, :, :, 1], lbv[:, :, :, 0], AF.Sigmoid, ["lbt"], ["lbt"], scale=-1.0)
        self.act(lbv[:, :, :, 0], lbv[:, :, :, 0], AF.Sigmoid, ["lbt"], ["lbt"])
        self.memset(self.lb0[:, 0:1], 0.0, ["lb0"])
        self.memset(self.lb0[:, 1:2], 1.0, ["lb0"])

    def lbs(self, j, d, h):
        if j == 0:
            return self.lb0[:, 0:1], self.lb0[:, 1:2], "lb0"
        o = (d * 8 + h) * 2
        return self.lbt[:, o:o + 1], self.lbt[:, o + 1:o + 2], "lbt"

    def mod(self, l, which, c, kind):
        o = (((l * 6 + which) * 16 + c) * 2) + kind
        return self.mods[:, o:o + 1]

    def modulation(self, wrot):
        io = self.io
        scv = self.scb[:].rearrange("p (c k) -> p c k", k=2)
        for l in range(DEPTH):
            bk, bkey = self.banks[7], ("pb", 7)
            for blk in range(24):
                wt, wkey = wrot.next()
                self.dma(wt[:, :, :], io["w_mod"][l, :, blk * 512:(blk + 1) * 512].rearrange("(k p) m -> p k m", p=128), (), [wkey], eng="pool")
                for jj in range(4):
                    oc = blk * 4 + jj
                    for k in range(16):
                        self.mm(bk[:, oc * 2:oc * 2 + 2], wt[:, k, jj * 128:(jj + 1) * 128], scv[:, k, :], k == 0, k == 15,
                                [wkey, "scb"], [bkey])
            o = PVL.off["bmod%d" % l]
            mt = self.modT[:, l * 192:(l + 1) * 192].rearrange("p (o k) -> p o k", k=2)
            self.tt(mt, bk[:, 0:192].rearrange("p (o k) -> p o k", k=2), self.pv[:, o:o + 96].unsqueeze(2).to_broadcast([128, 96, 2]),
                    ALU.add, [bkey, "pv"], ["modT"])
            go = PVL.off["gains%d" % l]
            md = self.mods[:, l * 192:(l + 1) * 192].rearrange("p (w c k) -> p w c k", w=6, c=16)
            mt6 = self.modT[:, l * 192:(l + 1) * 192].rearrange("p (w c k) -> p w c k", w=6, c=16)
            for (dst, gi, mi, plus1) in ((0, 0, 1, True), (2, 1, 2, False), (3, 2, 4, True), (5, 3, 5, False)):
                gb = self.pv[:, go + gi * 16:go + (gi + 1) * 16].unsqueeze(2).to_broadcast([128, 16, 2])
                if plus1:
                    self.stt(md[:, dst], mt6[:, mi], 1.0, gb, ALU.add, ALU.mult, ["modT", "pv"], ["mods"])
                else:
                    self.tt(md[:, dst], mt6[:, mi], gb, ALU.mult, ["modT", "pv"], ["mods"])
            self.copy(md[:, 1], mt6[:, 0], ["modT"], ["mods"])
            self.copy(md[:, 4], mt6[:, 3], ["modT"], ["mods"])

    def prenorm(self, l, sub):
        self.phase("prenorm")
        R = self.R
        wS, wB = (0, 1) if sub == 1 else (3, 4)
        for (c0, n, kind) in PIECES:
            xin = R["xin"]
            for c in range(16):
                self.dma(xin[:, c, :n], self.xT[c * 128:(c + 1) * 128, c0:c0 + n], ck("xT", c0, n), [("xin", c)])
            bk, bkey = self.bank()
            for c in range(16):
                sq, sqk = R["sq"].next()
                self.act(sq[:, :n], xin[:, c, :n], AF.Square, [("xin", c)], [sqk])
                self.mm(bk[:, :n], self.ones_f[:], sq[:, :n], c == 0, c == 15, ["ones_f", sqk], [bkey])
            rs, rsk = R["rstd"].next()
            self.rsqrt_from(rs[:, :n], bk[:, :n], 1.0 / D, [bkey, "epsb"], [rsk])
            for c in range(16):
                tmp, tk = R["tmpf"].next()
                self.tt(tmp[:, :n], xin[:, c, :n], rs[:, :n], ALU.mult, [("xin", c), rsk], [tk])
                ub, uk = R["ub"].next()
                self.act(ub[:, :n], tmp[:, :n], AF.Identity, [tk, "mods"], [uk], bias=self.mod(l, wB, c, kind), scale=self.mod(l, wS, c, kind))
                self.dma(self.uT[c * 128:(c + 1) * 128, c0:c0 + n], ub[:, :n], [uk], ck("uT", c0, n))

    def final_mm(self, l, sub, s, slot, aT, akeys, Kc, w_ap, bias_off, wblk, pump=None):
        R = self.R
        parts = [(0, 512, 6), (512, 256, 7)]
        yT = self.yT2[slot]
        nj = wblk // 128
        for blk in range(D // wblk):
            wt, wkey = R["wfin"].next()
            self.dma(wt[:, :Kc, :], w_ap[:, blk * wblk:(blk + 1) * wblk].rearrange("(k p) m -> p k m", p=128), (), [wkey], eng="pool")
            for jj in range(nj):
                c = blk * nj + jj
                for (n0, nn, sb_) in parts:
                    bk, bkey = self.bank()
                    for k in range(Kc):
                        self.mm(bk[:, :nn], wt[:, k, jj * 128:(jj + 1) * 128], aT[:, k, n0:n0 + nn], k == 0, k == Kc - 1,
                                [wkey] + akeys, [bkey])
                    ysb, yk = R["tmpf"].next()
                    bias = 0.0 if bias_off is None else self.pv[:, bias_off + c:bias_off + c + 1]
                    self.act(ysb[:, :nn], bk[:, :nn], AF.Identity, [bkey, "pv"], [yk], bias=bias)
                    sq, sqk = R["sq"].next()
                    self.act(sq[:, :nn], ysb[:, :nn], AF.Square, [yk], [sqk])
                    self.mm(self.banks[sb_][:, :nn], self.ones_f[:], sq[:, :nn], c == 0, c == 15, ["ones_f", sqk], [("pb", sb_)])
                    self.dma(yT[c * 128:(c + 1) * 128, n0:n0 + nn], ysb[:, :nn], [yk], [("yT", slot, c)])
                    if pump is not None:
                        pump()
        rs = R["rstd768"][slot]
        self.rsqrt_from(rs[:, 0:512], self.banks[6][:, :512], 1.0 / D, [("pb", 6), "epsb"], [("rs768a", slot)])
        self.rsqrt_from(rs[:, 512:768], self.banks[7][:, :256], 1.0 / D, [("pb", 7), "epsb"], [("rs768b", slot)])

    def final_tail(self, l, sub, s, slot):
        R = self.R
        wG = 2 if sub == 1 else 5
        yT = self.yT2[slot]
        rs = R["rstd768"][slot]
        for c in range(16):
            for (lo, c0, n, kind) in ST_RUNS[s]:
                yl, ylk = R["tmpf"].next()
                xl, xlk = R["sq"].next()
                self.dma(yl[:, :n], yT[c * 128:(c + 1) * 128, lo:lo + n], [("yT", slot, c)], [ylk])
                self.dma(xl[:, :n], self.xT[c * 128:(c + 1) * 128, c0:c0 + n], ck("xT", c0, n), [xlk])
                self.tt(yl[:, :n], yl[:, :n], rs[:, lo:lo + n], ALU.mult, [ylk, ("rs768a", slot), ("rs768b", slot)], [ylk])
                self.stt(xl[:, :n], yl[:, :n], self.mod(l, wG, c, kind), xl[:, :n], ALU.mult, ALU.add, [ylk, xlk, "mods"], [xlk])
                self.dma(self.xT[c * 128:(c + 1) * 128, c0:c0 + n], xl[:, :n], [xlk], ck("xT", c0, n))
                yield

    def final_from_dram(self, l, sub, w_ap, bias_off):
        self.phase("final16")
        R = self.R
        tail = None

        def pump():
            if tail is not None:
                next(tail, None)
                next(tail, None)

        for s in range(3):
            aT = R["aT16"]
            for (lo, c0, n, kind) in ST_RUNS[s]:
                self.dma(aT[:, :, lo:lo + n], self.hT[:, c0:c0 + n].rearrange("(k p) n -> p k n", p=128), ck("hT", c0, n), [("aT16", lo)])
            self.final_mm(l, sub, s, s % 2, aT, [("aT16", 0), ("aT16", 256), ("aT16", 512)], 16, w_ap, bias_off, 256, pump)
            if tail is not None:
                for _ in tail:
                    pass
            tail = self.final_tail(l, sub, s, s % 2)
        for _ in tail:
            pass

    def ffn(self, l):
        io = self.io
        R = self.R
        self.prenorm(l, 2)
        wo = PVL.off["fwc%d" % l]
        bo = PVL.off["fbc%d" % l]
        self.phase("ffn")
        aT = R["aT44"]
        akeys = [("aT44", 0), ("aT44", 1), ("aT44", 2)]
        tail = None
        unit = 0
        for s in range(3):
            uws = []
            for wi in range(3):
                c0, kind = WINS[3 * s + wi]
                uw = R["uw"][wi]
                self.dma(uw[:, :, :], self.uT[:, c0 - 1:c0 + 257].rearrange("(k p) n -> p k n", p=128), ck("uT", c0 - 1, 258), [("uw", wi)])
                uws.append(uw)
            for jj in range(22):
                wu, wkey = R["wup"].next()
                self.dma(wu[:, :, 0:256], io["ffn_w_up"][l, :, jj * 256:(jj + 1) * 256].rearrange("(k p) m -> p k m", p=128), (), [wkey + "a"], eng="pool")
                self.dma(wu[:, :, 256:512], io["ffn_w_up"][l, :, DFF + jj * 256:DFF + (jj + 1) * 256].rearrange("(k p) m -> p k m", p=128), (), [wkey + "b"], eng="pool")
                for pj in range(2):
                    j = jj * 2 + pj
                    for wi in range(3):
                        res = []
                        for half, chn in ((0, j), (1, 44 + j)):
                            bk, bkey = self.bank()
                            for k in range(16):
                                self.mm(bk[:, :258], wu[:, k, half * 256 + pj * 128:half * 256 + (pj + 1) * 128], uws[wi][:, k, :], k == 0, k == 15,
                                        [wkey + "a", wkey + "b", ("uw", wi)], [bkey])
                            hh, hk = R["hvg"].next()
                            w0 = self.pv[:, wo + chn:wo + chn + 1]
                            w1 = self.pv[:, wo + 88 + chn:wo + 88 + chn + 1]
                            w2 = self.pv[:, wo + 176 + chn:wo + 176 + chn + 1]
                            bb = self.pv[:, bo + chn:bo + chn + 1]
                            self.act(hh[:, :], bk[:, 1:257], AF.Identity, [bkey, "pv"], [hk], bias=bb, scale=w1)
                            self.stt(hh[:, :], bk[:, 0:256], w0, hh[:, :], ALU.mult, ALU.add, [bkey, hk, "pv"], [hk])
                            self.stt(hh[:, :], bk[:, 2:258], w2, hh[:, :], ALU.mult, ALU.add, [bkey, hk, "pv"], [hk])
                            res.append((hh, hk))
                        (hv, hvk), (hg, hgk) = res
                        self.act(hg[:, :], hg[:, :], AF.Silu, [hgk], [hgk])
                        self.tt(aT[:, j, wi * 256:(wi + 1) * 256], hg[:, :], hv[:, :], ALU.mult, [hgk, hvk], [("aT44", wi)])
                        unit += 1
                        if tail is not None and unit % 2 == 0:
                            next(tail, None)
            if tail is not None:
                for _ in tail:
                    pass
            self.final_mm(l, 2, s, s % 2, aT, akeys, 44, io["ffn_w_down"][l], None, 128)
            tail = self.final_tail(l, 2, s, s % 2)
        for _ in tail:
            pass

    def conformer(self, l):
        io = self.io
        R = self.R
        j = l // 2
        self.prenorm(l, 1)
        self.phase("odd1")
        b1 = PVL.off["bpw1%d" % j]
        wdw = PVL.off["wdw%d" % j]
        bdw = PVL.off["bdw%d" % j]
        for c in range(16):
            wt, wkey = R["wup"].next()
            self.dma(wt[:, :, 0:128], io["conv_w_pw1"][j, :, c * 128:(c + 1) * 128].rearrange("(k p) m -> p k m", p=128), (), [wkey + "a"], eng="pool")
            self.dma(wt[:, :, 128:256], io["conv_w_pw1"][j, :, D + c * 128:D + (c + 1) * 128].rearrange("(k p) m -> p k m", p=128), (), [wkey + "b"], eng="pool")
            hf, hfk = R["hfull"].next()
            for (c0, n, kind) in PIECES:
                up, upk = R["upiece"].next()
                self.dma(up[:, :, :n], self.uT[:, c0:c0 + n].rearrange("(k p) n -> p k n", p=128), ck("uT", c0, n), [upk])
                ba, bak = self.bank()
                bg, bgk = self.bank()
                for k in range(16):
                    self.mm(ba[:, :n], wt[:, k, 0:128], up[:, k, :n], k == 0, k == 15, [wkey + "a", wkey + "b", upk], [bak])
                for k in range(16):
                    self.mm(bg[:, :n], wt[:, k, 128:256], up[:, k, :n], k == 0, k == 15, [wkey + "a", wkey + "b", upk], [bgk])
                sg, sgk = R["tmpf"].next()
                self.act(sg[:, :n], bg[:, :n], AF.Sigmoid, [bgk, "pv"], [sgk], bias=self.pv[:, b1 + 16 + c:b1 + 16 + c + 1])
                self.stt(hf[:, c0:c0 + n], ba[:, :n], self.pv[:, b1 + c:b1 + c + 1], sg[:, :n], ALU.add, ALU.mult, [bak, sgk, "pv"], [hfk])
            dg, dgk = R["dg"].next()
            wv = self.pv[:, wdw:wdw + 496].rearrange("p (t c) -> p t c", c=16)[:, :, c:c + 1]
            self.tt(dg[:, :, :], self.id_b[:].unsqueeze(1).to_broadcast([128, 31, 128]), wv.to_broadcast([128, 31, 128]), ALU.mult,
                    ["id_b", "pv"], [dgk])
            for (c0, n, kind) in PIECES:
                bk, bkey = self.bank()
                for t in range(31):
                    self.mm(bk[:, :n], dg[:, t, :], hf[:, c0 - 15 + t:c0 - 15 + t + n], t == 0, t == 30, [dgk, hfk], [bkey])
                oc, ock = R["tmpf"].next()
                self.act(oc[:, :n], bk[:, :n], AF.Identity, [bkey, "pv"], [ock], bias=self.pv[:, bdw + c:bdw + c + 1])
                self.dma(self.hc[c * 128:(c + 1) * 128, c0:c0 + n], oc[:, :n], [ock], ck("hc", c0, n))
        self.phase("ln")
        lg = PVL.off["lng%d" % j]
        lbo = PVL.off["lnb%d" % j]
        for (c0, n, kind) in PIECES:
            xin = R["xin"]
            for c in range(16):
                self.dma(xin[:, c, :n], self.hc[c * 128:(c + 1) * 128, c0:c0 + n], ck("hc", c0, n), [("xin", c)])
            bm, bmk = self.bank()
            bq, bqk = self.bank()
            for c in range(16):
                sq, sqk = R["sq"].next()
                self.act(sq[:, :n], xin[:, c, :n], AF.Square, [("xin", c)], [sqk])
                self.mm(bm[:, :n], self.ones_f[:], xin[:, c, :n], c == 0, c == 15, ["ones_f", ("xin", c)], [bmk])
                self.mm(bq[:, :n], self.ones_f[:], sq[:, :n], c == 0, c == 15, ["ones_f", sqk], [bqk])
            mean, mk = R["rstd"].next()
            rs, rsk = R["rstd"].next()
            self.act(mean[:, :n], bm[:, :n], AF.Identity, [bmk], [mk], scale=1.0 / D)
            m2, m2k = R["tmpf"].next()
            self.tt(m2[:, :n], mean[:, :n], mean[:, :n], ALU.mult, [mk], [m2k])
            self.stt(m2[:, :n], bq[:, :n], 1.0 / D, m2[:, :n], ALU.mult, ALU.subtract, [bqk, m2k], [m2k])
            self.rsqrt_from(rs[:, :n], m2[:, :n], 1.0, [m2k, "epsb"], [rsk])
            for c in range(16):
                tmp, tk = R["tmpf"].next()
                self.tt(tmp[:, :n], xin[:, c, :n], mean[:, :n], ALU.subtract, [("xin", c), mk], [tk])
                self.tt(tmp[:, :n], tmp[:, :n], rs[:, :n], ALU.mult, [tk, rsk], [tk])
                ub, uk = R["ub"].next()
                self.act(ub[:, :n], tmp[:, :n], AF.Silu, [tk, "pv"], [uk], bias=self.pv[:, lbo + c:lbo + c + 1], scale=self.pv[:, lg + c:lg + c + 1])
                self.dma(self.hT[c * 128:(c + 1) * 128, c0:c0 + n], ub[:, :n], [uk], ck("hT", c0, n))
        self.final_from_dram(l, 1, io["conv_w_pw2"][j], PVL.off["bpw2%d" % j])

    def even(self, l):
        io = self.io
        R, E = self.R, self.E
        j = l // 2
        self.prenorm(l, 1)
        if self.stop_at == "pre":
            return
        self.phase("mla_prep")
        wp = E["wprep"]
        self.dma(wp[:, :, 0:832], io["w_in_ab"][j, :, 0:832].rearrange("(k p) m -> p k m", p=128), (), ["wprep_a"], eng="pool")
        self.dma(wp[:, :, 832:896], io["w_kr_sw"][j].rearrange("(k p) m -> p k m", p=128), (), ["wprep_b"], eng="pool")
        self.dma(E["rope"][:, :, :], io["rope"], (), ["rope"])
        wpk = ["wprep_a", "wprep_b"]
        qn_o = PVL.off["qn%d" % j]
        kvn_o = PVL.off["kvn%d" % j]
        cqn, ckvn, krot = E["cqn"], E["ckvn"], E["krot"]
        for (c0, n, kind) in PIECES:
            up, upk = R["upiece"].next()
            self.dma(up[:, :, :n], self.uT[:, c0:c0 + n].rearrange("(k p) n -> p k n", p=128), ck("uT", c0, n), [upk])
            for (nm, nch, col_off, dst, gain_o) in (("cq", 4, 0, cqn, qn_o), ("ckv", 2, 512, ckvn, kvn_o)):
                sbk, sbkey = self.bank()
                cf = E["cf"]
                for c in range(nch):
                    bk, bkey = self.bank()
                    for k in range(16):
                        self.mm(bk[:, :n], wp[:, k, col_off + c * 128:col_off + (c + 1) * 128], up[:, k, :n], k == 0, k == 15, wpk + [upk], [bkey])
                    self.act(cf[:, c, :n], bk[:, :n], AF.Identity, [bkey], [("xin", c)])
                    sq, sqk = R["sq"].next()
                    self.act(sq[:, :n], bk[:, :n], AF.Square, [bkey], [sqk])
                    self.mm(sbk[:, :n], self.ones_f[:], sq[:, :n], c == 0, c == nch - 1, ["ones_f", sqk], [sbkey])
                rs, rsk = R["rstd"].next()
                self.rsqrt_from(rs[:, :n], sbk[:, :n], 1.0 / (nch * 128), [sbkey, "epsb"], [rsk])
                for c in range(nch):
                    tmp, tk = R["tmpf"].next()
                    self.tt(tmp[:, :n], cf[:, c, :n], rs[:, :n], ALU.mult, [("xin", c), rsk], [tk])
                    self.act(dst[:, c, c0:c0 + n], tmp[:, :n], AF.Identity, [tk, "pv"], ck(nm + "n", c0, n), scale=self.pv[:, gain_o + c:gain_o + c + 1])
            bk, bkey = self.bank()
            bs, bskey = self.bank()
            for k in range(16):
                self.mm(bk[0:64, :n], wp[:, k, 768:832], up[:, k, :n], k == 0, k == 15, wpk + [upk], [bkey])
            for k in range(16):
                self.mm(bs[0:64, :n], wp[:, k, 832:896], up[:, k, :n], k == 0, k == 15, wpk + [upk], [bskey])
            self.rope_combine(krot[:, c0:c0 + n], bk, bkey, bs, bskey, c0, n, ck("krot", c0, n))
        if self.stop_at == "prep":
            return
        self.phase("mla_heads")
        cqn, ckvn, krot = E["cqn"], E["ckvn"], E["krot"]
        self.dma(E["wq"][:, :, :], io["w_q_full"][j].rearrange("(k p) m -> p k m", p=128), (), ["wq"], eng="pool")
        self.dma(E["wkv"][:, :, :], io["w_kv_up"][j].rearrange("(k p) m -> p k m", p=128), (), ["wkv"], eng="pool")
        wq, wkv = E["wq"], E["wkv"]
        for h in range(8):
            qn, qrot, kn, V = E["qn"], E["qrot"], E["kn"], E["V"]
            for pi, (c0, n, kind) in enumerate(PIECES):
                bk, bkey = self.bank()
                for k in range(4):
                    self.mm(bk[:, :n], wq[:, k, h * 256:h * 256 + 128], cqn[:, k, c0:c0 + n], k == 0, k == 3, ["wq"] + ck("cqn", c0, n), [bkey])
                self.copy(qn[:, c0:c0 + n], bk[:, :n], [bkey], ck("qn", c0, n), eng="act")
                b1, b1k = self.bank()
                b2, b2k = self.bank()
                for k in range(4):
                    self.mm(b1[0:64, :n], wq[:, k, h * 256 + 128:h * 256 + 192], cqn[:, k, c0:c0 + n], k == 0, k == 3, ["wq"] + ck("cqn", c0, n), [b1k])
                for k in range(4):
                    self.mm(b2[0:64, :n], wq[:, k, h * 256 + 192:h * 256 + 256], cqn[:, k, c0:c0 + n], k == 0, k == 3, ["wq"] + ck("cqn", c0, n), [b2k])
                self.rope_combine(qrot[:, c0:c0 + n], b1, b1k, b2, b2k, c0, n, ck("qrot", c0, n))
                bk, bkey = self.bank()
                for k in range(2):
                    self.mm(bk[:, :n], wkv[:, k, h * 256:h * 256 + 128], ckvn[:, k, c0:c0 + n], k == 0, k == 1, ["wkv"] + ck("ckvn", c0, n), [bkey])
                self.copy(kn[:, c0:c0 + n], bk[:, :n], [bkey], ck("kn", c0, n), eng="act")
                bk, bkey = self.bank()
                for k in range(2):
                    self.mm(bk[:, :n], wkv[:, k, h * 256 + 128:h * 256 + 256], ckvn[:, k, c0:c0 + n], k == 0, k == 1, ["wkv"] + ck("ckvn", c0, n), [bkey])
                vt, vtk = R["ub"].next()
                self.copy(vt[:, :n], bk[:, :n], [bkey], [vtk], eng="act")
                self.to_tokmajor(vt, vtk, c0, n, V, "V")
            for (c0, n, kind) in PIECES:
                kbs = [0, 1] if kind == 1 else list(range(18))
                ob, obk = self.banks[6], ("pb", 6)
                db, dbk = self.banks[7], ("pb", 7)
                pend = None
                nk = len(kbs)
                for i in range(nk + 1):
                    if i < nk:
                        kb = kbs[i]
                        kc = blkcol(kb)
                        sbk, sbkey = self.bank()
                        self.mm(sbk[:, :n], kn[:, kc:kc + 128], qn[:, c0:c0 + n], True, False, ck("kn", kc, 128) + ck("qn", c0, n), [sbkey])
                        self.mm(sbk[:, :n], krot[:, kc:kc + 128], qrot[:, c0:c0 + n], False, True, ck("krot", kc, 128) + ck("qrot", c0, n), [sbkey])
                        pt, ptk = R["ub"].next()
                        self.act(pt[:, :n], sbk[:, :n], AF.Exp, [sbkey], [ptk], scale=ATTN_SCALE)
                        cur = (i, kb, pt, ptk)
                    if pend is not None:
                        pi, pkb, ppt, pptk = pend
                        self.mm(ob[:, :n], V[:, pkb, :], ppt[:, :n], pi == 0, pi == nk - 1, [("V", pkb), pptk], [obk])
                        self.mm(db[:, :n], self.ones_b[:], ppt[:, :n], pi == 0, pi == nk - 1, ["ones_b", pptk], [dbk])
                    pend = cur if i < nk else None
                rd, rdk = R["tmpf"].next()
                self.recip(rd[:, :n], db[:, :n], [dbk], [rdk])
                at, atk = R["ub"].next()
                self.tt(at[:, :n], ob[:, :n], rd[:, :n], ALU.mult, [obk, rdk], [atk])
                self.dma(self.hT[h * 128:(h + 1) * 128, c0:c0 + n], at[:, :n], [atk], ck("hT", c0, n))
        if self.stop_at == "mla":
            return
        self.phase("hgrn")
        for h in range(8):
            self.hgrn_head(l, j, h)
        if self.stop_at == "hgrn":
            return
        self.final_from_dram(l, 1, io["w_out_ab"][j], None)

    def rope_combine(self, dst, b1, b1k, b2, b2k, c0, n, wkeys):
        R, E = self.R, self.E
        rope = E["rope"]
        t1, t1k = R["tmpf"].next()
        t2, t2k = R["tmpf"].next()
        self.tt(t1[0:64, :n], b1[0:64, :n], rope[:, 0, c0:c0 + n], ALU.mult, [b1k, "rope"], [t1k])
        self.tt(t2[0:64, :n], b2[0:64, :n], rope[:, 1, c0:c0 + n], ALU.mult, [b2k, "rope"], [t2k])
        self.tt(dst, t1[0:64, :n], t2[0:64, :n], ALU.add, [t1k, t2k], wkeys)

    def to_tokmajor(self, src, srck, c0, n, dst, dname, blk=128, banks=None):
        per = 128 // blk
        for q in range(n // blk):
            col = c0 + q * blk
            tb = (col - CTX0) // blk if col < LAT0 else 2 * per + (col - LAT0) // blk
            if banks is None:
                bk, bkey = self.bank()
            else:
                bi = banks[q % len(banks)]
                bk, bkey = self.banks[bi], ("pb", bi)
            bkb = bk.bitcast(BF16)
            self.P.op("pe", lambda h, bkb=bkb, q=q: h.transpose(bkb[0:blk, 0:128], src[:, q * blk:(q + 1) * blk], self.id_b[:]), [srck, "id_b"], [bkey])
            self.copy(dst[:, tb, :], bkb[0:blk, 0:128], [bkey], [(dname, tb)])

    def hgrn_head(self, l, j, h):
        io = self.io
        R, E = self.R, self.E
        wh = E["wh"]
        for gi, base in enumerate((832, 1856, 2880, 3904, 4928)):
            self.dma(wh[:, :, gi * 128:(gi + 1) * 128], io["w_in_ab"][j, :, base + h * 128:base + (h + 1) * 128].rearrange("(k p) m -> p k m", p=128),
                     (), [("wh", gi)], eng="pool")
        whk = [("wh", g) for g in range(5)]
        SG, OT, Vh = E["SG"], E["OT"], E["Vh"]
        QT, KT, KH = [E["QT0"], E["QT1"]], [E["KT0"], E["KT1"]], [E["KH0"], E["KH1"]]
        MS, FDc = [E["MS0"], E["MS1"]], [E["FD0"], E["FD1"]]
        lo, hi = CTX0, LAT0 + NLAT
        CH, MID, LAST = 32, 15, 31
        maskP = E["maskP"]
        def piece_gen(pi, c0, kind):
            n = 256
            ci0 = (c0 - CTX0) // 32 if kind == 1 else 8 + (c0 - LAT0) // 32
            up, upk = R["upiece"].next()
            self.dma(up[:, :, :n], self.uT[:, c0:c0 + n].rearrange("(k p) n -> p k n", p=128), ck("uT", c0, n), [upk])
            bks = []
            for gi in range(5):
                bk, bkey = self.bank(0, 6)
                for k in range(16):
                    self.mm(bk[:, :n], wh[:, k, gi * 128:(gi + 1) * 128], up[:, k, :n], k == 0, k == 15, whk + [upk], [bkey])
                bks.append((bk, bkey))
            yield
            gt = E["gt"][pi % 3]
            gk = lambda i: ("gt", pi % 3, i)
            qs, A, K, G = gt[0], [gt[1], gt[2]], [gt[3], gt[4]], [gt[5], gt[6]]
            self.copy(qs[:, :], bks[0][0][:, :n], [bks[0][1]], [gk(0)], eng="act")
            for d in range(2):
                bk, bkey = bks[1 + d]
                self.act(A[d][:, :], bk[:, :n], AF.Sigmoid, [bkey], [gk(1 + d)])
                self.act(K[d][:, :], bk[:, :n], AF.Sigmoid, [bkey], [gk(3 + d)], scale=-1.0)
            self.act(SG[:, c0:c0 + n], bks[4][0][:, :n], AF.Silu, [bks[4][1]], ["SG"])
            vt, vtk = R["ub"].next()
            self.copy(vt[:, :n], bks[3][0][:, :n], [bks[3][1]], [vtk], eng="act")
            yield
            for d in range(2):
                lb, oml, lbk = self.lbs(j, d, h)
                self.ts(A[d][:, :], A[d][:, :], oml, lb, ALU.mult, ALU.add, [gk(1 + d), lbk], [gk(1 + d)])
                self.P.op("dve", lambda hh, o=K[d][:, :], sc=oml: hh.tensor_scalar_mul(out=o, in0=o, scalar1=sc), [gk(3 + d), lbk], [gk(3 + d)])
            yield
            self.to_tokmajor(vt, vtk, c0, n, Vh, "Vh", blk=32, banks=(6, 7))
            for d in range(2):
                self.act(A[d][:, :], A[d][:, :], AF.Ln, [gk(1 + d)], [gk(1 + d)])
            yield
            for d in range(2):
                self.P.op("dve", lambda hh, d=d, A=A, G=G: hh.tensor_tensor_scan(out=G[d][:, :], data0=maskP[:, :], data1=A[d][:, :], initial=0.0,
                                                                                 op0=ALU.mult, op1=ALU.add), [gk(1 + d), "maskP"], [gk(5 + d)])
            yield
            for d in range(2):
                Gv = G[d][:, :].rearrange("p (c t) -> p c t", t=CH)
                Lv = A[d][:, :].rearrange("p (c t) -> p c t", t=CH)
                gG, gL = gk(5 + d), gk(1 + d)
                self.act(FDc[d][:, ci0:ci0 + 8], Gv[:, :, LAST], AF.Exp, [gG], ["FD%d" % d])
                if d == 0:
                    self.act(MS[d][:, ci0:ci0 + 8], Gv[:, :, MID], AF.Exp, [gG], ["MS%d" % d])
                    self.tt(Lv, Gv, Gv[:, :, MID:MID + 1].to_broadcast([128, 8, CH]), ALU.subtract, [gG], [gL])
                    self.tt(Gv, Gv[:, :, LAST:LAST + 1].to_broadcast([128, 8, CH]), Gv, ALU.subtract, [gG], [gG])
                    e1, e1k, e3, e3k = A[d], gL, G[d], gG
                else:
                    self.tt(Lv, Gv, Lv, ALU.subtract, [gG, gL], [gL])
                    tmpc = E["tmpc"]
                    self.tt(tmpc[:, 0:8], Gv[:, :, LAST], Lv[:, :, MID + 1], ALU.subtract, [gG, gL], ["tmpc"])
                    self.act(MS[d][:, ci0:ci0 + 8], tmpc[:, 0:8], AF.Exp, ["tmpc"], ["MS%d" % d])
                    self.tt(Gv, Lv[:, :, MID + 1:MID + 2].to_broadcast([128, 8, CH]), Lv, ALU.subtract, [gL], [gG])
                    e1, e1k, e3, e3k = G[d], gG, A[d], gL
                yield
                self.act(e3[:, :], e3[:, :], AF.Exp, [e3k], [e3k])
                yield
                self.tt(KH[d][:, c0:c0 + n], K[d][:, :], e3[:, :], ALU.mult, [gk(3 + d), e3k], ["KH%d" % d])
                self.act(e3[:, :], e1[:, :], AF.Exp, [e1k], [e3k])
                yield
                self.tt(QT[d][:, c0:c0 + n], qs[:, :], e3[:, :], ALU.mult, [gk(0), e3k], ["QT%d" % d])
                self.act(e3[:, :], e1[:, :], AF.Exp, [e1k], [e3k], scale=-1.0)
                yield
                self.tt(KT[d][:, c0:c0 + n], K[d][:, :], e3[:, :], ALU.mult, [gk(3 + d), e3k], ["KT%d" % d])

        pend = [piece_gen(pi, c0, kind) for pi, (c0, kind) in enumerate(WINS)]
        active = []
        while pend or active:
            if pend and len(active) < 3:
                active.append(pend.pop(0))
            for g in list(active):
                try:
                    next(g)
                except StopIteration:
                    active.remove(g)
        order = [list(range(18)), [1, 0] + list(range(17, 1, -1))]
        Sr = [E["Sr0"], E["Sr1"]]
        SMr = [E["SMr0"], E["SMr1"]]
        masks = [self.mask_f, self.mask_b]
        for d in range(2):
            self.memset(Sr[d][:, 0, :], 0.0, [("S", d, 0)], eng="dve")
        visited = set()

        def emit_PU(step, d):
            tb = order[d][step]
            col = blkcol(tb)
            pb0 = 4 * d
            atb = self.banks[pb0]
            trbb = self.banks[pb0 + 1].bitcast(BF16)
            ubk, ubkk = self.banks[pb0 + 3], ("pb", pb0 + 3)
            atm, kht = E["atm"][d], E["kht"][d]
            chs = (0, 1, 2, 3) if d == 0 else (3, 2, 1, 0)
            atk, trk = ("pb", pb0), ("pb", pb0 + 1)
            for ch in chs:
                cc = col + ch * 32
                self.mm(atb[0:32, ch * 32:(ch + 1) * 32], KT[d][:, cc:cc + 32], QT[d][:, cc:cc + 32], True, True, ["KT%d" % d, "QT%d" % d], [atk])
            for ch in chs:
                cc = col + ch * 32
                self.P.op("pe", lambda hh, trbb=trbb, d=d, cc=cc, ch=ch: hh.transpose(trbb[0:32, ch * 128:(ch + 1) * 128], KH[d][:, cc:cc + 32], self.id_b[:]),
                          ["KH%d" % d, "id_b"], [trk])
            self.tt(atm[:, step % 2, :, :], atb[0:32, 0:128].rearrange("p (c t) -> p c t", t=32), masks[d][:, :].unsqueeze(1).to_broadcast([32, 4, 32]),
                    ALU.mult, [atk, "mask%d" % d], [("atm", d, step % 2)])
            self.copy(kht[:, :, :], trbb[0:32, 0:512].rearrange("p (c t) -> p c t", t=128), [trk], [("kht", d)], eng="act")
            for qi, ch in enumerate(chs):
                ci = tb * 4 + ch
                self.mm(ubk[:, qi * 128:(qi + 1) * 128], kht[:, ch, :], Vh[:, ci, :], True, True, [("kht", d), ("Vh", ci)], [ubkk])

        def emit_S(step, d):
            tb = order[d][step]
            ubk, ubkk = self.banks[4 * d + 3], ("pb", 4 * d + 3)
            chs = (0, 1, 2, 3) if d == 0 else (3, 2, 1, 0)
            for qi, ch in enumerate(chs):
                ci = tb * 4 + ch
                g = step * 4 + qi
                self.act(SMr[d][:, g % 8, :], Sr[d][:, g % 8, :], AF.Identity, [("S", d, g % 8), "MS%d" % d], [("SM", d, g % 8)], scale=MS[d][:, ci:ci + 1])
                self.stt(Sr[d][:, (g + 1) % 8, :], Sr[d][:, g % 8, :], FDc[d][:, ci:ci + 1], ubk[:, qi * 128:(qi + 1) * 128], ALU.mult, ALU.add,
                         [("S", d, g % 8), "FD%d" % d, ubkk], [("S", d, (g + 1) % 8)])

        def emit_O(step, d):
            tb = order[d][step]
            col = blkcol(tb)
            ob, obk = self.banks[4 * d + 2], ("pb", 4 * d + 2)
            atm = E["atm"][d]
            chs = (0, 1, 2, 3) if d == 0 else (3, 2, 1, 0)
            for qi, ch in enumerate(chs):
                ci = tb * 4 + ch
                cc = col + ch * 32
                g = step * 4 + qi
                self.mm(ob[:, ch * 32:(ch + 1) * 32], Vh[:, ci, :], atm[:, step % 2, ch, :], True, False, [("Vh", ci), ("atm", d, step % 2)], [obk])
                self.mm(ob[:, ch * 32:(ch + 1) * 32], SMr[d][:, g % 8, :], QT[d][:, cc:cc + 32], False, True, [("SM", d, g % 8), "QT%d" % d], [obk])
            if tb not in visited:
                visited.add(tb)
                self.copy(OT[:, col:col + 128], ob[:, 0:128], [obk], [("OT", tb)], eng="act")
            else:
                self.tt(OT[:, col:col + 128], ob[:, 0:128], OT[:, col:col + 128], ALU.add, [obk, ("OT", tb)], [("OT", tb)])

        for step in range(18):
            for d in range(2):
                emit_PU(step, d)
            for d in range(2):
                emit_S(step, d)
            if step >= 1:
                for d in range(2):
                    emit_O(step - 1, d)
        for d in range(2):
            emit_O(17, d)
        hg_o = PVL.off["hgn%d" % j]
        otk = [("OT", tb) for tb in range(18)]
        for (c0, n, kind) in PIECES:
            sq, sqk = R["sq"].next()
            self.act(sq[:, :n], OT[:, c0:c0 + n], AF.Square, otk, [sqk])
            bk, bkey = self.bank()
            self.mm(bk[:, :n], self.ones_f[:], sq[:, :n], True, True, ["ones_f", sqk], [bkey])
            rs, rsk = R["rstd"].next()
            self.rsqrt_from(rs[:, :n], bk[:, :n], 1.0 / 128, [bkey, "epsb"], [rsk])
            tmp, tk = R["tmpf"].next()
            self.tt(tmp[:, :n], OT[:, c0:c0 + n], rs[:, :n], ALU.mult, otk + [rsk], [tk])
            ub, uk = R["ub"].next()
            self.stt(ub[:, :n], tmp[:, :n], self.pv[:, hg_o:hg_o + 1], SG[:, c0:c0 + n], ALU.mult, ALU.mult, [tk, "pv", "SG"], [uk])
            self.dma(self.hT[(8 + h) * 128:(9 + h) * 128, c0:c0 + n], ub[:, :n], [uk], ck("hT", c0, n))


_WSHAPES = {
    "w_mod": [DEPTH, D, 6 * D], "w_in_ab": [2, D, 5952], "w_kr_sw": [2, D, 64], "w_q_full": [2, 512, 2048],
    "w_kv_up": [2, 256, 2048], "w_out_ab": [2, D, D], "conv_w_pw1": [2, D, 2 * D], "conv_w_pw2": [2, D, D],
    "ffn_w_up": [DEPTH, D, 2 * DFF], "ffn_w_down": [DEPTH, DFF, D],
}


def build_program(stop=None, layers=None):
    nc = bass.Bass("TRN2", target_bir_lowering=False)
    io = {}
    io["xT0"] = nc.dram_tensor("xT0", [D, NCOL], F32, kind="ExternalInput").ap()
    io["pv"] = nc.dram_tensor("pv", [128, PVL.n], F32, kind="ExternalInput").ap()
    io["rope"] = nc.dram_tensor("rope", [64, 2, NCOL], F32, kind="ExternalInput").ap()
    for k, shp in _WSHAPES.items():
        io[k] = nc.dram_tensor(k, shp, F32, kind="ExternalInput").ap()
    outT = nc.dram_tensor("outT", [D, NCOL], F32, kind="ExternalOutput").ap()
    dbg_h = nc.dram_tensor("dbg_h", [D, NCOL], BF16, kind="ExternalOutput").ap() if stop is not None else None
    dbg_u = nc.dram_tensor("dbg_u", [D, NCOL], BF16, kind="ExternalOutput").ap() if stop is not None else None
    st = contextlib.ExitStack()
    with st:
        kb = KB(nc, st)
        kb.setup(io)
        R = kb.R
        R["sq"] = Rot(kb, "sq", 4, [128, 512], F32)
        R["tmpf"] = Rot(kb, "tmpf", 4, [128, 512], F32)
        R["ub"] = Rot(kb, "ub", 4, [128, 512], BF16)
        R["rstd"] = Rot(kb, "rstd", 3, [128, 512], F32)
        kb.arena = kb.sb("arena", [128, ARENA_BYTES // 2], BF16)
        kb.stop_at = stop[1] if stop else None
        if kb.stop_at != "setup":
            kb.phase("mod")
            kb.modulation(R["wup"])
        for l in (range(DEPTH) if layers is None else layers):
            if kb.stop_at in ("setup", "mod"):
                break
            if l % 2 == 0:
                kb.even(l)
            else:
                kb.conformer(l)
            if stop is not None and stop[0] == l and stop[1] != "ffn":
                break
            kb.ffn(l)
            if stop == (l, "ffn"):
                break
        kb.P.barrier()
        for c in range(16):
            kb.dma(outT[c * 128:(c + 1) * 128, :], kb.xT[c * 128:(c + 1) * 128, :], (), [("outT", c)])
            if dbg_h is not None:
                kb.dma(dbg_h[c * 128:(c + 1) * 128, :], kb.hT[c * 128:(c + 1) * 128, :], (), [("dbgh", c)])
                kb.dma(dbg_u[c * 128:(c + 1) * 128, :], kb.uT[c * 128:(c + 1) * 128, :], (), [("dbgu", c)])
        kb.P.emit(st)
    return nc


ARENA_BYTES = 158 * 1024


class Carver:
    def __init__(self, arena):
        self.arena = arena
        self.off = 0

    def take(self, shape, dtype):
        free = int(np.prod(shape[1:]))
        nbytes = free * (4 if dtype == F32 else 2)
        assert self.off + nbytes <= ARENA_BYTES, (self.off, nbytes)
        ap = self.arena[0:shape[0], self.off // 2:(self.off + nbytes) // 2]
        if dtype == F32:
            ap = ap.bitcast(F32)
        self.off += (nbytes + 63) // 64 * 64
        names = "abcdefg"[:len(shape) - 1]
        if len(shape) > 2:
            pat = "p (%s) -> p %s" % (" ".join(names), " ".join(names))
            ap = ap.rearrange(pat, **{names[i]: shape[1 + i] for i in range(1, len(shape) - 1)})
        return ap

    def rot(self, name, n, shape, dtype):
        return RotAP([(self.take(shape, dtype), "%s%d" % (name, i)) for i in range(n)])


class RotAP:
    def __init__(self, items):
        self.items = items
        self.i = 0

    def next(self):
        it = self.items[self.i % len(self.items)]
        self.i += 1
        return it


def rope_tables_host():
    tab = np.zeros((64, 2, NCOL), np.float32)
    tab[:, 0, :] = 1.0
    t = np.arange(NLAT)
    pos = np.stack([t // 64, t % 64], 0).astype(np.float32)
    freqs = (10000.0 ** (-np.arange(16, dtype=np.float32) / 16)).astype(np.float32)
    for i in range(64):
        a, half, p = i // 32, (i % 32) // 16, i % 16
        ang = (pos[a] * freqs[p]).astype(np.float32)
        tab[i, 0, LAT0:LAT0 + NLAT] = np.cos(ang)
        tab[i, 1, LAT0:LAT0 + NLAT] = np.sin(ang) * (-1.0 if half == 0 else 1.0)
    return tab


SWAP = np.array([(i // 32) * 32 + (1 - (i % 32) // 16) * 16 + i % 16 for i in range(64)])


def host_inputs(inputs, b):
    f = lambda k: np.asarray(inputs[k], np.float32)
    m = {}
    xT0 = np.zeros((D, NCOL), np.float32)
    xT0[:, CTX0:CTX0 + NCTX] = f("ctx")[b].T
    xT0[:, LAT0:LAT0 + NLAT] = f("x")[b].T
    m["xT0"] = xT0
    pv = np.zeros((128, PVL.n), np.float32)

    def put(name, arr):
        o = PVL.off[name]
        pv[:, o:o + arr.shape[1]] = arr

    cc = np.stack([fm(f("c")[b]), fm(f("c_ctx"))], -1).reshape(128, 32)
    put("cc", cc)
    for l in range(DEPTH):
        put("bmod%d" % l, fm(f("b_mod")[l]))
        put("gains%d" % l, np.concatenate([fm(f("norm_gains")[l, i]) for i in range(4)], 1))
        put("fwc%d" % l, np.concatenate([fm(f("ffn_w_conv")[l, k]) for k in range(3)], 1))
        put("fbc%d" % l, fm(f("ffn_b_conv")[l]))
    for j in range(2):
        put("qn%d" % j, fm(f("mla_q_norm")[j]))
        put("kvn%d" % j, fm(f("mla_kv_norm")[j]))
        put("hgn%d" % j, fm(f("hgrn_norm")[j]))
        put("bpw1%d" % j, fm(f("conv_b_pw1")[j]))
        wd = f("conv_w_dw")[j]
        put("wdw%d" % j, np.concatenate([fm(wd[k]) for k in range(31)], 1))
        put("bdw%d" % j, fm(f("conv_b_dw")[j]))
        put("lng%d" % j, fm(f("conv_ln_g")[j]))
        put("lnb%d" % j, fm(f("conv_ln_b")[j]))
        put("bpw2%d" % j, fm(f("conv_b_pw2")[j]))
    hl = f("hgrn_lb")
    put("hlb", np.concatenate([fm(hl[d, jj]) for d in range(2) for jj in range(2)], 1))
    m["pv"] = pv
    return m


def shared_inputs(inputs):
    f = lambda k: np.ascontiguousarray(np.asarray(inputs[k], np.float32))
    m = {k: f(k) for k in ("w_mod", "w_in_ab", "w_kv_up", "w_out_ab", "conv_w_pw1", "conv_w_pw2", "ffn_w_up", "ffn_w_down")}
    m["w_kr_sw"] = np.ascontiguousarray(m["w_in_ab"][:, :, 768:832][:, :, SWAP])
    wq = f("w_q_up").reshape(2, 512, 8, 192)
    m["w_q_full"] = np.ascontiguousarray(np.concatenate([wq, wq[:, :, :, 128:][:, :, :, SWAP]], -1).reshape(2, 512, 2048))
    m["rope"] = rope_tables_host()
    return m


_NC_CACHE = {}


def kernel(**inputs):
    n = 4
    if "full" not in _NC_CACHE:
        _NC_CACHE["full"] = build_program()
    nc = _NC_CACHE["full"]
    sh = shared_inputs(inputs)
    in_maps = []
    for b in range(n):
        m = dict(sh)
        m.update(host_inputs(inputs, b))
        in_maps.append(m)
    res = run_bass_kernel_spmd(nc, in_maps, core_ids=list(range(n)))
    out = np.stack([np.ascontiguousarray(res.results[b]["outT"][:, LAT0:LAT0 + NLAT].T) for b in range(n)], 0)
    return out.astype(np.float32)
```

```python
import contextlib
import numpy as np
import concourse.bass as bass
import concourse.mybir as mybir
from concourse.bass_utils import run_bass_kernel_spmd

dt = mybir.dt
F32 = dt.float32
BF16 = dt.bfloat16
AF = mybir.ActivationFunctionType
ALU = mybir.AluOpType

ENG_NAMES = ("pe", "act", "dve", "pool", "sp")

D = 2048
DEPTH = 4
NCTX = 256
NLAT = 2048
DFF = 5632
EPS = 1e-6
ATTN_SCALE = 192.0 ** -0.5
NCOL = 2352
CTX0 = 16
LAT0 = 288
PIECES = [(CTX0, 256, 1)] + [(LAT0 + 512 * i, 512, 0) for i in range(4)]
WINS = [(CTX0, 1)] + [(LAT0 + 256 * i, 0) for i in range(8)]
ST_RUNS = [[(0, CTX0, 256, 1), (256, LAT0, 512, 0)],
           [(0, 800, 512, 0), (512, 1312, 256, 0)],
           [(0, 1568, 512, 0), (512, 2080, 256, 0)]]
CELL_BOUNDS = np.array([0, 16, 272] + [288 + 256 * i for i in range(9)])


def cells(c0, n):
    a = int(np.searchsorted(CELL_BOUNDS, c0, side="right") - 1)
    b = int(np.searchsorted(CELL_BOUNDS, c0 + n - 1, side="right") - 1)
    return list(range(a, b + 1))


def ck(name, c0, n):
    return [(name, i) for i in cells(c0, n)]


def blkcol(tb):
    return CTX0 + 128 * tb if tb < 2 else LAT0 + 128 * (tb - 2)


class Op:
    __slots__ = ("eng", "fn", "deps", "signal", "tok", "lane")

    def __init__(self, eng, fn, lane=None):
        self.eng = eng
        self.fn = fn
        self.deps = []
        self.signal = lane is not None
        self.tok = None
        self.lane = lane


class Reg:
    __slots__ = ("w", "r")

    def __init__(self):
        self.w = None
        self.r = []


class Prog:
    def __init__(self, nc, n_lanes=8):
        self.nc = nc
        self.ops = {e: [] for e in ENG_NAMES}
        self.regs = {}
        self.n_lanes = n_lanes
        self.lane_last = {}
        self.lane_rr = {e: 0 for e in ENG_NAMES}

    def reg(self, key):
        r = self.regs.get(key)
        if r is None:
            r = self.regs[key] = Reg()
        return r

    def _add(self, op, reads, writes):
        deps = op.deps
        for k in reads:
            r = self.reg(k)
            if r.w is not None:
                deps.append(r.w)
        for k in writes:
            r = self.reg(k)
            if r.w is not None:
                deps.append(r.w)
            deps.extend(r.r)
        for k in reads:
            self.reg(k).r.append(op)
        for k in writes:
            r = self.reg(k)
            r.w = op
            r.r = []
        if op.eng == "pe":
            op.deps = deps = [d for d in deps if d.eng != "pe" or d.lane is not None]
        for d in deps:
            d.signal = True
        self.ops[op.eng].append(op)
        return op

    def op(self, eng, fn, reads=(), writes=()):
        return self._add(Op(eng, fn), reads, writes)

    def dma(self, eng, fn, reads=(), writes=()):
        lane = (eng, self.lane_rr[eng] % self.n_lanes)
        self.lane_rr[eng] += 1
        op = Op(eng, fn, lane=lane)
        prev = self.lane_last.get(lane)
        if prev is not None:
            op.deps.append(prev)
            prev.signal = True
        self.lane_last[lane] = op
        return self._add(op, reads, writes)

    def barrier(self):
        lasts = list(self.lane_last.values())
        for e in ENG_NAMES:
            for o in reversed(self.ops[e]):
                if o.fn is not None:
                    lasts.append(o)
                    break
        for e in ENG_NAMES:
            op = Op(e, None)
            op.deps = [d for d in lasts if not (d.eng == e and d.lane is None)]
            for d in op.deps:
                d.signal = True
            self.ops[e].append(op)
        self.regs = {}

    def emit(self, stack):
        nc = self.nc
        sems = {e: stack.enter_context(nc.semaphore("sem_" + e)) for e in ENG_NAMES}
        lane_sems = {}
        lane_cnt = {}
        for e in ENG_NAMES:
            for i in range(self.n_lanes):
                if (e, i) in self.lane_last:
                    lane_sems[(e, i)] = stack.enter_context(nc.semaphore("lane_%s%d" % (e, i)))
                    lane_cnt[(e, i)] = 0
        for e in ENG_NAMES:
            c = 0
            for op in self.ops[e]:
                if op.lane is not None:
                    lane_cnt[op.lane] += 16
                    op.tok = (lane_sems[op.lane], lane_cnt[op.lane])
                elif op.signal:
                    c += 1
                    op.tok = (sems[e], c)
        final_waits = [(lane_sems[l], lane_cnt[l]) for l in lane_sems]
        block = stack.enter_context(nc.Block())
        ops = self.ops

        def replay(e, h):
            seen = {}
            for op in ops[e]:
                for d in op.deps:
                    sem, val = d.tok
                    k = sem.num
                    if seen.get(k, 0) < val:
                        seen[k] = val
                        h.wait_ge(sem, val)
                if op.fn is None:
                    continue
                inst = op.fn(h)
                if op.lane is not None:
                    inst.then_inc(op.tok[0], 16)
                elif op.signal:
                    inst.then_inc(op.tok[0], 1)
            if e == "sp":
                for sem, val in final_waits:
                    if val:
                        h.wait_ge(sem, val)

        @block.tensor
        def _(h):
            replay("pe", h)

        @block.scalar
        def _(h):
            replay("act", h)

        @block.vector
        def _(h):
            replay("dve", h)

        @block.gpsimd
        def _(h):
            replay("pool", h)

        @block.sync
        def _(h):
            replay("sp", h)


class Rot:
    def __init__(self, kb, name, n, shape, dtype):
        self.items = [(kb.sb("%s%d" % (name, i), shape, dtype), "%s%d" % (name, i)) for i in range(n)]
        self.i = 0

    def next(self):
        it = self.items[self.i % len(self.items)]
        self.i += 1
        return it


class PVLayout:
    def __init__(self):
        self.off = {}
        self.n = 0

    def add(self, name, ncols):
        self.off[name] = self.n
        self.n += ncols


def pv_layout():
    L = PVLayout()
    L.add("cc", 32)
    for l in range(DEPTH):
        L.add("bmod%d" % l, 96)
        L.add("gains%d" % l, 64)
        L.add("fwc%d" % l, 3 * 88)
        L.add("fbc%d" % l, 88)
    for j in range(2):
        L.add("qn%d" % j, 4)
        L.add("kvn%d" % j, 2)
        L.add("hgn%d" % j, 1)
        L.add("bpw1%d" % j, 32)
        L.add("wdw%d" % j, 31 * 16)
        L.add("bdw%d" % j, 16)
        L.add("lng%d" % j, 16)
        L.add("lnb%d" % j, 16)
        L.add("bpw2%d" % j, 16)
    L.add("hlb", 32)
    return L


PVL = pv_layout()


def fm(v):
    v = np.asarray(v, np.float32)
    return np.ascontiguousarray(v.reshape(-1, 128).T)


class KB:
    def __init__(self, nc, st):
        self.nc = nc
        self.st = st
        self.P = Prog(nc)
        self.banks = [st.enter_context(nc.psum_tensor("pb%d" % i, [128, 512], F32)) for i in range(8)]
        self.brr = 0
        self.R = {}
        self.E = {}
        self.ctx_in = True
        self.ctx_out = True

    def phase(self, name):
        self.P.barrier()
        C = Carver(self.arena)
        R, E = self.R, self.E
        if name == "mod":
            R["wup"] = C.rot("wup", 2, [128, 16, 512], BF16)
        elif name in ("prenorm", "ln"):
            R["xin2"] = [C.take([128, 16, 512], F32) for _ in range(2)]
        elif name == "ffn":
            R["aT44"] = C.take([128, 44, 768], BF16)
            R["uw"] = [C.take([128, 16, 258], BF16) for _ in range(3)]
            R["hvg"] = C.rot("hvg", 6, [128, 256], F32)
            R["wup"] = C.rot("wup", 2, [128, 16, 512], BF16)
            R["wfin"] = C.rot("wfin", 2, [128, 44, 128], BF16)
            R["rstd768"] = [C.take([128, 768], F32) for _ in range(2)]
        elif name == "final16":
            R["aT16"] = C.take([128, 16, 768], BF16)
            R["wfin"] = C.rot("wfin", 2, [128, 16, 256], BF16)
            R["rstd768"] = [C.take([128, 768], F32) for _ in range(2)]
        elif name == "odd1":
            R["upiece"] = C.rot("upiece", 2, [128, 16, 512], BF16)
            R["hfull"] = C.rot("hfull", 2, [128, NCOL], BF16)
            R["dg"] = C.rot("dg", 2, [128, 31, 128], BF16)
            R["wup"] = C.rot("wup", 2, [128, 16, 256], BF16)
            for ap, key in R["hfull"].items:
                self.memset(ap, 0.0, [key], eng="dve")
        elif name in ("mla_prep", "mla_heads"):
            E["rope"] = C.take([64, 2, NCOL], F32)
            E["cqn"] = C.take([128, 4, NCOL], BF16)
            E["ckvn"] = C.take([128, 2, NCOL], BF16)
            E["krot"] = C.take([64, NCOL], BF16)
            if name == "mla_prep":
                R["upiece"] = C.rot("upiece", 2, [128, 16, 512], BF16)
                E["wprep"] = C.take([128, 16, 896], BF16)
                E["cf"] = C.take([128, 4, 512], F32)
            else:
                E["wq"] = C.take([128, 4, 2048], BF16)
                E["wkv"] = C.take([128, 2, 2048], BF16)
                E["qn"] = C.take([128, NCOL], BF16)
                E["qrot"] = C.take([64, NCOL], BF16)
                E["kn"] = C.take([128, NCOL], BF16)
                E["V"] = C.take([128, 18, 128], BF16)
        elif name == "hgrn":
            R["upiece"] = C.rot("upiece", 3, [128, 16, 256], BF16)
            E["wh"] = C.take([128, 16, 640], BF16)
            E["OT"] = C.take([128, NCOL], F32)
            for nm in ("SG", "QT0", "QT1", "KT0", "KT1", "KH0", "KH1"):
                E[nm] = C.take([128, NCOL], BF16)
            E["Vh"] = C.take([32, 72, 128], BF16)
            for nm in ("MS0", "MS1", "FD0", "FD1", "tmpc"):
                E[nm] = C.take([128, 72], F32)
            for nm in ("Sr0", "Sr1"):
                E[nm] = C.take([128, 8, 128], F32)
            for nm in ("SMr0", "SMr1"):
                E[nm] = C.take([128, 8, 128], BF16)
            E["atm"] = [C.take([32, 2, 4, 32], BF16) for _ in range(2)]
            E["kht"] = [C.take([32, 4, 128], BF16) for _ in range(2)]
            E["gt"] = [[C.take([128, 256], F32) for _ in range(7)] for _ in range(3)]
            E["maskP"] = C.take([128, 256], F32)
            self.memset(E["maskP"], 1.0, ["maskP"], eng="dve")
            self.memset(E["maskP"].rearrange("p (c t) -> p c t", t=32)[:, :, 0:1], 0.0, ["maskP"], eng="dve")
        else:
            raise ValueError(name)

    def sb(self, name, shape, dtype):
        return self.st.enter_context(self.nc.sbuf_tensor("s_" + name, shape, dtype))

    def dram(self, name, shape, dtype):
        return self.nc.dram_tensor(name, shape, dtype, kind="Internal").ap()

    def bank(self, lo=0, hi=6):
        i = lo + self.brr % (hi - lo)
        self.brr += 1
        return self.banks[i], ("pb", i)

    def mm(self, out, lhsT, rhs, start, stop, reads, writes):
        self.P.op("pe", lambda h: h.matmul(out, lhsT=lhsT, rhs=rhs, start=start, stop=stop), reads, writes)

    def act(self, out, in_, func, reads, writes, bias=0.0, scale=1.0):
        self.P.op("act", lambda h: h.activation(out=out, in_=in_, func=func, bias=bias, scale=scale), reads, writes)

    def tt(self, out, in0, in1, op, reads, writes, eng="dve"):
        self.P.op(eng, lambda h: h.tensor_tensor(out=out, in0=in0, in1=in1, op=op), reads, writes)

    def stt(self, out, in0, scalar, in1, op0, op1, reads, writes):
        self.P.op("dve", lambda h: h.scalar_tensor_tensor(out=out, in0=in0, scalar=scalar, in1=in1, op0=op0, op1=op1), reads, writes)

    def ts(self, out, in0, s1, s2, op0, op1, reads, writes):
        self.P.op("dve", lambda h: h.tensor_scalar(out=out, in0=in0, scalar1=s1, scalar2=s2, op0=op0, op1=op1), reads, writes)

    def recip(self, out, in_, reads, writes):
        self.P.op("dve", lambda h: h.reciprocal(out=out, in_=in_), reads, writes)

    def copy(self, out, in_, reads, writes, eng="dve"):
        if eng == "act":
            self.P.op("act", lambda h: h.activation(out=out, in_=in_, func=AF.Identity), reads, writes)
        else:
            self.P.op(eng, lambda h: h.tensor_copy(out=out, in_=in_), reads, writes)

    def memset(self, ap, val, writes, eng="pool"):
        self.P.op(eng, lambda h: h.memset(ap, val), (), writes)

    def dma(self, out, in_, reads, writes, eng="sp"):
        self.P.dma(eng, lambda h: h.dma_start(out=out, in_=in_), reads, writes)

    def rsqrt_from(self, out, in_, scale, reads, writes):
        self.act(out, in_, AF.Sqrt, reads, writes, bias=self.epsb[:, 0:1], scale=scale)
        self.recip(out, out, writes, writes)

    def setup(self, io):
        self.io = io
        P = self.P
        self.xT = self.dram("xT_d", [D, NCOL], F32)
        self.uT = self.dram("uT_d", [D, NCOL], BF16)
        self.hT = self.dram("hT_d", [D, NCOL], BF16)
        self.hc = self.dram("hc_d", [D, NCOL], F32)
        self.yT2 = [self.dram("yT_d%d" % i, [D, 768], F32) for i in range(2)]
        self.pv = self.sb("pv", [128, PVL.n], F32)
        self.ones_f = self.sb("ones_f", [128, 128], F32)
        self.ones_b = self.sb("ones_b", [128, 128], BF16)
        self.id_f = self.sb("id_f", [128, 128], F32)
        self.id_b = self.sb("id_b", [128, 128], BF16)
        self.epsb = self.sb("epsb", [128, 1], F32)
        self.zer = self.sb("zer", [128, 16], BF16)
        self.mods = self.sb("mods", [128, DEPTH * 6 * 16 * 2], F32)
        self.modT = self.sb("modT", [128, DEPTH * 96 * 2], F32)
        self.scb = self.sb("scb", [128, 32], BF16)
        self.lbt = self.sb("lbt", [128, 2 * 8 * 2], F32)
        self.lb0 = self.sb("lb0", [128, 2], F32)
        self.mask_f = self.sb("mask_f", [32, 32], F32)
        self.mask_b = self.sb("mask_b", [32, 32], F32)
        self.dma(self.pv[:], io["pv"], (), ["pv"])
        self.memset(self.ones_f[:], 1.0, ["ones_f"])
        self.memset(self.ones_b[:], 1.0, ["ones_b"])
        self.memset(self.epsb[:], EPS, ["epsb"])
        self.memset(self.zer[:], 0.0, ["zer"])
        self.memset(self.id_f[:], 0.0, ["id_f"])
        P.op("pool", lambda h: h.affine_select(out=self.id_f[:], in_=self.id_f[:], pattern=[[-1, 128]], compare_op=ALU.not_equal,
                                               fill=1.0, base=0, channel_multiplier=1), ["id_f"], ["id_f"])
        self.copy(self.id_b[:], self.id_f[:], ["id_f"], ["id_b"])
        for di, (m, pstep, cmul) in enumerate(((self.mask_f, 1, -1), (self.mask_b, -1, 1))):
            key = "mask%d" % di
            self.memset(m[:], 1.0, [key])
            P.op("pool", lambda h, m=m, ps=pstep, cm=cmul: h.affine_select(out=m[:], in_=m[:], pattern=[[ps, 32]], compare_op=ALU.is_ge,
                                                                            fill=0.0, base=0, channel_multiplier=cm), [key], [key])
        for c in range(16):
            self.dma(self.xT[c * 128:(c + 1) * 128, :], io["xT0"][c * 128:(c + 1) * 128, :], (), [("xT", i) for i in range(12)])
        for c in range(16):
            for p0 in (0, 272, 2336):
                self.dma(self.uT[c * 128:(c + 1) * 128, p0:p0 + 16], self.zer[:], ["zer"], ck("uT", p0, 16))
        o = PVL.off["cc"]
        self.act(self.scb[:], self.pv[:, o:o + 32], AF.Silu, ["pv"], ["scb"])
        o = PVL.off["hlb"]
        hl = self.pv[:, o:o + 32].rearrange("p (d j h) -> p d j h", d=2, j=2)
        lbv = self.lbt[:].rearrange("p (d h t) -> p d h t", d=2, h=8)
        self.tt(lbv[:, :, :, 0], hl[:, :, 1, :], hl[:, :, 0, :], ALU.subtract, ["pv"], ["lbt"])
        self.act(lbv[:, :, :, 1], lbv[:, :, :, 0], AF.Sigmoid, ["lbt"], ["lbt"], scale=-1.0)
        self.act(lbv[:, :, :, 0], lbv[:, :, :, 0], AF.Sigmoid, ["lbt"], ["lbt"])
        self.memset(self.lb0[:, 0:1], 0.0, ["lb0"])
        self.memset(self.lb0[:, 1:2], 1.0, ["lb0"])

    def lbs(self, j, d, h):
        if j == 0:
            return self.lb0[:, 0:1], self.lb0[:, 1:2], "lb0"
        o = (d * 8 + h) * 2
        return self.lbt[:, o:o + 1], self.lbt[:, o + 1:o + 2], "lbt"

    def mod(self, l, which, c, kind):
        o = (((l * 6 + which) * 16 + c) * 2) + kind
        return self.mods[:, o:o + 1]

    def modulation(self, wrot):
        io = self.io
        scv = self.scb[:].rearrange("p (c k) -> p c k", k=2)
        for l in range(DEPTH):
            bk, bkey = self.banks[7], ("pb", 7)
            for blk in range(24):
                wt, wkey = wrot.next()
                self.dma(wt[:, :, :], io["w_mod"][l, :, blk * 512:(blk + 1) * 512].rearrange("(k p) m -> p k m", p=128), (), [wkey], eng="pool")
                for jj in range(4):
                    oc = blk * 4 + jj
                    for k in range(16):
                        self.mm(bk[:, oc * 2:oc * 2 + 2], wt[:, k, jj * 128:(jj + 1) * 128], scv[:, k, :], k == 0, k == 15,
                                [wkey, "scb"], [bkey])
            o = PVL.off["bmod%d" % l]
            mt = self.modT[:, l * 192:(l + 1) * 192].rearrange("p (o k) -> p o k", k=2)
            self.tt(mt, bk[:, 0:192].rearrange("p (o k) -> p o k", k=2), self.pv[:, o:o + 96].unsqueeze(2).to_broadcast([128, 96, 2]),
                    ALU.add, [bkey, "pv"], ["modT"])
            go = PVL.off["gains%d" % l]
            md = self.mods[:, l * 192:(l + 1) * 192].rearrange("p (w c k) -> p w c k", w=6, c=16)
            mt6 = self.modT[:, l * 192:(l + 1) * 192].rearrange("p (w c k) -> p w c k", w=6, c=16)
            for (dst, gi, mi, plus1) in ((0, 0, 1, True), (2, 1, 2, False), (3, 2, 4, True), (5, 3, 5, False)):
                gb = self.pv[:, go + gi * 16:go + (gi + 1) * 16].unsqueeze(2).to_broadcast([128, 16, 2])
                if plus1:
                    self.stt(md[:, dst], mt6[:, mi], 1.0, gb, ALU.add, ALU.mult, ["modT", "pv"], ["mods"])
                else:
                    self.tt(md[:, dst], mt6[:, mi], gb, ALU.mult, ["modT", "pv"], ["mods"])
            self.copy(md[:, 1], mt6[:, 0], ["modT"], ["mods"])
            self.copy(md[:, 4], mt6[:, 3], ["modT"], ["mods"])

    def prenorm(self, l, sub):
        self.phase("prenorm")
        R = self.R
        wS, wB = (0, 1) if sub == 1 else (3, 4)
        ctx_on = self.ctx_in if sub == 1 else self.ctx_out
        for pi, (c0, n, kind) in enumerate(PIECES if ctx_on else PIECES[1:]):
            xin = R["xin2"][pi % 2]
            xs = pi % 2
            for c in range(16):
                self.dma(xin[:, c, :n], self.xT[c * 128:(c + 1) * 128, c0:c0 + n], ck("xT", c0, n), [("xin", xs, c)])
            bk, bkey = self.bank()
            for c in range(16):
                sq, sqk = R["sq"].next()
                self.act(sq[:, :n], xin[:, c, :n], AF.Square, [("xin", xs, c)], [sqk])
                self.mm(bk[:, :n], self.ones_f[:], sq[:, :n], c == 0, c == 15, ["ones_f", sqk], [bkey])
            rs, rsk = R["rstd"].next()
            self.rsqrt_from(rs[:, :n], bk[:, :n], 1.0 / D, [bkey, "epsb"], [rsk])
            for c in range(16):
                tmp, tk = R["tmpf"].next()
                self.tt(tmp[:, :n], xin[:, c, :n], rs[:, :n], ALU.mult, [("xin", xs, c), rsk], [tk])
                ub, uk = R["ub"].next()
                self.act(ub[:, :n], tmp[:, :n], AF.Identity, [tk, "mods"], [uk], bias=self.mod(l, wB, c, kind), scale=self.mod(l, wS, c, kind))
                self.dma(self.uT[c * 128:(c + 1) * 128, c0:c0 + n], ub[:, :n], [uk], ck("uT", c0, n))

    def final_mm(self, l, sub, s, slot, aT, akeys, Kc, w_ap, bias_off, wblk, pump=None):
        R = self.R
        skip_ctx = (s == 0 and not self.ctx_out)
        parts = [(256, 512, 6)] if skip_ctx else [(0, 512, 6), (512, 256, 7)]
        yT = self.yT2[slot]
        nj = wblk // 128
        for blk in range(D // wblk):
            wt, wkey = R["wfin"].next()
            self.dma(wt[:, :Kc, :], w_ap[:, blk * wblk:(blk + 1) * wblk].rearrange("(k p) m -> p k m", p=128), (), [wkey], eng="pool")
            for jj in range(nj):
                c = blk * nj + jj
                for (n0, nn, sb_) in parts:
                    bk, bkey = self.bank()
                    for k in range(Kc):
                        self.mm(bk[:, :nn], wt[:, k, jj * 128:(jj + 1) * 128], aT[:, k, n0:n0 + nn], k == 0, k == Kc - 1,
                                [wkey] + akeys, [bkey])
                    ysb, yk = R["tmpf"].next()
                    bias = 0.0 if bias_off is None else self.pv[:, bias_off + c:bias_off + c + 1]
                    self.act(ysb[:, :nn], bk[:, :nn], AF.Identity, [bkey, "pv"], [yk], bias=bias)
                    sq, sqk = R["sq"].next()
                    self.act(sq[:, :nn], ysb[:, :nn], AF.Square, [yk], [sqk])
                    self.mm(self.banks[sb_][:, :nn], self.ones_f[:], sq[:, :nn], c == 0, c == 15, ["ones_f", sqk], [("pb", sb_)])
                    self.dma(yT[c * 128:(c + 1) * 128, n0:n0 + nn], ysb[:, :nn], [yk], [("yT", slot, c)])
                    if pump is not None:
                        pump()
        rs = R["rstd768"][slot]
        if skip_ctx:
            self.rsqrt_from(rs[:, 256:768], self.banks[6][:, :512], 1.0 / D, [("pb", 6), "epsb"], [("rs768a", slot)])
        else:
            self.rsqrt_from(rs[:, 0:512], self.banks[6][:, :512], 1.0 / D, [("pb", 6), "epsb"], [("rs768a", slot)])
            self.rsqrt_from(rs[:, 512:768], self.banks[7][:, :256], 1.0 / D, [("pb", 7), "epsb"], [("rs768b", slot)])

    def final_tail(self, l, sub, s, slot):
        R = self.R
        wG = 2 if sub == 1 else 5
        yT = self.yT2[slot]
        rs = R["rstd768"][slot]
        runs = [r for r in ST_RUNS[s] if r[3] == 0 or self.ctx_out]
        rkeys = [("rs768a", slot)] if (s == 0 and not self.ctx_out) else [("rs768a", slot), ("rs768b", slot)]
        for c in range(16):
            for (lo, c0, n, kind) in runs:
                yl, ylk = R["tmpf"].next()
                xl, xlk = R["sq"].next()
                self.dma(yl[:, :n], yT[c * 128:(c + 1) * 128, lo:lo + n], [("yT", slot, c)], [ylk])
                self.dma(xl[:, :n], self.xT[c * 128:(c + 1) * 128, c0:c0 + n], ck("xT", c0, n), [xlk])
                self.tt(yl[:, :n], yl[:, :n], rs[:, lo:lo + n], ALU.mult, [ylk] + rkeys, [ylk])
                self.stt(xl[:, :n], yl[:, :n], self.mod(l, wG, c, kind), xl[:, :n], ALU.mult, ALU.add, [ylk, xlk, "mods"], [xlk])
                self.dma(self.xT[c * 128:(c + 1) * 128, c0:c0 + n], xl[:, :n], [xlk], ck("xT", c0, n))
                yield

    def final_from_dram(self, l, sub, w_ap, bias_off):
        self.phase("final16")
        R = self.R
        tail = None

        def pump():
            if tail is not None:
                next(tail, None)
                next(tail, None)

        for s in range(3):
            aT = R["aT16"]
            for (lo, c0, n, kind) in ST_RUNS[s]:
                if kind == 1 and not self.ctx_out:
                    continue
                self.dma(aT[:, :, lo:lo + n], self.hT[:, c0:c0 + n].rearrange("(k p) n -> p k n", p=128), ck("hT", c0, n), [("aT16", lo)])
            self.final_mm(l, sub, s, s % 2, aT, [("aT16", 0), ("aT16", 256), ("aT16", 512)], 16, w_ap, bias_off, 256, pump)
            if tail is not None:
                for _ in tail:
                    pass
            tail = self.final_tail(l, sub, s, s % 2)
        for _ in tail:
            pass

    def ffn(self, l):
        io = self.io
        R = self.R
        self.prenorm(l, 2)
        wo = PVL.off["fwc%d" % l]
        bo = PVL.off["fbc%d" % l]
        self.phase("ffn")
        aT = R["aT44"]
        akeys = [("aT44", 0), ("aT44", 1), ("aT44", 2)]
        tail = None
        unit = 0
        for s in range(3):
            uws = []
            wis = [wi for wi in range(3) if WINS[3 * s + wi][1] == 0 or self.ctx_out]
            for wi in range(3):
                c0, kind = WINS[3 * s + wi]
                uw = R["uw"][wi]
                if wi in wis:
                    self.dma(uw[:, :, :], self.uT[:, c0 - 1:c0 + 257].rearrange("(k p) n -> p k n", p=128), ck("uT", c0 - 1, 258), [("uw", wi)])
                uws.append(uw)
            for jj in range(22):
                wu, wkey = R["wup"].next()
                self.dma(wu[:, :, 0:256], io["ffn_w_up"][l, :, jj * 256:(jj + 1) * 256].rearrange("(k p) m -> p k m", p=128), (), [wkey + "a"], eng="pool")
                self.dma(wu[:, :, 256:512], io["ffn_w_up"][l, :, DFF + jj * 256:DFF + (jj + 1) * 256].rearrange("(k p) m -> p k m", p=128), (), [wkey + "b"], eng="pool")
                for pj in range(2):
                    j = jj * 2 + pj
                    for wi in wis:
                        res = []
                        for half, chn in ((0, j), (1, 44 + j)):
                            bk, bkey = self.bank()
                            for k in range(16):
                                self.mm(bk[:, :258], wu[:, k, half * 256 + pj * 128:half * 256 + (pj + 1) * 128], uws[wi][:, k, :], k == 0, k == 15,
                                        [wkey + "a", wkey + "b", ("uw", wi)], [bkey])
                            hh, hk = R["hvg"].next()
                            w0 = self.pv[:, wo + chn:wo + chn + 1]
                            w1 = self.pv[:, wo + 88 + chn:wo + 88 + chn + 1]
                            w2 = self.pv[:, wo + 176 + chn:wo + 176 + chn + 1]
                            bb = self.pv[:, bo + chn:bo + chn + 1]
                            self.act(hh[:, :], bk[:, 1:257], AF.Identity, [bkey, "pv"], [hk], bias=bb, scale=w1)
                            self.stt(hh[:, :], bk[:, 0:256], w0, hh[:, :], ALU.mult, ALU.add, [bkey, hk, "pv"], [hk])
                            self.stt(hh[:, :], bk[:, 2:258], w2, hh[:, :], ALU.mult, ALU.add, [bkey, hk, "pv"], [hk])
                            res.append((hh, hk))
                        (hv, hvk), (hg, hgk) = res
                        self.act(hg[:, :], hg[:, :], AF.Silu, [hgk], [hgk])
                        self.tt(aT[:, j, wi * 256:(wi + 1) * 256], hg[:, :], hv[:, :], ALU.mult, [hgk, hvk], [("aT44", wi)])
                        unit += 1
                        if tail is not None and unit % 2 == 0:
                            next(tail, None)
            if tail is not None:
                for _ in tail:
                    pass
            self.final_mm(l, 2, s, s % 2, aT, akeys, 44, io["ffn_w_down"][l], None, 128)
            tail = self.final_tail(l, 2, s, s % 2)
        for _ in tail:
            pass

    def conformer(self, l):
        io = self.io
        R = self.R
        j = l // 2
        self.prenorm(l, 1)
        self.phase("odd1")
        pieces = PIECES if self.ctx_in else PIECES[1:]
        b1 = PVL.off["bpw1%d" % j]
        wdw = PVL.off["wdw%d" % j]
        bdw = PVL.off["bdw%d" % j]
        for c in range(16):
            wt, wkey = R["wup"].next()
            self.dma(wt[:, :, 0:128], io["conv_w_pw1"][j, :, c * 128:(c + 1) * 128].rearrange("(k p) m -> p k m", p=128), (), [wkey + "a"], eng="pool")
            self.dma(wt[:, :, 128:256], io["conv_w_pw1"][j, :, D + c * 128:D + (c + 1) * 128].rearrange("(k p) m -> p k m", p=128), (), [wkey + "b"], eng="pool")
            hf, hfk = R["hfull"].next()
            for (c0, n, kind) in pieces:
                up, upk = R["upiece"].next()
                self.dma(up[:, :, :n], self.uT[:, c0:c0 + n].rearrange("(k p) n -> p k n", p=128), ck("uT", c0, n), [upk])
                ba, bak = self.bank()
                bg, bgk = self.bank()
                for k in range(16):
                    self.mm(ba[:, :n], wt[:, k, 0:128], up[:, k, :n], k == 0, k == 15, [wkey + "a", wkey + "b", upk], [bak])
                for k in range(16):
                    self.mm(bg[:, :n], wt[:, k, 128:256], up[:, k, :n], k == 0, k == 15, [wkey + "a", wkey + "b", upk], [bgk])
                sg, sgk = R["tmpf"].next()
                self.act(sg[:, :n], bg[:, :n], AF.Sigmoid, [bgk, "pv"], [sgk], bias=self.pv[:, b1 + 16 + c:b1 + 16 + c + 1])
                self.stt(hf[:, c0:c0 + n], ba[:, :n], self.pv[:, b1 + c:b1 + c + 1], sg[:, :n], ALU.add, ALU.mult, [bak, sgk, "pv"], [hfk])
            dg, dgk = R["dg"].next()
            wv = self.pv[:, wdw:wdw + 496].rearrange("p (t c) -> p t c", c=16)[:, :, c:c + 1]
            self.tt(dg[:, :, :], self.id_b[:].unsqueeze(1).to_broadcast([128, 31, 128]), wv.to_broadcast([128, 31, 128]), ALU.mult,
                    ["id_b", "pv"], [dgk])
            for (c0, n, kind) in pieces:
                bk, bkey = self.bank()
                for t in range(31):
                    self.mm(bk[:, :n], dg[:, t, :], hf[:, c0 - 15 + t:c0 - 15 + t + n], t == 0, t == 30, [dgk, hfk], [bkey])
                oc, ock = R["tmpf"].next()
                self.act(oc[:, :n], bk[:, :n], AF.Identity, [bkey, "pv"], [ock], bias=self.pv[:, bdw + c:bdw + c + 1])
                self.dma(self.hc[c * 128:(c + 1) * 128, c0:c0 + n], oc[:, :n], [ock], ck("hc", c0, n))
        self.phase("ln")
        lg = PVL.off["lng%d" % j]
        lbo = PVL.off["lnb%d" % j]
        for pi, (c0, n, kind) in enumerate(pieces):
            xin = R["xin2"][pi % 2]
            xs = pi % 2
            for c in range(16):
                self.dma(xin[:, c, :n], self.hc[c * 128:(c + 1) * 128, c0:c0 + n], ck("hc", c0, n), [("xin", xs, c)])
            bm, bmk = self.bank()
            bq, bqk = self.bank()
            for c in range(16):
                sq, sqk = R["sq"].next()
                self.act(sq[:, :n], xin[:, c, :n], AF.Square, [("xin", xs, c)], [sqk])
                self.mm(bm[:, :n], self.ones_f[:], xin[:, c, :n], c == 0, c == 15, ["ones_f", ("xin", xs, c)], [bmk])
                self.mm(bq[:, :n], self.ones_f[:], sq[:, :n], c == 0, c == 15, ["ones_f", sqk], [bqk])
            mean, mk = R["rstd"].next()
            rs, rsk = R["rstd"].next()
            self.act(mean[:, :n], bm[:, :n], AF.Identity, [bmk], [mk], scale=1.0 / D)
            m2, m2k = R["tmpf"].next()
            self.tt(m2[:, :n], mean[:, :n], mean[:, :n], ALU.mult, [mk], [m2k])
            self.stt(m2[:, :n], bq[:, :n], 1.0 / D, m2[:, :n], ALU.mult, ALU.subtract, [bqk, m2k], [m2k])
            self.rsqrt_from(rs[:, :n], m2[:, :n], 1.0, [m2k, "epsb"], [rsk])
            for c in range(16):
                tmp, tk = R["tmpf"].next()
                self.tt(tmp[:, :n], xin[:, c, :n], mean[:, :n], ALU.subtract, [("xin", xs, c), mk], [tk])
                self.tt(tmp[:, :n], tmp[:, :n], rs[:, :n], ALU.mult, [tk, rsk], [tk])
                ub, uk = R["ub"].next()
                self.act(ub[:, :n], tmp[:, :n], AF.Silu, [tk, "pv"], [uk], bias=self.pv[:, lbo + c:lbo + c + 1], scale=self.pv[:, lg + c:lg + c + 1])
                self.dma(self.hT[c * 128:(c + 1) * 128, c0:c0 + n], ub[:, :n], [uk], ck("hT", c0, n))
        self.final_from_dram(l, 1, io["conv_w_pw2"][j], PVL.off["bpw2%d" % j])

    def even(self, l):
        io = self.io
        R, E = self.R, self.E
        j = l // 2
        self.prenorm(l, 1)
        if self.stop_at == "pre":
            return
        self.phase("mla_prep")
        wp = E["wprep"]
        self.dma(wp[:, :, 0:832], io["w_in_ab"][j, :, 0:832].rearrange("(k p) m -> p k m", p=128), (), ["wprep_a"], eng="pool")
        self.dma(wp[:, :, 832:896], io["w_kr_sw"][j].rearrange("(k p) m -> p k m", p=128), (), ["wprep_b"], eng="pool")
        self.dma(E["rope"][:, :, :], io["rope"], (), ["rope"])
        wpk = ["wprep_a", "wprep_b"]
        qn_o = PVL.off["qn%d" % j]
        kvn_o = PVL.off["kvn%d" % j]
        cqn, ckvn, krot = E["cqn"], E["ckvn"], E["krot"]
        for (c0, n, kind) in PIECES:
            up, upk = R["upiece"].next()
            self.dma(up[:, :, :n], self.uT[:, c0:c0 + n].rearrange("(k p) n -> p k n", p=128), ck("uT", c0, n), [upk])
            for (nm, nch, col_off, dst, gain_o) in (("cq", 4, 0, cqn, qn_o), ("ckv", 2, 512, ckvn, kvn_o)):
                sbk, sbkey = self.bank()
                cf = E["cf"]
                for c in range(nch):
                    bk, bkey = self.bank()
                    for k in range(16):
                        self.mm(bk[:, :n], wp[:, k, col_off + c * 128:col_off + (c + 1) * 128], up[:, k, :n], k == 0, k == 15, wpk + [upk], [bkey])
                    self.act(cf[:, c, :n], bk[:, :n], AF.Identity, [bkey], [("xin", c)])
                    sq, sqk = R["sq"].next()
                    self.act(sq[:, :n], bk[:, :n], AF.Square, [bkey], [sqk])
                    self.mm(sbk[:, :n], self.ones_f[:], sq[:, :n], c == 0, c == nch - 1, ["ones_f", sqk], [sbkey])
                rs, rsk = R["rstd"].next()
                self.rsqrt_from(rs[:, :n], sbk[:, :n], 1.0 / (nch * 128), [sbkey, "epsb"], [rsk])
                for c in range(nch):
                    tmp, tk = R["tmpf"].next()
                    self.tt(tmp[:, :n], cf[:, c, :n], rs[:, :n], ALU.mult, [("xin", c), rsk], [tk])
                    self.act(dst[:, c, c0:c0 + n], tmp[:, :n], AF.Identity, [tk, "pv"], ck(nm + "n", c0, n), scale=self.pv[:, gain_o + c:gain_o + c + 1])
            bk, bkey = self.bank()
            bs, bskey = self.bank()
            for k in range(16):
                self.mm(bk[0:64, :n], wp[:, k, 768:832], up[:, k, :n], k == 0, k == 15, wpk + [upk], [bkey])
            for k in range(16):
                self.mm(bs[0:64, :n], wp[:, k, 832:896], up[:, k, :n], k == 0, k == 15, wpk + [upk], [bskey])
            self.rope_combine(krot[:, c0:c0 + n], bk, bkey, bs, bskey, c0, n, ck("krot", c0, n))
        if self.stop_at == "prep":
            return
        self.phase("mla_heads")
        cqn, ckvn, krot = E["cqn"], E["ckvn"], E["krot"]
        self.dma(E["wq"][:, :, :], io["w_q_full"][j].rearrange("(k p) m -> p k m", p=128), (), ["wq"], eng="pool")
        self.dma(E["wkv"][:, :, :], io["w_kv_up"][j].rearrange("(k p) m -> p k m", p=128), (), ["wkv"], eng="pool")
        wq, wkv = E["wq"], E["wkv"]
        for h in range(8):
            qn, qrot, kn, V = E["qn"], E["qrot"], E["kn"], E["V"]
            for pi, (c0, n, kind) in enumerate(PIECES):
                bk, bkey = self.bank()
                for k in range(4):
                    self.mm(bk[:, :n], wq[:, k, h * 256:h * 256 + 128], cqn[:, k, c0:c0 + n], k == 0, k == 3, ["wq"] + ck("cqn", c0, n), [bkey])
                self.copy(qn[:, c0:c0 + n], bk[:, :n], [bkey], ck("qn", c0, n), eng="act")
                b1, b1k = self.bank()
                b2, b2k = self.bank()
                for k in range(4):
                    self.mm(b1[0:64, :n], wq[:, k, h * 256 + 128:h * 256 + 192], cqn[:, k, c0:c0 + n], k == 0, k == 3, ["wq"] + ck("cqn", c0, n), [b1k])
                for k in range(4):
                    self.mm(b2[0:64, :n], wq[:, k, h * 256 + 192:h * 256 + 256], cqn[:, k, c0:c0 + n], k == 0, k == 3, ["wq"] + ck("cqn", c0, n), [b2k])
                self.rope_combine(qrot[:, c0:c0 + n], b1, b1k, b2, b2k, c0, n, ck("qrot", c0, n))
                bk, bkey = self.bank()
                for k in range(2):
                    self.mm(bk[:, :n], wkv[:, k, h * 256:h * 256 + 128], ckvn[:, k, c0:c0 + n], k == 0, k == 1, ["wkv"] + ck("ckvn", c0, n), [bkey])
                self.copy(kn[:, c0:c0 + n], bk[:, :n], [bkey], ck("kn", c0, n), eng="act")
                bk, bkey = self.bank()
                for k in range(2):
                    self.mm(bk[:, :n], wkv[:, k, h * 256 + 128:h * 256 + 256], ckvn[:, k, c0:c0 + n], k == 0, k == 1, ["wkv"] + ck("ckvn", c0, n), [bkey])
                vt, vtk = R["ub"].next()
                self.copy(vt[:, :n], bk[:, :n], [bkey], [vtk], eng="act")
                self.to_tokmajor(vt, vtk, c0, n, V, "V")
            for (c0, n, kind) in (PIECES if self.ctx_out else PIECES[1:]):
                kbs = [0, 1] if kind == 1 else list(range(18))
                ob, obk = self.banks[6], ("pb", 6)
                db, dbk = self.banks[7], ("pb", 7)
                pend = None
                nk = len(kbs)
                for i in range(nk + 1):
                    if i < nk:
                        kb = kbs[i]
                        kc = blkcol(kb)
                        sbk, sbkey = self.bank()
                        self.mm(sbk[:, :n], kn[:, kc:kc + 128], qn[:, c0:c0 + n], True, False, ck("kn", kc, 128) + ck("qn", c0, n), [sbkey])
                        self.mm(sbk[:, :n], krot[:, kc:kc + 128], qrot[:, c0:c0 + n], False, True, ck("krot", kc, 128) + ck("qrot", c0, n), [sbkey])
                        pt, ptk = R["ub"].next()
                        self.act(pt[:, :n], sbk[:, :n], AF.Exp, [sbkey], [ptk], scale=ATTN_SCALE)
                        cur = (i, kb, pt, ptk)
                    if pend is not None:
                        pi, pkb, ppt, pptk = pend
                        self.mm(ob[:, :n], V[:, pkb, :], ppt[:, :n], pi == 0, pi == nk - 1, [("V", pkb), pptk], [obk])
                        self.mm(db[:, :n], self.ones_b[:], ppt[:, :n], pi == 0, pi == nk - 1, ["ones_b", pptk], [dbk])
                    pend = cur if i < nk else None
                rd, rdk = R["tmpf"].next()
                self.recip(rd[:, :n], db[:, :n], [dbk], [rdk])
                at, atk = R["ub"].next()
                self.tt(at[:, :n], ob[:, :n], rd[:, :n], ALU.mult, [obk, rdk], [atk])
                self.dma(self.hT[h * 128:(h + 1) * 128, c0:c0 + n], at[:, :n], [atk], ck("hT", c0, n))
        if self.stop_at == "mla":
            return
        self.phase("hgrn")
        for h in range(8):
            self.hgrn_head(l, j, h)
        if self.stop_at == "hgrn":
            return
        self.final_from_dram(l, 1, io["w_out_ab"][j], None)

    def rope_combine(self, dst, b1, b1k, b2, b2k, c0, n, wkeys):
        R, E = self.R, self.E
        rope = E["rope"]
        t1, t1k = R["tmpf"].next()
        t2, t2k = R["tmpf"].next()
        self.tt(t1[0:64, :n], b1[0:64, :n], rope[:, 0, c0:c0 + n], ALU.mult, [b1k, "rope"], [t1k])
        self.tt(t2[0:64, :n], b2[0:64, :n], rope[:, 1, c0:c0 + n], ALU.mult, [b2k, "rope"], [t2k])
        self.tt(dst, t1[0:64, :n], t2[0:64, :n], ALU.add, [t1k, t2k], wkeys)

    def to_tokmajor(self, src, srck, c0, n, dst, dname, blk=128, banks=None):
        per = 128 // blk
        for q in range(n // blk):
            col = c0 + q * blk
            tb = (col - CTX0) // blk if col < LAT0 else 2 * per + (col - LAT0) // blk
            if banks is None:
                bk, bkey = self.bank()
            else:
                bi = banks[q % len(banks)]
                bk, bkey = self.banks[bi], ("pb", bi)
            bkb = bk.bitcast(BF16)
            self.P.op("pe", lambda h, bkb=bkb, q=q: h.transpose(bkb[0:blk, 0:128], src[:, q * blk:(q + 1) * blk], self.id_b[:]), [srck, "id_b"], [bkey])
            self.copy(dst[:, tb, :], bkb[0:blk, 0:128], [bkey], [(dname, tb)])

    def hgrn_head(self, l, j, h):
        io = self.io
        R, E = self.R, self.E
        wh = E["wh"]
        for gi, base in enumerate((832, 1856, 2880, 3904, 4928)):
            self.dma(wh[:, :, gi * 128:(gi + 1) * 128], io["w_in_ab"][j, :, base + h * 128:base + (h + 1) * 128].rearrange("(k p) m -> p k m", p=128),
                     (), [("wh", gi)], eng="pool")
        whk = [("wh", g) for g in range(5)]
        SG, OT, Vh = E["SG"], E["OT"], E["Vh"]
        QT, KT, KH = [E["QT0"], E["QT1"]], [E["KT0"], E["KT1"]], [E["KH0"], E["KH1"]]
        MS, FDc = [E["MS0"], E["MS1"]], [E["FD0"], E["FD1"]]
        lo, hi = CTX0, LAT0 + NLAT
        CH, MID, LAST = 32, 15, 31
        maskP = E["maskP"]
        def piece_gen(pi, c0, kind):
            n = 256
            ci0 = (c0 - CTX0) // 32 if kind == 1 else 8 + (c0 - LAT0) // 32
            up, upk = R["upiece"].next()
            self.dma(up[:, :, :n], self.uT[:, c0:c0 + n].rearrange("(k p) n -> p k n", p=128), ck("uT", c0, n), [upk])
            bks = []
            for gi in range(5):
                bk, bkey = self.bank(0, 6)
                for k in range(16):
                    self.mm(bk[:, :n], wh[:, k, gi * 128:(gi + 1) * 128], up[:, k, :n], k == 0, k == 15, whk + [upk], [bkey])
                bks.append((bk, bkey))
            yield
            gt = E["gt"][pi % 3]
            gk = lambda i: ("gt", pi % 3, i)
            qs, A, K, G = gt[0], [gt[1], gt[2]], [gt[3], gt[4]], [gt[5], gt[6]]
            self.copy(qs[:, :], bks[0][0][:, :n], [bks[0][1]], [gk(0)], eng="act")
            for d in range(2):
                bk, bkey = bks[1 + d]
                self.act(A[d][:, :], bk[:, :n], AF.Sigmoid, [bkey], [gk(1 + d)])
                self.act(K[d][:, :], bk[:, :n], AF.Sigmoid, [bkey], [gk(3 + d)], scale=-1.0)
            self.act(SG[:, c0:c0 + n], bks[4][0][:, :n], AF.Silu, [bks[4][1]], ["SG"])
            vt, vtk = R["ub"].next()
            self.copy(vt[:, :n], bks[3][0][:, :n], [bks[3][1]], [vtk], eng="act")
            yield
            for d in range(2):
                lb, oml, lbk = self.lbs(j, d, h)
                self.ts(A[d][:, :], A[d][:, :], oml, lb, ALU.mult, ALU.add, [gk(1 + d), lbk], [gk(1 + d)])
                self.P.op("dve", lambda hh, o=K[d][:, :], sc=oml: hh.tensor_scalar_mul(out=o, in0=o, scalar1=sc), [gk(3 + d), lbk], [gk(3 + d)])
            yield
            self.to_tokmajor(vt, vtk, c0, n, Vh, "Vh", blk=32, banks=(6, 7))
            for d in range(2):
                self.act(A[d][:, :], A[d][:, :], AF.Ln, [gk(1 + d)], [gk(1 + d)])
            yield
            for d in range(2):
                self.P.op("dve", lambda hh, d=d, A=A, G=G: hh.tensor_tensor_scan(out=G[d][:, :], data0=maskP[:, :], data1=A[d][:, :], initial=0.0,
                                                                                 op0=ALU.mult, op1=ALU.add), [gk(1 + d), "maskP"], [gk(5 + d)])
            yield
            for d in range(2):
                Gv = G[d][:, :].rearrange("p (c t) -> p c t", t=CH)
                Lv = A[d][:, :].rearrange("p (c t) -> p c t", t=CH)
                gG, gL = gk(5 + d), gk(1 + d)
                self.act(FDc[d][:, ci0:ci0 + 8], Gv[:, :, LAST], AF.Exp, [gG], ["FD%d" % d])
                if d == 0:
                    self.act(MS[d][:, ci0:ci0 + 8], Gv[:, :, MID], AF.Exp, [gG], ["MS%d" % d])
                    self.tt(Lv, Gv, Gv[:, :, MID:MID + 1].to_broadcast([128, 8, CH]), ALU.subtract, [gG], [gL])
                    self.tt(Gv, Gv[:, :, LAST:LAST + 1].to_broadcast([128, 8, CH]), Gv, ALU.subtract, [gG], [gG])
                    e1, e1k, e3, e3k = A[d], gL, G[d], gG
                else:
                    self.tt(Lv, Gv, Lv, ALU.subtract, [gG, gL], [gL])
                    tmpc = E["tmpc"]
                    self.tt(tmpc[:, 0:8], Gv[:, :, LAST], Lv[:, :, MID + 1], ALU.subtract, [gG, gL], ["tmpc"])
                    self.act(MS[d][:, ci0:ci0 + 8], tmpc[:, 0:8], AF.Exp, ["tmpc"], ["MS%d" % d])
                    self.tt(Gv, Lv[:, :, MID + 1:MID + 2].to_broadcast([128, 8, CH]), Lv, ALU.subtract, [gL], [gG])
                    e1, e1k, e3, e3k = G[d], gG, A[d], gL
                yield
                self.act(e3[:, :], e3[:, :], AF.Exp, [e3k], [e3k])
                yield
                self.tt(KH[d][:, c0:c0 + n], K[d][:, :], e3[:, :], ALU.mult, [gk(3 + d), e3k], ["KH%d" % d])
                self.act(e3[:, :], e1[:, :], AF.Exp, [e1k], [e3k])
                yield
                self.tt(QT[d][:, c0:c0 + n], qs[:, :], e3[:, :], ALU.mult, [gk(0), e3k], ["QT%d" % d])
                self.act(e3[:, :], e1[:, :], AF.Exp, [e1k], [e3k], scale=-1.0)
                yield
                self.tt(KT[d][:, c0:c0 + n], K[d][:, :], e3[:, :], ALU.mult, [gk(3 + d), e3k], ["KT%d" % d])

        pend = [piece_gen(pi, c0, kind) for pi, (c0, kind) in enumerate(WINS)]
        active = []
        while pend or active:
            if pend and len(active) < 3:
                active.append(pend.pop(0))
            for g in list(active):
                try:
                    next(g)
                except StopIteration:
                    active.remove(g)
        order = [list(range(18)), [1, 0] + list(range(17, 1, -1))]
        Sr = [E["Sr0"], E["Sr1"]]
        SMr = [E["SMr0"], E["SMr1"]]
        masks = [self.mask_f, self.mask_b]
        for d in range(2):
            self.memset(Sr[d][:, 0, :], 0.0, [("S", d, 0)], eng="dve")
        visited = set()

        def emit_PU(step, d):
            tb = order[d][step]
            col = blkcol(tb)
            pb0 = 4 * d
            atb = self.banks[pb0]
            trbb = self.banks[pb0 + 1].bitcast(BF16)
            ubk, ubkk = self.banks[pb0 + 3], ("pb", pb0 + 3)
            atm, kht = E["atm"][d], E["kht"][d]
            chs = (0, 1, 2, 3) if d == 0 else (3, 2, 1, 0)
            atk, trk = ("pb", pb0), ("pb", pb0 + 1)
            for ch in chs:
                cc = col + ch * 32
                self.mm(atb[0:32, ch * 32:(ch + 1) * 32], KT[d][:, cc:cc + 32], QT[d][:, cc:cc + 32], True, True, ["KT%d" % d, "QT%d" % d], [atk])
            for ch in chs:
                cc = col + ch * 32
                self.P.op("pe", lambda hh, trbb=trbb, d=d, cc=cc, ch=ch: hh.transpose(trbb[0:32, ch * 128:(ch + 1) * 128], KH[d][:, cc:cc + 32], self.id_b[:]),
                          ["KH%d" % d, "id_b"], [trk])
            self.tt(atm[:, step % 2, :, :], atb[0:32, 0:128].rearrange("p (c t) -> p c t", t=32), masks[d][:, :].unsqueeze(1).to_broadcast([32, 4, 32]),
                    ALU.mult, [atk, "mask%d" % d], [("atm", d, step % 2)])
            self.copy(kht[:, :, :], trbb[0:32, 0:512].rearrange("p (c t) -> p c t", t=128), [trk], [("kht", d)], eng="act")
            for qi, ch in enumerate(chs):
                ci = tb * 4 + ch
                self.mm(ubk[:, qi * 128:(qi + 1) * 128], kht[:, ch, :], Vh[:, ci, :], True, True, [("kht", d), ("Vh", ci)], [ubkk])

        def emit_S(step, d):
            tb = order[d][step]
            ubk, ubkk = self.banks[4 * d + 3], ("pb", 4 * d + 3)
            chs = (0, 1, 2, 3) if d == 0 else (3, 2, 1, 0)
            for qi, ch in enumerate(chs):
                ci = tb * 4 + ch
                g = step * 4 + qi
                self.act(SMr[d][:, g % 8, :], Sr[d][:, g % 8, :], AF.Identity, [("S", d, g % 8), "MS%d" % d], [("SM", d, g % 8)], scale=MS[d][:, ci:ci + 1])
                self.stt(Sr[d][:, (g + 1) % 8, :], Sr[d][:, g % 8, :], FDc[d][:, ci:ci + 1], ubk[:, qi * 128:(qi + 1) * 128], ALU.mult, ALU.add,
                         [("S", d, g % 8), "FD%d" % d, ubkk], [("S", d, (g + 1) % 8)])

        def emit_O(step, d):
            tb = order[d][step]
            col = blkcol(tb)
            ob, obk = self.banks[4 * d + 2], ("pb", 4 * d + 2)
            atm = E["atm"][d]
            chs = (0, 1, 2, 3) if d == 0 else (3, 2, 1, 0)
            for qi, ch in enumerate(chs):
                ci = tb * 4 + ch
                cc = col + ch * 32
                g = step * 4 + qi
                self.mm(ob[:, ch * 32:(ch + 1) * 32], Vh[:, ci, :], atm[:, step % 2, ch, :], True, False, [("Vh", ci), ("atm", d, step % 2)], [obk])
                self.mm(ob[:, ch * 32:(ch + 1) * 32], SMr[d][:, g % 8, :], QT[d][:, cc:cc + 32], False, True, [("SM", d, g % 8), "QT%d" % d], [obk])
            if tb not in visited:
                visited.add(tb)
                self.copy(OT[:, col:col + 128], ob[:, 0:128], [obk], [("OT", tb)], eng="act")
            else:
                self.tt(OT[:, col:col + 128], ob[:, 0:128], OT[:, col:col + 128], ALU.add, [obk, ("OT", tb)], [("OT", tb)])

        for step in range(18):
            for d in range(2):
                emit_PU(step, d)
            for d in range(2):
                emit_S(step, d)
            if step >= 1:
                for d in range(2):
                    emit_O(step - 1, d)
        for d in range(2):
            emit_O(17, d)
        hg_o = PVL.off["hgn%d" % j]
        otk = [("OT", tb) for tb in range(18)]
        for (c0, n, kind) in (PIECES if self.ctx_out else PIECES[1:]):
            sq, sqk = R["sq"].next()
            self.act(sq[:, :n], OT[:, c0:c0 + n], AF.Square, otk, [sqk])
            bk, bkey = self.bank()
            self.mm(bk[:, :n], self.ones_f[:], sq[:, :n], True, True, ["ones_f", sqk], [bkey])
            rs, rsk = R["rstd"].next()
            self.rsqrt_from(rs[:, :n], bk[:, :n], 1.0 / 128, [bkey, "epsb"], [rsk])
            tmp, tk = R["tmpf"].next()
            self.tt(tmp[:, :n], OT[:, c0:c0 + n], rs[:, :n], ALU.mult, otk + [rsk], [tk])
            ub, uk = R["ub"].next()
            self.stt(ub[:, :n], tmp[:, :n], self.pv[:, hg_o:hg_o + 1], SG[:, c0:c0 + n], ALU.mult, ALU.mult, [tk, "pv", "SG"], [uk])
            self.dma(self.hT[(8 + h) * 128:(9 + h) * 128, c0:c0 + n], ub[:, :n], [uk], ck("hT", c0, n))


_WSHAPES = {
    "w_mod": [DEPTH, D, 6 * D], "w_in_ab": [2, D, 5952], "w_kr_sw": [2, D, 64], "w_q_full": [2, 512, 2048],
    "w_kv_up": [2, 256, 2048], "w_out_ab": [2, D, D], "conv_w_pw1": [2, D, 2 * D], "conv_w_pw2": [2, D, D],
    "ffn_w_up": [DEPTH, D, 2 * DFF], "ffn_w_down": [DEPTH, DFF, D],
}


def build_program(stop=None, layers=None):
    nc = bass.Bass("TRN2", target_bir_lowering=False)
    io = {}
    io["xT0"] = nc.dram_tensor("xT0", [D, NCOL], F32, kind="ExternalInput").ap()
    io["pv"] = nc.dram_tensor("pv", [128, PVL.n], F32, kind="ExternalInput").ap()
    io["rope"] = nc.dram_tensor("rope", [64, 2, NCOL], F32, kind="ExternalInput").ap()
    for k, shp in _WSHAPES.items():
        io[k] = nc.dram_tensor(k, shp, F32, kind="ExternalInput").ap()
    outT = nc.dram_tensor("outT", [D, NCOL], F32, kind="ExternalOutput").ap()
    dbg_h = nc.dram_tensor("dbg_h", [D, NCOL], BF16, kind="ExternalOutput").ap() if stop is not None else None
    dbg_u = nc.dram_tensor("dbg_u", [D, NCOL], BF16, kind="ExternalOutput").ap() if stop is not None else None
    st = contextlib.ExitStack()
    with st:
        kb = KB(nc, st)
        kb.setup(io)
        R = kb.R
        R["sq"] = Rot(kb, "sq", 4, [128, 512], F32)
        R["tmpf"] = Rot(kb, "tmpf", 4, [128, 512], F32)
        R["ub"] = Rot(kb, "ub", 4, [128, 512], BF16)
        R["rstd"] = Rot(kb, "rstd", 3, [128, 512], F32)
        kb.arena = kb.sb("arena", [128, ARENA_BYTES // 2], BF16)
        kb.stop_at = stop[1] if stop else None
        if kb.stop_at != "setup":
            kb.phase("mod")
            kb.modulation(R["wup"])
        for l in (range(DEPTH) if layers is None else layers):
            if kb.stop_at in ("setup", "mod"):
                break
            kb.ctx_in = l < DEPTH - 1
            kb.ctx_out = l < DEPTH - 2
            if l % 2 == 0:
                kb.even(l)
            else:
                kb.conformer(l)
            if stop is not None and stop[0] == l and stop[1] != "ffn":
                break
            kb.ffn(l)
            if stop == (l, "ffn"):
                break
        kb.P.barrier()
        for c in range(16):
            kb.dma(outT[c * 128:(c + 1) * 128, :], kb.xT[c * 128:(c + 1) * 128, :], (), [("outT", c)])
            if dbg_h is not None:
                kb.dma(dbg_h[c * 128:(c + 1) * 128, :], kb.hT[c * 128:(c + 1) * 128, :], (), [("dbgh", c)])
                kb.dma(dbg_u[c * 128:(c + 1) * 128, :], kb.uT[c * 128:(c + 1) * 128, :], (), [("dbgu", c)])
        kb.P.emit(st)
    return nc


ARENA_BYTES = 158 * 1024


class Carver:
    def __init__(self, arena):
        self.arena = arena
        self.off = 0

    def take(self, shape, dtype):
        free = int(np.prod(shape[1:]))
        nbytes = free * (4 if dtype == F32 else 2)
        assert self.off + nbytes <= ARENA_BYTES, (self.off, nbytes)
        ap = self.arena[0:shape[0], self.off // 2:(self.off + nbytes) // 2]
        if dtype == F32:
            ap = ap.bitcast(F32)
        self.off += (nbytes + 63) // 64 * 64
        names = "abcdefg"[:len(shape) - 1]
        if len(shape) > 2:
            pat = "p (%s) -> p %s" % (" ".join(names), " ".join(names))
            ap = ap.rearrange(pat, **{names[i]: shape[1 + i] for i in range(1, len(shape) - 1)})
        return ap

    def rot(self, name, n, shape, dtype):
        return RotAP([(self.take(shape, dtype), "%s%d" % (name, i)) for i in range(n)])


class RotAP:
    def __init__(self, items):
        self.items = items
        self.i = 0

    def next(self):
        it = self.items[self.i % len(self.items)]
        self.i += 1
        return it


def rope_tables_host():
    tab = np.zeros((64, 2, NCOL), np.float32)
    tab[:, 0, :] = 1.0
    t = np.arange(NLAT)
    pos = np.stack([t // 64, t % 64], 0).astype(np.float32)
    freqs = (10000.0 ** (-np.arange(16, dtype=np.float32) / 16)).astype(np.float32)
    for i in range(64):
        a, half, p = i // 32, (i % 32) // 16, i % 16
        ang = (pos[a] * freqs[p]).astype(np.float32)
        tab[i, 0, LAT0:LAT0 + NLAT] = np.cos(ang)
        tab[i, 1, LAT0:LAT0 + NLAT] = np.sin(ang) * (-1.0 if half == 0 else 1.0)
    return tab


SWAP = np.array([(i // 32) * 32 + (1 - (i % 32) // 16) * 16 + i % 16 for i in range(64)])


def host_inputs(inputs, b):
    f = lambda k: np.asarray(inputs[k], np.float32)
    m = {}
    xT0 = np.zeros((D, NCOL), np.float32)
    xT0[:, CTX0:CTX0 + NCTX] = f("ctx")[b].T
    xT0[:, LAT0:LAT0 + NLAT] = f("x")[b].T
    m["xT0"] = xT0
    pv = np.zeros((128, PVL.n), np.float32)

    def put(name, arr):
        o = PVL.off[name]
        pv[:, o:o + arr.shape[1]] = arr

    cc = np.stack([fm(f("c")[b]), fm(f("c_ctx"))], -1).reshape(128, 32)
    put("cc", cc)
    for l in range(DEPTH):
        put("bmod%d" % l, fm(f("b_mod")[l]))
        put("gains%d" % l, np.concatenate([fm(f("norm_gains")[l, i]) for i in range(4)], 1))
        put("fwc%d" % l, np.concatenate([fm(f("ffn_w_conv")[l, k]) for k in range(3)], 1))
        put("fbc%d" % l, fm(f("ffn_b_conv")[l]))
    for j in range(2):
        put("qn%d" % j, fm(f("mla_q_norm")[j]))
        put("kvn%d" % j, fm(f("mla_kv_norm")[j]))
        put("hgn%d" % j, fm(f("hgrn_norm")[j]))
        put("bpw1%d" % j, fm(f("conv_b_pw1")[j]))
        wd = f("conv_w_dw")[j]
        put("wdw%d" % j, np.concatenate([fm(wd[k]) for k in range(31)], 1))
        put("bdw%d" % j, fm(f("conv_b_dw")[j]))
        put("lng%d" % j, fm(f("conv_ln_g")[j]))
        put("lnb%d" % j, fm(f("conv_ln_b")[j]))
        put("bpw2%d" % j, fm(f("conv_b_pw2")[j]))
    hl = f("hgrn_lb")
    put("hlb", np.concatenate([fm(hl[d, jj]) for d in range(2) for jj in range(2)], 1))
    m["pv"] = pv
    return m


def shared_inputs(inputs):
    f = lambda k: np.ascontiguousarray(np.asarray(inputs[k], np.float32))
    m = {k: f(k) for k in ("w_mod", "w_in_ab", "w_kv_up", "w_out_ab", "conv_w_pw1", "conv_w_pw2", "ffn_w_up", "ffn_w_down")}
    m["w_kr_sw"] = np.ascontiguousarray(m["w_in_ab"][:, :, 768:832][:, :, SWAP])
    wq = f("w_q_up").reshape(2, 512, 8, 192)
    m["w_q_full"] = np.ascontiguousarray(np.concatenate([wq, wq[:, :, :, 128:][:, :, :, SWAP]], -1).reshape(2, 512, 2048))
    m["rope"] = rope_tables_host()
    return m


_NC_CACHE = {}


def kernel(**inputs):
    n = 4
    if "full" not in _NC_CACHE:
        _NC_CACHE["full"] = build_program()
    nc = _NC_CACHE["full"]
    sh = shared_inputs(inputs)
    in_maps = []
    for b in range(n):
        m = dict(sh)
        m.update(host_inputs(inputs, b))
        in_maps.append(m)
    res = run_bass_kernel_spmd(nc, in_maps, core_ids=list(range(n)))
    out = np.stack([np.ascontiguousarray(res.results[b]["outT"][:, LAT0:LAT0 + NLAT].T) for b in range(n)], 0)
    return out.astype(np.float32)
```

```python
import contextlib
import numpy as np
import concourse.bass as bass
import concourse.mybir as mybir
from concourse.bass_utils import run_bass_kernel_spmd

dt = mybir.dt
F32 = dt.float32
BF16 = dt.bfloat16
AF = mybir.ActivationFunctionType
ALU = mybir.AluOpType

ENG_NAMES = ("pe", "act", "dve", "pool", "sp")

D = 2048
DEPTH = 4
NCTX = 256
NLAT = 2048
DFF = 5632
EPS = 1e-6
ATTN_SCALE = 192.0 ** -0.5
NCOL = 2352
CTX0 = 16
LAT0 = 288
PIECES = [(CTX0, 256, 1)] + [(LAT0 + 512 * i, 512, 0) for i in range(4)]
WINS = [(CTX0, 1)] + [(LAT0 + 256 * i, 0) for i in range(8)]
ST_RUNS = [[(0, CTX0, 256, 1), (256, LAT0, 512, 0)],
           [(0, 800, 512, 0), (512, 1312, 256, 0)],
           [(0, 1568, 512, 0), (512, 2080, 256, 0)]]
CELL_BOUNDS = np.array([0, 16, 272] + [288 + 256 * i for i in range(9)])


def cells(c0, n):
    a = int(np.searchsorted(CELL_BOUNDS, c0, side="right") - 1)
    b = int(np.searchsorted(CELL_BOUNDS, c0 + n - 1, side="right") - 1)
    return list(range(a, b + 1))


def ck(name, c0, n):
    return [(name, i) for i in cells(c0, n)]


def blkcol(tb):
    return CTX0 + 128 * tb if tb < 2 else LAT0 + 128 * (tb - 2)


class Op:
    __slots__ = ("eng", "fn", "deps", "signal", "tok", "lane")

    def __init__(self, eng, fn, lane=None):
        self.eng = eng
        self.fn = fn
        self.deps = []
        self.signal = lane is not None
        self.tok = None
        self.lane = lane


class Reg:
    __slots__ = ("w", "r")

    def __init__(self):
        self.w = None
        self.r = []


class Prog:
    def __init__(self, nc, n_lanes=8):
        self.nc = nc
        self.ops = {e: [] for e in ENG_NAMES}
        self.regs = {}
        self.n_lanes = n_lanes
        self.lane_last = {}
        self.lane_rr = {e: 0 for e in ENG_NAMES}

    def reg(self, key):
        r = self.regs.get(key)
        if r is None:
            r = self.regs[key] = Reg()
        return r

    def _add(self, op, reads, writes):
        deps = op.deps
        for k in reads:
            r = self.reg(k)
            if r.w is not None:
                deps.append(r.w)
        for k in writes:
            r = self.reg(k)
            if r.w is not None:
                deps.append(r.w)
            deps.extend(r.r)
        for k in reads:
            self.reg(k).r.append(op)
        for k in writes:
            r = self.reg(k)
            r.w = op
            r.r = []
        if op.eng == "pe":
            op.deps = deps = [d for d in deps if d.eng != "pe" or d.lane is not None]
        for d in deps:
            d.signal = True
        self.ops[op.eng].append(op)
        return op

    def op(self, eng, fn, reads=(), writes=()):
        return self._add(Op(eng, fn), reads, writes)

    def dma(self, eng, fn, reads=(), writes=()):
        lane = (eng, self.lane_rr[eng] % self.n_lanes)
        self.lane_rr[eng] += 1
        op = Op(eng, fn, lane=lane)
        prev = self.lane_last.get(lane)
        if prev is not None:
            op.deps.append(prev)
            prev.signal = True
        self.lane_last[lane] = op
        return self._add(op, reads, writes)

    def barrier(self):
        lasts = list(self.lane_last.values())
        for e in ENG_NAMES:
            for o in reversed(self.ops[e]):
                if o.fn is not None:
                    lasts.append(o)
                    break
        for e in ENG_NAMES:
            op = Op(e, None)
            op.deps = [d for d in lasts if not (d.eng == e and d.lane is None)]
            for d in op.deps:
                d.signal = True
            self.ops[e].append(op)
        self.regs = {}

    def emit(self, stack):
        nc = self.nc
        sems = {e: stack.enter_context(nc.semaphore("sem_" + e)) for e in ENG_NAMES}
        lane_sems = {}
        lane_cnt = {}
        for e in ENG_NAMES:
            for i in range(self.n_lanes):
                if (e, i) in self.lane_last:
                    lane_sems[(e, i)] = stack.enter_context(nc.semaphore("lane_%s%d" % (e, i)))
                    lane_cnt[(e, i)] = 0
        for e in ENG_NAMES:
            c = 0
            for op in self.ops[e]:
                if op.lane is not None:
                    lane_cnt[op.lane] += 16
                    op.tok = (lane_sems[op.lane], lane_cnt[op.lane])
                elif op.signal:
                    c += 1
                    op.tok = (sems[e], c)
        final_waits = [(lane_sems[l], lane_cnt[l]) for l in lane_sems]
        block = stack.enter_context(nc.Block())
        ops = self.ops

        def replay(e, h):
            seen = {}
            for op in ops[e]:
                for d in op.deps:
                    sem, val = d.tok
                    k = sem.num
                    if seen.get(k, 0) < val:
                        seen[k] = val
                        h.wait_ge(sem, val)
                if op.fn is None:
                    continue
                inst = op.fn(h)
                if op.lane is not None:
                    inst.then_inc(op.tok[0], 16)
                elif op.signal:
                    inst.then_inc(op.tok[0], 1)
            if e == "sp":
                for sem, val in final_waits:
                    if val:
                        h.wait_ge(sem, val)

        @block.tensor
        def _(h):
            replay("pe", h)

        @block.scalar
        def _(h):
            replay("act", h)

        @block.vector
        def _(h):
            replay("dve", h)

        @block.gpsimd
        def _(h):
            replay("pool", h)

        @block.sync
        def _(h):
            replay("sp", h)


class Rot:
    def __init__(self, kb, name, n, shape, dtype):
        self.items = [(kb.sb("%s%d" % (name, i), shape, dtype), "%s%d" % (name, i)) for i in range(n)]
        self.i = 0

    def next(self):
        it = self.items[self.i % len(self.items)]
        self.i += 1
        return it


class PVLayout:
    def __init__(self):
        self.off = {}
        self.n = 0

    def add(self, name, ncols):
        self.off[name] = self.n
        self.n += ncols


def pv_layout():
    L = PVLayout()
    L.add("cc", 32)
    for l in range(DEPTH):
        L.add("bmod%d" % l, 96)
        L.add("gains%d" % l, 64)
        L.add("fwc%d" % l, 3 * 88)
        L.add("fbc%d" % l, 88)
    for j in range(2):
        L.add("qn%d" % j, 4)
        L.add("kvn%d" % j, 2)
        L.add("hgn%d" % j, 1)
        L.add("bpw1%d" % j, 32)
        L.add("wdw%d" % j, 31 * 16)
        L.add("bdw%d" % j, 16)
        L.add("lng%d" % j, 16)
        L.add("lnb%d" % j, 16)
        L.add("bpw2%d" % j, 16)
    L.add("hlb", 32)
    return L


PVL = pv_layout()


def fm(v):
    v = np.asarray(v, np.float32)
    return np.ascontiguousarray(v.reshape(-1, 128).T)


class KB:
    def __init__(self, nc, st):
        self.nc = nc
        self.st = st
        self.P = Prog(nc)
        self.banks = [st.enter_context(nc.psum_tensor("pb%d" % i, [128, 512], F32)) for i in range(8)]
        self.brr = 0
        self.R = {}
        self.E = {}
        self.ctx_in = True
        self.ctx_out = True

    def phase(self, name):
        self.P.barrier()
        C = Carver(self.arena)
        R, E = self.R, self.E
        if name == "mod":
            R["wmod"] = C.rot("wmod", 2, [128, 16, 256], BF16)
        elif name in ("prenorm", "ln"):
            R["xin2"] = [C.take([128, 16, 512], F32) for _ in range(2)]
        elif name == "ffn":
            R["aT44"] = C.take([128, 44, 768], BF16)
            R["uw"] = [C.take([128, 16, 258], BF16) for _ in range(3)]
            R["hvg"] = C.rot("hvg", 6, [128, 256], F32)
            R["wup"] = C.rot("wup", 2, [128, 16, 512], BF16)
            R["wfin"] = C.rot("wfin", 2, [128, 44, 128], BF16)
            R["rstd768"] = [C.take([128, 768], F32) for _ in range(2)]
        elif name == "final16":
            R["aT16"] = C.take([128, 16, 768], BF16)
            R["wfin"] = C.rot("wfin", 2, [128, 16, 256], BF16)
            R["rstd768"] = [C.take([128, 768], F32) for _ in range(2)]
        elif name == "odd1":
            R["upiece"] = C.rot("upiece", 2, [128, 16, 512], BF16)
            R["hfull"] = C.rot("hfull", 2, [128, NCOL], BF16)
            R["dg"] = C.rot("dg", 2, [128, 31, 128], BF16)
            R["wup"] = C.rot("wup", 2, [128, 16, 256], BF16)
            for ap, key in R["hfull"].items:
                self.memset(ap, 0.0, [key], eng="dve")
        elif name in ("mla_prep", "mla_heads"):
            E["rope"] = C.take([64, 2, NCOL], F32)
            E["cqn"] = C.take([128, 4, NCOL], BF16)
            E["ckvn"] = C.take([128, 2, NCOL], BF16)
            E["krot"] = C.take([64, NCOL], BF16)
            if name == "mla_prep":
                R["upiece"] = C.rot("upiece", 2, [128, 16, 512], BF16)
                E["wprep"] = C.take([128, 16, 896], BF16)
                E["cf"] = C.take([128, 4, 512], F32)
            else:
                E["wq"] = C.take([128, 4, 2048], BF16)
                E["wkv"] = C.take([128, 2, 2048], BF16)
                E["qn"] = C.take([128, NCOL], BF16)
                E["qrot"] = C.take([64, NCOL], BF16)
                E["kn"] = C.take([128, NCOL], BF16)
                E["V"] = C.take([128, 18, 128], BF16)
        elif name == "hgrn":
            R["upiece"] = C.rot("upiece", 2, [128, 16, 256], BF16)
            R["wmod"] = C.rot("wmod", 2, [128, 16, 256], BF16)
            E["wh"] = C.take([128, 16, 640], BF16)
            E["OT"] = C.take([128, NCOL], F32)
            for nm in ("SG", "QT0", "QT1", "KT0", "KT1", "KH0", "KH1"):
                E[nm] = C.take([128, NCOL], BF16)
            E["Vh"] = C.take([32, 72, 128], BF16)
            for nm in ("MS0", "MS1", "FD0", "FD1", "tmpc"):
                E[nm] = C.take([128, 72], F32)
            for nm in ("Sr0", "Sr1"):
                E[nm] = C.take([128, 8, 128], F32)
            for nm in ("SMr0", "SMr1"):
                E[nm] = C.take([128, 8, 128], BF16)
            E["atm"] = [C.take([32, 2, 4, 32], BF16) for _ in range(2)]
            E["kht"] = [C.take([32, 4, 128], BF16) for _ in range(2)]
            E["gt"] = [[C.take([128, 256], F32) for _ in range(7)] for _ in range(3)]
            E["maskP"] = C.take([128, 256], F32)
            self.memset(E["maskP"], 1.0, ["maskP"], eng="dve")
            self.memset(E["maskP"].rearrange("p (c t) -> p c t", t=32)[:, :, 0:1], 0.0, ["maskP"], eng="dve")
        else:
            raise ValueError(name)

    def sb(self, name, shape, dtype):
        return self.st.enter_context(self.nc.sbuf_tensor("s_" + name, shape, dtype))

    def dram(self, name, shape, dtype):
        return self.nc.dram_tensor(name, shape, dtype, kind="Internal").ap()

    def bank(self, lo=0, hi=6):
        i = lo + self.brr % (hi - lo)
        self.brr += 1
        return self.banks[i], ("pb", i)

    def mm(self, out, lhsT, rhs, start, stop, reads, writes):
        self.P.op("pe", lambda h: h.matmul(out, lhsT=lhsT, rhs=rhs, start=start, stop=stop), reads, writes)

    def act(self, out, in_, func, reads, writes, bias=0.0, scale=1.0):
        self.P.op("act", lambda h: h.activation(out=out, in_=in_, func=func, bias=bias, scale=scale), reads, writes)

    def tt(self, out, in0, in1, op, reads, writes, eng="dve"):
        self.P.op(eng, lambda h: h.tensor_tensor(out=out, in0=in0, in1=in1, op=op), reads, writes)

    def stt(self, out, in0, scalar, in1, op0, op1, reads, writes):
        self.P.op("dve", lambda h: h.scalar_tensor_tensor(out=out, in0=in0, scalar=scalar, in1=in1, op0=op0, op1=op1), reads, writes)

    def ts(self, out, in0, s1, s2, op0, op1, reads, writes):
        self.P.op("dve", lambda h: h.tensor_scalar(out=out, in0=in0, scalar1=s1, scalar2=s2, op0=op0, op1=op1), reads, writes)

    def recip(self, out, in_, reads, writes):
        self.P.op("dve", lambda h: h.reciprocal(out=out, in_=in_), reads, writes)

    def copy(self, out, in_, reads, writes, eng="dve"):
        if eng == "act":
            self.P.op("act", lambda h: h.activation(out=out, in_=in_, func=AF.Identity), reads, writes)
        else:
            self.P.op(eng, lambda h: h.tensor_copy(out=out, in_=in_), reads, writes)

    def memset(self, ap, val, writes, eng="pool"):
        self.P.op(eng, lambda h: h.memset(ap, val), (), writes)

    def dma(self, out, in_, reads, writes, eng="sp"):
        self.P.dma(eng, lambda h: h.dma_start(out=out, in_=in_), reads, writes)

    def rsqrt_from(self, out, in_, scale, reads, writes):
        self.act(out, in_, AF.Sqrt, reads, writes, bias=self.epsb[:, 0:1], scale=scale)
        self.recip(out, out, writes, writes)

    def setup(self, io):
        self.io = io
        P = self.P
        self.xT = self.dram("xT_d", [D, NCOL], F32)
        self.uT = self.dram("uT_d", [D, NCOL], BF16)
        self.hT = self.dram("hT_d", [D, NCOL], BF16)
        self.hc = self.dram("hc_d", [D, NCOL], F32)
        self.yT2 = [self.dram("yT_d%d" % i, [D, 768], F32) for i in range(2)]
        self.pv = self.sb("pv", [128, PVL.n], F32)
        self.ones_f = self.sb("ones_f", [128, 128], F32)
        self.ones_b = self.sb("ones_b", [128, 128], BF16)
        self.id_f = self.sb("id_f", [128, 128], F32)
        self.id_b = self.sb("id_b", [128, 128], BF16)
        self.epsb = self.sb("epsb", [128, 1], F32)
        self.zer = self.sb("zer", [128, 16], BF16)
        self.mods = self.sb("mods", [128, DEPTH * 6 * 16 * 2], F32)
        self.modT = self.sb("modT", [128, DEPTH * 96 * 2], F32)
        self.scb = self.sb("scb", [128, 32], BF16)
        self.lbt = self.sb("lbt", [128, 2 * 8 * 2], F32)
        self.lb0 = self.sb("lb0", [128, 2], F32)
        self.mask_f = self.sb("mask_f", [32, 32], F32)
        self.mask_b = self.sb("mask_b", [32, 32], F32)
        self.dma(self.pv[:], io["pv"], (), ["pv"])
        self.memset(self.ones_f[:], 1.0, ["ones_f"])
        self.memset(self.ones_b[:], 1.0, ["ones_b"])
        self.memset(self.epsb[:], EPS, ["epsb"])
        self.memset(self.zer[:], 0.0, ["zer"])
        self.memset(self.id_f[:], 0.0, ["id_f"])
        P.op("pool", lambda h: h.affine_select(out=self.id_f[:], in_=self.id_f[:], pattern=[[-1, 128]], compare_op=ALU.not_equal,
                                               fill=1.0, base=0, channel_multiplier=1), ["id_f"], ["id_f"])
        self.copy(self.id_b[:], self.id_f[:], ["id_f"], ["id_b"])
        for di, (m, pstep, cmul) in enumerate(((self.mask_f, 1, -1), (self.mask_b, -1, 1))):
            key = "mask%d" % di
            self.memset(m[:], 1.0, [key])
            P.op("pool", lambda h, m=m, ps=pstep, cm=cmul: h.affine_select(out=m[:], in_=m[:], pattern=[[ps, 32]], compare_op=ALU.is_ge,
                                                                            fill=0.0, base=0, channel_multiplier=cm), [key], [key])
        for c in range(16):
            self.dma(self.xT[c * 128:(c + 1) * 128, :], io["xT0"][c * 128:(c + 1) * 128, :], (), [("xT", i) for i in range(12)])
        for c in range(16):
            for p0 in (0, 272, 2336):
                self.dma(self.uT[c * 128:(c + 1) * 128, p0:p0 + 16], self.zer[:], ["zer"], ck("uT", p0, 16))
        o = PVL.off["cc"]
        self.act(self.scb[:], self.pv[:, o:o + 32], AF.Silu, ["pv"], ["scb"])
        o = PVL.off["hlb"]
        hl = self.pv[:, o:o + 32].rearrange("p (d j h) -> p d j h", d=2, j=2)
        lbv = self.lbt[:].rearrange("p (d h t) -> p d h t", d=2, h=8)
        self.tt(lbv[:, :, :, 0], hl[:, :, 1, :], hl[:, :, 0, :], ALU.subtract, ["pv"], ["lbt"])
        self.act(lbv[:, :, :, 1], lbv[:, :, :, 0], AF.Sigmoid, ["lbt"], ["lbt"], scale=-1.0)
        self.act(lbv[:, :, :, 0], lbv[:, :, :, 0], AF.Sigmoid, ["lbt"], ["lbt"])
        self.memset(self.lb0[:, 0:1], 0.0, ["lb0"])
        self.memset(self.lb0[:, 1:2], 1.0, ["lb0"])

    def lbs(self, j, d, h):
        if j == 0:
            return self.lb0[:, 0:1], self.lb0[:, 1:2], "lb0"
        o = (d * 8 + h) * 2
        return self.lbt[:, o:o + 1], self.lbt[:, o + 1:o + 2], "lbt"

    def mod(self, l, which, c, kind):
        o = (((l * 6 + which) * 16 + c) * 2) + kind
        return self.mods[:, o:o + 1]

    def mod_stream(self, layers):
        io = self.io
        scv = self.scb[:].rearrange("p (c k) -> p c k", k=2)
        for l in layers:
            o = PVL.off["bmod%d" % l]
            for blk in range(48):
                wt, wkey = self.R["wmod"].next()
                self.dma(wt[:, :, :], io["w_mod"][l, :, blk * 256:(blk + 1) * 256].rearrange("(k p) m -> p k m", p=128), (), [wkey], eng="pool")
                bk, bkey = self.bank()
                for jj in range(2):
                    for k in range(16):
                        self.mm(bk[:, jj * 2:jj * 2 + 2], wt[:, k, jj * 128:(jj + 1) * 128], scv[:, k, :], k == 0, k == 15, [wkey, "scb"], [bkey])
                oc = blk * 2
                mt = self.modT[:, l * 192 + oc * 2:l * 192 + oc * 2 + 4].rearrange("p (o k) -> p o k", k=2)
                self.tt(mt, bk[:, 0:4].rearrange("p (o k) -> p o k", k=2), self.pv[:, o + oc:o + oc + 2].unsqueeze(2).to_broadcast([128, 2, 2]),
                        ALU.add, [bkey, "pv"], [("modT", l)])
                yield
            go = PVL.off["gains%d" % l]
            md = self.mods[:, l * 192:(l + 1) * 192].rearrange("p (w c k) -> p w c k", w=6, c=16)
            mt6 = self.modT[:, l * 192:(l + 1) * 192].rearrange("p (w c k) -> p w c k", w=6, c=16)
            mk, tk_ = ("mods", l), ("modT", l)
            for (dst, gi, mi, plus1) in ((0, 0, 1, True), (2, 1, 2, False), (3, 2, 4, True), (5, 3, 5, False)):
                gb = self.pv[:, go + gi * 16:go + (gi + 1) * 16].unsqueeze(2).to_broadcast([128, 16, 2])
                if plus1:
                    self.stt(md[:, dst], mt6[:, mi], 1.0, gb, ALU.add, ALU.mult, [tk_, "pv"], [mk])
                else:
                    self.tt(md[:, dst], mt6[:, mi], gb, ALU.mult, [tk_, "pv"], [mk])
            self.copy(md[:, 1], mt6[:, 0], [tk_], [mk])
            self.copy(md[:, 4], mt6[:, 3], [tk_], [mk])
            self.mod_ready.add(l)
            yield

    def pump_mod(self, n):
        for _ in range(n):
            if self.modgen is None:
                return
            try:
                next(self.modgen)
            except StopIteration:
                self.modgen = None

    def need_mod(self, l):
        while l not in self.mod_ready:
            assert self.modgen is not None
            self.pump_mod(1)

    def prenorm(self, l, sub):
        self.phase("prenorm")
        R = self.R
        wS, wB = (0, 1) if sub == 1 else (3, 4)
        ctx_on = self.ctx_in if sub == 1 else self.ctx_out
        for pi, (c0, n, kind) in enumerate(PIECES if ctx_on else PIECES[1:]):
            xin = R["xin2"][pi % 2]
            xs = pi % 2
            for c in range(16):
                self.dma(xin[:, c, :n], self.xT[c * 128:(c + 1) * 128, c0:c0 + n], ck("xT", c0, n), [("xin", xs, c)])
            bk, bkey = self.bank()
            for c in range(16):
                sq, sqk = R["sq"].next()
                self.act(sq[:, :n], xin[:, c, :n], AF.Square, [("xin", xs, c)], [sqk])
                self.mm(bk[:, :n], self.ones_f[:], sq[:, :n], c == 0, c == 15, ["ones_f", sqk], [bkey])
            rs, rsk = R["rstd"].next()
            self.rsqrt_from(rs[:, :n], bk[:, :n], 1.0 / D, [bkey, "epsb"], [rsk])
            for c in range(16):
                tmp, tk = R["tmpf"].next()
                self.tt(tmp[:, :n], xin[:, c, :n], rs[:, :n], ALU.mult, [("xin", xs, c), rsk], [tk])
                ub, uk = R["ub"].next()
                self.act(ub[:, :n], tmp[:, :n], AF.Identity, [tk, ("mods", l)], [uk], bias=self.mod(l, wB, c, kind), scale=self.mod(l, wS, c, kind))
                self.dma(self.uT[c * 128:(c + 1) * 128, c0:c0 + n], ub[:, :n], [uk], ck("uT", c0, n))

    def final_mm(self, l, sub, s, slot, aT, akeys, Kc, w_ap, bias_off, wblk, pump=None):
        R = self.R
        skip_ctx = (s == 0 and not self.ctx_out)
        parts = [(256, 512, 6)] if skip_ctx else [(0, 512, 6), (512, 256, 7)]
        yT = self.yT2[slot]
        nj = wblk // 128
        for blk in range(D // wblk):
            wt, wkey = R["wfin"].next()
            self.dma(wt[:, :Kc, :], w_ap[:, blk * wblk:(blk + 1) * wblk].rearrange("(k p) m -> p k m", p=128), (), [wkey], eng="pool")
            for jj in range(nj):
                c = blk * nj + jj
                for (n0, nn, sb_) in parts:
                    bk, bkey = self.bank()
                    for k in range(Kc):
                        self.mm(bk[:, :nn], wt[:, k, jj * 128:(jj + 1) * 128], aT[:, k, n0:n0 + nn], k == 0, k == Kc - 1,
                                [wkey] + akeys, [bkey])
                    ysb, yk = R["tmpf"].next()
                    bias = 0.0 if bias_off is None else self.pv[:, bias_off + c:bias_off + c + 1]
                    self.act(ysb[:, :nn], bk[:, :nn], AF.Identity, [bkey, "pv"], [yk], bias=bias)
                    sq, sqk = R["sq"].next()
                    self.act(sq[:, :nn], ysb[:, :nn], AF.Square, [yk], [sqk])
                    self.mm(self.banks[sb_][:, :nn], self.ones_f[:], sq[:, :nn], c == 0, c == 15, ["ones_f", sqk], [("pb", sb_)])
                    self.dma(yT[c * 128:(c + 1) * 128, n0:n0 + nn], ysb[:, :nn], [yk], [("yT", slot, c)])
                    if pump is not None:
                        pump()
        rs = R["rstd768"][slot]
        if skip_ctx:
            self.rsqrt_from(rs[:, 256:768], self.banks[6][:, :512], 1.0 / D, [("pb", 6), "epsb"], [("rs768a", slot)])
        else:
            self.rsqrt_from(rs[:, 0:512], self.banks[6][:, :512], 1.0 / D, [("pb", 6), "epsb"], [("rs768a", slot)])
            self.rsqrt_from(rs[:, 512:768], self.banks[7][:, :256], 1.0 / D, [("pb", 7), "epsb"], [("rs768b", slot)])

    def final_tail(self, l, sub, s, slot):
        R = self.R
        wG = 2 if sub == 1 else 5
        yT = self.yT2[slot]
        rs = R["rstd768"][slot]
        runs = [r for r in ST_RUNS[s] if r[3] == 0 or self.ctx_out]
        rkeys = [("rs768a", slot)] if (s == 0 and not self.ctx_out) else [("rs768a", slot), ("rs768b", slot)]
        for c in range(16):
            for (lo, c0, n, kind) in runs:
                yl, ylk = R["tmpf"].next()
                xl, xlk = R["sq"].next()
                self.dma(yl[:, :n], yT[c * 128:(c + 1) * 128, lo:lo + n], [("yT", slot, c)], [ylk])
                self.dma(xl[:, :n], self.xT[c * 128:(c + 1) * 128, c0:c0 + n], ck("xT", c0, n), [xlk])
                self.tt(yl[:, :n], yl[:, :n], rs[:, lo:lo + n], ALU.mult, [ylk] + rkeys, [ylk])
                self.stt(xl[:, :n], yl[:, :n], self.mod(l, wG, c, kind), xl[:, :n], ALU.mult, ALU.add, [ylk, xlk, ("mods", l)], [xlk])
                self.dma(self.xT[c * 128:(c + 1) * 128, c0:c0 + n], xl[:, :n], [xlk], ck("xT", c0, n))
                yield

    def final_from_dram(self, l, sub, w_ap, bias_off):
        self.phase("final16")
        R = self.R
        tail = None

        def pump():
            if tail is not None:
                next(tail, None)
                next(tail, None)

        for s in range(3):
            aT = R["aT16"]
            for (lo, c0, n, kind) in ST_RUNS[s]:
                if kind == 1 and not self.ctx_out:
                    continue
                self.dma(aT[:, :, lo:lo + n], self.hT[:, c0:c0 + n].rearrange("(k p) n -> p k n", p=128), ck("hT", c0, n), [("aT16", lo)])
            self.final_mm(l, sub, s, s % 2, aT, [("aT16", 0), ("aT16", 256), ("aT16", 512)], 16, w_ap, bias_off, 256, pump)
            if tail is not None:
                for _ in tail:
                    pass
            tail = self.final_tail(l, sub, s, s % 2)
        for _ in tail:
            pass

    def ffn(self, l):
        io = self.io
        R = self.R
        self.prenorm(l, 2)
        wo = PVL.off["fwc%d" % l]
        bo = PVL.off["fbc%d" % l]
        self.phase("ffn")
        aT = R["aT44"]
        akeys = [("aT44", 0), ("aT44", 1), ("aT44", 2)]
        tail = None
        unit = 0
        for s in range(3):
            uws = []
            wis = [wi for wi in range(3) if WINS[3 * s + wi][1] == 0 or self.ctx_out]
            for wi in range(3):
                c0, kind = WINS[3 * s + wi]
                uw = R["uw"][wi]
                if wi in wis:
                    self.dma(uw[:, :, :], self.uT[:, c0 - 1:c0 + 257].rearrange("(k p) n -> p k n", p=128), ck("uT", c0 - 1, 258), [("uw", wi)])
                uws.append(uw)
            for jj in range(22):
                wu, wkey = R["wup"].next()
                self.dma(wu[:, :, 0:256], io["ffn_w_up"][l, :, jj * 256:(jj + 1) * 256].rearrange("(k p) m -> p k m", p=128), (), [wkey + "a"], eng="pool")
                self.dma(wu[:, :, 256:512], io["ffn_w_up"][l, :, DFF + jj * 256:DFF + (jj + 1) * 256].rearrange("(k p) m -> p k m", p=128), (), [wkey + "b"], eng="pool")
                for pj in range(2):
                    j = jj * 2 + pj
                    for wi in wis:
                        res = []
                        for half, chn in ((0, j), (1, 44 + j)):
                            bk, bkey = self.bank()
                            for k in range(16):
                                self.mm(bk[:, :258], wu[:, k, half * 256 + pj * 128:half * 256 + (pj + 1) * 128], uws[wi][:, k, :], k == 0, k == 15,
                                        [wkey + "a", wkey + "b", ("uw", wi)], [bkey])
                            hh, hk = R["hvg"].next()
                            w0 = self.pv[:, wo + chn:wo + chn + 1]
                            w1 = self.pv[:, wo + 88 + chn:wo + 88 + chn + 1]
                            w2 = self.pv[:, wo + 176 + chn:wo + 176 + chn + 1]
                            bb = self.pv[:, bo + chn:bo + chn + 1]
                            self.act(hh[:, :], bk[:, 1:257], AF.Identity, [bkey, "pv"], [hk], bias=bb, scale=w1)
                            self.stt(hh[:, :], bk[:, 0:256], w0, hh[:, :], ALU.mult, ALU.add, [bkey, hk, "pv"], [hk])
                            self.stt(hh[:, :], bk[:, 2:258], w2, hh[:, :], ALU.mult, ALU.add, [bkey, hk, "pv"], [hk])
                            res.append((hh, hk))
                        (hv, hvk), (hg, hgk) = res
                        self.act(hg[:, :], hg[:, :], AF.Silu, [hgk], [hgk])
                        self.tt(aT[:, j, wi * 256:(wi + 1) * 256], hg[:, :], hv[:, :], ALU.mult, [hgk, hvk], [("aT44", wi)])
                        unit += 1
                        if tail is not None and unit % 2 == 0:
                            next(tail, None)
            if tail is not None:
                for _ in tail:
                    pass
            self.final_mm(l, 2, s, s % 2, aT, akeys, 44, io["ffn_w_down"][l], None, 128)
            tail = self.final_tail(l, 2, s, s % 2)
        for _ in tail:
            pass

    def conformer(self, l):
        io = self.io
        R = self.R
        j = l // 2
        self.prenorm(l, 1)
        self.phase("odd1")
        pieces = PIECES if self.ctx_in else PIECES[1:]
        b1 = PVL.off["bpw1%d" % j]
        wdw = PVL.off["wdw%d" % j]
        bdw = PVL.off["bdw%d" % j]
        for c in range(16):
            wt, wkey = R["wup"].next()
            self.dma(wt[:, :, 0:128], io["conv_w_pw1"][j, :, c * 128:(c + 1) * 128].rearrange("(k p) m -> p k m", p=128), (), [wkey + "a"], eng="pool")
            self.dma(wt[:, :, 128:256], io["conv_w_pw1"][j, :, D + c * 128:D + (c + 1) * 128].rearrange("(k p) m -> p k m", p=128), (), [wkey + "b"], eng="pool")
            hf, hfk = R["hfull"].next()
            for (c0, n, kind) in pieces:
                up, upk = R["upiece"].next()
                self.dma(up[:, :, :n], self.uT[:, c0:c0 + n].rearrange("(k p) n -> p k n", p=128), ck("uT", c0, n), [upk])
                ba, bak = self.bank()
                bg, bgk = self.bank()
                for k in range(16):
                    self.mm(ba[:, :n], wt[:, k, 0:128], up[:, k, :n], k == 0, k == 15, [wkey + "a", wkey + "b", upk], [bak])
                for k in range(16):
                    self.mm(bg[:, :n], wt[:, k, 128:256], up[:, k, :n], k == 0, k == 15, [wkey + "a", wkey + "b", upk], [bgk])
                sg, sgk = R["tmpf"].next()
                self.act(sg[:, :n], bg[:, :n], AF.Sigmoid, [bgk, "pv"], [sgk], bias=self.pv[:, b1 + 16 + c:b1 + 16 + c + 1])
                self.stt(hf[:, c0:c0 + n], ba[:, :n], self.pv[:, b1 + c:b1 + c + 1], sg[:, :n], ALU.add, ALU.mult, [bak, sgk, "pv"], [hfk])
            dg, dgk = R["dg"].next()
            wv = self.pv[:, wdw:wdw + 496].rearrange("p (t c) -> p t c", c=16)[:, :, c:c + 1]
            self.tt(dg[:, :, :], self.id_b[:].unsqueeze(1).to_broadcast([128, 31, 128]), wv.to_broadcast([128, 31, 128]), ALU.mult,
                    ["id_b", "pv"], [dgk])
            for (c0, n, kind) in pieces:
                bk, bkey = self.bank()
                for t in range(31):
                    self.mm(bk[:, :n], dg[:, t, :], hf[:, c0 - 15 + t:c0 - 15 + t + n], t == 0, t == 30, [dgk, hfk], [bkey])
                oc, ock = R["tmpf"].next()
                self.act(oc[:, :n], bk[:, :n], AF.Identity, [bkey, "pv"], [ock], bias=self.pv[:, bdw + c:bdw + c + 1])
                self.dma(self.hc[c * 128:(c + 1) * 128, c0:c0 + n], oc[:, :n], [ock], ck("hc", c0, n))
        self.phase("ln")
        lg = PVL.off["lng%d" % j]
        lbo = PVL.off["lnb%d" % j]
        for pi, (c0, n, kind) in enumerate(pieces):
            xin = R["xin2"][pi % 2]
            xs = pi % 2
            for c in range(16):
                self.dma(xin[:, c, :n], self.hc[c * 128:(c + 1) * 128, c0:c0 + n], ck("hc", c0, n), [("xin", xs, c)])
            bm, bmk = self.bank()
            bq, bqk = self.bank()
            for c in range(16):
                sq, sqk = R["sq"].next()
                self.act(sq[:, :n], xin[:, c, :n], AF.Square, [("xin", xs, c)], [sqk])
                self.mm(bm[:, :n], self.ones_f[:], xin[:, c, :n], c == 0, c == 15, ["ones_f", ("xin", xs, c)], [bmk])
                self.mm(bq[:, :n], self.ones_f[:], sq[:, :n], c == 0, c == 15, ["ones_f", sqk], [bqk])
            mean, mk = R["rstd"].next()
            rs, rsk = R["rstd"].next()
            self.act(mean[:, :n], bm[:, :n], AF.Identity, [bmk], [mk], scale=1.0 / D)
            m2, m2k = R["tmpf"].next()
            self.tt(m2[:, :n], mean[:, :n], mean[:, :n], ALU.mult, [mk], [m2k])
            self.stt(m2[:, :n], bq[:, :n], 1.0 / D, m2[:, :n], ALU.mult, ALU.subtract, [bqk, m2k], [m2k])
            self.rsqrt_from(rs[:, :n], m2[:, :n], 1.0, [m2k, "epsb"], [rsk])
            for c in range(16):
                tmp, tk = R["tmpf"].next()
                self.tt(tmp[:, :n], xin[:, c, :n], mean[:, :n], ALU.subtract, [("xin", xs, c), mk], [tk])
                self.tt(tmp[:, :n], tmp[:, :n], rs[:, :n], ALU.mult, [tk, rsk], [tk])
                ub, uk = R["ub"].next()
                self.act(ub[:, :n], tmp[:, :n], AF.Silu, [tk, "pv"], [uk], bias=self.pv[:, lbo + c:lbo + c + 1], scale=self.pv[:, lg + c:lg + c + 1])
                self.dma(self.hT[c * 128:(c + 1) * 128, c0:c0 + n], ub[:, :n], [uk], ck("hT", c0, n))
        self.final_from_dram(l, 1, io["conv_w_pw2"][j], PVL.off["bpw2%d" % j])

    def even(self, l):
        io = self.io
        R, E = self.R, self.E
        j = l // 2
        self.prenorm(l, 1)
        if self.stop_at == "pre":
            return
        self.phase("mla_prep")
        wp = E["wprep"]
        self.dma(wp[:, :, 0:832], io["w_in_ab"][j, :, 0:832].rearrange("(k p) m -> p k m", p=128), (), ["wprep_a"], eng="pool")
        self.dma(wp[:, :, 832:896], io["w_kr_sw"][j].rearrange("(k p) m -> p k m", p=128), (), ["wprep_b"], eng="pool")
        self.dma(E["rope"][:, :, :], io["rope"], (), ["rope"])
        wpk = ["wprep_a", "wprep_b"]
        qn_o = PVL.off["qn%d" % j]
        kvn_o = PVL.off["kvn%d" % j]
        cqn, ckvn, krot = E["cqn"], E["ckvn"], E["krot"]
        for (c0, n, kind) in PIECES:
            up, upk = R["upiece"].next()
            self.dma(up[:, :, :n], self.uT[:, c0:c0 + n].rearrange("(k p) n -> p k n", p=128), ck("uT", c0, n), [upk])
            for (nm, nch, col_off, dst, gain_o) in (("cq", 4, 0, cqn, qn_o), ("ckv", 2, 512, ckvn, kvn_o)):
                sbk, sbkey = self.bank()
                cf = E["cf"]
                for c in range(nch):
                    bk, bkey = self.bank()
                    for k in range(16):
                        self.mm(bk[:, :n], wp[:, k, col_off + c * 128:col_off + (c + 1) * 128], up[:, k, :n], k == 0, k == 15, wpk + [upk], [bkey])
                    self.act(cf[:, c, :n], bk[:, :n], AF.Identity, [bkey], [("xin", c)])
                    sq, sqk = R["sq"].next()
                    self.act(sq[:, :n], bk[:, :n], AF.Square, [bkey], [sqk])
                    self.mm(sbk[:, :n], self.ones_f[:], sq[:, :n], c == 0, c == nch - 1, ["ones_f", sqk], [sbkey])
                rs, rsk = R["rstd"].next()
                self.rsqrt_from(rs[:, :n], sbk[:, :n], 1.0 / (nch * 128), [sbkey, "epsb"], [rsk])
                for c in range(nch):
                    tmp, tk = R["tmpf"].next()
                    self.tt(tmp[:, :n], cf[:, c, :n], rs[:, :n], ALU.mult, [("xin", c), rsk], [tk])
                    self.act(dst[:, c, c0:c0 + n], tmp[:, :n], AF.Identity, [tk, "pv"], ck(nm + "n", c0, n), scale=self.pv[:, gain_o + c:gain_o + c + 1])
            bk, bkey = self.bank()
            bs, bskey = self.bank()
            for k in range(16):
                self.mm(bk[0:64, :n], wp[:, k, 768:832], up[:, k, :n], k == 0, k == 15, wpk + [upk], [bkey])
            for k in range(16):
                self.mm(bs[0:64, :n], wp[:, k, 832:896], up[:, k, :n], k == 0, k == 15, wpk + [upk], [bskey])
            self.rope_combine(krot[:, c0:c0 + n], bk, bkey, bs, bskey, c0, n, ck("krot", c0, n))
        if self.stop_at == "prep":
            return
        self.phase("mla_heads")
        cqn, ckvn, krot = E["cqn"], E["ckvn"], E["krot"]
        self.dma(E["wq"][:, :, :], io["w_q_full"][j].rearrange("(k p) m -> p k m", p=128), (), ["wq"], eng="pool")
        self.dma(E["wkv"][:, :, :], io["w_kv_up"][j].rearrange("(k p) m -> p k m", p=128), (), ["wkv"], eng="pool")
        wq, wkv = E["wq"], E["wkv"]
        for h in range(8):
            qn, qrot, kn, V = E["qn"], E["qrot"], E["kn"], E["V"]
            for pi, (c0, n, kind) in enumerate(PIECES):
                bk, bkey = self.bank()
                for k in range(4):
                    self.mm(bk[:, :n], wq[:, k, h * 256:h * 256 + 128], cqn[:, k, c0:c0 + n], k == 0, k == 3, ["wq"] + ck("cqn", c0, n), [bkey])
                self.copy(qn[:, c0:c0 + n], bk[:, :n], [bkey], ck("qn", c0, n), eng="act")
                b1, b1k = self.bank()
                b2, b2k = self.bank()
                for k in range(4):
                    self.mm(b1[0:64, :n], wq[:, k, h * 256 + 128:h * 256 + 192], cqn[:, k, c0:c0 + n], k == 0, k == 3, ["wq"] + ck("cqn", c0, n), [b1k])
                for k in range(4):
                    self.mm(b2[0:64, :n], wq[:, k, h * 256 + 192:h * 256 + 256], cqn[:, k, c0:c0 + n], k == 0, k == 3, ["wq"] + ck("cqn", c0, n), [b2k])
                self.rope_combine(qrot[:, c0:c0 + n], b1, b1k, b2, b2k, c0, n, ck("qrot", c0, n))
                bk, bkey = self.bank()
                for k in range(2):
                    self.mm(bk[:, :n], wkv[:, k, h * 256:h * 256 + 128], ckvn[:, k, c0:c0 + n], k == 0, k == 1, ["wkv"] + ck("ckvn", c0, n), [bkey])
                self.copy(kn[:, c0:c0 + n], bk[:, :n], [bkey], ck("kn", c0, n), eng="act")
                bk, bkey = self.bank()
                for k in range(2):
                    self.mm(bk[:, :n], wkv[:, k, h * 256 + 128:h * 256 + 256], ckvn[:, k, c0:c0 + n], k == 0, k == 1, ["wkv"] + ck("ckvn", c0, n), [bkey])
                vt, vtk = R["ub"].next()
                self.copy(vt[:, :n], bk[:, :n], [bkey], [vtk], eng="act")
                self.to_tokmajor(vt, vtk, c0, n, V, "V")
            for (c0, n, kind) in (PIECES if self.ctx_out else PIECES[1:]):
                kbs = [0, 1] if kind == 1 else list(range(18))
                ob, obk = self.banks[6], ("pb", 6)
                db, dbk = self.banks[7], ("pb", 7)
                pend = None
                nk = len(kbs)
                for i in range(nk + 1):
                    if i < nk:
                        kb = kbs[i]
                        kc = blkcol(kb)
                        sbk, sbkey = self.bank()
                        self.mm(sbk[:, :n], kn[:, kc:kc + 128], qn[:, c0:c0 + n], True, False, ck("kn", kc, 128) + ck("qn", c0, n), [sbkey])
                        self.mm(sbk[:, :n], krot[:, kc:kc + 128], qrot[:, c0:c0 + n], False, True, ck("krot", kc, 128) + ck("qrot", c0, n), [sbkey])
                        pt, ptk = R["ub"].next()
                        self.act(pt[:, :n], sbk[:, :n], AF.Exp, [sbkey], [ptk], scale=ATTN_SCALE)
                        cur = (i, kb, pt, ptk)
                    if pend is not None:
                        pi, pkb, ppt, pptk = pend
                        self.mm(ob[:, :n], V[:, pkb, :], ppt[:, :n], pi == 0, pi == nk - 1, [("V", pkb), pptk], [obk])
                        self.mm(db[:, :n], self.ones_b[:], ppt[:, :n], pi == 0, pi == nk - 1, ["ones_b", pptk], [dbk])
                    pend = cur if i < nk else None
                rd, rdk = R["tmpf"].next()
                self.recip(rd[:, :n], db[:, :n], [dbk], [rdk])
                at, atk = R["ub"].next()
                self.tt(at[:, :n], ob[:, :n], rd[:, :n], ALU.mult, [obk, rdk], [atk])
                self.dma(self.hT[h * 128:(h + 1) * 128, c0:c0 + n], at[:, :n], [atk], ck("hT", c0, n))
        if self.stop_at == "mla":
            return
        self.phase("hgrn")
        for h in range(8):
            self.hgrn_head(l, j, h)
        if self.stop_at == "hgrn":
            return
        self.final_from_dram(l, 1, io["w_out_ab"][j], None)

    def rope_combine(self, dst, b1, b1k, b2, b2k, c0, n, wkeys):
        R, E = self.R, self.E
        rope = E["rope"]
        t1, t1k = R["tmpf"].next()
        t2, t2k = R["tmpf"].next()
        self.tt(t1[0:64, :n], b1[0:64, :n], rope[:, 0, c0:c0 + n], ALU.mult, [b1k, "rope"], [t1k])
        self.tt(t2[0:64, :n], b2[0:64, :n], rope[:, 1, c0:c0 + n], ALU.mult, [b2k, "rope"], [t2k])
        self.tt(dst, t1[0:64, :n], t2[0:64, :n], ALU.add, [t1k, t2k], wkeys)

    def to_tokmajor(self, src, srck, c0, n, dst, dname, blk=128, banks=None):
        per = 128 // blk
        for q in range(n // blk):
            col = c0 + q * blk
            tb = (col - CTX0) // blk if col < LAT0 else 2 * per + (col - LAT0) // blk
            if banks is None:
                bk, bkey = self.bank()
            else:
                bi = banks[q % len(banks)]
                bk, bkey = self.banks[bi], ("pb", bi)
            bkb = bk.bitcast(BF16)
            self.P.op("pe", lambda h, bkb=bkb, q=q: h.transpose(bkb[0:blk, 0:128], src[:, q * blk:(q + 1) * blk], self.id_b[:]), [srck, "id_b"], [bkey])
            self.copy(dst[:, tb, :], bkb[0:blk, 0:128], [bkey], [(dname, tb)])

    def hgrn_head(self, l, j, h):
        io = self.io
        R, E = self.R, self.E
        wh = E["wh"]
        for gi, base in enumerate((832, 1856, 2880, 3904, 4928)):
            self.dma(wh[:, :, gi * 128:(gi + 1) * 128], io["w_in_ab"][j, :, base + h * 128:base + (h + 1) * 128].rearrange("(k p) m -> p k m", p=128),
                     (), [("wh", gi)], eng="pool")
        whk = [("wh", g) for g in range(5)]
        SG, OT, Vh = E["SG"], E["OT"], E["Vh"]
        QT, KT, KH = [E["QT0"], E["QT1"]], [E["KT0"], E["KT1"]], [E["KH0"], E["KH1"]]
        MS, FDc = [E["MS0"], E["MS1"]], [E["FD0"], E["FD1"]]
        lo, hi = CTX0, LAT0 + NLAT
        CH, MID, LAST = 32, 15, 31
        maskP = E["maskP"]
        def piece_gen(pi, c0, kind):
            n = 256
            ci0 = (c0 - CTX0) // 32 if kind == 1 else 8 + (c0 - LAT0) // 32
            up, upk = R["upiece"].next()
            self.dma(up[:, :, :n], self.uT[:, c0:c0 + n].rearrange("(k p) n -> p k n", p=128), ck("uT", c0, n), [upk])
            bks = []
            for gi in range(5):
                bk, bkey = self.bank(0, 6)
                for k in range(16):
                    self.mm(bk[:, :n], wh[:, k, gi * 128:(gi + 1) * 128], up[:, k, :n], k == 0, k == 15, whk + [upk], [bkey])
                bks.append((bk, bkey))
            yield
            gt = E["gt"][pi % 3]
            gk = lambda i: ("gt", pi % 3, i)
            qs, A, K, G = gt[0], [gt[1], gt[2]], [gt[3], gt[4]], [gt[5], gt[6]]
            self.copy(qs[:, :], bks[0][0][:, :n], [bks[0][1]], [gk(0)], eng="act")
            for d in range(2):
                bk, bkey = bks[1 + d]
                self.act(A[d][:, :], bk[:, :n], AF.Sigmoid, [bkey], [gk(1 + d)])
                self.act(K[d][:, :], bk[:, :n], AF.Sigmoid, [bkey], [gk(3 + d)], scale=-1.0)
            self.act(SG[:, c0:c0 + n], bks[4][0][:, :n], AF.Silu, [bks[4][1]], ["SG"])
            vt, vtk = R["ub"].next()
            self.copy(vt[:, :n], bks[3][0][:, :n], [bks[3][1]], [vtk], eng="act")
            yield
            for d in range(2):
                lb, oml, lbk = self.lbs(j, d, h)
                self.ts(A[d][:, :], A[d][:, :], oml, lb, ALU.mult, ALU.add, [gk(1 + d), lbk], [gk(1 + d)])
                self.P.op("dve", lambda hh, o=K[d][:, :], sc=oml: hh.tensor_scalar_mul(out=o, in0=o, scalar1=sc), [gk(3 + d), lbk], [gk(3 + d)])
            yield
            self.to_tokmajor(vt, vtk, c0, n, Vh, "Vh", blk=32, banks=(6, 7))
            for d in range(2):
                self.act(A[d][:, :], A[d][:, :], AF.Ln, [gk(1 + d)], [gk(1 + d)])
            yield
            for d in range(2):
                self.P.op("dve", lambda hh, d=d, A=A, G=G: hh.tensor_tensor_scan(out=G[d][:, :], data0=maskP[:, :], data1=A[d][:, :], initial=0.0,
                                                                                 op0=ALU.mult, op1=ALU.add), [gk(1 + d), "maskP"], [gk(5 + d)])
            yield
            for d in range(2):
                Gv = G[d][:, :].rearrange("p (c t) -> p c t", t=CH)
                Lv = A[d][:, :].rearrange("p (c t) -> p c t", t=CH)
                gG, gL = gk(5 + d), gk(1 + d)
                self.act(FDc[d][:, ci0:ci0 + 8], Gv[:, :, LAST], AF.Exp, [gG], ["FD%d" % d])
                if d == 0:
                    self.act(MS[d][:, ci0:ci0 + 8], Gv[:, :, MID], AF.Exp, [gG], ["MS%d" % d])
                    self.tt(Lv, Gv, Gv[:, :, MID:MID + 1].to_broadcast([128, 8, CH]), ALU.subtract, [gG], [gL])
                    self.tt(Gv, Gv[:, :, LAST:LAST + 1].to_broadcast([128, 8, CH]), Gv, ALU.subtract, [gG], [gG])
                    e1, e1k, e3, e3k = A[d], gL, G[d], gG
                else:
                    self.tt(Lv, Gv, Lv, ALU.subtract, [gG, gL], [gL])
                    tmpc = E["tmpc"]
                    self.tt(tmpc[:, 0:8], Gv[:, :, LAST], Lv[:, :, MID + 1], ALU.subtract, [gG, gL], ["tmpc"])
                    self.act(MS[d][:, ci0:ci0 + 8], tmpc[:, 0:8], AF.Exp, ["tmpc"], ["MS%d" % d])
                    self.tt(Gv, Lv[:, :, MID + 1:MID + 2].to_broadcast([128, 8, CH]), Lv, ALU.subtract, [gL], [gG])
                    e1, e1k, e3, e3k = G[d], gG, A[d], gL
                yield
                self.act(e3[:, :], e3[:, :], AF.Exp, [e3k], [e3k])
                yield
                self.tt(KH[d][:, c0:c0 + n], K[d][:, :], e3[:, :], ALU.mult, [gk(3 + d), e3k], ["KH%d" % d])
                self.act(e3[:, :], e1[:, :], AF.Exp, [e1k], [e3k])
                yield
                self.tt(QT[d][:, c0:c0 + n], qs[:, :], e3[:, :], ALU.mult, [gk(0), e3k], ["QT%d" % d])
                self.act(e3[:, :], e1[:, :], AF.Exp, [e1k], [e3k], scale=-1.0)
                yield
                self.tt(KT[d][:, c0:c0 + n], K[d][:, :], e3[:, :], ALU.mult, [gk(3 + d), e3k], ["KT%d" % d])

        pend = [piece_gen(pi, c0, kind) for pi, (c0, kind) in enumerate(WINS)]
        active = []
        mod_left = [13 if l == 0 else 7]
        while pend or active:
            if pend and len(active) < 3:
                active.append(pend.pop(0))
            for g in list(active):
                try:
                    next(g)
                except StopIteration:
                    active.remove(g)
            if self.modgen is not None and mod_left[0] > 0:
                mod_left[0] -= 1
                self.pump_mod(1)
        order = [list(range(18)), [1, 0] + list(range(17, 1, -1))]
        Sr = [E["Sr0"], E["Sr1"]]
        SMr = [E["SMr0"], E["SMr1"]]
        masks = [self.mask_f, self.mask_b]
        for d in range(2):
            self.memset(Sr[d][:, 0, :], 0.0, [("S", d, 0)], eng="dve")
        visited = set()

        def emit_PU(step, d):
            tb = order[d][step]
            col = blkcol(tb)
            pb0 = 4 * d
            atb = self.banks[pb0]
            trbb = self.banks[pb0 + 1].bitcast(BF16)
            ubk, ubkk = self.banks[pb0 + 3], ("pb", pb0 + 3)
            atm, kht = E["atm"][d], E["kht"][d]
            chs = (0, 1, 2, 3) if d == 0 else (3, 2, 1, 0)
            atk, trk = ("pb", pb0), ("pb", pb0 + 1)
            for ch in chs:
                cc = col + ch * 32
                self.mm(atb[0:32, ch * 32:(ch + 1) * 32], KT[d][:, cc:cc + 32], QT[d][:, cc:cc + 32], True, True, ["KT%d" % d, "QT%d" % d], [atk])
            for ch in chs:
                cc = col + ch * 32
                self.P.op("pe", lambda hh, trbb=trbb, d=d, cc=cc, ch=ch: hh.transpose(trbb[0:32, ch * 128:(ch + 1) * 128], KH[d][:, cc:cc + 32], self.id_b[:]),
                          ["KH%d" % d, "id_b"], [trk])
            self.tt(atm[:, step % 2, :, :], atb[0:32, 0:128].rearrange("p (c t) -> p c t", t=32), masks[d][:, :].unsqueeze(1).to_broadcast([32, 4, 32]),
                    ALU.mult, [atk, "mask%d" % d], [("atm", d, step % 2)])
            self.copy(kht[:, :, :], trbb[0:32, 0:512].rearrange("p (c t) -> p c t", t=128), [trk], [("kht", d)], eng="act")
            for qi, ch in enumerate(chs):
                ci = tb * 4 + ch
                self.mm(ubk[:, qi * 128:(qi + 1) * 128], kht[:, ch, :], Vh[:, ci, :], True, True, [("kht", d), ("Vh", ci)], [ubkk])

        def emit_S(step, d):
            tb = order[d][step]
            ubk, ubkk = self.banks[4 * d + 3], ("pb", 4 * d + 3)
            chs = (0, 1, 2, 3) if d == 0 else (3, 2, 1, 0)
            for qi, ch in enumerate(chs):
                ci = tb * 4 + ch
                g = step * 4 + qi
                self.act(SMr[d][:, g % 8, :], Sr[d][:, g % 8, :], AF.Identity, [("S", d, g % 8), "MS%d" % d], [("SM", d, g % 8)], scale=MS[d][:, ci:ci + 1])
                self.stt(Sr[d][:, (g + 1) % 8, :], Sr[d][:, g % 8, :], FDc[d][:, ci:ci + 1], ubk[:, qi * 128:(qi + 1) * 128], ALU.mult, ALU.add,
                         [("S", d, g % 8), "FD%d" % d, ubkk], [("S", d, (g + 1) % 8)])

        def emit_O(step, d):
            tb = order[d][step]
            col = blkcol(tb)
            ob, obk = self.banks[4 * d + 2], ("pb", 4 * d + 2)
            atm = E["atm"][d]
            chs = (0, 1, 2, 3) if d == 0 else (3, 2, 1, 0)
            for qi, ch in enumerate(chs):
                ci = tb * 4 + ch
                cc = col + ch * 32
                g = step * 4 + qi
                self.mm(ob[:, ch * 32:(ch + 1) * 32], Vh[:, ci, :], atm[:, step % 2, ch, :], True, False, [("Vh", ci), ("atm", d, step % 2)], [obk])
                self.mm(ob[:, ch * 32:(ch + 1) * 32], SMr[d][:, g % 8, :], QT[d][:, cc:cc + 32], False, True, [("SM", d, g % 8), "QT%d" % d], [obk])
            if tb not in visited:
                visited.add(tb)
                self.copy(OT[:, col:col + 128], ob[:, 0:128], [obk], [("OT", tb)], eng="act")
            else:
                self.tt(OT[:, col:col + 128], ob[:, 0:128], OT[:, col:col + 128], ALU.add, [obk, ("OT", tb)], [("OT", tb)])

        for step in range(18):
            for d in range(2):
                emit_PU(step, d)
            for d in range(2):
                emit_S(step, d)
            if step >= 1:
                for d in range(2):
                    emit_O(step - 1, d)
        for d in range(2):
            emit_O(17, d)
        hg_o = PVL.off["hgn%d" % j]
        otk = [("OT", tb) for tb in range(18)]
        for (c0, n, kind) in (PIECES if self.ctx_out else PIECES[1:]):
            sq, sqk = R["sq"].next()
            self.act(sq[:, :n], OT[:, c0:c0 + n], AF.Square, otk, [sqk])
            bk, bkey = self.bank()
            self.mm(bk[:, :n], self.ones_f[:], sq[:, :n], True, True, ["ones_f", sqk], [bkey])
            rs, rsk = R["rstd"].next()
            self.rsqrt_from(rs[:, :n], bk[:, :n], 1.0 / 128, [bkey, "epsb"], [rsk])
            tmp, tk = R["tmpf"].next()
            self.tt(tmp[:, :n], OT[:, c0:c0 + n], rs[:, :n], ALU.mult, otk + [rsk], [tk])
            ub, uk = R["ub"].next()
            self.stt(ub[:, :n], tmp[:, :n], self.pv[:, hg_o:hg_o + 1], SG[:, c0:c0 + n], ALU.mult, ALU.mult, [tk, "pv", "SG"], [uk])
            self.dma(self.hT[(8 + h) * 128:(9 + h) * 128, c0:c0 + n], ub[:, :n], [uk], ck("hT", c0, n))


_WSHAPES = {
    "w_mod": [DEPTH, D, 6 * D], "w_in_ab": [2, D, 5952], "w_kr_sw": [2, D, 64], "w_q_full": [2, 512, 2048],
    "w_kv_up": [2, 256, 2048], "w_out_ab": [2, D, D], "conv_w_pw1": [2, D, 2 * D], "conv_w_pw2": [2, D, D],
    "ffn_w_up": [DEPTH, D, 2 * DFF], "ffn_w_down": [DEPTH, DFF, D],
}


def build_program(stop=None, layers=None):
    nc = bass.Bass("TRN2", target_bir_lowering=False)
    io = {}
    io["xT0"] = nc.dram_tensor("xT0", [D, NCOL], F32, kind="ExternalInput").ap()
    io["pv"] = nc.dram_tensor("pv", [128, PVL.n], F32, kind="ExternalInput").ap()
    io["rope"] = nc.dram_tensor("rope", [64, 2, NCOL], F32, kind="ExternalInput").ap()
    for k, shp in _WSHAPES.items():
        io[k] = nc.dram_tensor(k, shp, F32, kind="ExternalInput").ap()
    outT = nc.dram_tensor("outT", [D, NCOL], F32, kind="ExternalOutput").ap()
    dbg_h = nc.dram_tensor("dbg_h", [D, NCOL], BF16, kind="ExternalOutput").ap() if stop is not None else None
    dbg_u = nc.dram_tensor("dbg_u", [D, NCOL], BF16, kind="ExternalOutput").ap() if stop is not None else None
    st = contextlib.ExitStack()
    with st:
        kb = KB(nc, st)
        kb.setup(io)
        R = kb.R
        R["sq"] = Rot(kb, "sq", 4, [128, 512], F32)
        R["tmpf"] = Rot(kb, "tmpf", 4, [128, 512], F32)
        R["ub"] = Rot(kb, "ub", 4, [128, 512], BF16)
        R["rstd"] = Rot(kb, "rstd", 3, [128, 512], F32)
        kb.arena = kb.sb("arena", [128, ARENA_BYTES // 2], BF16)
        kb.stop_at = stop[1] if stop else None
        kb.mod_ready = set()
        kb.modgen = None
        if kb.stop_at != "setup":
            kb.phase("mod")
            for _ in kb.mod_stream([0]):
                pass
            kb.modgen = kb.mod_stream([1, 2, 3])
        for l in (range(DEPTH) if layers is None else layers):
            if kb.stop_at in ("setup", "mod"):
                break
            kb.ctx_in = l < DEPTH - 1
            kb.ctx_out = l < DEPTH - 2
            if l not in kb.mod_ready:
                kb.phase("mod")
                kb.need_mod(l)
            if l % 2 == 0:
                kb.even(l)
            else:
                kb.conformer(l)
            if stop is not None and stop[0] == l and stop[1] != "ffn":
                break
            kb.ffn(l)
            if stop == (l, "ffn"):
                break
        kb.P.barrier()
        for c in range(16):
            kb.dma(outT[c * 128:(c + 1) * 128, :], kb.xT[c * 128:(c + 1) * 128, :], (), [("outT", c)])
            if dbg_h is not None:
                kb.dma(dbg_h[c * 128:(c + 1) * 128, :], kb.hT[c * 128:(c + 1) * 128, :], (), [("dbgh", c)])
                kb.dma(dbg_u[c * 128:(c + 1) * 128, :], kb.uT[c * 128:(c + 1) * 128, :], (), [("dbgu", c)])
        kb.P.emit(st)
    return nc


ARENA_BYTES = 158 * 1024


class Carver:
    def __init__(self, arena):
        self.arena = arena
        self.off = 0

    def take(self, shape, dtype):
        free = int(np.prod(shape[1:]))
        nbytes = free * (4 if dtype == F32 else 2)
        assert self.off + nbytes <= ARENA_BYTES, (self.off, nbytes)
        ap = self.arena[0:shape[0], self.off // 2:(self.off + nbytes) // 2]
        if dtype == F32:
            ap = ap.bitcast(F32)
        self.off += (nbytes + 63) // 64 * 64
        names = "abcdefg"[:len(shape) - 1]
        if len(shape) > 2:
            pat = "p (%s) -> p %s" % (" ".join(names), " ".join(names))
            ap = ap.rearrange(pat, **{names[i]: shape[1 + i] for i in range(1, len(shape) - 1)})
        return ap

    def rot(self, name, n, shape, dtype):
        return RotAP([(self.take(shape, dtype), "%s%d" % (name, i)) for i in range(n)])


class RotAP:
    def __init__(self, items):
        self.items = items
        self.i = 0

    def next(self):
        it = self.items[self.i % len(self.items)]
        self.i += 1
        return it


def rope_tables_host():
    tab = np.zeros((64, 2, NCOL), np.float32)
    tab[:, 0, :] = 1.0
    t = np.arange(NLAT)
    pos = np.stack([t // 64, t % 64], 0).astype(np.float32)
    freqs = (10000.0 ** (-np.arange(16, dtype=np.float32) / 16)).astype(np.float32)
    for i in range(64):
        a, half, p = i // 32, (i % 32) // 16, i % 16
        ang = (pos[a] * freqs[p]).astype(np.float32)
        tab[i, 0, LAT0:LAT0 + NLAT] = np.cos(ang)
        tab[i, 1, LAT0:LAT0 + NLAT] = np.sin(ang) * (-1.0 if half == 0 else 1.0)
    return tab


SWAP = np.array([(i // 32) * 32 + (1 - (i % 32) // 16) * 16 + i % 16 for i in range(64)])


def host_inputs(inputs, b):
    f = lambda k: np.asarray(inputs[k], np.float32)
    m = {}
    xT0 = np.zeros((D, NCOL), np.float32)
    xT0[:, CTX0:CTX0 + NCTX] = f("ctx")[b].T
    xT0[:, LAT0:LAT0 + NLAT] = f("x")[b].T
    m["xT0"] = xT0
    pv = np.zeros((128, PVL.n), np.float32)

    def put(name, arr):
        o = PVL.off[name]
        pv[:, o:o + arr.shape[1]] = arr

    cc = np.stack([fm(f("c")[b]), fm(f("c_ctx"))], -1).reshape(128, 32)
    put("cc", cc)
    for l in range(DEPTH):
        put("bmod%d" % l, fm(f("b_mod")[l]))
        put("gains%d" % l, np.concatenate([fm(f("norm_gains")[l, i]) for i in range(4)], 1))
        put("fwc%d" % l, np.concatenate([fm(f("ffn_w_conv")[l, k]) for k in range(3)], 1))
        put("fbc%d" % l, fm(f("ffn_b_conv")[l]))
    for j in range(2):
        put("qn%d" % j, fm(f("mla_q_norm")[j]))
        put("kvn%d" % j, fm(f("mla_kv_norm")[j]))
        put("hgn%d" % j, fm(f("hgrn_norm")[j]))
        put("bpw1%d" % j, fm(f("conv_b_pw1")[j]))
        wd = f("conv_w_dw")[j]
        put("wdw%d" % j, np.concatenate([fm(wd[k]) for k in range(31)], 1))
        put("bdw%d" % j, fm(f("conv_b_dw")[j]))
        put("lng%d" % j, fm(f("conv_ln_g")[j]))
        put("lnb%d" % j, fm(f("conv_ln_b")[j]))
        put("bpw2%d" % j, fm(f("conv_b_pw2")[j]))
    hl = f("hgrn_lb")
    put("hlb", np.concatenate([fm(hl[d, jj]) for d in range(2) for jj in range(2)], 1))
    m["pv"] = pv
    return m


def shared_inputs(inputs):
    f = lambda k: np.ascontiguousarray(np.asarray(inputs[k], np.float32))
    m = {k: f(k) for k in ("w_mod", "w_in_ab", "w_kv_up", "w_out_ab", "conv_w_pw1", "conv_w_pw2", "ffn_w_up", "ffn_w_down")}
    m["w_kr_sw"] = np.ascontiguousarray(m["w_in_ab"][:, :, 768:832][:, :, SWAP])
    wq = f("w_q_up").reshape(2, 512, 8, 192)
    m["w_q_full"] = np.ascontiguousarray(np.concatenate([wq, wq[:, :, :, 128:][:, :, :, SWAP]], -1).reshape(2, 512, 2048))
    m["rope"] = rope_tables_host()
    return m


_NC_CACHE = {}


def kernel(**inputs):
    n = 4
    if "full" not in _NC_CACHE:
        _NC_CACHE["full"] = build_program()
    nc = _NC_CACHE["full"]
    sh = shared_inputs(inputs)
    in_maps = []
    for b in range(n):
        m = dict(sh)
        m.update(host_inputs(inputs, b))
        in_maps.append(m)
    res = run_bass_kernel_spmd(nc, in_maps, core_ids=list(range(n)))
    out = np.stack([np.ascontiguousarray(res.results[b]["outT"][:, LAT0:LAT0 + NLAT].T) for b in range(n)], 0)
    return out.astype(np.float32)
```

```python
import contextlib
import numpy as np
import concourse.bass as bass
import concourse.mybir as mybir
from concourse.bass_utils import run_bass_kernel_spmd

dt = mybir.dt
F32 = dt.float32
BF16 = dt.bfloat16
AF = mybir.ActivationFunctionType
ALU = mybir.AluOpType

ENG_NAMES = ("pe", "act", "dve", "pool", "sp")

D = 2048
DEPTH = 4
NCTX = 256
NLAT = 2048
DFF = 5632
EPS = 1e-6
ATTN_SCALE = 192.0 ** -0.5
NCOL = 2352
CTX0 = 16
LAT0 = 288
PIECES = [(CTX0, 256, 1)] + [(LAT0 + 512 * i, 512, 0) for i in range(4)]
WINS = [(CTX0, 1)] + [(LAT0 + 256 * i, 0) for i in range(8)]
ST_RUNS = [[(0, CTX0, 256, 1), (256, LAT0, 512, 0)],
           [(0, 800, 512, 0), (512, 1312, 256, 0)],
           [(0, 1568, 512, 0), (512, 2080, 256, 0)]]
CELL_BOUNDS = np.array([0, 16, 272] + [288 + 256 * i for i in range(9)])


def cells(c0, n):
    a = int(np.searchsorted(CELL_BOUNDS, c0, side="right") - 1)
    b = int(np.searchsorted(CELL_BOUNDS, c0 + n - 1, side="right") - 1)
    return list(range(a, b + 1))


def ck(name, c0, n):
    return [(name, i) for i in cells(c0, n)]


def blkcol(tb):
    return CTX0 + 128 * tb if tb < 2 else LAT0 + 128 * (tb - 2)


class Op:
    __slots__ = ("eng", "fn", "deps", "signal", "tok", "lane")

    def __init__(self, eng, fn, lane=None):
        self.eng = eng
        self.fn = fn
        self.deps = []
        self.signal = lane is not None
        self.tok = None
        self.lane = lane


class Reg:
    __slots__ = ("w", "r")

    def __init__(self):
        self.w = None
        self.r = []


class Prog:
    def __init__(self, nc, n_lanes=16):
        self.nc = nc
        self.ops = {e: [] for e in ENG_NAMES}
        self.regs = {}
        self.n_lanes = n_lanes
        self.lane_last = {}
        self.lane_rr = {e: 0 for e in ENG_NAMES}

    def reg(self, key):
        r = self.regs.get(key)
        if r is None:
            r = self.regs[key] = Reg()
        return r

    def _add(self, op, reads, writes):
        deps = op.deps
        for k in reads:
            r = self.reg(k)
            if r.w is not None:
                deps.append(r.w)
        for k in writes:
            r = self.reg(k)
            if r.w is not None:
                deps.append(r.w)
            deps.extend(r.r)
        for k in reads:
            self.reg(k).r.append(op)
        for k in writes:
            r = self.reg(k)
            r.w = op
            r.r = []
        if op.eng == "pe":
            op.deps = deps = [d for d in deps if d.eng != "pe" or d.lane is not None]
        for d in deps:
            d.signal = True
        self.ops[op.eng].append(op)
        return op

    def op(self, eng, fn, reads=(), writes=()):
        return self._add(Op(eng, fn), reads, writes)

    def dma(self, eng, fn, reads=(), writes=()):
        lane = (eng, self.lane_rr[eng] % self.n_lanes)
        self.lane_rr[eng] += 1
        op = Op(eng, fn, lane=lane)
        prev = self.lane_last.get(lane)
        if prev is not None:
            op.deps.append(prev)
            prev.signal = True
        self.lane_last[lane] = op
        return self._add(op, reads, writes)

    def barrier(self):
        lasts = list(self.lane_last.values())
        for e in ENG_NAMES:
            for o in reversed(self.ops[e]):
                if o.fn is not None:
                    lasts.append(o)
                    break
        for e in ENG_NAMES:
            op = Op(e, None)
            op.deps = [d for d in lasts if not (d.eng == e and d.lane is None)]
            for d in op.deps:
                d.signal = True
            self.ops[e].append(op)
        self.regs = {}

    def emit(self, stack):
        nc = self.nc
        sems = {e: stack.enter_context(nc.semaphore("sem_" + e)) for e in ENG_NAMES}
        lane_sems = {}
        lane_cnt = {}
        for e in ENG_NAMES:
            for i in range(self.n_lanes):
                if (e, i) in self.lane_last:
                    lane_sems[(e, i)] = stack.enter_context(nc.semaphore("lane_%s%d" % (e, i)))
                    lane_cnt[(e, i)] = 0
        for e in ENG_NAMES:
            c = 0
            for op in self.ops[e]:
                if op.lane is not None:
                    lane_cnt[op.lane] += 16
                    op.tok = (lane_sems[op.lane], lane_cnt[op.lane])
                elif op.signal:
                    c += 1
                    op.tok = (sems[e], c)
        final_waits = [(lane_sems[l], lane_cnt[l]) for l in lane_sems]
        block = stack.enter_context(nc.Block())
        ops = self.ops

        def replay(e, h):
            seen = {}
            for op in ops[e]:
                for d in op.deps:
                    sem, val = d.tok
                    k = sem.num
                    if seen.get(k, 0) < val:
                        seen[k] = val
                        h.wait_ge(sem, val)
                if op.fn is None:
                    continue
                inst = op.fn(h)
                if op.lane is not None:
                    inst.then_inc(op.tok[0], 16)
                elif op.signal:
                    inst.then_inc(op.tok[0], 1)
            if e == "sp":
                for sem, val in final_waits:
                    if val:
                        h.wait_ge(sem, val)

        @block.tensor
        def _(h):
            replay("pe", h)

        @block.scalar
        def _(h):
            replay("act", h)

        @block.vector
        def _(h):
            replay("dve", h)

        @block.gpsimd
        def _(h):
            replay("pool", h)

        @block.sync
        def _(h):
            replay("sp", h)


class Rot:
    def __init__(self, kb, name, n, shape, dtype):
        self.items = [(kb.sb("%s%d" % (name, i), shape, dtype), "%s%d" % (name, i)) for i in range(n)]
        self.i = 0

    def next(self):
        it = self.items[self.i % len(self.items)]
        self.i += 1
        return it


class PVLayout:
    def __init__(self):
        self.off = {}
        self.n = 0

    def add(self, name, ncols):
        self.off[name] = self.n
        self.n += ncols


def pv_layout():
    L = PVLayout()
    L.add("cc", 32)
    for l in range(DEPTH):
        L.add("bmod%d" % l, 96)
        L.add("gains%d" % l, 64)
        L.add("fwc%d" % l, 3 * 88)
        L.add("fbc%d" % l, 88)
    for j in range(2):
        L.add("qn%d" % j, 4)
        L.add("kvn%d" % j, 2)
        L.add("hgn%d" % j, 1)
        L.add("bpw1%d" % j, 32)
        L.add("wdw%d" % j, 31 * 16)
        L.add("bdw%d" % j, 16)
        L.add("lng%d" % j, 16)
        L.add("lnb%d" % j, 16)
        L.add("bpw2%d" % j, 16)
    L.add("hlb", 32)
    return L


PVL = pv_layout()


def fm(v):
    v = np.asarray(v, np.float32)
    return np.ascontiguousarray(v.reshape(-1, 128).T)


class KB:
    def __init__(self, nc, st):
        self.nc = nc
        self.st = st
        self.P = Prog(nc)
        self.banks = [st.enter_context(nc.psum_tensor("pb%d" % i, [128, 512], F32)) for i in range(8)]
        self.brr = 0
        self.R = {}
        self.E = {}
        self.ctx_in = True
        self.ctx_out = True

    def phase(self, name):
        self.P.barrier()
        C = Carver(self.arena)
        R, E = self.R, self.E
        if name == "mod":
            R["wmod"] = C.rot("wmod", 2, [128, 16, 256], BF16)
        elif name in ("prenorm", "ln"):
            R["xin2"] = [C.take([128, 16, 512], F32) for _ in range(2)]
        elif name == "ffn":
            R["aT44"] = C.take([128, 44, 768], BF16)
            R["uw"] = [C.take([128, 16, 258], BF16) for _ in range(3)]
            R["hvg"] = C.rot("hvg", 6, [128, 256], F32)
            R["wup"] = C.rot("wup", 2, [128, 16, 512], BF16)
            R["wfin"] = C.rot("wfin", 2, [128, 44, 128], BF16)
            R["rstd768"] = [C.take([128, 768], F32) for _ in range(2)]
        elif name == "final16":
            R["aT16"] = C.take([128, 16, 768], BF16)
            R["wfin"] = C.rot("wfin", 2, [128, 16, 256], BF16)
            R["rstd768"] = [C.take([128, 768], F32) for _ in range(2)]
        elif name == "odd1":
            R["upiece"] = C.rot("upiece", 2, [128, 16, 512], BF16)
            R["hfull"] = C.rot("hfull", 2, [128, NCOL], BF16)
            R["dg"] = C.rot("dg", 2, [128, 31, 128], BF16)
            R["wup"] = C.rot("wup", 2, [128, 16, 256], BF16)
            for ap, key in R["hfull"].items:
                self.memset(ap, 0.0, [key], eng="dve")
        elif name in ("mla_prep", "mla_heads"):
            E["rope"] = C.take([64, 2, NCOL], F32)
            E["cqn"] = C.take([128, 4, NCOL], BF16)
            E["ckvn"] = C.take([128, 2, NCOL], BF16)
            E["krot"] = C.take([64, NCOL], BF16)
            if name == "mla_prep":
                R["upiece"] = C.rot("upiece", 2, [128, 16, 512], BF16)
                E["wprep"] = C.take([128, 16, 896], BF16)
                E["cf"] = C.take([128, 4, 512], F32)
            else:
                E["wq"] = C.take([128, 4, 2048], BF16)
                E["wkv"] = C.take([128, 2, 2048], BF16)
                E["qn"] = C.take([128, NCOL], BF16)
                E["qrot"] = C.take([64, NCOL], BF16)
                E["kn"] = C.take([128, NCOL], BF16)
                E["V"] = C.take([128, 18, 128], BF16)
        elif name == "hgrn":
            R["upiece"] = C.rot("upiece", 2, [128, 16, 256], BF16)
            R["wmod"] = C.rot("wmod", 2, [128, 16, 256], BF16)
            E["wh"] = C.take([128, 16, 640], BF16)
            E["OT"] = C.take([128, NCOL], F32)
            for nm in ("SG", "QT0", "QT1", "KT0", "KT1", "KH0", "KH1"):
                E[nm] = C.take([128, NCOL], BF16)
            E["Vh"] = C.take([32, 72, 128], BF16)
            for nm in ("MS0", "MS1", "FD0", "FD1", "tmpc"):
                E[nm] = C.take([128, 72], F32)
            for nm in ("Sr0", "Sr1"):
                E[nm] = C.take([128, 8, 128], F32)
            for nm in ("SMr0", "SMr1"):
                E[nm] = C.take([128, 8, 128], BF16)
            E["atm"] = [C.take([32, 2, 4, 32], BF16) for _ in range(2)]
            E["kht"] = [C.take([32, 4, 128], BF16) for _ in range(2)]
            E["gt"] = [[C.take([128, 256], F32) for _ in range(7)] for _ in range(3)]
            E["maskP"] = C.take([128, 256], F32)
            self.memset(E["maskP"], 1.0, ["maskP"], eng="dve")
            self.memset(E["maskP"].rearrange("p (c t) -> p c t", t=32)[:, :, 0:1], 0.0, ["maskP"], eng="dve")
        else:
            raise ValueError(name)

    def sb(self, name, shape, dtype):
        return self.st.enter_context(self.nc.sbuf_tensor("s_" + name, shape, dtype))

    def dram(self, name, shape, dtype):
        return self.nc.dram_tensor(name, shape, dtype, kind="Internal").ap()

    def bank(self, lo=0, hi=6):
        i = lo + self.brr % (hi - lo)
        self.brr += 1
        return self.banks[i], ("pb", i)

    def mm(self, out, lhsT, rhs, start, stop, reads, writes):
        self.P.op("pe", lambda h: h.matmul(out, lhsT=lhsT, rhs=rhs, start=start, stop=stop), reads, writes)

    def act(self, out, in_, func, reads, writes, bias=0.0, scale=1.0):
        self.P.op("act", lambda h: h.activation(out=out, in_=in_, func=func, bias=bias, scale=scale), reads, writes)

    def tt(self, out, in0, in1, op, reads, writes, eng="dve"):
        self.P.op(eng, lambda h: h.tensor_tensor(out=out, in0=in0, in1=in1, op=op), reads, writes)

    def stt(self, out, in0, scalar, in1, op0, op1, reads, writes):
        self.P.op("dve", lambda h: h.scalar_tensor_tensor(out=out, in0=in0, scalar=scalar, in1=in1, op0=op0, op1=op1), reads, writes)

    def ts(self, out, in0, s1, s2, op0, op1, reads, writes):
        self.P.op("dve", lambda h: h.tensor_scalar(out=out, in0=in0, scalar1=s1, scalar2=s2, op0=op0, op1=op1), reads, writes)

    def recip(self, out, in_, reads, writes):
        self.P.op("dve", lambda h: h.reciprocal(out=out, in_=in_), reads, writes)

    def copy(self, out, in_, reads, writes, eng="dve"):
        if eng == "act":
            self.P.op("act", lambda h: h.activation(out=out, in_=in_, func=AF.Identity), reads, writes)
        else:
            self.P.op(eng, lambda h: h.tensor_copy(out=out, in_=in_), reads, writes)

    def memset(self, ap, val, writes, eng="pool"):
        self.P.op(eng, lambda h: h.memset(ap, val), (), writes)

    def dma(self, out, in_, reads, writes, eng="sp"):
        self.P.dma(eng, lambda h: h.dma_start(out=out, in_=in_), reads, writes)

    def rsqrt_from(self, out, in_, scale, reads, writes):
        self.act(out, in_, AF.Sqrt, reads, writes, bias=self.epsb[:, 0:1], scale=scale)
        self.recip(out, out, writes, writes)

    def setup(self, io):
        self.io = io
        P = self.P
        self.xT = self.dram("xT_d", [D, NCOL], F32)
        self.uT = self.dram("uT_d", [D, NCOL], BF16)
        self.hT = self.dram("hT_d", [D, NCOL], BF16)
        self.hc = self.dram("hc_d", [D, NCOL], F32)
        self.yT2 = [self.dram("yT_d%d" % i, [D, 768], F32) for i in range(2)]
        self.pv = self.sb("pv", [128, PVL.n], F32)
        self.ones_f = self.sb("ones_f", [128, 128], F32)
        self.ones_b = self.sb("ones_b", [128, 128], BF16)
        self.id_f = self.sb("id_f", [128, 128], F32)
        self.id_b = self.sb("id_b", [128, 128], BF16)
        self.epsb = self.sb("epsb", [128, 1], F32)
        self.zer = self.sb("zer", [128, 16], BF16)
        self.mods = self.sb("mods", [128, DEPTH * 6 * 16 * 2], F32)
        self.modT = self.sb("modT", [128, DEPTH * 96 * 2], F32)
        self.scb = self.sb("scb", [128, 32], BF16)
        self.lbt = self.sb("lbt", [128, 2 * 8 * 2], F32)
        self.lb0 = self.sb("lb0", [128, 2], F32)
        self.mask_f = self.sb("mask_f", [32, 32], F32)
        self.mask_b = self.sb("mask_b", [32, 32], F32)
        self.dma(self.pv[:], io["pv"], (), ["pv"])
        self.memset(self.ones_f[:], 1.0, ["ones_f"])
        self.memset(self.ones_b[:], 1.0, ["ones_b"])
        self.memset(self.epsb[:], EPS, ["epsb"])
        self.memset(self.zer[:], 0.0, ["zer"])
        self.memset(self.id_f[:], 0.0, ["id_f"])
        P.op("pool", lambda h: h.affine_select(out=self.id_f[:], in_=self.id_f[:], pattern=[[-1, 128]], compare_op=ALU.not_equal,
                                               fill=1.0, base=0, channel_multiplier=1), ["id_f"], ["id_f"])
        self.copy(self.id_b[:], self.id_f[:], ["id_f"], ["id_b"])
        for di, (m, pstep, cmul) in enumerate(((self.mask_f, 1, -1), (self.mask_b, -1, 1))):
            key = "mask%d" % di
            self.memset(m[:], 1.0, [key])
            P.op("pool", lambda h, m=m, ps=pstep, cm=cmul: h.affine_select(out=m[:], in_=m[:], pattern=[[ps, 32]], compare_op=ALU.is_ge,
                                                                            fill=0.0, base=0, channel_multiplier=cm), [key], [key])
        for c in range(16):
            self.dma(self.xT[c * 128:(c + 1) * 128, :], io["xT0"][c * 128:(c + 1) * 128, :], (), [("xT", i) for i in range(12)])
        for c in range(16):
            for p0 in (0, 272, 2336):
                self.dma(self.uT[c * 128:(c + 1) * 128, p0:p0 + 16], self.zer[:], ["zer"], ck("uT", p0, 16))
        o = PVL.off["cc"]
        self.act(self.scb[:], self.pv[:, o:o + 32], AF.Silu, ["pv"], ["scb"])
        o = PVL.off["hlb"]
        hl = self.pv[:, o:o + 32].rearrange("p (d j h) -> p d j h", d=2, j=2)
        lbv = self.lbt[:].rearrange("p (d h t) -> p d h t", d=2, h=8)
        self.tt(lbv[:, :, :, 0], hl[:, :, 1, :], hl[:, :, 0, :], ALU.subtract, ["pv"], ["lbt"])
        self.act(lbv[:, :, :, 1], lbv[:, :, :, 0], AF.Sigmoid, ["lbt"], ["lbt"], scale=-1.0)
        self.act(lbv[:, :, :, 0], lbv[:, :, :, 0], AF.Sigmoid, ["lbt"], ["lbt"])
        self.memset(self.lb0[:, 0:1], 0.0, ["lb0"])
        self.memset(self.lb0[:, 1:2], 1.0, ["lb0"])

    def lbs(self, j, d, h):
        if j == 0:
            return self.lb0[:, 0:1], self.lb0[:, 1:2], "lb0"
        o = (d * 8 + h) * 2
        return self.lbt[:, o:o + 1], self.lbt[:, o + 1:o + 2], "lbt"

    def mod(self, l, which, c, kind):
        o = (((l * 6 + which) * 16 + c) * 2) + kind
        return self.mods[:, o:o + 1]

    def mod_stream(self, layers):
        io = self.io
        scv = self.scb[:].rearrange("p (c k) -> p c k", k=2)
        for l in layers:
            o = PVL.off["bmod%d" % l]
            for blk in range(48):
                wt, wkey = self.R["wmod"].next()
                self.dma(wt[:, :, :], io["w_mod"][l, :, blk * 256:(blk + 1) * 256].rearrange("(k p) m -> p k m", p=128), (), [wkey], eng="pool")
                bk, bkey = self.bank()
                for jj in range(2):
                    for k in range(16):
                        self.mm(bk[:, jj * 2:jj * 2 + 2], wt[:, k, jj * 128:(jj + 1) * 128], scv[:, k, :], k == 0, k == 15, [wkey, "scb"], [bkey])
                oc = blk * 2
                mt = self.modT[:, l * 192 + oc * 2:l * 192 + oc * 2 + 4].rearrange("p (o k) -> p o k", k=2)
                self.tt(mt, bk[:, 0:4].rearrange("p (o k) -> p o k", k=2), self.pv[:, o + oc:o + oc + 2].unsqueeze(2).to_broadcast([128, 2, 2]),
                        ALU.add, [bkey, "pv"], [("modT", l)])
                yield
            go = PVL.off["gains%d" % l]
            md = self.mods[:, l * 192:(l + 1) * 192].rearrange("p (w c k) -> p w c k", w=6, c=16)
            mt6 = self.modT[:, l * 192:(l + 1) * 192].rearrange("p (w c k) -> p w c k", w=6, c=16)
            mk, tk_ = ("mods", l), ("modT", l)
            for (dst, gi, mi, plus1) in ((0, 0, 1, True), (2, 1, 2, False), (3, 2, 4, True), (5, 3, 5, False)):
                gb = self.pv[:, go + gi * 16:go + (gi + 1) * 16].unsqueeze(2).to_broadcast([128, 16, 2])
                if plus1:
                    self.stt(md[:, dst], mt6[:, mi], 1.0, gb, ALU.add, ALU.mult, [tk_, "pv"], [mk])
                else:
                    self.tt(md[:, dst], mt6[:, mi], gb, ALU.mult, [tk_, "pv"], [mk])
            self.copy(md[:, 1], mt6[:, 0], [tk_], [mk])
            self.copy(md[:, 4], mt6[:, 3], [tk_], [mk])
            self.mod_ready.add(l)
            yield

    def pump_mod(self, n):
        for _ in range(n):
            if self.modgen is None:
                return
            try:
                next(self.modgen)
            except StopIteration:
                self.modgen = None

    def need_mod(self, l):
        while l not in self.mod_ready:
            assert self.modgen is not None
            self.pump_mod(1)

    def prenorm(self, l, sub):
        self.phase("prenorm")
        R = self.R
        wS, wB = (0, 1) if sub == 1 else (3, 4)
        ctx_on = self.ctx_in if sub == 1 else self.ctx_out
        for pi, (c0, n, kind) in enumerate(PIECES if ctx_on else PIECES[1:]):
            xin = R["xin2"][pi % 2]
            xs = pi % 2
            for c in range(16):
                self.dma(xin[:, c, :n], self.xT[c * 128:(c + 1) * 128, c0:c0 + n], ck("xT", c0, n), [("xin", xs, c)])
            bk, bkey = self.bank()
            for c in range(16):
                sq, sqk = R["sq"].next()
                self.act(sq[:, :n], xin[:, c, :n], AF.Square, [("xin", xs, c)], [sqk])
                self.mm(bk[:, :n], self.ones_f[:], sq[:, :n], c == 0, c == 15, ["ones_f", sqk], [bkey])
            rs, rsk = R["rstd"].next()
            self.rsqrt_from(rs[:, :n], bk[:, :n], 1.0 / D, [bkey, "epsb"], [rsk])
            for c in range(16):
                tmp, tk = R["tmpf"].next()
                self.tt(tmp[:, :n], xin[:, c, :n], rs[:, :n], ALU.mult, [("xin", xs, c), rsk], [tk])
                ub, uk = R["ub"].next()
                self.act(ub[:, :n], tmp[:, :n], AF.Identity, [tk, ("mods", l)], [uk], bias=self.mod(l, wB, c, kind), scale=self.mod(l, wS, c, kind))
                self.dma(self.uT[c * 128:(c + 1) * 128, c0:c0 + n], ub[:, :n], [uk], ck("uT", c0, n))

    def final_mm(self, l, sub, s, slot, aT, akeys, Kc, w_ap, bias_off, wblk, pump=None):
        R = self.R
        skip_ctx = (s == 0 and not self.ctx_out)
        parts = [(256, 512, 6)] if skip_ctx else [(0, 512, 6), (512, 256, 7)]
        yT = self.yT2[slot]
        nj = wblk // 128
        for blk in range(D // wblk):
            wt, wkey = R["wfin"].next()
            self.dma(wt[:, :Kc, :], w_ap[:, blk * wblk:(blk + 1) * wblk].rearrange("(k p) m -> p k m", p=128), (), [wkey], eng="pool")
            for jj in range(nj):
                c = blk * nj + jj
                for (n0, nn, sb_) in parts:
                    bk, bkey = self.bank()
                    for k in range(Kc):
                        self.mm(bk[:, :nn], wt[:, k, jj * 128:(jj + 1) * 128], aT[:, k, n0:n0 + nn], k == 0, k == Kc - 1,
                                [wkey] + akeys, [bkey])
                    ysb, yk = R["tmpf"].next()
                    bias = 0.0 if bias_off is None else self.pv[:, bias_off + c:bias_off + c + 1]
                    self.act(ysb[:, :nn], bk[:, :nn], AF.Identity, [bkey, "pv"], [yk], bias=bias)
                    sq, sqk = R["sq"].next()
                    self.act(sq[:, :nn], ysb[:, :nn], AF.Square, [yk], [sqk])
                    self.mm(self.banks[sb_][:, :nn], self.ones_f[:], sq[:, :nn], c == 0, c == 15, ["ones_f", sqk], [("pb", sb_)])
                    self.dma(yT[c * 128:(c + 1) * 128, n0:n0 + nn], ysb[:, :nn], [yk], [("yT", slot, c)])
                    if pump is not None:
                        pump()
        rs = R["rstd768"][slot]
        if skip_ctx:
            self.rsqrt_from(rs[:, 256:768], self.banks[6][:, :512], 1.0 / D, [("pb", 6), "epsb"], [("rs768a", slot)])
        else:
            self.rsqrt_from(rs[:, 0:512], self.banks[6][:, :512], 1.0 / D, [("pb", 6), "epsb"], [("rs768a", slot)])
            self.rsqrt_from(rs[:, 512:768], self.banks[7][:, :256], 1.0 / D, [("pb", 7), "epsb"], [("rs768b", slot)])

    def final_tail(self, l, sub, s, slot):
        R = self.R
        wG = 2 if sub == 1 else 5
        yT = self.yT2[slot]
        rs = R["rstd768"][slot]
        runs = [r for r in ST_RUNS[s] if r[3] == 0 or self.ctx_out]
        rkeys = [("rs768a", slot)] if (s == 0 and not self.ctx_out) else [("rs768a", slot), ("rs768b", slot)]
        for c in range(16):
            for (lo, c0, n, kind) in runs:
                yl, ylk = R["tmpf"].next()
                xl, xlk = R["sq"].next()
                self.dma(yl[:, :n], yT[c * 128:(c + 1) * 128, lo:lo + n], [("yT", slot, c)], [ylk])
                self.dma(xl[:, :n], self.xT[c * 128:(c + 1) * 128, c0:c0 + n], ck("xT", c0, n), [xlk])
                self.tt(yl[:, :n], yl[:, :n], rs[:, lo:lo + n], ALU.mult, [ylk] + rkeys, [ylk])
                self.stt(xl[:, :n], yl[:, :n], self.mod(l, wG, c, kind), xl[:, :n], ALU.mult, ALU.add, [ylk, xlk, ("mods", l)], [xlk])
                self.dma(self.xT[c * 128:(c + 1) * 128, c0:c0 + n], xl[:, :n], [xlk], ck("xT", c0, n))
                yield

    def final_from_dram(self, l, sub, w_ap, bias_off):
        self.phase("final16")
        R = self.R
        tail = None

        def pump():
            if tail is not None:
                next(tail, None)
                next(tail, None)

        for s in range(3):
            aT = R["aT16"]
            for (lo, c0, n, kind) in ST_RUNS[s]:
                if kind == 1 and not self.ctx_out:
                    continue
                self.dma(aT[:, :, lo:lo + n], self.hT[:, c0:c0 + n].rearrange("(k p) n -> p k n", p=128), ck("hT", c0, n), [("aT16", lo)])
            self.final_mm(l, sub, s, s % 2, aT, [("aT16", 0), ("aT16", 256), ("aT16", 512)], 16, w_ap, bias_off, 256, pump)
            if tail is not None:
                for _ in tail:
                    pass
            tail = self.final_tail(l, sub, s, s % 2)
        for _ in tail:
            pass

    def ffn(self, l):
        io = self.io
        R = self.R
        self.prenorm(l, 2)
        wo = PVL.off["fwc%d" % l]
        bo = PVL.off["fbc%d" % l]
        self.phase("ffn")
        aT = R["aT44"]
        akeys = [("aT44", 0), ("aT44", 1), ("aT44", 2)]
        tail = None
        unit = 0
        for s in range(3):
            uws = []
            wis = [wi for wi in range(3) if WINS[3 * s + wi][1] == 0 or self.ctx_out]
            for wi in range(3):
                c0, kind = WINS[3 * s + wi]
                uw = R["uw"][wi]
                if wi in wis:
                    self.dma(uw[:, :, :], self.uT[:, c0 - 1:c0 + 257].rearrange("(k p) n -> p k n", p=128), ck("uT", c0 - 1, 258), [("uw", wi)])
                uws.append(uw)
            for jj in range(22):
                wu, wkey = R["wup"].next()
                self.dma(wu[:, :, 0:256], io["ffn_w_up"][l, :, jj * 256:(jj + 1) * 256].rearrange("(k p) m -> p k m", p=128), (), [wkey + "a"], eng="pool")
                self.dma(wu[:, :, 256:512], io["ffn_w_up"][l, :, DFF + jj * 256:DFF + (jj + 1) * 256].rearrange("(k p) m -> p k m", p=128), (), [wkey + "b"], eng="pool")
                for pj in range(2):
                    j = jj * 2 + pj
                    for wi in wis:
                        res = []
                        for half, chn in ((0, j), (1, 44 + j)):
                            bk, bkey = self.bank()
                            for k in range(16):
                                self.mm(bk[:, :258], wu[:, k, half * 256 + pj * 128:half * 256 + (pj + 1) * 128], uws[wi][:, k, :], k == 0, k == 15,
                                        [wkey + "a", wkey + "b", ("uw", wi)], [bkey])
                            hh, hk = R["hvg"].next()
                            w0 = self.pv[:, wo + chn:wo + chn + 1]
                            w1 = self.pv[:, wo + 88 + chn:wo + 88 + chn + 1]
                            w2 = self.pv[:, wo + 176 + chn:wo + 176 + chn + 1]
                            bb = self.pv[:, bo + chn:bo + chn + 1]
                            self.act(hh[:, :], bk[:, 1:257], AF.Identity, [bkey, "pv"], [hk], bias=bb, scale=w1)
                            self.stt(hh[:, :], bk[:, 0:256], w0, hh[:, :], ALU.mult, ALU.add, [bkey, hk, "pv"], [hk])
                            self.stt(hh[:, :], bk[:, 2:258], w2, hh[:, :], ALU.mult, ALU.add, [bkey, hk, "pv"], [hk])
                            res.append((hh, hk))
                        (hv, hvk), (hg, hgk) = res
                        self.act(hg[:, :], hg[:, :], AF.Silu, [hgk], [hgk])
                        self.tt(aT[:, j, wi * 256:(wi + 1) * 256], hg[:, :], hv[:, :], ALU.mult, [hgk, hvk], [("aT44", wi)])
                        unit += 1
                        if tail is not None and unit % 2 == 0:
                            next(tail, None)
            if tail is not None:
                for _ in tail:
                    pass
            self.final_mm(l, 2, s, s % 2, aT, akeys, 44, io["ffn_w_down"][l], None, 128)
            tail = self.final_tail(l, 2, s, s % 2)
        for _ in tail:
            pass

    def conformer(self, l):
        io = self.io
        R = self.R
        j = l // 2
        self.prenorm(l, 1)
        self.phase("odd1")
        pieces = PIECES if self.ctx_in else PIECES[1:]
        b1 = PVL.off["bpw1%d" % j]
        wdw = PVL.off["wdw%d" % j]
        bdw = PVL.off["bdw%d" % j]
        for c in range(16):
            wt, wkey = R["wup"].next()
            self.dma(wt[:, :, 0:128], io["conv_w_pw1"][j, :, c * 128:(c + 1) * 128].rearrange("(k p) m -> p k m", p=128), (), [wkey + "a"], eng="pool")
            self.dma(wt[:, :, 128:256], io["conv_w_pw1"][j, :, D + c * 128:D + (c + 1) * 128].rearrange("(k p) m -> p k m", p=128), (), [wkey + "b"], eng="pool")
            hf, hfk = R["hfull"].next()
            for (c0, n, kind) in pieces:
                up, upk = R["upiece"].next()
                self.dma(up[:, :, :n], self.uT[:, c0:c0 + n].rearrange("(k p) n -> p k n", p=128), ck("uT", c0, n), [upk])
                ba, bak = self.bank()
                bg, bgk = self.bank()
                for k in range(16):
                    self.mm(ba[:, :n], wt[:, k, 0:128], up[:, k, :n], k == 0, k == 15, [wkey + "a", wkey + "b", upk], [bak])
                for k in range(16):
                    self.mm(bg[:, :n], wt[:, k, 128:256], up[:, k, :n], k == 0, k == 15, [wkey + "a", wkey + "b", upk], [bgk])
                sg, sgk = R["tmpf"].next()
                self.act(sg[:, :n], bg[:, :n], AF.Sigmoid, [bgk, "pv"], [sgk], bias=self.pv[:, b1 + 16 + c:b1 + 16 + c + 1])
                self.stt(hf[:, c0:c0 + n], ba[:, :n], self.pv[:, b1 + c:b1 + c + 1], sg[:, :n], ALU.add, ALU.mult, [bak, sgk, "pv"], [hfk])
            dg, dgk = R["dg"].next()
            wv = self.pv[:, wdw:wdw + 496].rearrange("p (t c) -> p t c", c=16)[:, :, c:c + 1]
            self.tt(dg[:, :, :], self.id_b[:].unsqueeze(1).to_broadcast([128, 31, 128]), wv.to_broadcast([128, 31, 128]), ALU.mult,
                    ["id_b", "pv"], [dgk])
            for (c0, n, kind) in pieces:
                bk, bkey = self.bank()
                for t in range(31):
                    self.mm(bk[:, :n], dg[:, t, :], hf[:, c0 - 15 + t:c0 - 15 + t + n], t == 0, t == 30, [dgk, hfk], [bkey])
                oc, ock = R["tmpf"].next()
                self.act(oc[:, :n], bk[:, :n], AF.Identity, [bkey, "pv"], [ock], bias=self.pv[:, bdw + c:bdw + c + 1])
                self.dma(self.hc[c * 128:(c + 1) * 128, c0:c0 + n], oc[:, :n], [ock], ck("hc", c0, n))
        self.phase("ln")
        lg = PVL.off["lng%d" % j]
        lbo = PVL.off["lnb%d" % j]
        for pi, (c0, n, kind) in enumerate(pieces):
            xin = R["xin2"][pi % 2]
            xs = pi % 2
            for c in range(16):
                self.dma(xin[:, c, :n], self.hc[c * 128:(c + 1) * 128, c0:c0 + n], ck("hc", c0, n), [("xin", xs, c)])
            bm, bmk = self.bank()
            bq, bqk = self.bank()
            for c in range(16):
                sq, sqk = R["sq"].next()
                self.act(sq[:, :n], xin[:, c, :n], AF.Square, [("xin", xs, c)], [sqk])
                self.mm(bm[:, :n], self.ones_f[:], xin[:, c, :n], c == 0, c == 15, ["ones_f", ("xin", xs, c)], [bmk])
                self.mm(bq[:, :n], self.ones_f[:], sq[:, :n], c == 0, c == 15, ["ones_f", sqk], [bqk])
            mean, mk = R["rstd"].next()
            rs, rsk = R["rstd"].next()
            self.act(mean[:, :n], bm[:, :n], AF.Identity, [bmk], [mk], scale=1.0 / D)
            m2, m2k = R["tmpf"].next()
            self.tt(m2[:, :n], mean[:, :n], mean[:, :n], ALU.mult, [mk], [m2k])
            self.stt(m2[:, :n], bq[:, :n], 1.0 / D, m2[:, :n], ALU.mult, ALU.subtract, [bqk, m2k], [m2k])
            self.rsqrt_from(rs[:, :n], m2[:, :n], 1.0, [m2k, "epsb"], [rsk])
            for c in range(16):
                tmp, tk = R["tmpf"].next()
                self.tt(tmp[:, :n], xin[:, c, :n], mean[:, :n], ALU.subtract, [("xin", xs, c), mk], [tk])
                self.tt(tmp[:, :n], tmp[:, :n], rs[:, :n], ALU.mult, [tk, rsk], [tk])
                ub, uk = R["ub"].next()
                self.act(ub[:, :n], tmp[:, :n], AF.Silu, [tk, "pv"], [uk], bias=self.pv[:, lbo + c:lbo + c + 1], scale=self.pv[:, lg + c:lg + c + 1])
                self.dma(self.hT[c * 128:(c + 1) * 128, c0:c0 + n], ub[:, :n], [uk], ck("hT", c0, n))
        self.final_from_dram(l, 1, io["conv_w_pw2"][j], PVL.off["bpw2%d" % j])

    def even(self, l):
        io = self.io
        R, E = self.R, self.E
        j = l // 2
        self.prenorm(l, 1)
        if self.stop_at == "pre":
            return
        self.phase("mla_prep")
        wp = E["wprep"]
        self.dma(wp[:, :, 0:832], io["w_in_ab"][j, :, 0:832].rearrange("(k p) m -> p k m", p=128), (), ["wprep_a"], eng="pool")
        self.dma(wp[:, :, 832:896], io["w_kr_sw"][j].rearrange("(k p) m -> p k m", p=128), (), ["wprep_b"], eng="pool")
        self.dma(E["rope"][:, :, :], io["rope"], (), ["rope"])
        wpk = ["wprep_a", "wprep_b"]
        qn_o = PVL.off["qn%d" % j]
        kvn_o = PVL.off["kvn%d" % j]
        cqn, ckvn, krot = E["cqn"], E["ckvn"], E["krot"]
        for (c0, n, kind) in PIECES:
            up, upk = R["upiece"].next()
            self.dma(up[:, :, :n], self.uT[:, c0:c0 + n].rearrange("(k p) n -> p k n", p=128), ck("uT", c0, n), [upk])
            for (nm, nch, col_off, dst, gain_o) in (("cq", 4, 0, cqn, qn_o), ("ckv", 2, 512, ckvn, kvn_o)):
                sbk, sbkey = self.bank()
                cf = E["cf"]
                for c in range(nch):
                    bk, bkey = self.bank()
                    for k in range(16):
                        self.mm(bk[:, :n], wp[:, k, col_off + c * 128:col_off + (c + 1) * 128], up[:, k, :n], k == 0, k == 15, wpk + [upk], [bkey])
                    self.act(cf[:, c, :n], bk[:, :n], AF.Identity, [bkey], [("xin", c)])
                    sq, sqk = R["sq"].next()
                    self.act(sq[:, :n], bk[:, :n], AF.Square, [bkey], [sqk])
                    self.mm(sbk[:, :n], self.ones_f[:], sq[:, :n], c == 0, c == nch - 1, ["ones_f", sqk], [sbkey])
                rs, rsk = R["rstd"].next()
                self.rsqrt_from(rs[:, :n], sbk[:, :n], 1.0 / (nch * 128), [sbkey, "epsb"], [rsk])
                for c in range(nch):
                    tmp, tk = R["tmpf"].next()
                    self.tt(tmp[:, :n], cf[:, c, :n], rs[:, :n], ALU.mult, [("xin", c), rsk], [tk])
                    self.act(dst[:, c, c0:c0 + n], tmp[:, :n], AF.Identity, [tk, "pv"], ck(nm + "n", c0, n), scale=self.pv[:, gain_o + c:gain_o + c + 1])
            bk, bkey = self.bank()
            bs, bskey = self.bank()
            for k in range(16):
                self.mm(bk[0:64, :n], wp[:, k, 768:832], up[:, k, :n], k == 0, k == 15, wpk + [upk], [bkey])
            for k in range(16):
                self.mm(bs[0:64, :n], wp[:, k, 832:896], up[:, k, :n], k == 0, k == 15, wpk + [upk], [bskey])
            self.rope_combine(krot[:, c0:c0 + n], bk, bkey, bs, bskey, c0, n, ck("krot", c0, n))
        if self.stop_at == "prep":
            return
        self.phase("mla_heads")
        cqn, ckvn, krot = E["cqn"], E["ckvn"], E["krot"]
        self.dma(E["wq"][:, :, :], io["w_q_full"][j].rearrange("(k p) m -> p k m", p=128), (), ["wq"], eng="pool")
        self.dma(E["wkv"][:, :, :], io["w_kv_up"][j].rearrange("(k p) m -> p k m", p=128), (), ["wkv"], eng="pool")
        wq, wkv = E["wq"], E["wkv"]
        for h in range(8):
            qn, qrot, kn, V = E["qn"], E["qrot"], E["kn"], E["V"]
            deferred = None
            for pi, (c0, n, kind) in enumerate(PIECES):
                bk, bkey = self.bank()
                for k in range(4):
                    self.mm(bk[:, :n], wq[:, k, h * 256:h * 256 + 128], cqn[:, k, c0:c0 + n], k == 0, k == 3, ["wq"] + ck("cqn", c0, n), [bkey])
                self.copy(qn[:, c0:c0 + n], bk[:, :n], [bkey], ck("qn", c0, n), eng="act")
                b1, b1k = self.bank()
                b2, b2k = self.bank()
                for k in range(4):
                    self.mm(b1[0:64, :n], wq[:, k, h * 256 + 128:h * 256 + 192], cqn[:, k, c0:c0 + n], k == 0, k == 3, ["wq"] + ck("cqn", c0, n), [b1k])
                for k in range(4):
                    self.mm(b2[0:64, :n], wq[:, k, h * 256 + 192:h * 256 + 256], cqn[:, k, c0:c0 + n], k == 0, k == 3, ["wq"] + ck("cqn", c0, n), [b2k])
                self.rope_combine(qrot[:, c0:c0 + n], b1, b1k, b2, b2k, c0, n, ck("qrot", c0, n))
                bk, bkey = self.bank()
                for k in range(2):
                    self.mm(bk[:, :n], wkv[:, k, h * 256:h * 256 + 128], ckvn[:, k, c0:c0 + n], k == 0, k == 1, ["wkv"] + ck("ckvn", c0, n), [bkey])
                self.copy(kn[:, c0:c0 + n], bk[:, :n], [bkey], ck("kn", c0, n), eng="act")
                bk, bkey = self.bank()
                for k in range(2):
                    self.mm(bk[:, :n], wkv[:, k, h * 256 + 128:h * 256 + 256], ckvn[:, k, c0:c0 + n], k == 0, k == 1, ["wkv"] + ck("ckvn", c0, n), [bkey])
                vt, vtk = R["ub"].next()
                self.copy(vt[:, :n], bk[:, :n], [bkey], [vtk], eng="act")
                if deferred is not None:
                    self.to_tokmajor(*deferred, banks=(6, 7))
                deferred = (vt, vtk, c0, n, V, "V")
            self.to_tokmajor(*deferred, banks=(6, 7))
            for (c0, n, kind) in (PIECES if self.ctx_out else PIECES[1:]):
                kbs = [0, 1] if kind == 1 else list(range(18))
                ob, obk = self.banks[6], ("pb", 6)
                db, dbk = self.banks[7], ("pb", 7)
                pend = None
                nk = len(kbs)
                for i in range(nk + 1):
                    if i < nk:
                        kb = kbs[i]
                        kc = blkcol(kb)
                        sbk, sbkey = self.bank()
                        self.mm(sbk[:, :n], kn[:, kc:kc + 128], qn[:, c0:c0 + n], True, False, ck("kn", kc, 128) + ck("qn", c0, n), [sbkey])
                        self.mm(sbk[:, :n], krot[:, kc:kc + 128], qrot[:, c0:c0 + n], False, True, ck("krot", kc, 128) + ck("qrot", c0, n), [sbkey])
                        pt, ptk = R["ub"].next()
                        self.act(pt[:, :n], sbk[:, :n], AF.Exp, [sbkey], [ptk], scale=ATTN_SCALE)
                        cur = (i, kb, pt, ptk)
                    if pend is not None:
                        pi, pkb, ppt, pptk = pend
                        self.mm(ob[:, :n], V[:, pkb, :], ppt[:, :n], pi == 0, pi == nk - 1, [("V", pkb), pptk], [obk])
                        self.mm(db[:, :n], self.ones_b[:], ppt[:, :n], pi == 0, pi == nk - 1, ["ones_b", pptk], [dbk])
                    pend = cur if i < nk else None
                rd, rdk = R["tmpf"].next()
                self.recip(rd[:, :n], db[:, :n], [dbk], [rdk])
                at, atk = R["ub"].next()
                self.tt(at[:, :n], ob[:, :n], rd[:, :n], ALU.mult, [obk, rdk], [atk])
                self.dma(self.hT[h * 128:(h + 1) * 128, c0:c0 + n], at[:, :n], [atk], ck("hT", c0, n))
        if self.stop_at == "mla":
            return
        self.phase("hgrn")
        for h in range(8):
            self.hgrn_head(l, j, h)
        if self.stop_at == "hgrn":
            return
        self.final_from_dram(l, 1, io["w_out_ab"][j], None)

    def rope_combine(self, dst, b1, b1k, b2, b2k, c0, n, wkeys):
        R, E = self.R, self.E
        rope = E["rope"]
        t1, t1k = R["tmpf"].next()
        t2, t2k = R["tmpf"].next()
        self.tt(t1[0:64, :n], b1[0:64, :n], rope[:, 0, c0:c0 + n], ALU.mult, [b1k, "rope"], [t1k])
        self.tt(t2[0:64, :n], b2[0:64, :n], rope[:, 1, c0:c0 + n], ALU.mult, [b2k, "rope"], [t2k])
        self.tt(dst, t1[0:64, :n], t2[0:64, :n], ALU.add, [t1k, t2k], wkeys)

    def to_tokmajor(self, src, srck, c0, n, dst, dname, blk=128, banks=None):
        per = 128 // blk
        for q in range(n // blk):
            col = c0 + q * blk
            tb = (col - CTX0) // blk if col < LAT0 else 2 * per + (col - LAT0) // blk
            if banks is None:
                bk, bkey = self.bank()
            else:
                bi = banks[q % len(banks)]
                bk, bkey = self.banks[bi], ("pb", bi)
            bkb = bk.bitcast(BF16)
            self.P.op("pe", lambda h, bkb=bkb, q=q: h.transpose(bkb[0:blk, 0:128], src[:, q * blk:(q + 1) * blk], self.id_b[:]), [srck, "id_b"], [bkey])
            self.copy(dst[:, tb, :], bkb[0:blk, 0:128], [bkey], [(dname, tb)])

    def hgrn_head(self, l, j, h):
        io = self.io
        R, E = self.R, self.E
        wh = E["wh"]
        for gi, base in enumerate((832, 1856, 2880, 3904, 4928)):
            self.dma(wh[:, :, gi * 128:(gi + 1) * 128], io["w_in_ab"][j, :, base + h * 128:base + (h + 1) * 128].rearrange("(k p) m -> p k m", p=128),
                     (), [("wh", gi)], eng="pool")
        whk = [("wh", g) for g in range(5)]
        SG, OT, Vh = E["SG"], E["OT"], E["Vh"]
        QT, KT, KH = [E["QT0"], E["QT1"]], [E["KT0"], E["KT1"]], [E["KH0"], E["KH1"]]
        MS, FDc = [E["MS0"], E["MS1"]], [E["FD0"], E["FD1"]]
        lo, hi = CTX0, LAT0 + NLAT
        CH, MID, LAST = 32, 15, 31
        maskP = E["maskP"]
        def piece_gen(pi, c0, kind):
            n = 256
            ci0 = (c0 - CTX0) // 32 if kind == 1 else 8 + (c0 - LAT0) // 32
            up, upk = R["upiece"].next()
            self.dma(up[:, :, :n], self.uT[:, c0:c0 + n].rearrange("(k p) n -> p k n", p=128), ck("uT", c0, n), [upk])
            bks = []
            for gi in range(5):
                bk, bkey = self.bank(0, 6)
                for k in range(16):
                    self.mm(bk[:, :n], wh[:, k, gi * 128:(gi + 1) * 128], up[:, k, :n], k == 0, k == 15, whk + [upk], [bkey])
                bks.append((bk, bkey))
            yield
            gt = E["gt"][pi % 3]
            gk = lambda i: ("gt", pi % 3, i)
            qs, A, K, G = gt[0], [gt[1], gt[2]], [gt[3], gt[4]], [gt[5], gt[6]]
            self.copy(qs[:, :], bks[0][0][:, :n], [bks[0][1]], [gk(0)], eng="act")
            for d in range(2):
                bk, bkey = bks[1 + d]
                self.act(A[d][:, :], bk[:, :n], AF.Sigmoid, [bkey], [gk(1 + d)])
                self.act(K[d][:, :], bk[:, :n], AF.Sigmoid, [bkey], [gk(3 + d)], scale=-1.0)
            self.act(SG[:, c0:c0 + n], bks[4][0][:, :n], AF.Silu, [bks[4][1]], ["SG"])
            vt, vtk = R["ub"].next()
            self.copy(vt[:, :n], bks[3][0][:, :n], [bks[3][1]], [vtk], eng="act")
            yield
            for d in range(2):
                lb, oml, lbk = self.lbs(j, d, h)
                self.ts(A[d][:, :], A[d][:, :], oml, lb, ALU.mult, ALU.add, [gk(1 + d), lbk], [gk(1 + d)])
                self.P.op("dve", lambda hh, o=K[d][:, :], sc=oml: hh.tensor_scalar_mul(out=o, in0=o, scalar1=sc), [gk(3 + d), lbk], [gk(3 + d)])
            yield
            self.to_tokmajor(vt, vtk, c0, n, Vh, "Vh", blk=32, banks=(6, 7))
            for d in range(2):
                self.act(A[d][:, :], A[d][:, :], AF.Ln, [gk(1 + d)], [gk(1 + d)])
            yield
            for d in range(2):
                self.P.op("dve", lambda hh, d=d, A=A, G=G: hh.tensor_tensor_scan(out=G[d][:, :], data0=maskP[:, :], data1=A[d][:, :], initial=0.0,
                                                                                 op0=ALU.mult, op1=ALU.add), [gk(1 + d), "maskP"], [gk(5 + d)])
            yield
            for d in range(2):
                Gv = G[d][:, :].rearrange("p (c t) -> p c t", t=CH)
                Lv = A[d][:, :].rearrange("p (c t) -> p c t", t=CH)
                gG, gL = gk(5 + d), gk(1 + d)
                self.act(FDc[d][:, ci0:ci0 + 8], Gv[:, :, LAST], AF.Exp, [gG], ["FD%d" % d])
                if d == 0:
                    self.act(MS[d][:, ci0:ci0 + 8], Gv[:, :, MID], AF.Exp, [gG], ["MS%d" % d])
                    self.tt(Lv, Gv, Gv[:, :, MID:MID + 1].to_broadcast([128, 8, CH]), ALU.subtract, [gG], [gL])
                    self.tt(Gv, Gv[:, :, LAST:LAST + 1].to_broadcast([128, 8, CH]), Gv, ALU.subtract, [gG], [gG])
                    e1, e1k, e3, e3k = A[d], gL, G[d], gG
                else:
                    self.tt(Lv, Gv, Lv, ALU.subtract, [gG, gL], [gL])
                    tmpc = E["tmpc"]
                    self.tt(tmpc[:, 0:8], Gv[:, :, LAST], Lv[:, :, MID + 1], ALU.subtract, [gG, gL], ["tmpc"])
                    self.act(MS[d][:, ci0:ci0 + 8], tmpc[:, 0:8], AF.Exp, ["tmpc"], ["MS%d" % d])
                    self.tt(Gv, Lv[:, :, MID + 1:MID + 2].to_broadcast([128, 8, CH]), Lv, ALU.subtract, [gL], [gG])
                    e1, e1k, e3, e3k = G[d], gG, A[d], gL
                yield
                self.act(e3[:, :], e3[:, :], AF.Exp, [e3k], [e3k])
                yield
                self.tt(KH[d][:, c0:c0 + n], K[d][:, :], e3[:, :], ALU.mult, [gk(3 + d), e3k], ["KH%d" % d])
                self.act(e3[:, :], e1[:, :], AF.Exp, [e1k], [e3k])
                yield
                self.tt(QT[d][:, c0:c0 + n], qs[:, :], e3[:, :], ALU.mult, [gk(0), e3k], ["QT%d" % d])
                self.act(e3[:, :], e1[:, :], AF.Exp, [e1k], [e3k], scale=-1.0)
                yield
                self.tt(KT[d][:, c0:c0 + n], K[d][:, :], e3[:, :], ALU.mult, [gk(3 + d), e3k], ["KT%d" % d])

        pend = [piece_gen(pi, c0, kind) for pi, (c0, kind) in enumerate(WINS)]
        active = []
        mod_left = [13 if l == 0 else 7]
        while pend or active:
            if pend and len(active) < 3:
                active.append(pend.pop(0))
            for g in list(active):
                try:
                    next(g)
                except StopIteration:
                    active.remove(g)
            if self.modgen is not None and mod_left[0] > 0:
                mod_left[0] -= 1
                self.pump_mod(1)
        order = [list(range(18)), [1, 0] + list(range(17, 1, -1))]
        Sr = [E["Sr0"], E["Sr1"]]
        SMr = [E["SMr0"], E["SMr1"]]
        masks = [self.mask_f, self.mask_b]
        for d in range(2):
            self.memset(Sr[d][:, 0, :], 0.0, [("S", d, 0)], eng="dve")
        visited = set()

        def emit_PU(step, d):
            tb = order[d][step]
            col = blkcol(tb)
            pb0 = 4 * d
            atb = self.banks[pb0]
            trbb = self.banks[pb0 + 1].bitcast(BF16)
            ubk, ubkk = self.banks[pb0 + 3], ("pb", pb0 + 3)
            atm, kht = E["atm"][d], E["kht"][d]
            chs = (0, 1, 2, 3) if d == 0 else (3, 2, 1, 0)
            atk, trk = ("pb", pb0), ("pb", pb0 + 1)
            for ch in chs:
                cc = col + ch * 32
                self.mm(atb[0:32, ch * 32:(ch + 1) * 32], KT[d][:, cc:cc + 32], QT[d][:, cc:cc + 32], True, True, ["KT%d" % d, "QT%d" % d], [atk])
            for ch in chs:
                cc = col + ch * 32
                self.P.op("pe", lambda hh, trbb=trbb, d=d, cc=cc, ch=ch: hh.transpose(trbb[0:32, ch * 128:(ch + 1) * 128], KH[d][:, cc:cc + 32], self.id_b[:]),
                          ["KH%d" % d, "id_b"], [trk])
            self.tt(atm[:, step % 2, :, :], atb[0:32, 0:128].rearrange("p (c t) -> p c t", t=32), masks[d][:, :].unsqueeze(1).to_broadcast([32, 4, 32]),
                    ALU.mult, [atk, "mask%d" % d], [("atm", d, step % 2)])
            self.copy(kht[:, :, :], trbb[0:32, 0:512].rearrange("p (c t) -> p c t", t=128), [trk], [("kht", d)], eng="act")
            for qi, ch in enumerate(chs):
                ci = tb * 4 + ch
                self.mm(ubk[:, qi * 128:(qi + 1) * 128], kht[:, ch, :], Vh[:, ci, :], True, True, [("kht", d), ("Vh", ci)], [ubkk])

        def emit_S(step, d):
            tb = order[d][step]
            ubk, ubkk = self.banks[4 * d + 3], ("pb", 4 * d + 3)
            chs = (0, 1, 2, 3) if d == 0 else (3, 2, 1, 0)
            for qi, ch in enumerate(chs):
                ci = tb * 4 + ch
                g = step * 4 + qi
                self.act(SMr[d][:, g % 8, :], Sr[d][:, g % 8, :], AF.Identity, [("S", d, g % 8), "MS%d" % d], [("SM", d, g % 8)], scale=MS[d][:, ci:ci + 1])
                self.stt(Sr[d][:, (g + 1) % 8, :], Sr[d][:, g % 8, :], FDc[d][:, ci:ci + 1], ubk[:, qi * 128:(qi + 1) * 128], ALU.mult, ALU.add,
                         [("S", d, g % 8), "FD%d" % d, ubkk], [("S", d, (g + 1) % 8)])

        def emit_O(step, d):
            tb = order[d][step]
            col = blkcol(tb)
            ob, obk = self.banks[4 * d + 2], ("pb", 4 * d + 2)
            atm = E["atm"][d]
            chs = (0, 1, 2, 3) if d == 0 else (3, 2, 1, 0)
            for qi, ch in enumerate(chs):
                ci = tb * 4 + ch
                cc = col + ch * 32
                g = step * 4 + qi
                self.mm(ob[:, ch * 32:(ch + 1) * 32], Vh[:, ci, :], atm[:, step % 2, ch, :], True, False, [("Vh", ci), ("atm", d, step % 2)], [obk])
                self.mm(ob[:, ch * 32:(ch + 1) * 32], SMr[d][:, g % 8, :], QT[d][:, cc:cc + 32], False, True, [("SM", d, g % 8), "QT%d" % d], [obk])
            if tb not in visited:
                visited.add(tb)
                self.copy(OT[:, col:col + 128], ob[:, 0:128], [obk], [("OT", tb)], eng="act")
            else:
                self.tt(OT[:, col:col + 128], ob[:, 0:128], OT[:, col:col + 128], ALU.add, [obk, ("OT", tb)], [("OT", tb)])

        for step in range(18):
            for d in range(2):
                emit_PU(step, d)
            for d in range(2):
                emit_S(step, d)
            if step >= 1:
                for d in range(2):
                    emit_O(step - 1, d)
        for d in range(2):
            emit_O(17, d)
        hg_o = PVL.off["hgn%d" % j]
        otk = [("OT", tb) for tb in range(18)]
        for (c0, n, kind) in (PIECES if self.ctx_out else PIECES[1:]):
            sq, sqk = R["sq"].next()
            self.act(sq[:, :n], OT[:, c0:c0 + n], AF.Square, otk, [sqk])
            bk, bkey = self.bank()
            self.mm(bk[:, :n], self.ones_f[:], sq[:, :n], True, True, ["ones_f", sqk], [bkey])
            rs, rsk = R["rstd"].next()
            self.rsqrt_from(rs[:, :n], bk[:, :n], 1.0 / 128, [bkey, "epsb"], [rsk])
            tmp, tk = R["tmpf"].next()
            self.tt(tmp[:, :n], OT[:, c0:c0 + n], rs[:, :n], ALU.mult, otk + [rsk], [tk])
            ub, uk = R["ub"].next()
            self.stt(ub[:, :n], tmp[:, :n], self.pv[:, hg_o:hg_o + 1], SG[:, c0:c0 + n], ALU.mult, ALU.mult, [tk, "pv", "SG"], [uk])
            self.dma(self.hT[(8 + h) * 128:(9 + h) * 128, c0:c0 + n], ub[:, :n], [uk], ck("hT", c0, n))


_WSHAPES = {
    "w_mod": [DEPTH, D, 6 * D], "w_in_ab": [2, D, 5952], "w_kr_sw": [2, D, 64], "w_q_full": [2, 512, 2048],
    "w_kv_up": [2, 256, 2048], "w_out_ab": [2, D, D], "conv_w_pw1": [2, D, 2 * D], "conv_w_pw2": [2, D, D],
    "ffn_w_up": [DEPTH, D, 2 * DFF], "ffn_w_down": [DEPTH, DFF, D],
}


def build_program(stop=None, layers=None):
    nc = bass.Bass("TRN2", target_bir_lowering=False)
    io = {}
    io["xT0"] = nc.dram_tensor("xT0", [D, NCOL], F32, kind="ExternalInput").ap()
    io["pv"] = nc.dram_tensor("pv", [128, PVL.n], F32, kind="ExternalInput").ap()
    io["rope"] = nc.dram_tensor("rope", [64, 2, NCOL], F32, kind="ExternalInput").ap()
    for k, shp in _WSHAPES.items():
        io[k] = nc.dram_tensor(k, shp, F32, kind="ExternalInput").ap()
    outT = nc.dram_tensor("outT", [D, NCOL], F32, kind="ExternalOutput").ap()
    dbg_h = nc.dram_tensor("dbg_h", [D, NCOL], BF16, kind="ExternalOutput").ap() if stop is not None else None
    dbg_u = nc.dram_tensor("dbg_u", [D, NCOL], BF16, kind="ExternalOutput").ap() if stop is not None else None
    st = contextlib.ExitStack()
    with st:
        kb = KB(nc, st)
        kb.setup(io)
        R = kb.R
        R["sq"] = Rot(kb, "sq", 4, [128, 512], F32)
        R["tmpf"] = Rot(kb, "tmpf", 4, [128, 512], F32)
        R["ub"] = Rot(kb, "ub", 4, [128, 512], BF16)
        R["rstd"] = Rot(kb, "rstd", 3, [128, 512], F32)
        kb.arena = kb.sb("arena", [128, ARENA_BYTES // 2], BF16)
        kb.stop_at = stop[1] if stop else None
        kb.mod_ready = set()
        kb.modgen = None
        if kb.stop_at != "setup":
            kb.phase("mod")
            for _ in kb.mod_stream([0]):
                pass
            kb.modgen = kb.mod_stream([1, 2, 3])
        for l in (range(DEPTH) if layers is None else layers):
            if kb.stop_at in ("setup", "mod"):
                break
            kb.ctx_in = l < DEPTH - 1
            kb.ctx_out = l < DEPTH - 2
            if l not in kb.mod_ready:
                kb.phase("mod")
                kb.need_mod(l)
            if l % 2 == 0:
                kb.even(l)
            else:
                kb.conformer(l)
            if stop is not None and stop[0] == l and stop[1] != "ffn":
                break
            kb.ffn(l)
            if stop == (l, "ffn"):
                break
        kb.P.barrier()
        for c in range(16):
            kb.dma(outT[c * 128:(c + 1) * 128, :], kb.xT[c * 128:(c + 1) * 128, :], (), [("outT", c)])
            if dbg_h is not None:
                kb.dma(dbg_h[c * 128:(c + 1) * 128, :], kb.hT[c * 128:(c + 1) * 128, :], (), [("dbgh", c)])
                kb.dma(dbg_u[c * 128:(c + 1) * 128, :], kb.uT[c * 128:(c + 1) * 128, :], (), [("dbgu", c)])
        kb.P.emit(st)
    return nc


ARENA_BYTES = 158 * 1024


class Carver:
    def __init__(self, arena):
        self.arena = arena
        self.off = 0

    def take(self, shape, dtype):
        free = int(np.prod(shape[1:]))
        nbytes = free * (4 if dtype == F32 else 2)
        assert self.off + nbytes <= ARENA_BYTES, (self.off, nbytes)
        ap = self.arena[0:shape[0], self.off // 2:(self.off + nbytes) // 2]
        if dtype == F32:
            ap = ap.bitcast(F32)
        self.off += (nbytes + 63) // 64 * 64
        names = "abcdefg"[:len(shape) - 1]
        if len(shape) > 2:
            pat = "p (%s) -> p %s" % (" ".join(names), " ".join(names))
            ap = ap.rearrange(pat, **{names[i]: shape[1 + i] for i in range(1, len(shape) - 1)})
        return ap

    def rot(self, name, n, shape, dtype):
        return RotAP([(self.take(shape, dtype), "%s%d" % (name, i)) for i in range(n)])


class RotAP:
    def __init__(self, items):
        self.items = items
        self.i = 0

    def next(self):
        it = self.items[self.i % len(self.items)]
        self.i += 1
        return it


def rope_tables_host():
    tab = np.zeros((64, 2, NCOL), np.float32)
    tab[:, 0, :] = 1.0
    t = np.arange(NLAT)
    pos = np.stack([t // 64, t % 64], 0).astype(np.float32)
    freqs = (10000.0 ** (-np.arange(16, dtype=np.float32) / 16)).astype(np.float32)
    for i in range(64):
        a, half, p = i // 32, (i % 32) // 16, i % 16
        ang = (pos[a] * freqs[p]).astype(np.float32)
        tab[i, 0, LAT0:LAT0 + NLAT] = np.cos(ang)
        tab[i, 1, LAT0:LAT0 + NLAT] = np.sin(ang) * (-1.0 if half == 0 else 1.0)
    return tab


SWAP = np.array([(i // 32) * 32 + (1 - (i % 32) // 16) * 16 + i % 16 for i in range(64)])


def host_inputs(inputs, b):
    f = lambda k: np.asarray(inputs[k], np.float32)
    m = {}
    xT0 = np.zeros((D, NCOL), np.float32)
    xT0[:, CTX0:CTX0 + NCTX] = f("ctx")[b].T
    xT0[:, LAT0:LAT0 + NLAT] = f("x")[b].T
    m["xT0"] = xT0
    pv = np.zeros((128, PVL.n), np.float32)

    def put(name, arr):
        o = PVL.off[name]
        pv[:, o:o + arr.shape[1]] = arr

    cc = np.stack([fm(f("c")[b]), fm(f("c_ctx"))], -1).reshape(128, 32)
    put("cc", cc)
    for l in range(DEPTH):
        put("bmod%d" % l, fm(f("b_mod")[l]))
        put("gains%d" % l, np.concatenate([fm(f("norm_gains")[l, i]) for i in range(4)], 1))
        put("fwc%d" % l, np.concatenate([fm(f("ffn_w_conv")[l, k]) for k in range(3)], 1))
        put("fbc%d" % l, fm(f("ffn_b_conv")[l]))
    for j in range(2):
        put("qn%d" % j, fm(f("mla_q_norm")[j]))
        put("kvn%d" % j, fm(f("mla_kv_norm")[j]))
        put("hgn%d" % j, fm(f("hgrn_norm")[j]))
        put("bpw1%d" % j, fm(f("conv_b_pw1")[j]))
        wd = f("conv_w_dw")[j]
        put("wdw%d" % j, np.concatenate([fm(wd[k]) for k in range(31)], 1))
        put("bdw%d" % j, fm(f("conv_b_dw")[j]))
        put("lng%d" % j, fm(f("conv_ln_g")[j]))
        put("lnb%d" % j, fm(f("conv_ln_b")[j]))
        put("bpw2%d" % j, fm(f("conv_b_pw2")[j]))
    hl = f("hgrn_lb")
    put("hlb", np.concatenate([fm(hl[d, jj]) for d in range(2) for jj in range(2)], 1))
    m["pv"] = pv
    return m


def shared_inputs(inputs):
    f = lambda k: np.ascontiguousarray(np.asarray(inputs[k], np.float32))
    m = {k: f(k) for k in ("w_mod", "w_in_ab", "w_kv_up", "w_out_ab", "conv_w_pw1", "conv_w_pw2", "ffn_w_up", "ffn_w_down")}
    m["w_kr_sw"] = np.ascontiguousarray(m["w_in_ab"][:, :, 768:832][:, :, SWAP])
    wq = f("w_q_up").reshape(2, 512, 8, 192)
    m["w_q_full"] = np.ascontiguousarray(np.concatenate([wq, wq[:, :, :, 128:][:, :, :, SWAP]], -1).reshape(2, 512, 2048))
    m["rope"] = rope_tables_host()
    return m


_NC_CACHE = {}


def kernel(**inputs):
    n = 4
    if "full" not in _NC_CACHE:
        _NC_CACHE["full"] = build_program()
    nc = _NC_CACHE["full"]
    sh = shared_inputs(inputs)
    in_maps = []
    for b in range(n):
        m = dict(sh)
        m.update(host_inputs(inputs, b))
        in_maps.append(m)
    res = run_bass_kernel_spmd(nc, in_maps, core_ids=list(range(n)))
    out = np.stack([np.ascontiguousarray(res.results[b]["outT"][:, LAT0:LAT0 + NLAT].T) for b in range(n)], 0)
    return out.astype(np.float32)
```

```python
import contextlib
import numpy as np
import concourse.bass as bass
import concourse.mybir as mybir
from concourse.bass_utils import run_bass_kernel_spmd

dt = mybir.dt
F32 = dt.float32
BF16 = dt.bfloat16
AF = mybir.ActivationFunctionType
ALU = mybir.AluOpType

ENG_NAMES = ("pe", "act", "dve", "pool", "sp")

D = 2048
DEPTH = 4
NCTX = 256
NLAT = 2048
DFF = 5632
EPS = 1e-6
ATTN_SCALE = 192.0 ** -0.5
NCOL = 2352
CTX0 = 16
LAT0 = 288
PIECES = [(CTX0, 256, 1)] + [(LAT0 + 512 * i, 512, 0) for i in range(4)]
WINS = [(CTX0, 1)] + [(LAT0 + 256 * i, 0) for i in range(8)]
ST_RUNS = [[(0, CTX0, 256, 1), (256, LAT0, 512, 0)],
           [(0, 800, 512, 0), (512, 1312, 256, 0)],
           [(0, 1568, 512, 0), (512, 2080, 256, 0)]]
CELL_BOUNDS = np.array([0, 16, 272] + [288 + 256 * i for i in range(9)])


def cells(c0, n):
    a = int(np.searchsorted(CELL_BOUNDS, c0, side="right") - 1)
    b = int(np.searchsorted(CELL_BOUNDS, c0 + n - 1, side="right") - 1)
    return list(range(a, b + 1))


def ck(name, c0, n):
    return [(name, i) for i in cells(c0, n)]


def blkcol(tb):
    return CTX0 + 128 * tb if tb < 2 else LAT0 + 128 * (tb - 2)


class Op:
    __slots__ = ("eng", "fn", "deps", "signal", "tok", "lane")

    def __init__(self, eng, fn, lane=None):
        self.eng = eng
        self.fn = fn
        self.deps = []
        self.signal = lane is not None
        self.tok = None
        self.lane = lane


class Reg:
    __slots__ = ("w", "r")

    def __init__(self):
        self.w = None
        self.r = []


class Prog:
    def __init__(self, nc, n_lanes=8):
        self.nc = nc
        self.ops = {e: [] for e in ENG_NAMES}
        self.regs = {}
        self.n_lanes = n_lanes
        self.lane_last = {}
        self.lane_rr = {e: 0 for e in ENG_NAMES}

    def reg(self, key):
        r = self.regs.get(key)
        if r is None:
            r = self.regs[key] = Reg()
        return r

    def _add(self, op, reads, writes):
        deps = op.deps
        for k in reads:
            r = self.reg(k)
            if r.w is not None:
                deps.append(r.w)
        for k in writes:
            r = self.reg(k)
            if r.w is not None:
                deps.append(r.w)
            deps.extend(r.r)
        for k in reads:
            self.reg(k).r.append(op)
        for k in writes:
            r = self.reg(k)
            r.w = op
            r.r = []
        if op.eng == "pe":
            op.deps = deps = [d for d in deps if d.eng != "pe" or d.lane is not None]
        for d in deps:
            d.signal = True
        self.ops[op.eng].append(op)
        return op

    def op(self, eng, fn, reads=(), writes=()):
        return self._add(Op(eng, fn), reads, writes)

    def dma(self, eng, fn, reads=(), writes=()):
        lane = (eng, self.lane_rr[eng] % self.n_lanes)
        self.lane_rr[eng] += 1
        op = Op(eng, fn, lane=lane)
        prev = self.lane_last.get(lane)
        if prev is not None:
            op.deps.append(prev)
            prev.signal = True
        self.lane_last[lane] = op
        return self._add(op, reads, writes)

    def barrier(self):
        lasts = list(self.lane_last.values())
        for e in ENG_NAMES:
            for o in reversed(self.ops[e]):
                if o.fn is not None:
                    lasts.append(o)
                    break
        for e in ENG_NAMES:
            op = Op(e, None)
            op.deps = [d for d in lasts if not (d.eng == e and d.lane is None)]
            for d in op.deps:
                d.signal = True
            self.ops[e].append(op)
        self.regs = {}

    def emit(self, stack):
        nc = self.nc
        sems = {e: stack.enter_context(nc.semaphore("sem_" + e)) for e in ENG_NAMES}
        lane_sems = {}
        lane_cnt = {}
        for e in ENG_NAMES:
            for i in range(self.n_lanes):
                if (e, i) in self.lane_last:
                    lane_sems[(e, i)] = stack.enter_context(nc.semaphore("lane_%s%d" % (e, i)))
                    lane_cnt[(e, i)] = 0
        for e in ENG_NAMES:
            c = 0
            for op in self.ops[e]:
                if op.lane is not None:
                    lane_cnt[op.lane] += 16
                    op.tok = (lane_sems[op.lane], lane_cnt[op.lane])
                elif op.signal:
                    c += 1
                    op.tok = (sems[e], c)
        final_waits = [(lane_sems[l], lane_cnt[l]) for l in lane_sems]
        block = stack.enter_context(nc.Block())
        ops = self.ops

        def replay(e, h):
            seen = {}
            for op in ops[e]:
                for d in op.deps:
                    sem, val = d.tok
                    k = sem.num
                    if seen.get(k, 0) < val:
                        seen[k] = val
                        h.wait_ge(sem, val)
                if op.fn is None:
                    continue
                inst = op.fn(h)
                if op.lane is not None:
                    inst.then_inc(op.tok[0], 16)
                elif op.signal:
                    inst.then_inc(op.tok[0], 1)
            if e == "sp":
                for sem, val in final_waits:
                    if val:
                        h.wait_ge(sem, val)

        @block.tensor
        def _(h):
            replay("pe", h)

        @block.scalar
        def _(h):
            replay("act", h)

        @block.vector
        def _(h):
            replay("dve", h)

        @block.gpsimd
        def _(h):
            replay("pool", h)

        @block.sync
        def _(h):
            replay("sp", h)


class Rot:
    def __init__(self, kb, name, n, shape, dtype):
        self.items = [(kb.sb("%s%d" % (name, i), shape, dtype), "%s%d" % (name, i)) for i in range(n)]
        self.i = 0

    def next(self):
        it = self.items[self.i % len(self.items)]
        self.i += 1
        return it


class PVLayout:
    def __init__(self):
        self.off = {}
        self.n = 0

    def add(self, name, ncols):
        self.off[name] = self.n
        self.n += ncols


def pv_layout():
    L = PVLayout()
    L.add("cc", 32)
    for l in range(DEPTH):
        L.add("bmod%d" % l, 96)
        L.add("gains%d" % l, 64)
        L.add("fwc%d" % l, 3 * 88)
        L.add("fbc%d" % l, 88)
    for j in range(2):
        L.add("qn%d" % j, 4)
        L.add("kvn%d" % j, 2)
        L.add("hgn%d" % j, 1)
        L.add("bpw1%d" % j, 32)
        L.add("wdw%d" % j, 31 * 16)
        L.add("bdw%d" % j, 16)
        L.add("lng%d" % j, 16)
        L.add("lnb%d" % j, 16)
        L.add("bpw2%d" % j, 16)
    L.add("hlb", 32)
    return L


PVL = pv_layout()


def fm(v):
    v = np.asarray(v, np.float32)
    return np.ascontiguousarray(v.reshape(-1, 128).T)


class KB:
    def __init__(self, nc, st):
        self.nc = nc
        self.st = st
        self.P = Prog(nc)
        self.banks = [st.enter_context(nc.psum_tensor("pb%d" % i, [128, 512], F32)) for i in range(8)]
        self.brr = 0
        self.R = {}
        self.E = {}
        self.ctx_in = True
        self.ctx_out = True

    def phase(self, name):
        self.P.barrier()
        C = Carver(self.arena)
        R, E = self.R, self.E
        if name == "mod":
            R["wmod"] = C.rot("wmod", 2, [128, 16, 256], BF16)
        elif name in ("prenorm", "ln"):
            R["xin2"] = [C.take([128, 16, 512], F32) for _ in range(2)]
        elif name == "ffn":
            R["aT44"] = C.take([128, 44, 768], BF16)
            R["uw"] = [C.take([128, 16, 258], BF16) for _ in range(3)]
            R["hvg"] = C.rot("hvg", 6, [128, 256], F32)
            R["wup"] = C.rot("wup", 2, [128, 16, 512], BF16)
            R["wfin"] = C.rot("wfin", 2, [128, 44, 128], BF16)
            R["rstd768"] = [C.take([128, 768], F32) for _ in range(2)]
        elif name == "final16":
            R["aT16"] = C.take([128, 16, 768], BF16)
            R["wfin"] = C.rot("wfin", 2, [128, 16, 256], BF16)
            R["rstd768"] = [C.take([128, 768], F32) for _ in range(2)]
        elif name == "odd1":
            R["upiece"] = C.rot("upiece", 2, [128, 16, 512], BF16)
            R["hfull"] = C.rot("hfull", 2, [128, NCOL], BF16)
            R["dg"] = C.rot("dg", 2, [128, 31, 128], BF16)
            R["wup"] = C.rot("wup", 2, [128, 16, 256], BF16)
            for ap, key in R["hfull"].items:
                self.memset(ap, 0.0, [key], eng="dve")
        elif name in ("mla_prep", "mla_heads"):
            E["rope"] = C.take([64, 2, NCOL], F32)
            E["cqn"] = C.take([128, 4, NCOL], BF16)
            E["ckvn"] = C.take([128, 2, NCOL], BF16)
            E["krot"] = C.take([64, NCOL], BF16)
            if name == "mla_prep":
                R["upiece"] = C.rot("upiece", 2, [128, 16, 512], BF16)
                E["wprep"] = C.take([128, 16, 896], BF16)
                E["cf"] = C.take([128, 4, 512], F32)
            else:
                E["wq"] = C.take([128, 4, 2048], BF16)
                E["wkv"] = C.take([128, 2, 2048], BF16)
                E["qn"] = C.take([128, NCOL], BF16)
                E["qrot"] = C.take([64, NCOL], BF16)
                E["kn"] = C.take([128, NCOL], BF16)
                E["V"] = C.take([128, 18, 128], BF16)
        elif name == "hgrn":
            R["upiece"] = C.rot("upiece", 2, [128, 16, 256], BF16)
            R["wmod"] = C.rot("wmod", 2, [128, 16, 256], BF16)
            E["wh"] = C.take([128, 16, 640], BF16)
            E["OT"] = C.take([128, NCOL], F32)
            for nm in ("SG", "QT0", "QT1", "KT0", "KT1", "KH0", "KH1"):
                E[nm] = C.take([128, NCOL], BF16)
            E["Vh"] = C.take([32, 72, 128], BF16)
            for nm in ("MS0", "MS1", "FD0", "FD1", "tmpc"):
                E[nm] = C.take([128, 72], F32)
            for nm in ("Sr0", "Sr1"):
                E[nm] = C.take([128, 8, 128], F32)
            for nm in ("SMr0", "SMr1"):
                E[nm] = C.take([128, 8, 128], BF16)
            E["atm"] = [C.take([32, 2, 4, 32], BF16) for _ in range(2)]
            E["kht"] = [C.take([32, 4, 128], BF16) for _ in range(2)]
            E["gt"] = [[C.take([128, 256], F32) for _ in range(7)] for _ in range(3)]
            E["maskP"] = C.take([128, 256], F32)
            self.memset(E["maskP"], 1.0, ["maskP"], eng="dve")
            self.memset(E["maskP"].rearrange("p (c t) -> p c t", t=32)[:, :, 0:1], 0.0, ["maskP"], eng="dve")
        else:
            raise ValueError(name)

    def sb(self, name, shape, dtype):
        return self.st.enter_context(self.nc.sbuf_tensor("s_" + name, shape, dtype))

    def dram(self, name, shape, dtype):
        return self.nc.dram_tensor(name, shape, dtype, kind="Internal").ap()

    def bank(self, lo=0, hi=6):
        i = lo + self.brr % (hi - lo)
        self.brr += 1
        return self.banks[i], ("pb", i)

    def mm(self, out, lhsT, rhs, start, stop, reads, writes):
        self.P.op("pe", lambda h: h.matmul(out, lhsT=lhsT, rhs=rhs, start=start, stop=stop), reads, writes)

    def act(self, out, in_, func, reads, writes, bias=0.0, scale=1.0):
        self.P.op("act", lambda h: h.activation(out=out, in_=in_, func=func, bias=bias, scale=scale), reads, writes)

    def tt(self, out, in0, in1, op, reads, writes, eng="dve"):
        self.P.op(eng, lambda h: h.tensor_tensor(out=out, in0=in0, in1=in1, op=op), reads, writes)

    def stt(self, out, in0, scalar, in1, op0, op1, reads, writes):
        self.P.op("dve", lambda h: h.scalar_tensor_tensor(out=out, in0=in0, scalar=scalar, in1=in1, op0=op0, op1=op1), reads, writes)

    def ts(self, out, in0, s1, s2, op0, op1, reads, writes):
        self.P.op("dve", lambda h: h.tensor_scalar(out=out, in0=in0, scalar1=s1, scalar2=s2, op0=op0, op1=op1), reads, writes)

    def recip(self, out, in_, reads, writes):
        self.P.op("dve", lambda h: h.reciprocal(out=out, in_=in_), reads, writes)

    def copy(self, out, in_, reads, writes, eng="dve"):
        if eng == "act":
            self.P.op("act", lambda h: h.activation(out=out, in_=in_, func=AF.Identity), reads, writes)
        else:
            self.P.op(eng, lambda h: h.tensor_copy(out=out, in_=in_), reads, writes)

    def memset(self, ap, val, writes, eng="pool"):
        self.P.op(eng, lambda h: h.memset(ap, val), (), writes)

    def dma(self, out, in_, reads, writes, eng="sp"):
        self.P.dma(eng, lambda h: h.dma_start(out=out, in_=in_), reads, writes)

    def rsqrt_from(self, out, in_, scale, reads, writes):
        self.act(out, in_, AF.Sqrt, reads, writes, bias=self.epsb[:, 0:1], scale=scale)
        self.recip(out, out, writes, writes)

    def setup(self, io):
        self.io = io
        P = self.P
        self.xT = self.dram("xT_d", [D, NCOL], F32)
        self.uT = self.dram("uT_d", [D, NCOL], BF16)
        self.hT = self.dram("hT_d", [D, NCOL], BF16)
        self.hc = self.dram("hc_d", [D, NCOL], F32)
        self.yT2 = [self.dram("yT_d%d" % i, [D, 768], F32) for i in range(2)]
        self.pv = self.sb("pv", [128, PVL.n], F32)
        self.ones_f = self.sb("ones_f", [128, 128], F32)
        self.ones_b = self.sb("ones_b", [128, 128], BF16)
        self.id_f = self.sb("id_f", [128, 128], F32)
        self.id_b = self.sb("id_b", [128, 128], BF16)
        self.epsb = self.sb("epsb", [128, 1], F32)
        self.zer = self.sb("zer", [128, 16], BF16)
        self.mods = self.sb("mods", [128, DEPTH * 6 * 16 * 2], F32)
        self.modT = self.sb("modT", [128, DEPTH * 96 * 2], F32)
        self.scb = self.sb("scb", [128, 32], BF16)
        self.lbt = self.sb("lbt", [128, 2 * 8 * 2], F32)
        self.lb0 = self.sb("lb0", [128, 2], F32)
        self.mask_f = self.sb("mask_f", [32, 32], F32)
        self.mask_b = self.sb("mask_b", [32, 32], F32)
        self.dma(self.pv[:], io["pv"], (), ["pv"])
        self.memset(self.ones_f[:], 1.0, ["ones_f"])
        self.memset(self.ones_b[:], 1.0, ["ones_b"])
        self.memset(self.epsb[:], EPS, ["epsb"])
        self.memset(self.zer[:], 0.0, ["zer"])
        self.memset(self.id_f[:], 0.0, ["id_f"])
        P.op("pool", lambda h: h.affine_select(out=self.id_f[:], in_=self.id_f[:], pattern=[[-1, 128]], compare_op=ALU.not_equal,
                                               fill=1.0, base=0, channel_multiplier=1), ["id_f"], ["id_f"])
        self.copy(self.id_b[:], self.id_f[:], ["id_f"], ["id_b"])
        for di, (m, pstep, cmul) in enumerate(((self.mask_f, 1, -1), (self.mask_b, -1, 1))):
            key = "mask%d" % di
            self.memset(m[:], 1.0, [key])
            P.op("pool", lambda h, m=m, ps=pstep, cm=cmul: h.affine_select(out=m[:], in_=m[:], pattern=[[ps, 32]], compare_op=ALU.is_ge,
                                                                            fill=0.0, base=0, channel_multiplier=cm), [key], [key])
        for c in range(16):
            self.dma(self.xT[c * 128:(c + 1) * 128, :], io["xT0"][c * 128:(c + 1) * 128, :], (), [("xT", i) for i in range(12)])
        for c in range(16):
            for p0 in (0, 272, 2336):
                self.dma(self.uT[c * 128:(c + 1) * 128, p0:p0 + 16], self.zer[:], ["zer"], ck("uT", p0, 16))
        o = PVL.off["cc"]
        self.act(self.scb[:], self.pv[:, o:o + 32], AF.Silu, ["pv"], ["scb"])
        o = PVL.off["hlb"]
        hl = self.pv[:, o:o + 32].rearrange("p (d j h) -> p d j h", d=2, j=2)
        lbv = self.lbt[:].rearrange("p (d h t) -> p d h t", d=2, h=8)
        self.tt(lbv[:, :, :, 0], hl[:, :, 1, :], hl[:, :, 0, :], ALU.subtract, ["pv"], ["lbt"])
        self.act(lbv[:, :, :, 1], lbv[:, :, :, 0], AF.Sigmoid, ["lbt"], ["lbt"], scale=-1.0)
        self.act(lbv[:, :, :, 0], lbv[:, :, :, 0], AF.Sigmoid, ["lbt"], ["lbt"])
        self.memset(self.lb0[:, 0:1], 0.0, ["lb0"])
        self.memset(self.lb0[:, 1:2], 1.0, ["lb0"])

    def lbs(self, j, d, h):
        if j == 0:
            return self.lb0[:, 0:1], self.lb0[:, 1:2], "lb0"
        o = (d * 8 + h) * 2
        return self.lbt[:, o:o + 1], self.lbt[:, o + 1:o + 2], "lbt"

    def mod(self, l, which, c, kind):
        o = (((l * 6 + which) * 16 + c) * 2) + kind
        return self.mods[:, o:o + 1]

    def mod_stream(self, layers):
        io = self.io
        scv = self.scb[:].rearrange("p (c k) -> p c k", k=2)
        for l in layers:
            o = PVL.off["bmod%d" % l]
            for blk in range(48):
                wt, wkey = self.R["wmod"].next()
                self.dma(wt[:, :, :], io["w_mod"][l, :, blk * 256:(blk + 1) * 256].rearrange("(k p) m -> p k m", p=128), (), [wkey], eng="pool")
                bk, bkey = self.bank()
                for jj in range(2):
                    for k in range(16):
                        self.mm(bk[:, jj * 2:jj * 2 + 2], wt[:, k, jj * 128:(jj + 1) * 128], scv[:, k, :], k == 0, k == 15, [wkey, "scb"], [bkey])
                oc = blk * 2
                mt = self.modT[:, l * 192 + oc * 2:l * 192 + oc * 2 + 4].rearrange("p (o k) -> p o k", k=2)
                self.tt(mt, bk[:, 0:4].rearrange("p (o k) -> p o k", k=2), self.pv[:, o + oc:o + oc + 2].unsqueeze(2).to_broadcast([128, 2, 2]),
                        ALU.add, [bkey, "pv"], [("modT", l)])
                yield
            go = PVL.off["gains%d" % l]
            md = self.mods[:, l * 192:(l + 1) * 192].rearrange("p (w c k) -> p w c k", w=6, c=16)
            mt6 = self.modT[:, l * 192:(l + 1) * 192].rearrange("p (w c k) -> p w c k", w=6, c=16)
            mk, tk_ = ("mods", l), ("modT", l)
            for (dst, gi, mi, plus1) in ((0, 0, 1, True), (2, 1, 2, False), (3, 2, 4, True), (5, 3, 5, False)):
                gb = self.pv[:, go + gi * 16:go + (gi + 1) * 16].unsqueeze(2).to_broadcast([128, 16, 2])
                if plus1:
                    self.stt(md[:, dst], mt6[:, mi], 1.0, gb, ALU.add, ALU.mult, [tk_, "pv"], [mk])
                else:
                    self.tt(md[:, dst], mt6[:, mi], gb, ALU.mult, [tk_, "pv"], [mk])
            self.copy(md[:, 1], mt6[:, 0], [tk_], [mk])
            self.copy(md[:, 4], mt6[:, 3], [tk_], [mk])
            self.mod_ready.add(l)
            yield

    def pump_mod(self, n):
        for _ in range(n):
            if self.modgen is None:
                return
            try:
                next(self.modgen)
            except StopIteration:
                self.modgen = None

    def need_mod(self, l):
        while l not in self.mod_ready:
            assert self.modgen is not None
            self.pump_mod(1)

    def prenorm(self, l, sub):
        self.phase("prenorm")
        R = self.R
        wS, wB = (0, 1) if sub == 1 else (3, 4)
        ctx_on = self.ctx_in if sub == 1 else self.ctx_out
        for pi, (c0, n, kind) in enumerate(PIECES if ctx_on else PIECES[1:]):
            xin = R["xin2"][pi % 2]
            xs = pi % 2
            for c in range(16):
                self.dma(xin[:, c, :n], self.xT[c * 128:(c + 1) * 128, c0:c0 + n], ck("xT", c0, n), [("xin", xs, c)])
            bk, bkey = self.bank()
            for c in range(16):
                sq, sqk = R["sq"].next()
                self.act(sq[:, :n], xin[:, c, :n], AF.Square, [("xin", xs, c)], [sqk])
                self.mm(bk[:, :n], self.ones_f[:], sq[:, :n], c == 0, c == 15, ["ones_f", sqk], [bkey])
            rs, rsk = R["rstd"].next()
            self.rsqrt_from(rs[:, :n], bk[:, :n], 1.0 / D, [bkey, "epsb"], [rsk])
            for c in range(16):
                tmp, tk = R["tmpf"].next()
                self.tt(tmp[:, :n], xin[:, c, :n], rs[:, :n], ALU.mult, [("xin", xs, c), rsk], [tk])
                ub, uk = R["ub"].next()
                self.act(ub[:, :n], tmp[:, :n], AF.Identity, [tk, ("mods", l)], [uk], bias=self.mod(l, wB, c, kind), scale=self.mod(l, wS, c, kind))
                self.dma(self.uT[c * 128:(c + 1) * 128, c0:c0 + n], ub[:, :n], [uk], ck("uT", c0, n))

    def final_mm(self, l, sub, s, slot, aT, akeys, Kc, w_ap, bias_off, wblk, pump=None):
        R = self.R
        skip_ctx = (s == 0 and not self.ctx_out)
        parts = [(256, 512, 6)] if skip_ctx else [(0, 512, 6), (512, 256, 7)]
        yT = self.yT2[slot]
        nj = wblk // 128
        for blk in range(D // wblk):
            wt, wkey = R["wfin"].next()
            self.dma(wt[:, :Kc, :], w_ap[:, blk * wblk:(blk + 1) * wblk].rearrange("(k p) m -> p k m", p=128), (), [wkey], eng="pool")
            for jj in range(nj):
                c = blk * nj + jj
                for (n0, nn, sb_) in parts:
                    bk, bkey = self.bank()
                    for k in range(Kc):
                        self.mm(bk[:, :nn], wt[:, k, jj * 128:(jj + 1) * 128], aT[:, k, n0:n0 + nn], k == 0, k == Kc - 1,
                                [wkey] + akeys, [bkey])
                    ysb, yk = R["tmpf"].next()
                    bias = 0.0 if bias_off is None else self.pv[:, bias_off + c:bias_off + c + 1]
                    self.act(ysb[:, :nn], bk[:, :nn], AF.Identity, [bkey, "pv"], [yk], bias=bias)
                    sq, sqk = R["sq"].next()
                    self.act(sq[:, :nn], ysb[:, :nn], AF.Square, [yk], [sqk])
                    self.mm(self.banks[sb_][:, :nn], self.ones_f[:], sq[:, :nn], c == 0, c == 15, ["ones_f", sqk], [("pb", sb_)])
                    self.dma(yT[c * 128:(c + 1) * 128, n0:n0 + nn], ysb[:, :nn], [yk], [("yT", slot, c)])
                    if pump is not None:
                        pump()
        rs = R["rstd768"][slot]
        if skip_ctx:
            self.rsqrt_from(rs[:, 256:768], self.banks[6][:, :512], 1.0 / D, [("pb", 6), "epsb"], [("rs768a", slot)])
        else:
            self.rsqrt_from(rs[:, 0:512], self.banks[6][:, :512], 1.0 / D, [("pb", 6), "epsb"], [("rs768a", slot)])
            self.rsqrt_from(rs[:, 512:768], self.banks[7][:, :256], 1.0 / D, [("pb", 7), "epsb"], [("rs768b", slot)])

    def final_tail(self, l, sub, s, slot):
        R = self.R
        wG = 2 if sub == 1 else 5
        yT = self.yT2[slot]
        rs = R["rstd768"][slot]
        runs = [r for r in ST_RUNS[s] if r[3] == 0 or self.ctx_out]
        rkeys = [("rs768a", slot)] if (s == 0 and not self.ctx_out) else [("rs768a", slot), ("rs768b", slot)]
        for c in range(16):
            for (lo, c0, n, kind) in runs:
                yl, ylk = R["tmpf"].next()
                xl, xlk = R["sq"].next()
                self.dma(yl[:, :n], yT[c * 128:(c + 1) * 128, lo:lo + n], [("yT", slot, c)], [ylk])
                self.dma(xl[:, :n], self.xT[c * 128:(c + 1) * 128, c0:c0 + n], ck("xT", c0, n), [xlk])
                self.tt(yl[:, :n], yl[:, :n], rs[:, lo:lo + n], ALU.mult, [ylk] + rkeys, [ylk])
                self.stt(xl[:, :n], yl[:, :n], self.mod(l, wG, c, kind), xl[:, :n], ALU.mult, ALU.add, [ylk, xlk, ("mods", l)], [xlk])
                self.dma(self.xT[c * 128:(c + 1) * 128, c0:c0 + n], xl[:, :n], [xlk], ck("xT", c0, n))
                yield

    def final_from_dram(self, l, sub, w_ap, bias_off):
        self.phase("final16")
        R = self.R
        tail = None

        def pump():
            if tail is not None:
                next(tail, None)
                next(tail, None)

        for s in range(3):
            aT = R["aT16"]
            for (lo, c0, n, kind) in ST_RUNS[s]:
                if kind == 1 and not self.ctx_out:
                    continue
                self.dma(aT[:, :, lo:lo + n], self.hT[:, c0:c0 + n].rearrange("(k p) n -> p k n", p=128), ck("hT", c0, n), [("aT16", lo)])
            self.final_mm(l, sub, s, s % 2, aT, [("aT16", 0), ("aT16", 256), ("aT16", 512)], 16, w_ap, bias_off, 256, pump)
            if tail is not None:
                for _ in tail:
                    pass
            tail = self.final_tail(l, sub, s, s % 2)
        for _ in tail:
            pass

    def ffn(self, l):
        io = self.io
        R = self.R
        self.prenorm(l, 2)
        wo = PVL.off["fwc%d" % l]
        bo = PVL.off["fbc%d" % l]
        self.phase("ffn")
        aT = R["aT44"]
        akeys = [("aT44", 0), ("aT44", 1), ("aT44", 2)]
        tail = None
        unit = 0
        for s in range(3):
            uws = []
            wis = [wi for wi in range(3) if WINS[3 * s + wi][1] == 0 or self.ctx_out]
            for wi in range(3):
                c0, kind = WINS[3 * s + wi]
                uw = R["uw"][wi]
                if wi in wis:
                    self.dma(uw[:, :, :], self.uT[:, c0 - 1:c0 + 257].rearrange("(k p) n -> p k n", p=128), ck("uT", c0 - 1, 258), [("uw", wi)])
                uws.append(uw)
            for jj in range(22):
                wu, wkey = R["wup"].next()
                self.dma(wu[:, :, 0:256], io["ffn_w_up"][l, :, jj * 256:(jj + 1) * 256].rearrange("(k p) m -> p k m", p=128), (), [wkey + "a"], eng="pool")
                self.dma(wu[:, :, 256:512], io["ffn_w_up"][l, :, DFF + jj * 256:DFF + (jj + 1) * 256].rearrange("(k p) m -> p k m", p=128), (), [wkey + "b"], eng="pool")
                for pj in range(2):
                    j = jj * 2 + pj
                    for wi in wis:
                        res = []
                        for half, chn in ((0, j), (1, 44 + j)):
                            bk, bkey = self.bank(0, 8)
                            for k in range(16):
                                self.mm(bk[:, :258], wu[:, k, half * 256 + pj * 128:half * 256 + (pj + 1) * 128], uws[wi][:, k, :], k == 0, k == 15,
                                        [wkey + "a", wkey + "b", ("uw", wi)], [bkey])
                            hh, hk = R["hvg"].next()
                            w0 = self.pv[:, wo + chn:wo + chn + 1]
                            w1 = self.pv[:, wo + 88 + chn:wo + 88 + chn + 1]
                            w2 = self.pv[:, wo + 176 + chn:wo + 176 + chn + 1]
                            bb = self.pv[:, bo + chn:bo + chn + 1]
                            self.act(hh[:, :], bk[:, 1:257], AF.Identity, [bkey, "pv"], [hk], bias=bb, scale=w1)
                            self.stt(hh[:, :], bk[:, 0:256], w0, hh[:, :], ALU.mult, ALU.add, [bkey, hk, "pv"], [hk])
                            self.stt(hh[:, :], bk[:, 2:258], w2, hh[:, :], ALU.mult, ALU.add, [bkey, hk, "pv"], [hk])
                            res.append((hh, hk))
                        (hv, hvk), (hg, hgk) = res
                        self.act(hg[:, :], hg[:, :], AF.Silu, [hgk], [hgk])
                        self.tt(aT[:, j, wi * 256:(wi + 1) * 256], hg[:, :], hv[:, :], ALU.mult, [hgk, hvk], [("aT44", wi)])
                        unit += 1
                        if tail is not None and unit % 2 == 0:
                            next(tail, None)
            if tail is not None:
                for _ in tail:
                    pass
            self.final_mm(l, 2, s, s % 2, aT, akeys, 44, io["ffn_w_down"][l], None, 128)
            tail = self.final_tail(l, 2, s, s % 2)
        for _ in tail:
            pass

    def conformer(self, l):
        io = self.io
        R = self.R
        j = l // 2
        self.prenorm(l, 1)
        self.phase("odd1")
        pieces = PIECES if self.ctx_in else PIECES[1:]
        b1 = PVL.off["bpw1%d" % j]
        wdw = PVL.off["wdw%d" % j]
        bdw = PVL.off["bdw%d" % j]
        for c in range(16):
            wt, wkey = R["wup"].next()
            self.dma(wt[:, :, 0:128], io["conv_w_pw1"][j, :, c * 128:(c + 1) * 128].rearrange("(k p) m -> p k m", p=128), (), [wkey + "a"], eng="pool")
            self.dma(wt[:, :, 128:256], io["conv_w_pw1"][j, :, D + c * 128:D + (c + 1) * 128].rearrange("(k p) m -> p k m", p=128), (), [wkey + "b"], eng="pool")
            hf, hfk = R["hfull"].next()
            for (c0, n, kind) in pieces:
                up, upk = R["upiece"].next()
                self.dma(up[:, :, :n], self.uT[:, c0:c0 + n].rearrange("(k p) n -> p k n", p=128), ck("uT", c0, n), [upk])
                ba, bak = self.bank()
                bg, bgk = self.bank()
                for k in range(16):
                    self.mm(ba[:, :n], wt[:, k, 0:128], up[:, k, :n], k == 0, k == 15, [wkey + "a", wkey + "b", upk], [bak])
                for k in range(16):
                    self.mm(bg[:, :n], wt[:, k, 128:256], up[:, k, :n], k == 0, k == 15, [wkey + "a", wkey + "b", upk], [bgk])
                sg, sgk = R["tmpf"].next()
                self.act(sg[:, :n], bg[:, :n], AF.Sigmoid, [bgk, "pv"], [sgk], bias=self.pv[:, b1 + 16 + c:b1 + 16 + c + 1])
                self.stt(hf[:, c0:c0 + n], ba[:, :n], self.pv[:, b1 + c:b1 + c + 1], sg[:, :n], ALU.add, ALU.mult, [bak, sgk, "pv"], [hfk])
            dg, dgk = R["dg"].next()
            wv = self.pv[:, wdw:wdw + 496].rearrange("p (t c) -> p t c", c=16)[:, :, c:c + 1]
            self.tt(dg[:, :, :], self.id_b[:].unsqueeze(1).to_broadcast([128, 31, 128]), wv.to_broadcast([128, 31, 128]), ALU.mult,
                    ["id_b", "pv"], [dgk])
            for (c0, n, kind) in pieces:
                bk, bkey = self.bank()
                for t in range(31):
                    self.mm(bk[:, :n], dg[:, t, :], hf[:, c0 - 15 + t:c0 - 15 + t + n], t == 0, t == 30, [dgk, hfk], [bkey])
                oc, ock = R["tmpf"].next()
                self.act(oc[:, :n], bk[:, :n], AF.Identity, [bkey, "pv"], [ock], bias=self.pv[:, bdw + c:bdw + c + 1])
                self.dma(self.hc[c * 128:(c + 1) * 128, c0:c0 + n], oc[:, :n], [ock], ck("hc", c0, n))
        self.phase("ln")
        lg = PVL.off["lng%d" % j]
        lbo = PVL.off["lnb%d" % j]
        for pi, (c0, n, kind) in enumerate(pieces):
            xin = R["xin2"][pi % 2]
            xs = pi % 2
            for c in range(16):
                self.dma(xin[:, c, :n], self.hc[c * 128:(c + 1) * 128, c0:c0 + n], ck("hc", c0, n), [("xin", xs, c)])
            bm, bmk = self.bank()
            bq, bqk = self.bank()
            for c in range(16):
                sq, sqk = R["sq"].next()
                self.act(sq[:, :n], xin[:, c, :n], AF.Square, [("xin", xs, c)], [sqk])
                self.mm(bm[:, :n], self.ones_f[:], xin[:, c, :n], c == 0, c == 15, ["ones_f", ("xin", xs, c)], [bmk])
                self.mm(bq[:, :n], self.ones_f[:], sq[:, :n], c == 0, c == 15, ["ones_f", sqk], [bqk])
            mean, mk = R["rstd"].next()
            rs, rsk = R["rstd"].next()
            self.act(mean[:, :n], bm[:, :n], AF.Identity, [bmk], [mk], scale=1.0 / D)
            m2, m2k = R["tmpf"].next()
            self.tt(m2[:, :n], mean[:, :n], mean[:, :n], ALU.mult, [mk], [m2k])
            self.stt(m2[:, :n], bq[:, :n], 1.0 / D, m2[:, :n], ALU.mult, ALU.subtract, [bqk, m2k], [m2k])
            self.rsqrt_from(rs[:, :n], m2[:, :n], 1.0, [m2k, "epsb"], [rsk])
            for c in range(16):
                tmp, tk = R["tmpf"].next()
                self.tt(tmp[:, :n], xin[:, c, :n], mean[:, :n], ALU.subtract, [("xin", xs, c), mk], [tk])
                self.tt(tmp[:, :n], tmp[:, :n], rs[:, :n], ALU.mult, [tk, rsk], [tk])
                ub, uk = R["ub"].next()
                self.act(ub[:, :n], tmp[:, :n], AF.Silu, [tk, "pv"], [uk], bias=self.pv[:, lbo + c:lbo + c + 1], scale=self.pv[:, lg + c:lg + c + 1])
                self.dma(self.hT[c * 128:(c + 1) * 128, c0:c0 + n], ub[:, :n], [uk], ck("hT", c0, n))
        self.final_from_dram(l, 1, io["conv_w_pw2"][j], PVL.off["bpw2%d" % j])

    def even(self, l):
        io = self.io
        R, E = self.R, self.E
        j = l // 2
        self.prenorm(l, 1)
        if self.stop_at == "pre":
            return
        self.phase("mla_prep")
        wp = E["wprep"]
        self.dma(wp[:, :, 0:832], io["w_in_ab"][j, :, 0:832].rearrange("(k p) m -> p k m", p=128), (), ["wprep_a"], eng="pool")
        self.dma(wp[:, :, 832:896], io["w_kr_sw"][j].rearrange("(k p) m -> p k m", p=128), (), ["wprep_b"], eng="pool")
        self.dma(E["rope"][:, :, :], io["rope"], (), ["rope"])
        wpk = ["wprep_a", "wprep_b"]
        qn_o = PVL.off["qn%d" % j]
        kvn_o = PVL.off["kvn%d" % j]
        cqn, ckvn, krot = E["cqn"], E["ckvn"], E["krot"]
        for (c0, n, kind) in PIECES:
            up, upk = R["upiece"].next()
            self.dma(up[:, :, :n], self.uT[:, c0:c0 + n].rearrange("(k p) n -> p k n", p=128), ck("uT", c0, n), [upk])
            for (nm, nch, col_off, dst, gain_o) in (("cq", 4, 0, cqn, qn_o), ("ckv", 2, 512, ckvn, kvn_o)):
                sbk, sbkey = self.bank()
                cf = E["cf"]
                for c in range(nch):
                    bk, bkey = self.bank()
                    for k in range(16):
                        self.mm(bk[:, :n], wp[:, k, col_off + c * 128:col_off + (c + 1) * 128], up[:, k, :n], k == 0, k == 15, wpk + [upk], [bkey])
                    self.act(cf[:, c, :n], bk[:, :n], AF.Identity, [bkey], [("xin", c)])
                    sq, sqk = R["sq"].next()
                    self.act(sq[:, :n], bk[:, :n], AF.Square, [bkey], [sqk])
                    self.mm(sbk[:, :n], self.ones_f[:], sq[:, :n], c == 0, c == nch - 1, ["ones_f", sqk], [sbkey])
                rs, rsk = R["rstd"].next()
                self.rsqrt_from(rs[:, :n], sbk[:, :n], 1.0 / (nch * 128), [sbkey, "epsb"], [rsk])
                for c in range(nch):
                    tmp, tk = R["tmpf"].next()
                    self.tt(tmp[:, :n], cf[:, c, :n], rs[:, :n], ALU.mult, [("xin", c), rsk], [tk])
                    self.act(dst[:, c, c0:c0 + n], tmp[:, :n], AF.Identity, [tk, "pv"], ck(nm + "n", c0, n), scale=self.pv[:, gain_o + c:gain_o + c + 1])
            bk, bkey = self.bank()
            bs, bskey = self.bank()
            for k in range(16):
                self.mm(bk[0:64, :n], wp[:, k, 768:832], up[:, k, :n], k == 0, k == 15, wpk + [upk], [bkey])
            for k in range(16):
                self.mm(bs[0:64, :n], wp[:, k, 832:896], up[:, k, :n], k == 0, k == 15, wpk + [upk], [bskey])
            self.rope_combine(krot[:, c0:c0 + n], bk, bkey, bs, bskey, c0, n, ck("krot", c0, n))
        if self.stop_at == "prep":
            return
        self.phase("mla_heads")
        cqn, ckvn, krot = E["cqn"], E["ckvn"], E["krot"]
        self.dma(E["wq"][:, :, :], io["w_q_full"][j].rearrange("(k p) m -> p k m", p=128), (), ["wq"], eng="pool")
        self.dma(E["wkv"][:, :, :], io["w_kv_up"][j].rearrange("(k p) m -> p k m", p=128), (), ["wkv"], eng="pool")
        wq, wkv = E["wq"], E["wkv"]
        for h in range(8):
            qn, qrot, kn, V = E["qn"], E["qrot"], E["kn"], E["V"]
            for pi, (c0, n, kind) in enumerate(PIECES):
                bk, bkey = self.bank()
                for k in range(4):
                    self.mm(bk[:, :n], wq[:, k, h * 256:h * 256 + 128], cqn[:, k, c0:c0 + n], k == 0, k == 3, ["wq"] + ck("cqn", c0, n), [bkey])
                self.copy(qn[:, c0:c0 + n], bk[:, :n], [bkey], ck("qn", c0, n), eng="act")
                b1, b1k = self.bank()
                b2, b2k = self.bank()
                for k in range(4):
                    self.mm(b1[0:64, :n], wq[:, k, h * 256 + 128:h * 256 + 192], cqn[:, k, c0:c0 + n], k == 0, k == 3, ["wq"] + ck("cqn", c0, n), [b1k])
                for k in range(4):
                    self.mm(b2[0:64, :n], wq[:, k, h * 256 + 192:h * 256 + 256], cqn[:, k, c0:c0 + n], k == 0, k == 3, ["wq"] + ck("cqn", c0, n), [b2k])
                self.rope_combine(qrot[:, c0:c0 + n], b1, b1k, b2, b2k, c0, n, ck("qrot", c0, n))
                bk, bkey = self.bank()
                for k in range(2):
                    self.mm(bk[:, :n], wkv[:, k, h * 256:h * 256 + 128], ckvn[:, k, c0:c0 + n], k == 0, k == 1, ["wkv"] + ck("ckvn", c0, n), [bkey])
                self.copy(kn[:, c0:c0 + n], bk[:, :n], [bkey], ck("kn", c0, n), eng="act")
                bk, bkey = self.bank()
                for k in range(2):
                    self.mm(bk[:, :n], wkv[:, k, h * 256 + 128:h * 256 + 256], ckvn[:, k, c0:c0 + n], k == 0, k == 1, ["wkv"] + ck("ckvn", c0, n), [bkey])
                vt, vtk = R["ub"].next()
                self.copy(vt[:, :n], bk[:, :n], [bkey], [vtk], eng="act")
                self.to_tokmajor(vt, vtk, c0, n, V, "V")
            for (c0, n, kind) in (PIECES if self.ctx_out else PIECES[1:]):
                kbs = [0, 1] if kind == 1 else list(range(18))
                ob, obk = self.banks[6], ("pb", 6)
                db, dbk = self.banks[7], ("pb", 7)
                pend = None
                nk = len(kbs)
                for i in range(nk + 1):
                    if i < nk:
                        kb = kbs[i]
                        kc = blkcol(kb)
                        sbk, sbkey = self.bank()
                        self.mm(sbk[:, :n], kn[:, kc:kc + 128], qn[:, c0:c0 + n], True, False, ck("kn", kc, 128) + ck("qn", c0, n), [sbkey])
                        self.mm(sbk[:, :n], krot[:, kc:kc + 128], qrot[:, c0:c0 + n], False, True, ck("krot", kc, 128) + ck("qrot", c0, n), [sbkey])
                        pt, ptk = R["ub"].next()
                        self.act(pt[:, :n], sbk[:, :n], AF.Exp, [sbkey], [ptk], scale=ATTN_SCALE)
                        cur = (i, kb, pt, ptk)
                    if pend is not None:
                        pi, pkb, ppt, pptk = pend
                        self.mm(ob[:, :n], V[:, pkb, :], ppt[:, :n], pi == 0, pi == nk - 1, [("V", pkb), pptk], [obk])
                        self.mm(db[:, :n], self.ones_b[:], ppt[:, :n], pi == 0, pi == nk - 1, ["ones_b", pptk], [dbk])
                    pend = cur if i < nk else None
                rd, rdk = R["tmpf"].next()
                self.recip(rd[:, :n], db[:, :n], [dbk], [rdk])
                at, atk = R["ub"].next()
                self.tt(at[:, :n], ob[:, :n], rd[:, :n], ALU.mult, [obk, rdk], [atk])
                self.dma(self.hT[h * 128:(h + 1) * 128, c0:c0 + n], at[:, :n], [atk], ck("hT", c0, n))
        if self.stop_at == "mla":
            return
        self.phase("hgrn")
        for h in range(8):
            self.hgrn_head(l, j, h)
        if self.stop_at == "hgrn":
            return
        self.final_from_dram(l, 1, io["w_out_ab"][j], None)

    def rope_combine(self, dst, b1, b1k, b2, b2k, c0, n, wkeys):
        R, E = self.R, self.E
        rope = E["rope"]
        t1, t1k = R["tmpf"].next()
        t2, t2k = R["tmpf"].next()
        self.tt(t1[0:64, :n], b1[0:64, :n], rope[:, 0, c0:c0 + n], ALU.mult, [b1k, "rope"], [t1k])
        self.tt(t2[0:64, :n], b2[0:64, :n], rope[:, 1, c0:c0 + n], ALU.mult, [b2k, "rope"], [t2k])
        self.tt(dst, t1[0:64, :n], t2[0:64, :n], ALU.add, [t1k, t2k], wkeys)

    def to_tokmajor(self, src, srck, c0, n, dst, dname, blk=128, banks=None):
        per = 128 // blk
        for q in range(n // blk):
            col = c0 + q * blk
            tb = (col - CTX0) // blk if col < LAT0 else 2 * per + (col - LAT0) // blk
            if banks is None:
                bk, bkey = self.bank()
            else:
                bi = banks[q % len(banks)]
                bk, bkey = self.banks[bi], ("pb", bi)
            bkb = bk.bitcast(BF16)
            self.P.op("pe", lambda h, bkb=bkb, q=q: h.transpose(bkb[0:blk, 0:128], src[:, q * blk:(q + 1) * blk], self.id_b[:]), [srck, "id_b"], [bkey])
            self.copy(dst[:, tb, :], bkb[0:blk, 0:128], [bkey], [(dname, tb)])

    def hgrn_head(self, l, j, h):
        io = self.io
        R, E = self.R, self.E
        wh = E["wh"]
        for gi, base in enumerate((832, 1856, 2880, 3904, 4928)):
            self.dma(wh[:, :, gi * 128:(gi + 1) * 128], io["w_in_ab"][j, :, base + h * 128:base + (h + 1) * 128].rearrange("(k p) m -> p k m", p=128),
                     (), [("wh", gi)], eng="pool")
        whk = [("wh", g) for g in range(5)]
        SG, OT, Vh = E["SG"], E["OT"], E["Vh"]
        QT, KT, KH = [E["QT0"], E["QT1"]], [E["KT0"], E["KT1"]], [E["KH0"], E["KH1"]]
        MS, FDc = [E["MS0"], E["MS1"]], [E["FD0"], E["FD1"]]
        lo, hi = CTX0, LAT0 + NLAT
        CH, MID, LAST = 32, 15, 31
        maskP = E["maskP"]
        def piece_gen(pi, c0, kind):
            n = 256
            ci0 = (c0 - CTX0) // 32 if kind == 1 else 8 + (c0 - LAT0) // 32
            up, upk = R["upiece"].next()
            self.dma(up[:, :, :n], self.uT[:, c0:c0 + n].rearrange("(k p) n -> p k n", p=128), ck("uT", c0, n), [upk])
            bks = []
            for gi in range(5):
                bk, bkey = self.bank(0, 6)
                for k in range(16):
                    self.mm(bk[:, :n], wh[:, k, gi * 128:(gi + 1) * 128], up[:, k, :n], k == 0, k == 15, whk + [upk], [bkey])
                bks.append((bk, bkey))
            yield
            gt = E["gt"][pi % 3]
            gk = lambda i: ("gt", pi % 3, i)
            qs, A, K, G = gt[0], [gt[1], gt[2]], [gt[3], gt[4]], [gt[5], gt[6]]
            self.copy(qs[:, :], bks[0][0][:, :n], [bks[0][1]], [gk(0)], eng="act")
            for d in range(2):
                bk, bkey = bks[1 + d]
                self.act(A[d][:, :], bk[:, :n], AF.Sigmoid, [bkey], [gk(1 + d)])
                self.act(K[d][:, :], bk[:, :n], AF.Sigmoid, [bkey], [gk(3 + d)], scale=-1.0)
            self.act(SG[:, c0:c0 + n], bks[4][0][:, :n], AF.Silu, [bks[4][1]], ["SG"])
            vt, vtk = R["ub"].next()
            self.copy(vt[:, :n], bks[3][0][:, :n], [bks[3][1]], [vtk], eng="act")
            yield
            for d in range(2):
                lb, oml, lbk = self.lbs(j, d, h)
                self.ts(A[d][:, :], A[d][:, :], oml, lb, ALU.mult, ALU.add, [gk(1 + d), lbk], [gk(1 + d)])
                self.P.op("dve", lambda hh, o=K[d][:, :], sc=oml: hh.tensor_scalar_mul(out=o, in0=o, scalar1=sc), [gk(3 + d), lbk], [gk(3 + d)])
            yield
            self.to_tokmajor(vt, vtk, c0, n, Vh, "Vh", blk=32, banks=(6, 7))
            for d in range(2):
                self.act(A[d][:, :], A[d][:, :], AF.Ln, [gk(1 + d)], [gk(1 + d)])
            yield
            for d in range(2):
                self.P.op("dve", lambda hh, d=d, A=A, G=G: hh.tensor_tensor_scan(out=G[d][:, :], data0=maskP[:, :], data1=A[d][:, :], initial=0.0,
                                                                                 op0=ALU.mult, op1=ALU.add), [gk(1 + d), "maskP"], [gk(5 + d)])
            yield
            for d in range(2):
                Gv = G[d][:, :].rearrange("p (c t) -> p c t", t=CH)
                Lv = A[d][:, :].rearrange("p (c t) -> p c t", t=CH)
                gG, gL = gk(5 + d), gk(1 + d)
                self.act(FDc[d][:, ci0:ci0 + 8], Gv[:, :, LAST], AF.Exp, [gG], ["FD%d" % d])
                if d == 0:
                    self.act(MS[d][:, ci0:ci0 + 8], Gv[:, :, MID], AF.Exp, [gG], ["MS%d" % d])
                    self.tt(Lv, Gv, Gv[:, :, MID:MID + 1].to_broadcast([128, 8, CH]), ALU.subtract, [gG], [gL])
                    self.tt(Gv, Gv[:, :, LAST:LAST + 1].to_broadcast([128, 8, CH]), Gv, ALU.subtract, [gG], [gG])
                    e1, e1k, e3, e3k = A[d], gL, G[d], gG
                else:
                    self.tt(Lv, Gv, Lv, ALU.subtract, [gG, gL], [gL])
                    tmpc = E["tmpc"]
                    self.tt(tmpc[:, 0:8], Gv[:, :, LAST], Lv[:, :, MID + 1], ALU.subtract, [gG, gL], ["tmpc"])
                    self.act(MS[d][:, ci0:ci0 + 8], tmpc[:, 0:8], AF.Exp, ["tmpc"], ["MS%d" % d])
                    self.tt(Gv, Lv[:, :, MID + 1:MID + 2].to_broadcast([128, 8, CH]), Lv, ALU.subtract, [gL], [gG])
                    e1, e1k, e3, e3k = G[d], gG, A[d], gL
                yield
                self.act(e3[:, :], e3[:, :], AF.Exp, [e3k], [e3k])
                yield
                self.tt(KH[d][:, c0:c0 + n], K[d][:, :], e3[:, :], ALU.mult, [gk(3 + d), e3k], ["KH%d" % d])
                self.act(e3[:, :], e1[:, :], AF.Exp, [e1k], [e3k])
                yield
                self.tt(QT[d][:, c0:c0 + n], qs[:, :], e3[:, :], ALU.mult, [gk(0), e3k], ["QT%d" % d])
                self.act(e3[:, :], e1[:, :], AF.Exp, [e1k], [e3k], scale=-1.0)
                yield
                self.tt(KT[d][:, c0:c0 + n], K[d][:, :], e3[:, :], ALU.mult, [gk(3 + d), e3k], ["KT%d" % d])

        pend = [piece_gen(pi, c0, kind) for pi, (c0, kind) in enumerate(WINS)]
        active = []
        mod_left = [13 if l == 0 else 7]
        while pend or active:
            if pend and len(active) < 3:
                active.append(pend.pop(0))
            for g in list(active):
                try:
                    next(g)
                except StopIteration:
                    active.remove(g)
            if self.modgen is not None and mod_left[0] > 0:
                mod_left[0] -= 1
                self.pump_mod(1)
        order = [list(range(18)), [1, 0] + list(range(17, 1, -1))]
        Sr = [E["Sr0"], E["Sr1"]]
        SMr = [E["SMr0"], E["SMr1"]]
        masks = [self.mask_f, self.mask_b]
        for d in range(2):
            self.memset(Sr[d][:, 0, :], 0.0, [("S", d, 0)], eng="dve")
        visited = set()

        def emit_PU(step, d):
            tb = order[d][step]
            col = blkcol(tb)
            pb0 = 4 * d
            atb = self.banks[pb0]
            trbb = self.banks[pb0 + 1].bitcast(BF16)
            ubk, ubkk = self.banks[pb0 + 3], ("pb", pb0 + 3)
            atm, kht = E["atm"][d], E["kht"][d]
            chs = (0, 1, 2, 3) if d == 0 else (3, 2, 1, 0)
            atk, trk = ("pb", pb0), ("pb", pb0 + 1)
            for ch in chs:
                cc = col + ch * 32
                self.mm(atb[0:32, ch * 32:(ch + 1) * 32], KT[d][:, cc:cc + 32], QT[d][:, cc:cc + 32], True, True, ["KT%d" % d, "QT%d" % d], [atk])
            for ch in chs:
                cc = col + ch * 32
                self.P.op("pe", lambda hh, trbb=trbb, d=d, cc=cc, ch=ch: hh.transpose(trbb[0:32, ch * 128:(ch + 1) * 128], KH[d][:, cc:cc + 32], self.id_b[:]),
                          ["KH%d" % d, "id_b"], [trk])
            self.tt(atm[:, step % 2, :, :], atb[0:32, 0:128].rearrange("p (c t) -> p c t", t=32), masks[d][:, :].unsqueeze(1).to_broadcast([32, 4, 32]),
                    ALU.mult, [atk, "mask%d" % d], [("atm", d, step % 2)])
            self.copy(kht[:, :, :], trbb[0:32, 0:512].rearrange("p (c t) -> p c t", t=128), [trk], [("kht", d)], eng="act")
            for qi, ch in enumerate(chs):
                ci = tb * 4 + ch
                self.mm(ubk[:, qi * 128:(qi + 1) * 128], kht[:, ch, :], Vh[:, ci, :], True, True, [("kht", d), ("Vh", ci)], [ubkk])

        def emit_S(step, d):
            tb = order[d][step]
            ubk, ubkk = self.banks[4 * d + 3], ("pb", 4 * d + 3)
            chs = (0, 1, 2, 3) if d == 0 else (3, 2, 1, 0)
            for qi, ch in enumerate(chs):
                ci = tb * 4 + ch
                g = step * 4 + qi
                self.act(SMr[d][:, g % 8, :], Sr[d][:, g % 8, :], AF.Identity, [("S", d, g % 8), "MS%d" % d], [("SM", d, g % 8)], scale=MS[d][:, ci:ci + 1])
                self.stt(Sr[d][:, (g + 1) % 8, :], Sr[d][:, g % 8, :], FDc[d][:, ci:ci + 1], ubk[:, qi * 128:(qi + 1) * 128], ALU.mult, ALU.add,
                         [("S", d, g % 8), "FD%d" % d, ubkk], [("S", d, (g + 1) % 8)])

        def emit_O(step, d):
            tb = order[d][step]
            col = blkcol(tb)
            ob, obk = self.banks[4 * d + 2], ("pb", 4 * d + 2)
            atm = E["atm"][d]
            chs = (0, 1, 2, 3) if d == 0 else (3, 2, 1, 0)
            for qi, ch in enumerate(chs):
                ci = tb * 4 + ch
                cc = col + ch * 32
                g = step * 4 + qi
                self.mm(ob[:, ch * 32:(ch + 1) * 32], Vh[:, ci, :], atm[:, step % 2, ch, :], True, False, [("Vh", ci), ("atm", d, step % 2)], [obk])
                self.mm(ob[:, ch * 32:(ch + 1) * 32], SMr[d][:, g % 8, :], QT[d][:, cc:cc + 32], False, True, [("SM", d, g % 8), "QT%d" % d], [obk])
            if tb not in visited:
                visited.add(tb)
                self.copy(OT[:, col:col + 128], ob[:, 0:128], [obk], [("OT", tb)], eng="act")
            else:
                self.tt(OT[:, col:col + 128], ob[:, 0:128], OT[:, col:col + 128], ALU.add, [obk, ("OT", tb)], [("OT", tb)])

        for step in range(18):
            for d in range(2):
                emit_PU(step, d)
            for d in range(2):
                emit_S(step, d)
            if step >= 1:
                for d in range(2):
                    emit_O(step - 1, d)
        for d in range(2):
            emit_O(17, d)
        hg_o = PVL.off["hgn%d" % j]
        otk = [("OT", tb) for tb in range(18)]
        for (c0, n, kind) in (PIECES if self.ctx_out else PIECES[1:]):
            sq, sqk = R["sq"].next()
            self.act(sq[:, :n], OT[:, c0:c0 + n], AF.Square, otk, [sqk])
            bk, bkey = self.bank()
            self.mm(bk[:, :n], self.ones_f[:], sq[:, :n], True, True, ["ones_f", sqk], [bkey])
            rs, rsk = R["rstd"].next()
            self.rsqrt_from(rs[:, :n], bk[:, :n], 1.0 / 128, [bkey, "epsb"], [rsk])
            tmp, tk = R["tmpf"].next()
            self.tt(tmp[:, :n], OT[:, c0:c0 + n], rs[:, :n], ALU.mult, otk + [rsk], [tk])
            ub, uk = R["ub"].next()
            self.stt(ub[:, :n], tmp[:, :n], self.pv[:, hg_o:hg_o + 1], SG[:, c0:c0 + n], ALU.mult, ALU.mult, [tk, "pv", "SG"], [uk])
            self.dma(self.hT[(8 + h) * 128:(9 + h) * 128, c0:c0 + n], ub[:, :n], [uk], ck("hT", c0, n))


_WSHAPES = {
    "w_mod": [DEPTH, D, 6 * D], "w_in_ab": [2, D, 5952], "w_kr_sw": [2, D, 64], "w_q_full": [2, 512, 2048],
    "w_kv_up": [2, 256, 2048], "w_out_ab": [2, D, D], "conv_w_pw1": [2, D, 2 * D], "conv_w_pw2": [2, D, D],
    "ffn_w_up": [DEPTH, D, 2 * DFF], "ffn_w_down": [DEPTH, DFF, D],
}


def build_program(stop=None, layers=None):
    nc = bass.Bass("TRN2", target_bir_lowering=False)
    io = {}
    io["xT0"] = nc.dram_tensor("xT0", [D, NCOL], F32, kind="ExternalInput").ap()
    io["pv"] = nc.dram_tensor("pv", [128, PVL.n], F32, kind="ExternalInput").ap()
    io["rope"] = nc.dram_tensor("rope", [64, 2, NCOL], F32, kind="ExternalInput").ap()
    for k, shp in _WSHAPES.items():
        io[k] = nc.dram_tensor(k, shp, F32, kind="ExternalInput").ap()
    outT = nc.dram_tensor("outT", [D, NCOL], F32, kind="ExternalOutput").ap()
    dbg_h = nc.dram_tensor("dbg_h", [D, NCOL], BF16, kind="ExternalOutput").ap() if stop is not None else None
    dbg_u = nc.dram_tensor("dbg_u", [D, NCOL], BF16, kind="ExternalOutput").ap() if stop is not None else None
    st = contextlib.ExitStack()
    with st:
        kb = KB(nc, st)
        kb.setup(io)
        R = kb.R
        R["sq"] = Rot(kb, "sq", 4, [128, 512], F32)
        R["tmpf"] = Rot(kb, "tmpf", 4, [128, 512], F32)
        R["ub"] = Rot(kb, "ub", 4, [128, 512], BF16)
        R["rstd"] = Rot(kb, "rstd", 3, [128, 512], F32)
        kb.arena = kb.sb("arena", [128, ARENA_BYTES // 2], BF16)
        kb.stop_at = stop[1] if stop else None
        kb.mod_ready = set()
        kb.modgen = None
        if kb.stop_at != "setup":
            kb.phase("mod")
            for _ in kb.mod_stream([0]):
                pass
            kb.modgen = kb.mod_stream([1, 2, 3])
        for l in (range(DEPTH) if layers is None else layers):
            if kb.stop_at in ("setup", "mod"):
                break
            kb.ctx_in = l < DEPTH - 1
            kb.ctx_out = l < DEPTH - 2
            if l not in kb.mod_ready:
                kb.phase("mod")
                kb.need_mod(l)
            if l % 2 == 0:
                kb.even(l)
            else:
                kb.conformer(l)
            if stop is not None and stop[0] == l and stop[1] != "ffn":
                break
            kb.ffn(l)
            if stop == (l, "ffn"):
                break
        kb.P.barrier()
        for c in range(16):
            kb.dma(outT[c * 128:(c + 1) * 128, :], kb.xT[c * 128:(c + 1) * 128, :], (), [("outT", c)])
            if dbg_h is not None:
                kb.dma(dbg_h[c * 128:(c + 1) * 128, :], kb.hT[c * 128:(c + 1) * 128, :], (), [("dbgh", c)])
                kb.dma(dbg_u[c * 128:(c + 1) * 128, :], kb.uT[c * 128:(c + 1) * 128, :], (), [("dbgu", c)])
        kb.P.emit(st)
    return nc


ARENA_BYTES = 158 * 1024


class Carver:
    def __init__(self, arena):
        self.arena = arena
        self.off = 0

    def take(self, shape, dtype):
        free = int(np.prod(shape[1:]))
        nbytes = free * (4 if dtype == F32 else 2)
        assert self.off + nbytes <= ARENA_BYTES, (self.off, nbytes)
        ap = self.arena[0:shape[0], self.off // 2:(self.off + nbytes) // 2]
        if dtype == F32:
            ap = ap.bitcast(F32)
        self.off += (nbytes + 63) // 64 * 64
        names = "abcdefg"[:len(shape) - 1]
        if len(shape) > 2:
            pat = "p (%s) -> p %s" % (" ".join(names), " ".join(names))
            ap = ap.rearrange(pat, **{names[i]: shape[1 + i] for i in range(1, len(shape) - 1)})
        return ap

    def rot(self, name, n, shape, dtype):
        return RotAP([(self.take(shape, dtype), "%s%d" % (name, i)) for i in range(n)])


class RotAP:
    def __init__(self, items):
        self.items = items
        self.i = 0

    def next(self):
        it = self.items[self.i % len(self.items)]
        self.i += 1
        return it


def rope_tables_host():
    tab = np.zeros((64, 2, NCOL), np.float32)
    tab[:, 0, :] = 1.0
    t = np.arange(NLAT)
    pos = np.stack([t // 64, t % 64], 0).astype(np.float32)
    freqs = (10000.0 ** (-np.arange(16, dtype=np.float32) / 16)).astype(np.float32)
    for i in range(64):
        a, half, p = i // 32, (i % 32) // 16, i % 16
        ang = (pos[a] * freqs[p]).astype(np.float32)
        tab[i, 0, LAT0:LAT0 + NLAT] = np.cos(ang)
        tab[i, 1, LAT0:LAT0 + NLAT] = np.sin(ang) * (-1.0 if half == 0 else 1.0)
    return tab


SWAP = np.array([(i // 32) * 32 + (1 - (i % 32) // 16) * 16 + i % 16 for i in range(64)])


def host_inputs(inputs, b):
    f = lambda k: np.asarray(inputs[k], np.float32)
    m = {}
    xT0 = np.zeros((D, NCOL), np.float32)
    xT0[:, CTX0:CTX0 + NCTX] = f("ctx")[b].T
    xT0[:, LAT0:LAT0 + NLAT] = f("x")[b].T
    m["xT0"] = xT0
    pv = np.zeros((128, PVL.n), np.float32)

    def put(name, arr):
        o = PVL.off[name]
        pv[:, o:o + arr.shape[1]] = arr

    cc = np.stack([fm(f("c")[b]), fm(f("c_ctx"))], -1).reshape(128, 32)
    put("cc", cc)
    for l in range(DEPTH):
        put("bmod%d" % l, fm(f("b_mod")[l]))
        put("gains%d" % l, np.concatenate([fm(f("norm_gains")[l, i]) for i in range(4)], 1))
        put("fwc%d" % l, np.concatenate([fm(f("ffn_w_conv")[l, k]) for k in range(3)], 1))
        put("fbc%d" % l, fm(f("ffn_b_conv")[l]))
    for j in range(2):
        put("qn%d" % j, fm(f("mla_q_norm")[j]))
        put("kvn%d" % j, fm(f("mla_kv_norm")[j]))
        put("hgn%d" % j, fm(f("hgrn_norm")[j]))
        put("bpw1%d" % j, fm(f("conv_b_pw1")[j]))
        wd = f("conv_w_dw")[j]
        put("wdw%d" % j, np.concatenate([fm(wd[k]) for k in range(31)], 1))
        put("bdw%d" % j, fm(f("conv_b_dw")[j]))
        put("lng%d" % j, fm(f("conv_ln_g")[j]))
        put("lnb%d" % j, fm(f("conv_ln_b")[j]))
        put("bpw2%d" % j, fm(f("conv_b_pw2")[j]))
    hl = f("hgrn_lb")
    put("hlb", np.concatenate([fm(hl[d, jj]) for d in range(2) for jj in range(2)], 1))
    m["pv"] = pv
    return m


def shared_inputs(inputs):
    f = lambda k: np.ascontiguousarray(np.asarray(inputs[k], np.float32))
    m = {k: f(k) for k in ("w_mod", "w_in_ab", "w_kv_up", "w_out_ab", "conv_w_pw1", "conv_w_pw2", "ffn_w_up", "ffn_w_down")}
    m["w_kr_sw"] = np.ascontiguousarray(m["w_in_ab"][:, :, 768:832][:, :, SWAP])
    wq = f("w_q_up").reshape(2, 512, 8, 192)
    m["w_q_full"] = np.ascontiguousarray(np.concatenate([wq, wq[:, :, :, 128:][:, :, :, SWAP]], -1).reshape(2, 512, 2048))
    m["rope"] = rope_tables_host()
    return m


_NC_CACHE = {}


def kernel(**inputs):
    n = 4
    if "full" not in _NC_CACHE:
        _NC_CACHE["full"] = build_program()
    nc = _NC_CACHE["full"]
    sh = shared_inputs(inputs)
    in_maps = []
    for b in range(n):
        m = dict(sh)
        m.update(host_inputs(inputs, b))
        in_maps.append(m)
    res = run_bass_kernel_spmd(nc, in_maps, core_ids=list(range(n)))
    out = np.stack([np.ascontiguousarray(res.results[b]["outT"][:, LAT0:LAT0 + NLAT].T) for b in range(n)], 0)
    return out.astype(np.float32)
```
